# Optimizing a Trainium2 kernel written in Bass

```python
import jax, jax.numpy as jnp
from jax import lax
import numpy as np

D_MODEL = 1024
BATCH = 2
SEQ = 8192
DEPTH = 4
DEC_BATCH = 128
DEC_SEQ = 1
PAST_LEN = 8192
PAGE_SIZE = 128

HEAD_DIM = 64
A_HEADS = (D_MODEL // 2) // HEAD_DIM
A_KV_HEADS = 2
A_GROUP = A_HEADS // A_KV_HEADS
WINDOW = 128
ROPE_THETA = 10000.0
A_Q = A_HEADS * HEAD_DIM
A_KV = A_KV_HEADS * HEAD_DIM
A_PROJ = A_Q + 2 * A_KV
B_HEAD_DIM = 64
B_WIDTH = D_MODEL // 2
B_HEADS = B_WIDTH // B_HEAD_DIM
DECAY_RANK = 64
ICLR_RANK = 64
GATE_RANK = 128
B_PROJ = 3 * B_WIDTH + DECAY_RANK + ICLR_RANK + GATE_RANK
DECAY_SCALE = 0.606531
GN_EPS = 64e-5
EVEN_PROJ = A_PROJ + B_PROJ
EVEN_OUT = A_Q + B_WIDTH
CHUNK = 128
C_WIDTH = D_MODEL
C_HEADS = 8
C_HEAD_DIM = C_WIDTH // C_HEADS
D_FF = 2816
CONV_W = 3
N_EVEN = (DEPTH + 1) // 2
N_ODD = DEPTH // 2
RMS_EPS = 1e-6
LN_EPS = 1e-5
NEG = -1e30

kernel_name = "hybrid_swa_rwkv7_gmlp_convffn_step"


def rmsnorm(x, g):
    xf = x.astype(jnp.float32)
    y = xf * lax.rsqrt(jnp.mean(xf * xf, axis=-1, keepdims=True) + RMS_EPS)
    return (y * g.astype(jnp.float32)).astype(x.dtype)


def layernorm(x, g, b):
    xf = x.astype(jnp.float32)
    mu = jnp.mean(xf, axis=-1, keepdims=True)
    var = jnp.mean(jnp.square(xf - mu), axis=-1, keepdims=True)
    return ((xf - mu) * lax.rsqrt(var + LN_EPS) * g.astype(jnp.float32) + b.astype(jnp.float32)).astype(x.dtype)


def rope(x, pos):
    half = x.shape[-1] // 2
    inv = ROPE_THETA ** (-jnp.arange(half, dtype=jnp.float32) / half)
    ang = pos.astype(jnp.float32)[:, None] * inv[None, :]
    shape = (1, pos.shape[0]) + (1,) * (x.ndim - 3) + (half,)
    cos, sin = jnp.cos(ang).reshape(shape), jnp.sin(ang).reshape(shape)
    xf = x.astype(jnp.float32)
    x1, x2 = xf[..., :half], xf[..., half:]
    return jnp.concatenate([x1 * cos - x2 * sin, x2 * cos + x1 * sin], axis=-1).astype(x.dtype)


def sink_attention(q, k, v, mask, sinks):
    s = jnp.einsum('...qkgd,...skd->...kgqs', q, k).astype(jnp.float32) * (HEAD_DIM ** -0.5)
    s = jnp.where(mask, s, NEG)
    sink = sinks.astype(jnp.float32).reshape(A_KV_HEADS, A_GROUP, 1, 1)
    m = jnp.maximum(jnp.max(s, axis=-1, keepdims=True), sink)
    p = jnp.exp(s - m)
    p = p / (jnp.sum(p, axis=-1, keepdims=True) + jnp.exp(sink - m))
    return jnp.einsum('...kgqs,...skd->...qkgd', p.astype(v.dtype), v)


def banded_window_attention(q, k, v, sinks):
    n, t = q.shape[:2]
    nb = t // WINDOW
    qb = q.reshape(n, nb, WINDOW, A_KV_HEADS, A_GROUP, HEAD_DIM)
    kb = k.reshape(n, nb, WINDOW, A_KV_HEADS, HEAD_DIM)
    vb = v.reshape(n, nb, WINDOW, A_KV_HEADS, HEAD_DIM)
    shift = lambda u: jnp.concatenate([jnp.zeros_like(u[:, :1]), u[:, :-1]], axis=1)
    kk = jnp.concatenate([shift(kb), kb], axis=2)
    vv = jnp.concatenate([shift(vb), vb], axis=2)
    qi = jnp.arange(WINDOW)[:, None]
    kj = jnp.arange(2 * WINDOW)[None, :]
    diff = WINDOW + qi - kj
    blk = jnp.arange(nb)[:, None, None]
    mask = (diff >= 0) & (diff <= WINDOW) & ((blk > 0) | (kj >= WINDOW))
    o = sink_attention(qb, kk, vv, mask[:, None, None], sinks)
    return o.reshape(n, t, A_KV_HEADS, A_GROUP, HEAD_DIM)


def cached_window_attention(q, k, v, k_buf, v_buf, sinks):
    t = q.shape[1]
    wb = k_buf.shape[1]
    kk = jnp.concatenate([k_buf.astype(k.dtype), k], axis=1)
    vv = jnp.concatenate([v_buf.astype(v.dtype), v], axis=1)
    q_pos = PAST_LEN + jnp.arange(t)
    k_pos = jnp.concatenate([PAST_LEN - wb + jnp.arange(wb), q_pos])
    diff = q_pos[:, None] - k_pos[None, :]
    mask = (diff >= 0) & (diff <= WINDOW)
    return sink_attention(q, kk, vv, mask, sinks)


def rwkv7_mix(zb, shift, wkv, mu, w0, w2, a0, a2, g2, k_k, k_a, r_k, gn_g, gn_b):
    f32 = jnp.float32
    n, t, _ = zb.shape
    prev = jnp.concatenate([shift[:, None].astype(zb.dtype), zb[:, :-1]], axis=1)
    zs = zb + (prev - zb) * mu
    o1, o2, o3 = B_WIDTH, 2 * B_WIDTH, 3 * B_WIDTH
    o4, o5 = o3 + DECAY_RANK, o3 + DECAY_RANK + ICLR_RANK
    r, k, v = zs[..., :o1], zs[..., o1:o2], zs[..., o2:o3]
    wd, ad, gd = zs[..., o3:o4], zs[..., o4:o5], zs[..., o5:]
    decay = jnp.exp(-DECAY_SCALE * jax.nn.sigmoid((w0 + jnp.tanh(wd) @ w2).astype(f32)))
    a = jax.nn.sigmoid((a0 + ad @ a2).astype(f32))
    g = jax.nn.sigmoid(gd) @ g2
    heads = lambda u: u.astype(f32).reshape(n, t, B_HEADS, B_HEAD_DIM)
    kk = heads(k * k_k)
    kk = kk / jnp.maximum(jnp.sqrt(jnp.sum(kk * kk, axis=-1, keepdims=True)), 1e-12)
    k = heads(k.astype(f32) * (1.0 + (a - 1.0) * k_a.astype(f32)))
    r, v, decay, a = heads(r), heads(v), heads(decay), heads(a)

    def step(s, inp):
        r_t, w_t, k_t, v_t, kk_t, a_t = inp
        sa = jnp.einsum('bhij,bhj->bhi', s, -kk_t)
        s = (s * w_t[:, :, None, :] + sa[..., None] * (kk_t * a_t)[:, :, None, :]
             + v_t[..., None] * k_t[:, :, None, :])
        return s, jnp.einsum('bhij,bhj->bhi', s, r_t)

    xs = tuple(jnp.moveaxis(u, 1, 0) for u in (r, decay, k, v, kk, a))
    s_last, o = lax.scan(step, wkv.astype(f32), xs)
    o = jnp.moveaxis(o, 0, 1)
    mean = jnp.mean(o, axis=-1, keepdims=True)
    var = jnp.mean(jnp.square(o - mean), axis=-1, keepdims=True)
    o = ((o - mean) * lax.rsqrt(var + GN_EPS)).reshape(n, t, B_WIDTH) * gn_g.astype(f32) + gn_b.astype(f32)
    bonus = jnp.sum(r * k * r_k.astype(f32), axis=-1, keepdims=True) * v
    o = (o + bonus.reshape(n, t, B_WIDTH)) * g.astype(f32)
    return o.astype(zb.dtype), zb[:, -1], s_last.astype(wkv.dtype)


def even_mix(xn, pos, k_buf, v_buf, shift, wkv, w_in, sinks, mu, w0, w2, a0, a2, g2, k_k, k_a, r_k, gn_g, gn_b, w_out):
    n, t, _ = xn.shape
    z = xn @ w_in
    q = z[..., :A_Q].reshape(n, t, A_KV_HEADS, A_GROUP, HEAD_DIM)
    k = z[..., A_Q:A_Q + A_KV].reshape(n, t, A_KV_HEADS, HEAD_DIM)
    v = z[..., A_Q + A_KV:A_PROJ].reshape(n, t, A_KV_HEADS, HEAD_DIM)
    q, k = rope(q, pos), rope(k, pos)
    if k_buf is None:
        o_a = banded_window_attention(q, k, v, sinks)
    else:
        o_a = cached_window_attention(q, k, v, k_buf, v_buf, sinks)
    o_b, new_shift, new_wkv = rwkv7_mix(z[..., A_PROJ:], shift, wkv, mu, w0, w2, a0, a2, g2, k_k, k_a, r_k, gn_g, gn_b)
    o = jnp.concatenate([o_a.reshape(n, t, A_Q), o_b], axis=-1) @ w_out
    return o, k, v, new_shift, new_wkv


def chunk_gmlp_mix(xn, w_in, ln_g, ln_b, w_s, b_s, w_out):
    n, t, _ = xn.shape
    z = jax.nn.gelu(xn @ w_in, approximate=False)
    u, v = z[..., :C_WIDTH], z[..., C_WIDTH:]
    v = layernorm(v, ln_g, ln_b)
    L = min(t, CHUNK)
    vc = v.reshape(n, t // L, L, C_HEADS, C_HEAD_DIM)
    ws = jnp.tril(w_s[:, :L, :L])
    mixed = jnp.einsum('hts,bnshc->bnthc', ws, vc) + b_s[:, :L].T[None, None, :, :, None]
    y = u * mixed.reshape(n, t, C_WIDTH)
    return y @ w_out, v


def conv_ffn(xn, conv_state, w_gate, w_up, conv_w, conv_b, w_down):
    t = xn.shape[1]
    gpre = xn @ w_gate
    ext = jnp.concatenate([conv_state.astype(gpre.dtype), gpre], axis=1)
    conv = conv_b + conv_w[CONV_W - 1] * ext[:, CONV_W - 1:CONV_W - 1 + t]
    for j in range(CONV_W - 1):
        conv = conv + conv_w[j] * ext[:, j:j + t]
    h = jax.nn.gelu(conv, approximate=True) * (xn @ w_up)
    return h @ w_down, ext[:, -(CONV_W - 1):]


def setup_inputs(seed: int = 0) -> dict:
    key = jax.random.key(seed)
    ks = iter(jax.random.split(key, 40))
    nrm = lambda shape, scale=1.0: scale * jax.random.normal(next(ks), shape, jnp.float32)
    unif = lambda shape, lo, hi: jax.random.uniform(next(ks), shape, jnp.float32, lo, hi)
    w_buf = min(WINDOW, PAST_LEN)
    return {
        "x_prompt": nrm((BATCH, SEQ, D_MODEL)),
        "x_sample": nrm((DEC_BATCH, DEC_SEQ, D_MODEL)),
        "cache_win_k": nrm((N_EVEN, DEC_BATCH, w_buf, A_KV_HEADS, HEAD_DIM)),
        "cache_win_v": nrm((N_EVEN, DEC_BATCH, w_buf, A_KV_HEADS, HEAD_DIM)),
        "state_wkv": nrm((N_EVEN, DEC_BATCH, B_HEADS, B_HEAD_DIM, B_HEAD_DIM), 0.5),
        "state_shift": nrm((N_EVEN, DEC_BATCH, B_PROJ)),
        "state_ffn_conv": nrm((DEPTH, DEC_BATCH, CONV_W - 1, D_FF)),
        "norm_mix_pre": 1.0 + nrm((DEPTH, D_MODEL), 0.05),
        "norm_mix_post": 1.0 + nrm((DEPTH, D_MODEL), 0.05),
        "norm_ffn_pre": 1.0 + nrm((DEPTH, D_MODEL), 0.05),
        "norm_ffn_post": 1.0 + nrm((DEPTH, D_MODEL), 0.05),
        "w_in_even": nrm((N_EVEN, D_MODEL, EVEN_PROJ), D_MODEL ** -0.5),
        "attn_sinks": nrm((N_EVEN, A_HEADS), 0.5),
        "shift_mu": unif((N_EVEN, B_PROJ), 0.0, 1.0),
        "decay_w0": unif((N_EVEN, B_WIDTH), -4.0, 1.0),
        "decay_w2": nrm((N_EVEN, DECAY_RANK, B_WIDTH), 0.1),
        "iclr_a0": nrm((N_EVEN, B_WIDTH), 0.1),
        "iclr_a2": nrm((N_EVEN, ICLR_RANK, B_WIDTH), 0.5 * ICLR_RANK ** -0.5),
        "gate_g2": nrm((N_EVEN, GATE_RANK, B_WIDTH), 2.0 * GATE_RANK ** -0.5),
        "key_k": 1.0 + nrm((N_EVEN, B_WIDTH), 0.1),
        "key_a": 1.0 + nrm((N_EVEN, B_WIDTH), 0.1),
        "bonus_r_k": nrm((N_EVEN, B_HEADS, B_HEAD_DIM), 0.1),
        "gn_gain": 1.0 + nrm((N_EVEN, B_WIDTH), 0.05),
        "gn_bias": nrm((N_EVEN, B_WIDTH), 0.02),
        "w_out_even": nrm((N_EVEN, EVEN_OUT, D_MODEL), EVEN_OUT ** -0.5),
        "w_in_odd": nrm((N_ODD, D_MODEL, 2 * C_WIDTH), D_MODEL ** -0.5),
        "sgu_ln_gain": 1.0 + nrm((N_ODD, C_WIDTH), 0.05),
        "sgu_ln_bias": nrm((N_ODD, C_WIDTH), 0.02),
        "sgu_w": nrm((N_ODD, C_HEADS, CHUNK, CHUNK), CHUNK ** -0.5),
        "sgu_b": 1.0 + nrm((N_ODD, C_HEADS, CHUNK), 0.1),
        "w_out_odd": nrm((N_ODD, C_WIDTH, D_MODEL), C_WIDTH ** -0.5),
        "ffn_w_gate": nrm((DEPTH, D_MODEL, D_FF), D_MODEL ** -0.5),
        "ffn_w_up": nrm((DEPTH, D_MODEL, D_FF), D_MODEL ** -0.5),
        "ffn_conv_w": nrm((DEPTH, CONV_W, D_FF), CONV_W ** -0.5),
        "ffn_conv_b": nrm((DEPTH, D_FF), 0.02),
        "ffn_w_down": nrm((DEPTH, D_FF, D_MODEL), D_FF ** -0.5),
    }


def reference(x_prompt, x_sample, cache_win_k, cache_win_v, state_wkv, state_shift, state_ffn_conv,
              norm_mix_pre, norm_mix_post, norm_ffn_pre, norm_ffn_post,
              w_in_even, attn_sinks, shift_mu, decay_w0, decay_w2, iclr_a0, iclr_a2, gate_g2,
              key_k, key_a, bonus_r_k, gn_gain, gn_bias, w_out_even,
              w_in_odd, sgu_ln_gain, sgu_ln_bias, sgu_w, sgu_b, w_out_odd,
              ffn_w_gate, ffn_w_up, ffn_conv_w, ffn_conv_b, ffn_w_down):
    pos_p = jnp.arange(SEQ, dtype=jnp.int32)
    pos_s = PAST_LEN + jnp.arange(DEC_SEQ, dtype=jnp.int32)
    hp, hs = x_prompt, x_sample
    kp_l, vp_l, ks_l, vs_l = [], [], [], []
    sp_l, ss_l, shp_l, shs_l = [], [], [], []
    vsgu_l, cp_l, cs_l = [], [], []
    for layer in range(DEPTH):
        j = layer // 2
        xp = rmsnorm(hp, norm_mix_pre[layer])
        xs = rmsnorm(hs, norm_mix_pre[layer])
        if layer % 2 == 0:
            ep = (w_in_even[j], attn_sinks[j], shift_mu[j], decay_w0[j], decay_w2[j], iclr_a0[j], iclr_a2[j],
                  gate_g2[j], key_k[j], key_a[j], bonus_r_k[j], gn_gain[j], gn_bias[j], w_out_even[j])
            shift0 = jnp.zeros((BATCH, B_PROJ), hp.dtype)
            wkv0 = jnp.zeros((BATCH, B_HEADS, B_HEAD_DIM, B_HEAD_DIM), hp.dtype)
            mp, kp, vp, shp, sp = even_mix(xp, pos_p, None, None, shift0, wkv0, *ep)
            ms, kn, vn, shs, sn = even_mix(xs, pos_s, cache_win_k[j], cache_win_v[j], state_shift[j], state_wkv[j], *ep)
            wp = min(WINDOW, SEQ)
            kp_l.append(kp[:, -wp:]); vp_l.append(vp[:, -wp:])
            ks_l.append(kn); vs_l.append(vn)
            sp_l.append(sp); ss_l.append(sn)
            shp_l.append(shp); shs_l.append(shs)
        else:
            op = (w_in_odd[j], sgu_ln_gain[j], sgu_ln_bias[j], sgu_w[j], sgu_b[j], w_out_odd[j])
            mp, _ = chunk_gmlp_mix(xp, *op)
            ms, vsg = chunk_gmlp_mix(xs, *op)
            vsgu_l.append(vsg)
        hp = hp + rmsnorm(mp, norm_mix_post[layer])
        hs = hs + rmsnorm(ms, norm_mix_post[layer])
        fprm = (ffn_w_gate[layer], ffn_w_up[layer], ffn_conv_w[layer], ffn_conv_b[layer], ffn_w_down[layer])
        fp, cp = conv_ffn(rmsnorm(hp, norm_ffn_pre[layer]), jnp.zeros((BATCH, CONV_W - 1, D_FF), hp.dtype), *fprm)
        fs, cs = conv_ffn(rmsnorm(hs, norm_ffn_pre[layer]), state_ffn_conv[layer], *fprm)
        hp = hp + rmsnorm(fp, norm_ffn_post[layer])
        hs = hs + rmsnorm(fs, norm_ffn_post[layer])
        cp_l.append(cp); cs_l.append(cs)
    return (hp, hs,
            jnp.stack(kp_l), jnp.stack(vp_l), jnp.stack(ks_l), jnp.stack(vs_l),
            jnp.stack(sp_l), jnp.stack(ss_l), jnp.stack(shp_l), jnp.stack(shs_l),
            jnp.stack(vsgu_l), jnp.stack(cp_l), jnp.stack(cs_l))
```

```python
import numpy as np
import concourse.bass as bass
import concourse.mybir as mybir
from concourse.bass_utils import run_bass_kernel_spmd

F32 = mybir.dt.float32
BF16 = mybir.dt.bfloat16
AF = mybir.ActivationFunctionType
ALU = mybir.AluOpType
AX = mybir.AxisListType

NT = 16
NS = 16
D = 1024
DFF = 2816
NFC = 22
EVEN_PROJ = 2560
BPROJ = 1792
RMS_EPS = 1e-6
LN_EPS = 1e-5
GN_EPS = 64e-5
DECAY_SCALE = 0.606531
NEG = -1e30

C_ID = 0
C_TRI = 128
C_TRS = 256
C_LOW = 384
C_ONE = 512
C_MA = 640
C_M0 = 896
C_SELP = 1152
C_SELH = 1156
NCST = 1160
C_COS = 1160
C_SIN = 1672
C_COSS = 2184
C_SINS = 2216
NCONST = 2248


class _Eng:
    def __init__(self, name, handle, is_pe=False):
        self.name = name
        self.h = handle
        self.is_pe = is_pe
        self.sem = None
        self.count = 0
        self.waited = {}
        self.dsems = []
        self.dcount = []
        self.dnext = 0


class KB:
    def __init__(self, nc, n_dma_sems=8, n_cc=12):
        self.nc = nc
        self.E = {
            "pe": _Eng("pe", nc.tensor, True),
            "act": _Eng("act", nc.scalar),
            "dve": _Eng("dve", nc.vector),
            "pool": _Eng("pool", nc.gpsimd),
            "sp": _Eng("sp", nc.sync),
        }
        self.lastw = {}
        self.readers = {}
        self.ctx = []
        self.sem_by_id = {}
        self.out_events = []
        self.n_ops = 0
        self.block = self.enter(nc.Block())
        for name, e in self.E.items():
            e.sem = self.enter(nc.semaphore("sem_" + name))
            self.sem_by_id[id(e.sem)] = e.sem
        for name in ("sp", "pool"):
            e = self.E[name]
            for i in range(n_dma_sems):
                s = self.enter(nc.semaphore("dsem_%s_%d" % (name, i)))
                self.sem_by_id[id(s)] = s
                e.dsems.append(s)
                e.dcount.append(0)
        self.cc_sems = []
        for i in range(n_cc):
            s = self.enter(nc.semaphore("ccsem_%d" % i))
            self.sem_by_id[id(s)] = s
            self.cc_sems.append(s)
        self.cc_used = 0
        self.all_events = {}
        self.exclusive = set()

    def enter(self, cm):
        v = cm.__enter__()
        self.ctx.append(cm)
        return v

    def push_scope(self):
        return len(self.ctx)

    def pop_scope(self, mark):
        while len(self.ctx) > mark:
            self.ctx.pop().__exit__(None, None, None)

    def sbuf(self, name, shape, dtype=F32):
        return self.enter(self.nc.sbuf_tensor(name, list(shape), dtype))

    def psum(self, name, shape, dtype=F32):
        self.exclusive.add(name)
        return self.enter(self.nc.psum_tensor(name, list(shape), dtype))

    def _deps(self, reads, writes, raw_keys=()):
        deps = {}
        raw = {}

        def add(sid, val, is_raw):
            if deps.get(sid, 0) < val:
                deps[sid] = val
            if is_raw and raw.get(sid, 0) < val:
                raw[sid] = val

        for r in reads:
            ev = self.lastw.get(r)
            if ev is not None:
                add(ev[0], ev[1], True)
        for w in writes:
            ev = self.lastw.get(w)
            if ev is not None:
                add(ev[0], ev[1], w in raw_keys)
            for sid, val in self.readers.get(w, {}).items():
                add(sid, val, False)
        self._raw = raw
        return deps

    def _record(self, ev, reads, writes):
        sid, val = ev
        if self.all_events.get(sid, 0) < val:
            self.all_events[sid] = val
        for r in reads:
            d = self.readers.setdefault(r, {})
            if d.get(sid, 0) < val:
                d[sid] = val
        for w in writes:
            self.lastw[w] = ev
            self.readers[w] = {}

    def _emit_waits(self, e, deps, raw=None):
        for sid, val in deps.items():
            if sid == id(e.sem):
                if e.is_pe:
                    continue
                if raw is not None:
                    val = raw.get(sid, 0)
                    if val == 0:
                        continue
            if e.waited.get(sid, 0) >= val:
                continue
            e.waited[sid] = val
            e.h.wait_ge(self.sem_by_id[sid], val)

    def op(self, eng, fn, reads=(), writes=(), inc=True):
        e = self.E[eng]
        ex = [r for r in reads if r in self.exclusive]
        if ex:
            writes = list(writes) + ex
        deps = self._deps(reads, writes, raw_keys=ex)
        self._emit_waits(e, deps, self._raw)
        if inc:
            e.count += 1
            val = e.count
        else:
            val = e.count + 1
        ins = fn(e.h)
        if inc:
            ins.then_inc(e.sem, 1)
        ev = (id(e.sem), val)
        self._record(ev, reads, writes)
        self.n_ops += 1
        return ev

    def dma(self, eng, out, in_, reads=(), writes=(), is_output=False, **kw):
        e = self.E[eng]
        deps = self._deps(reads, writes)
        i = e.dnext
        e.dnext = (e.dnext + 1) % len(e.dsems)
        s = e.dsems[i]
        if e.dcount[i] > 0:
            deps[id(s)] = max(deps.get(id(s), 0), 16 * e.dcount[i])
        self._emit_waits(e, deps)
        e.dcount[i] += 1
        val = 16 * e.dcount[i]
        e.h.dma_start(out=out, in_=in_, **kw).then_inc(s, 16)
        ev = (id(s), val)
        self._record(ev, reads, writes)
        if is_output:
            self.out_events.append(ev)
        self.n_ops += 1
        return ev

    def collective(self, in_ap, out_ap, groups, reads=(), writes=()):
        e = self.E["pool"]
        self._emit_waits(e, self._deps(reads, writes))
        s = self.cc_sems[self.cc_used]
        self.cc_used += 1
        e.h.collective_compute("AllGather", ALU.bypass, replica_groups=groups,
                               ins=[in_ap], outs=[out_ap]).then_inc(s)
        ev = (id(s), 1)
        self._record(ev, reads, writes)
        return ev

    def barrier(self):
        for name, e in self.E.items():
            self._emit_waits(e, dict(self.all_events))

    def finish(self):
        e = self.E["sp"]
        self._emit_waits(e, dict(self.all_events))

    def close(self):
        self.pop_scope(0)


def _bcast_row(ap1d, n=128):
    return ap1d.partition_broadcast(n)


class Prog:
    def __init__(self, plan, dbg=()):
        self.plan = plan
        self.dbg = dbg
        nc = self.nc = bass.Bass("TRN2", target_bir_lowering=False)
        self.kb = KB(nc)
        self.inp = {}
        self.out = {}
        self.uid = 0

    def din(self, name, shape):
        self.inp[name] = self.nc.dram_tensor(name, list(shape), F32, kind="ExternalInput").ap()
        return self.inp[name]

    def dout(self, name, shape):
        self.out[name] = self.nc.dram_tensor(name, list(shape), F32, kind="ExternalOutput").ap()
        return self.out[name]

    def nm(self, s):
        self.uid += 1
        return "%s_%d" % (s, self.uid)

    def declare(self):
        di = self.din
        di("xp", [NT * 128, D]); di("xs", [NS, D])
        di("ck", [2, NS, 128, 128]); di("cv", [2, NS, 128, 128])
        di("swkv", [2, NS, 8, 4096]); di("sshift", [2, NS, BPROJ]); di("sconv", [4, NS, 2, DFF])
        di("norm_mix_pre", [4, D]); di("norm_mix_post", [4, D]); di("norm_ffn_pre", [4, D]); di("norm_ffn_post", [4, D])
        di("w_in_even", [2, D, EVEN_PROJ]); di("attn_sinks", [2, 8]); di("shift_mu", [2, BPROJ])
        di("decay_w0", [2, 512]); di("decay_w2", [2, 64, 512]); di("iclr_a0", [2, 512]); di("iclr_a2", [2, 64, 512])
        di("gate_g2", [2, 128, 512]); di("key_k", [2, 512]); di("key_a", [2, 512]); di("bonus_r_k", [2, 512])
        di("gn_gain", [2, 512]); di("gn_bias", [2, 512]); di("w_out_even", [2, D, D])
        di("w_in_odd", [2, D, 2 * D]); di("sgu_ln_gain", [2, D]); di("sgu_ln_bias", [2, D])
        di("sgu_w", [2, 8, 128, 128]); di("sgu_b", [2, 8, 128]); di("w_out_odd", [2, D, D])
        di("ffn_w_gate", [4, D, DFF]); di("ffn_w_up", [4, D, DFF]); di("ffn_conv_w", [4, 3, DFF])
        di("ffn_conv_b", [4, DFF]); di("ffn_w_down", [4, DFF, D])
        di("consts", [128, NCONST])
        do = self.dout
        do("yp", [NT * 128, D]); do("ys", [NS, D])
        do("wkp", [2, 128, 128]); do("wvp", [2, 128, 128]); do("wks", [2, NS, 128]); do("wvs", [2, NS, 128])
        do("wkvp", [2, 8, 64, 64]); do("wkvs", [2, NS, 8, 4096]); do("shp", [2, BPROJ]); do("shs", [2, NS, BPROJ])
        do("sguv", [2, NS, D]); do("convp", [4, 2, DFF]); do("convs", [4, NS, 2, DFF])
        nc = self.nc
        self.cc_ffn_in = [nc.dram_tensor("ccfi%d" % l, [128, 44], F32) for l in range(4)]
        self.cc_ffn_out = [nc.dram_tensor("ccfo%d" % l, [512, 44], F32) for l in range(4)]
        self.cc_att_in = [nc.dram_tensor("ccai%d" % l, [128, 256], F32) for l in range(2)]
        self.cc_att_out = [nc.dram_tensor("ccao%d" % l, [512, 256], F32) for l in range(2)]
        self.cc_sh_in = [nc.dram_tensor("ccshi%d" % l, [1, BPROJ], F32) for l in range(2)]
        self.cc_sh_out = [nc.dram_tensor("ccsho%d" % l, [4, BPROJ], F32) for l in range(2)]
        self.zl_dram = [nc.dram_tensor("zl%d" % l, [1, BPROJ], F32) for l in range(2)]
        self.rec_op = nc.dram_tensor("rec_op", [NT, 128, 4, 128], BF16)
        if "oa" in self.dbg:
            self.dout("dbg_oa", [NT * 128, 512])
        if "ob" in self.dbg:
            self.dout("dbg_ob", [NT * 128, 512])
        self.cc_st_in = [nc.dram_tensor("ccsi%d" % l, [128, 512], F32) for l in range(2)]
        self.cc_st_out = [nc.dram_tensor("ccso%d" % l, [512, 512], F32) for l in range(2)]
        self.rec_oa = nc.dram_tensor("rec_oa", [NT + 1, 128, 512], F32)
        self.rec_ol = nc.dram_tensor("rec_ol", [NT + 1, 128, 512], F32)
        self.rec_bo = nc.dram_tensor("rec_bo", [NT + 1, 128, 512], F32)
        self.rec_g = nc.dram_tensor("rec_g", [NT + 1, 128, 512], F32)

    def setup(self):
        kb = self.kb
        self.hp = kb.sbuf("hp", [128, NT, D])
        self.hs = kb.sbuf("hs", [128, D])
        self.cst = kb.sbuf("cst", [128, NCST])
        self.identb = kb.sbuf("identb", [128, 128], BF16)
        self.epsr = kb.sbuf("epsr", [128, 4])
        kb.dma("sp", self.cst[:], self.inp["consts"][:, 0:NCST], writes=["cst"])
        for t in range(NT):
            kb.dma("sp", self.hp[:, t, :], self.inp["xp"][t * 128:(t + 1) * 128, :], writes=[("hp", t)])
        kb.dma("sp", self.hs[0:NS, :], self.inp["xs"][:, :], writes=["hs"])
        kb.op("dve", lambda e: e.tensor_copy(out=self.identb[:], in_=self.cst[:, C_ID:C_ID + 128]),
              reads=["cst"], writes=["identb"])
        kb.op("pool", lambda e: e.memset(self.epsr[:, 0:1], RMS_EPS), writes=["epsr"])
        kb.op("pool", lambda e: e.memset(self.epsr[:, 1:2], LN_EPS), writes=["epsr"])
        kb.op("pool", lambda e: e.memset(self.epsr[:, 2:3], GN_EPS), writes=["epsr"])
        kb.op("pool", lambda e: e.memset(self.epsr[:, 3:4], 0.0), writes=["epsr"])

    def ident_f(self):
        return self.cst[:, C_ID:C_ID + 128]

    def prenorm_stats(self, pfx, junk=None):
        kb = self.kb
        ss = kb.sbuf(pfx + "ss", [128, NT + 1])
        rstd = kb.sbuf(pfx + "rstd", [128, NT + 1])
        mj = kb.push_scope()
        junk = kb.sbuf(pfx + "junk", [128, D], BF16)
        kb.op("pool", lambda e: e.memset(ss[:], 0.0), writes=[pfx + "ss"])
        for t in range(NT):
            kb.op("act", lambda e, t=t: e.activation(out=junk[:], in_=self.hp[:, t, :], func=AF.Square,
                                                     accum_out=ss[:, t:t + 1]),
                  reads=[("hp", t)], writes=[pfx + "junk", pfx + "ss"])
        kb.op("act", lambda e: e.activation(out=junk[0:NS, :], in_=self.hs[0:NS, :], func=AF.Square,
                                            accum_out=ss[0:NS, NT:NT + 1]),
              reads=["hs"], writes=[pfx + "junk", pfx + "ss"])
        kb.op("act", lambda e: e.activation(out=rstd[:], in_=ss[:], func=AF.Sqrt, bias=self.epsr[:, 0:1],
                                            scale=1.0 / D), reads=[pfx + "ss", "epsr"], writes=[pfx + "rstd"])
        kb.op("dve", lambda e: e.reciprocal(out=rstd[:], in_=rstd[:]), reads=[pfx + "rstd"], writes=[pfx + "rstd"])
        kb.barrier()
        kb.pop_scope(mj)
        return rstd

    def load_bcast(self, name, src1d, n):
        kb = self.kb
        t = kb.sbuf(name, [128, n])
        kb.dma("sp", t[:], _bcast_row(src1d), writes=[name])
        return t

    def norm_transpose(self, pfx, src_ap, src_key, rstd_col, gain, xn, xn_key, psT, ps_key, dst_ap, dst_key,
                       rows=128, c0=0, c1=None):
        kb = self.kb
        kb.op("dve", lambda e: e.scalar_tensor_tensor(out=xn[0:rows, :], in0=src_ap, scalar=rstd_col,
                                                      in1=gain[0:rows, :], op0=ALU.mult, op1=ALU.mult),
              reads=[src_key, pfx + "rstd", pfx + "gain"], writes=[xn_key])
        pv = psT[:].bitcast(BF16).rearrange("p (k t) -> p k t", k=8)
        for k in range(8):
            kb.op("pe", lambda e, k=k: e.transpose(pv[:, k, 0:rows], xn[0:rows, k * 128:(k + 1) * 128],
                                                   self.identb[0:rows, 0:rows]),
                  reads=[xn_key, "identb"], writes=[ps_key], inc=(k == 7))
        if c1 is None:
            c1 = rows
        kb.op("act", lambda e: e.copy(out=dst_ap, in_=pv[:, :, c0:c1]), reads=[ps_key], writes=[dst_key])

    def post_norm_residual(self, pfx, ps2, ps_key, gain, dst_ap, dst_key, tmp, tmp_key, small, rows=128):
        kb = self.kb
        sk = pfx + "small"
        kb.op("act", lambda e: e.activation(out=tmp[0:rows, :], in_=ps2[0:rows, :], func=AF.Square,
                                            accum_out=small[0:rows, 0:1]),
              reads=[ps_key], writes=[tmp_key, sk])
        kb.op("act", lambda e: e.activation(out=small[0:rows, 1:2], in_=small[0:rows, 0:1], func=AF.Sqrt,
                                            bias=self.epsr[0:rows, 0:1], scale=1.0 / D),
              reads=[sk, "epsr"], writes=[sk])
        kb.op("dve", lambda e: e.reciprocal(out=small[0:rows, 2:3], in_=small[0:rows, 1:2]), reads=[sk], writes=[sk])
        kb.op("dve", lambda e: e.scalar_tensor_tensor(out=tmp[0:rows, :], in0=ps2[0:rows, :],
                                                      scalar=small[0:rows, 2:3], in1=gain[0:rows, :],
                                                      op0=ALU.mult, op1=ALU.mult),
              reads=[ps_key, sk, pfx + "gpost"], writes=[tmp_key])
        kb.op("pool", lambda e: e.tensor_tensor(out=dst_ap, in0=dst_ap, in1=tmp[0:rows, :], op=ALU.add),
              reads=[tmp_key, dst_key], writes=[dst_key])

    def load_w_rows(self, name, src2d, kchunks, ncols, c0=0):
        kb = self.kb
        w = kb.sbuf(name, [128, kchunks, ncols], BF16)
        for k in range(kchunks):
            kb.dma("pool", w[:, k, :], src2d[k * 128:(k + 1) * 128, c0:c0 + ncols], writes=[(name, k)])
        return w

    def ffn(self, L):
        kb = self.kb
        P = "f%d" % L
        I = self.inp
        mark = kb.push_scope()
        gain = self.load_bcast(P + "gain", I["norm_ffn_pre"][L, :], D)
        gpost = self.load_bcast(P + "gpost", I["norm_ffn_post"][L, :], D)
        rstd = self.prenorm_stats(P)
        cwb = kb.sbuf(P + "cwb", [128, NFC, 4])
        stf = kb.sbuf(P + "stf", [128, NFC, 2 * NS])
        gl = kb.sbuf(P + "gl", [128, 2, NFC])
        gsall = kb.sbuf(P + "gsall", [128, NFC, NS])
        halo = kb.sbuf(P + "halo", [128, 2, NFC])
        m2 = kb.push_scope()
        cwt = kb.sbuf(P + "cwt", [4, DFF])
        kb.dma("sp", cwt[0:3, :], I["ffn_conv_w"][L, :, :], writes=[P + "cwt"])
        kb.dma("sp", cwt[3:4, :], I["ffn_conv_b"][L:L + 1, :], writes=[P + "cwt"])
        stt = kb.sbuf(P + "stt", [2 * NS, DFF])
        kb.dma("sp", stt[:], I["sconv"][L].rearrange("s j f -> (s j) f"), writes=[P + "stt"])
        psA = kb.psum(P + "psA", [128, 512])
        psB = kb.psum(P + "psB", [128, 1024])
        for fc in range(NFC):
            kb.op("pe", lambda e, fc=fc: e.transpose(psA[:, fc * 4:fc * 4 + 4], cwt[0:4, fc * 128:(fc + 1) * 128],
                                                     self.ident_f()[0:4, 0:4]),
                  reads=[P + "cwt", "cst"], writes=[P + "psA"], inc=(fc == NFC - 1))
            kb.op("pe", lambda e, fc=fc: e.transpose(psB[:, fc * 32:fc * 32 + 32],
                                                     stt[0:32, fc * 128:(fc + 1) * 128],
                                                     self.ident_f()[0:32, 0:32]),
                  reads=[P + "stt", "cst"], writes=[P + "psB"], inc=(fc == NFC - 1))
        kb.op("dve", lambda e: e.tensor_copy(out=cwb[:].rearrange("p a b -> p (a b)"), in_=psA[:, 0:NFC * 4]),
              reads=[P + "psA"], writes=[P + "cwb"])
        kb.op("dve", lambda e: e.tensor_copy(out=stf[:].rearrange("p a b -> p (a b)"), in_=psB[:, 0:NFC * 32]),
              reads=[P + "psB"], writes=[P + "stf"])
        kb.barrier()
        kb.pop_scope(m2)
        kb.dma("sp", self.out["convs"][L, :, 0, :], I["sconv"][L, :, 1, :], is_output=True)

        for sb in (1, 0):
            self.ffn_sb(L, sb, P, gain, gpost, rstd, cwb, stf, gl, gsall, halo)
            if sb == 1:
                I_ = self.cc_ffn_in[L]
                O_ = self.cc_ffn_out[L]
                kb.dma("sp", I_.ap()[:, :], gl[:].rearrange("p a b -> p (a b)"), reads=[P + "gl"],
                       writes=[P + "ccin"])
                kb.collective(I_.ap(), O_.ap(), [[0, 1, 2, 3], [4, 5, 6, 7]], reads=[P + "ccin"],
                              writes=[P + "ccout"])
                g4 = kb.sbuf(P + "g4", [128, 4, 44])
                kb.dma("sp", g4[:], O_.ap().rearrange("(r p) c -> p r c", p=128), reads=[P + "ccout"],
                       writes=[P + "g4"])
                hv = halo[:].rearrange("p a b -> p (a b)")
                kb.op("dve", lambda e: e.tensor_scalar(out=hv, in0=g4[:, 0, :], scalar1=self.cst[:, C_SELP:C_SELP + 1],
                                                       scalar2=None, op0=ALU.mult),
                      reads=[P + "g4", "cst"], writes=[P + "halo"])
                for r in range(1, 4):
                    kb.op("dve", lambda e, r=r: e.scalar_tensor_tensor(
                        out=hv, in0=g4[:, r, :], scalar=self.cst[:, C_SELP + r:C_SELP + r + 1], in1=hv,
                        op0=ALU.mult, op1=ALU.add), reads=[P + "g4", "cst", P + "halo"], writes=[P + "halo"])
                m3 = kb.push_scope()
                psC = kb.psum(P + "psC", [128, 512])
                otr = kb.sbuf(P + "otr", [44, 128])
                kb.op("pe", lambda e: e.transpose(psC[0:44, 0:128], gl[:].rearrange("p a b -> p (a b)"),
                                                  self.ident_f()), reads=[P + "gl", "cst"], writes=[P + "psC"])
                kb.op("dve", lambda e: e.tensor_copy(out=otr[:], in_=psC[0:44, 0:128]), reads=[P + "psC"],
                      writes=[P + "otr"])
                for j in range(2):
                    kb.dma("sp", self.out["convp"][L, j, :].rearrange("(c p) -> c p", p=128),
                           otr[j * NFC:(j + 1) * NFC, :], reads=[P + "otr"], is_output=True)
                psD = kb.psum(P + "psD", [128, 3, 1024])
                osr = kb.sbuf(P + "osr", [NS, DFF])
                pv = psD[:].rearrange("p a b -> p (a b)")
                for fc in range(NFC):
                    kb.op("pe", lambda e, fc=fc: e.transpose(pv[0:NS, fc * 128:(fc + 1) * 128], gsall[:, fc, :],
                                                             self.ident_f()),
                          reads=[P + "gsall", "cst"], writes=[P + "psD"], inc=(fc == NFC - 1))
                kb.op("dve", lambda e: e.tensor_copy(out=osr[:], in_=pv[0:NS, 0:DFF]), reads=[P + "psD"],
                      writes=[P + "osr"])
                kb.dma("sp", self.out["convs"][L, :, 1, :], osr[:], reads=[P + "osr"], is_output=True)
                kb.barrier()
                kb.pop_scope(m3)
        kb.barrier()
        kb.pop_scope(mark)

    def ffn_sb(self, L, sb, P0, gain, gpost, rstd, cwb, stf, gl, gsall, halo):
        kb = self.kb
        I = self.inp
        P = "%ss%d" % (P0, sb)
        t0 = sb * 8
        npr = 1024
        ncol = npr + (NS if sb == 1 else 0)
        mark = kb.push_scope()
        hff = kb.sbuf(P + "hff", [128, NFC, ncol], BF16)
        m1 = kb.push_scope()
        xnT = kb.sbuf(P + "xnT", [128, 8, ncol + 2], BF16)
        xn = [kb.sbuf(P + "xn%d" % i, [128, D], BF16) for i in range(2)]
        psT = [kb.psum(P + "psT%d" % i, [128, 512]) for i in range(2)]
        pg = [kb.psum(P + "pg%d" % i, [128, 512]) for i in range(3)]
        pu = [kb.psum(P + "pu%d" % i, [128, 512]) for i in range(3)]
        tiles = list(range(t0, t0 + 8))
        for i, t in enumerate(tiles):
            b = i % 2
            self.norm_transpose(P0, self.hp[:, t, :], ("hp", t), rstd[:, t:t + 1], gain, xn[b], P + "xn%d" % b,
                                psT[b], P + "psT%d" % b, xnT[:, :, i * 128:(i + 1) * 128], (P + "xnT", i))
        if sb == 1:
            self.norm_transpose(P0, self.hs[0:NS, :], "hs", rstd[0:NS, NT:NT + 1], gain, xn[0], P + "xn0",
                                psT[0], P + "psT0", xnT[:, :, npr:npr + NS], (P + "xnT", 8), rows=NS)
            self.norm_transpose(P0, self.hp[:, 7, :], ("hp", 7), rstd[:, 7:8], gain, xn[1], P + "xn1",
                                psT[1], P + "psT1", xnT[:, :, ncol:ncol + 2], (P + "xnT", 9), c0=126, c1=128)
        xkeys = [(P + "xnT", i) for i in range(10 if sb == 1 else 8)]
        NB3 = 3
        G = [kb.sbuf(P + "G%d" % i, [128, npr + 2]) for i in range(NB3)]
        U = [kb.sbuf(P + "U%d" % i, [128, ncol]) for i in range(NB3)]
        C = [kb.sbuf(P + "C%d" % i, [128, npr]) for i in range(2)]
        TM = [kb.sbuf(P + "TM%d" % i, [128, npr]) for i in range(1)]
        GS = [kb.sbuf(P + "GS%d" % i, [128, 2 * NS]) for i in range(NB3)]
        NWB = 2
        wg = [kb.sbuf(P + "wg%d" % i, [128, 8, 256], BF16) for i in range(NWB)]
        wu = [kb.sbuf(P + "wu%d" % i, [128, 8, 256], BF16) for i in range(NWB)]
        slot = 0

        def issue_w(fg):
            wb = fg % NWB
            kb.dma("pool", wg[wb][:], I["ffn_w_gate"][L, :, fg * 256:(fg + 1) * 256].rearrange("(k p) n -> p k n", p=128),
                   writes=[P + "wg%d" % wb])
            kb.dma("pool", wu[wb][:], I["ffn_w_up"][L, :, fg * 256:(fg + 1) * 256].rearrange("(k p) n -> p k n", p=128),
                   writes=[P + "wu%d" % wb])
        issue_w(0)
        slot_of = {}

        def front(fc):
            nonlocal slot
            fg, fi = fc // 2, fc % 2
            wb = fg % NWB
            if fi == 0 and fg + 1 < NFC // 2:
                issue_w(fg + 1)
            gb = fc % NB3
            Gk, Uk, Ck, GSk = P + "G%d" % gb, P + "U%d" % gb, P + "C%d" % gb, P + "GS%d" % gb
            for tg in range(2):
                s3 = slot % 3
                slot += 1
                for k in range(8):
                    kb.op("pe", lambda e, k=k, tg=tg, s3=s3, wb=wb, fi=fi: e.matmul(
                        pg[s3][:, :], wg[wb][:, k, fi * 128:(fi + 1) * 128], xnT[:, k, tg * 512:(tg + 1) * 512],
                        start=(k == 0), stop=(k == 7)),
                        reads=[P + "wg%d" % wb] + xkeys, writes=[P + "pg%d" % s3], inc=(k == 7))
                for k in range(8):
                    kb.op("pe", lambda e, k=k, tg=tg, s3=s3, wb=wb, fi=fi: e.matmul(
                        pu[s3][:, :], wu[wb][:, k, fi * 128:(fi + 1) * 128], xnT[:, k, tg * 512:(tg + 1) * 512],
                        start=(k == 0), stop=(k == 7)),
                        reads=[P + "wu%d" % wb] + xkeys, writes=[P + "pu%d" % s3], inc=(k == 7))
                kb.op("act", lambda e, tg=tg, s3=s3, gb=gb: e.copy(out=G[gb][:, 2 + tg * 512:2 + (tg + 1) * 512],
                                                                  in_=pg[s3][:, :]),
                      reads=[P + "pg%d" % s3], writes=[Gk])
                kb.op("act", lambda e, tg=tg, s3=s3, gb=gb: e.copy(out=U[gb][:, tg * 512:(tg + 1) * 512],
                                                                  in_=pu[s3][:, :]),
                      reads=[P + "pu%d" % s3], writes=[Uk])
            if sb == 1:
                s3 = slot % 3
                slot += 1
                for k in range(8):
                    kb.op("pe", lambda e, k=k, s3=s3, wb=wb, fi=fi: e.matmul(
                        pg[s3][:, 0:NS + 2], wg[wb][:, k, fi * 128:(fi + 1) * 128], xnT[:, k, npr:npr + NS + 2],
                        start=(k == 0), stop=(k == 7)),
                        reads=[P + "wg%d" % wb] + xkeys, writes=[P + "pg%d" % s3], inc=(k == 7))
                for k in range(8):
                    kb.op("pe", lambda e, k=k, s3=s3, wb=wb, fi=fi: e.matmul(
                        pu[s3][:, 0:NS], wu[wb][:, k, fi * 128:(fi + 1) * 128], xnT[:, k, npr:npr + NS],
                        start=(k == 0), stop=(k == 7)),
                        reads=[P + "wu%d" % wb] + xkeys, writes=[P + "pu%d" % s3], inc=(k == 7))
                kb.op("act", lambda e, s3=s3, gb=gb: e.copy(out=GS[gb][:, 0:NS], in_=pg[s3][:, 0:NS]),
                      reads=[P + "pg%d" % s3], writes=[GSk])
                kb.op("act", lambda e, s3=s3, gb=gb: e.copy(out=G[gb][:, 0:2], in_=pg[s3][:, NS:NS + 2]),
                      reads=[P + "pg%d" % s3], writes=[Gk])
                kb.op("act", lambda e, s3=s3, gb=gb: e.copy(out=U[gb][:, npr:npr + NS], in_=pu[s3][:, 0:NS]),
                      reads=[P + "pu%d" % s3], writes=[Uk])
            else:
                kb.op("pool", lambda e, gb=gb, fc=fc: e.tensor_copy(out=G[gb][:, 0:2], in_=halo[:, :, fc]),
                      reads=[P0 + "halo"], writes=[Gk])

        def tail(fc):
            gb = fc % NB3
            Gk, Uk, Ck, GSk = P + "G%d" % gb, P + "U%d" % gb, P + "C%d" % gb, P + "GS%d" % gb
            w0, w1, w2, bb = (cwb[:, fc, i:i + 1] for i in range(4))
            cb = fc % 2
            Ck = P + "C%d" % cb
            kb.op("pool", lambda e, gb=gb, w1=w1: e.tensor_tensor(
                out=TM[0][:, :], in0=G[gb][:, 1:1 + npr], in1=w1.to_broadcast([128, npr]), op=ALU.mult),
                reads=[Gk, P0 + "cwb"], writes=[P + "TM0"])
            kb.op("dve", lambda e, gb=gb, cb=cb, w2=w2, bb=bb: e.tensor_scalar(
                out=C[cb][:, :], in0=G[gb][:, 2:2 + npr], scalar1=w2, scalar2=bb, op0=ALU.mult, op1=ALU.add),
                reads=[Gk, P0 + "cwb"], writes=[Ck])
            kb.op("dve", lambda e, gb=gb, cb=cb, w0=w0: e.scalar_tensor_tensor(
                out=C[cb][:, :], in0=G[gb][:, 0:npr], scalar=w0, in1=C[cb][:, :], op0=ALU.mult, op1=ALU.add),
                reads=[Gk, P0 + "cwb", Ck], writes=[Ck])
            kb.op("dve", lambda e, cb=cb: e.tensor_tensor(
                out=C[cb][:, :], in0=C[cb][:, :], in1=TM[0][:, :], op=ALU.add),
                reads=[P + "TM0", Ck], writes=[Ck])
            kb.op("act", lambda e, cb=cb: e.activation(out=C[cb][:, :], in_=C[cb][:, :], func=AF.Gelu_apprx_tanh),
                  reads=[Ck], writes=[Ck])
            kb.op("dve", lambda e, gb=gb, cb=cb, fc=fc: e.tensor_tensor(out=hff[:, fc, 0:npr], in0=C[cb][:, :],
                                                                in1=U[gb][:, 0:npr], op=ALU.mult),
                  reads=[Ck, Uk], writes=[(P + "hff", fc)])
            if sb == 1:
                kb.op("pool", lambda e, gb=gb, fc=fc: e.tensor_copy(out=gl[:, :, fc], in_=G[gb][:, npr:npr + 2]),
                      reads=[Gk], writes=[P0 + "gl"])
                kb.op("pool", lambda e, gb=gb, fc=fc: e.tensor_copy(out=gsall[:, fc, :], in_=GS[gb][:, 0:NS]),
                      reads=[GSk], writes=[P0 + "gsall"])
                stv = stf[:, fc, :].rearrange("p (s j) -> p s j", j=2)
                cs = GS[gb][:, NS:2 * NS]
                kb.op("dve", lambda e, gb=gb, w2=w2, bb=bb, cs=cs: e.tensor_scalar(
                    out=cs, in0=GS[gb][:, 0:NS], scalar1=w2, scalar2=bb, op0=ALU.mult, op1=ALU.add),
                    reads=[GSk, P0 + "cwb"], writes=[GSk])
                kb.op("dve", lambda e, w1=w1, cs=cs, stv=stv: e.scalar_tensor_tensor(
                    out=cs, in0=stv[:, :, 1], scalar=w1, in1=cs, op0=ALU.mult, op1=ALU.add),
                    reads=[GSk, P0 + "cwb", P0 + "stf"], writes=[GSk])
                kb.op("dve", lambda e, w0=w0, cs=cs, stv=stv: e.scalar_tensor_tensor(
                    out=cs, in0=stv[:, :, 0], scalar=w0, in1=cs, op0=ALU.mult, op1=ALU.add),
                    reads=[GSk, P0 + "cwb", P0 + "stf"], writes=[GSk])
                kb.op("act", lambda e, cs=cs: e.activation(out=cs, in_=cs, func=AF.Gelu_apprx_tanh),
                      reads=[GSk], writes=[GSk])
                kb.op("dve", lambda e, gb=gb, fc=fc, cs=cs: e.tensor_tensor(
                    out=hff[:, fc, npr:npr + NS], in0=cs, in1=U[gb][:, npr:npr + NS], op=ALU.mult),
                    reads=[GSk, Uk], writes=[(P + "hff", fc)])

        for fc in range(NFC):
            front(fc)
            if fc >= 1:
                tail(fc - 1)
        tail(NFC - 1)
        kb.barrier()
        kb.pop_scope(m1)
        m2 = kb.push_scope()
        wd = kb.sbuf(P + "wd", [128, NFC, D], BF16)
        for f2 in range(NFC // 2):
            kb.dma("pool", wd[:, 2 * f2:2 * f2 + 2, :],
                   I["ffn_w_down"][L, f2 * 256:(f2 + 1) * 256, :].rearrange("(f p) n -> p f n", p=128),
                   writes=[(P + "wd", f2)])
        wkeys = [(P + "wd", f2) for f2 in range(NFC // 2)]
        hkeys = [(P + "hff", fc) for fc in range(NFC)]
        py = [kb.psum(P + "py%d" % i, [128, 1024]) for i in range(2)]
        tmp = [kb.sbuf(P + "tmp%d" % i, [128, D]) for i in range(2)]
        small = kb.sbuf(P + "small", [128, 4])
        units = [(t, 128) for t in tiles] + ([("s", NS)] if sb == 1 else [])
        for i, (t, rows) in enumerate(units):
            b = i % 2
            off = (i * 128) if t != "s" else npr
            for half in range(2):
                for f in range(NFC):
                    kb.op("pe", lambda e, f=f, half=half, b=b, off=off, rows=rows: e.matmul(
                        py[b][0:rows, half * 512:(half + 1) * 512], hff[:, f, off:off + rows],
                        wd[:, f, half * 512:(half + 1) * 512], start=(f == 0), stop=(f == NFC - 1)),
                        reads=wkeys + hkeys, writes=[P + "py%d" % b], inc=(f == NFC - 1))
            if t == "s":
                dst, dk = self.hs[0:NS, :], "hs"
            else:
                dst, dk = self.hp[:, t, :], ("hp", t)
            self.post_norm_residual(P0, py[b], P + "py%d" % b, gpost, dst, dk, tmp[b], P + "tmp%d" % b, small, rows)
        kb.barrier()
        kb.pop_scope(m2)
        kb.pop_scope(mark)

    def odd(self, L):
        kb = self.kb
        I = self.inp
        P = "o%d" % L
        j = L // 2
        mark = kb.push_scope()
        gain = self.load_bcast(P + "gain", I["norm_mix_pre"][L, :], D)
        gpost = self.load_bcast(P + "gpost", I["norm_mix_post"][L, :], D)
        lng = self.load_bcast(P + "lng", I["sgu_ln_gain"][j, :], D)
        lnb = self.load_bcast(P + "lnb", I["sgu_ln_bias"][j, :], D)
        rstd = self.prenorm_stats(P)
        win = self.load_w_rows(P + "win", I["w_in_odd"][j], 8, 2 * D)
        wout = self.load_w_rows(P + "wout", I["w_out_odd"][j], 8, D)
        winkeys = [(P + "win", k) for k in range(8)]
        woutkeys = [(P + "wout", k) for k in range(8)]
        wsT = kb.sbuf(P + "wsT", [128, 8, 128], BF16)
        bsb = kb.sbuf(P + "bsb", [128, 8])
        w00 = kb.sbuf(P + "w00", [NS, 8])
        b0 = kb.sbuf(P + "b0", [NS, 8])
        kb.dma("sp", w00[:], I["sgu_w"][j, :, 0, 0].partition_broadcast(NS), writes=[P + "w00"],
               allow_slow_non_contiguous=True)
        kb.dma("sp", b0[:], I["sgu_b"][j, :, 0].partition_broadcast(NS), writes=[P + "b0"],
               allow_slow_non_contiguous=True)
        m0 = kb.push_scope()
        wsn = kb.sbuf(P + "wsn", [128, 8, 128])
        kb.dma("sp", wsn[:], I["sgu_w"][j].rearrange("h t s -> t h s"), writes=[P + "wsn"])
        bsn = kb.sbuf(P + "bsn", [8, 128])
        kb.dma("sp", bsn[:], I["sgu_b"][j, :, :], writes=[P + "bsn"])
        p0 = kb.psum(P + "p0", [128, 1024])
        p1 = kb.psum(P + "p1", [128, 512])
        for h in range(8):
            kb.op("pe", lambda e, h=h: e.transpose(p0[:, h * 128:(h + 1) * 128], wsn[:, h, :], self.ident_f()),
                  reads=[P + "wsn", "cst"], writes=[P + "p0"], inc=(h == 7))
        kb.op("pe", lambda e: e.transpose(p1[:, 0:8], bsn[0:8, :], self.ident_f()[0:8, 0:8]),
              reads=[P + "bsn", "cst"], writes=[P + "p1"])
        kb.op("dve", lambda e: e.tensor_tensor(
            out=wsT[:], in0=p0[:].rearrange("p (h t) -> p h t", h=8),
            in1=self.cst[:, C_TRI:C_TRI + 128].unsqueeze(1).to_broadcast([128, 8, 128]), op=ALU.mult),
            reads=[P + "p0", "cst"], writes=[P + "wsT"])
        kb.op("dve", lambda e: e.tensor_copy(out=bsb[:], in_=p1[:, 0:8]), reads=[P + "p1"], writes=[P + "bsb"])
        kb.barrier()
        kb.pop_scope(m0)

        xn = kb.sbuf(P + "xn", [128, D], BF16)
        xnT = [kb.sbuf(P + "xnT%d" % i, [128, 8, 128], BF16) for i in range(2)]
        u = kb.sbuf(P + "u", [128, D])
        v = kb.sbuf(P + "v", [128, D])
        vn = kb.sbuf(P + "vn", [128, D])
        vnb = kb.sbuf(P + "vnb", [128, D], BF16)
        y = kb.sbuf(P + "y", [128, D])
        yb = kb.sbuf(P + "yb", [128, D], BF16)
        yT = kb.sbuf(P + "yT", [128, 8, 128], BF16)
        tmp = kb.sbuf(P + "tmp", [128, D])
        sm = kb.sbuf(P + "sm", [128, 8])
        small = kb.sbuf(P + "small", [128, 4])
        pT = kb.psum(P + "pT", [128, 512])
        pz = [kb.psum(P + "pz%d" % i, [128, 512]) for i in range(2)]
        pm = kb.psum(P + "pm", [128, 1024])
        py = kb.psum(P + "py", [128, 1024])
        pyT = kb.psum(P + "pyT", [128, 512])
        units = [(t, 128) for t in range(NT)] + [("s", NS)]
        for i, (t, rows) in enumerate(units):
            b = i % 2
            if t == "s":
                src, sk, rc = self.hs[0:NS, :], "hs", rstd[0:NS, NT:NT + 1]
            else:
                src, sk, rc = self.hp[:, t, :], ("hp", t), rstd[:, t:t + 1]
            self.norm_transpose(P, src, sk, rc, gain, xn, P + "xn", pT, P + "pT", xnT[b][:, :, 0:rows],
                                P + "xnT%d" % b, rows=rows)
            for g in range(4):
                pb = g % 2
                for k in range(8):
                    kb.op("pe", lambda e, k=k, g=g, pb=pb, b=b, rows=rows: e.matmul(
                        pz[pb][0:rows, :], xnT[b][:, k, 0:rows], win[:, k, g * 512:(g + 1) * 512],
                        start=(k == 0), stop=(k == 7)),
                        reads=[P + "xnT%d" % b] + winkeys, writes=[P + "pz%d" % pb], inc=(k == 7))
                if g < 2:
                    kb.op("act", lambda e, g=g, pb=pb, rows=rows: e.activation(
                        out=u[0:rows, g * 512:(g + 1) * 512], in_=pz[pb][0:rows, :], func=AF.Gelu),
                        reads=[P + "pz%d" % pb], writes=[P + "u"])
                else:
                    kb.op("act", lambda e, g=g, pb=pb, rows=rows: e.activation(
                        out=v[0:rows, (g - 2) * 512:(g - 1) * 512], in_=pz[pb][0:rows, :], func=AF.Gelu,
                        accum_out=sm[0:rows, g - 2:g - 1]),
                        reads=[P + "pz%d" % pb], writes=[P + "v", P + "sm"])
            kb.op("dve", lambda e, rows=rows: e.tensor_scalar(
                out=sm[0:rows, 2:3], in0=sm[0:rows, 0:1], scalar1=sm[0:rows, 1:2], scalar2=-1.0 / D,
                op0=ALU.add, op1=ALU.mult), reads=[P + "sm"], writes=[P + "sm"])
            kb.op("act", lambda e, rows=rows: e.activation(
                out=tmp[0:rows, :], in_=v[0:rows, :], func=AF.Square, bias=sm[0:rows, 2:3],
                accum_out=sm[0:rows, 3:4]), reads=[P + "v", P + "sm"], writes=[P + "tmp", P + "sm"])
            kb.op("act", lambda e, rows=rows: e.activation(
                out=sm[0:rows, 4:5], in_=sm[0:rows, 3:4], func=AF.Sqrt, bias=self.epsr[0:rows, 1:2],
                scale=1.0 / D), reads=[P + "sm", "epsr"], writes=[P + "sm"])
            kb.op("dve", lambda e, rows=rows: e.reciprocal(out=sm[0:rows, 5:6], in_=sm[0:rows, 4:5]),
                  reads=[P + "sm"], writes=[P + "sm"])
            kb.op("dve", lambda e, rows=rows: e.tensor_scalar(
                out=vn[0:rows, :], in0=v[0:rows, :], scalar1=sm[0:rows, 2:3], scalar2=sm[0:rows, 5:6],
                op0=ALU.add, op1=ALU.mult), reads=[P + "v", P + "sm"], writes=[P + "vn"])
            kb.op("dve", lambda e, rows=rows: e.tensor_tensor(out=vn[0:rows, :], in0=vn[0:rows, :],
                                                              in1=lng[0:rows, :], op=ALU.mult),
                  reads=[P + "vn", P + "lng"], writes=[P + "vn"])
            kb.op("pool", lambda e, rows=rows: e.tensor_tensor(out=vn[0:rows, :], in0=vn[0:rows, :],
                                                               in1=lnb[0:rows, :], op=ALU.add),
                  reads=[P + "vn", P + "lnb"], writes=[P + "vn"])
            if t == "s":
                kb.dma("sp", self.out["sguv"][j, :, :], vn[0:NS, :], reads=[P + "vn"], is_output=True)
                y3 = y[0:NS, :].rearrange("p (h c) -> p h c", h=8)
                kb.op("dve", lambda e: e.tensor_tensor(
                    out=y3, in0=vn[0:NS, :].rearrange("p (h c) -> p h c", h=8),
                    in1=w00[:].unsqueeze(2).to_broadcast([NS, 8, 128]), op=ALU.mult),
                    reads=[P + "vn", P + "w00"], writes=[P + "y"])
                kb.op("dve", lambda e: e.tensor_tensor(
                    out=y3, in0=y3, in1=b0[:].unsqueeze(2).to_broadcast([NS, 8, 128]), op=ALU.add),
                    reads=[P + "y", P + "b0"], writes=[P + "y"])
            else:
                kb.op("act", lambda e: e.copy(out=vnb[:], in_=vn[:]), reads=[P + "vn"], writes=[P + "vnb"])
                for h in range(8):
                    kb.op("pe", lambda e, h=h: e.matmul(pm[:, h * 128:(h + 1) * 128], wsT[:, h, :],
                                                        vnb[:, h * 128:(h + 1) * 128], start=True, stop=True),
                          reads=[P + "wsT", P + "vnb"], writes=[P + "pm"], inc=(h == 7))
                kb.op("dve", lambda e: e.tensor_tensor(
                    out=y[:].rearrange("p (h c) -> p h c", h=8), in0=pm[:].rearrange("p (h c) -> p h c", h=8),
                    in1=bsb[:].unsqueeze(2).to_broadcast([128, 8, 128]), op=ALU.add),
                    reads=[P + "pm", P + "bsb"], writes=[P + "y"])
            kb.op("dve", lambda e, rows=rows: e.tensor_tensor(out=yb[0:rows, :], in0=y[0:rows, :],
                                                              in1=u[0:rows, :], op=ALU.mult),
                  reads=[P + "y", P + "u"], writes=[P + "yb"])
            pv = pyT[:].bitcast(BF16).rearrange("p (k t) -> p k t", k=8)
            for k in range(8):
                kb.op("pe", lambda e, k=k, rows=rows: e.transpose(pv[:, k, 0:rows], yb[0:rows, k * 128:(k + 1) * 128],
                                                                  self.identb[0:rows, 0:rows]),
                      reads=[P + "yb", "identb"], writes=[P + "pyT"], inc=(k == 7))
            kb.op("act", lambda e, rows=rows: e.copy(out=yT[:, :, 0:rows], in_=pv[:, :, 0:rows]),
                  reads=[P + "pyT"], writes=[P + "yT"])
            for half in range(2):
                for k in range(8):
                    kb.op("pe", lambda e, k=k, half=half, rows=rows: e.matmul(
                        py[0:rows, half * 512:(half + 1) * 512], yT[:, k, 0:rows],
                        wout[:, k, half * 512:(half + 1) * 512], start=(k == 0), stop=(k == 7)),
                        reads=[P + "yT"] + woutkeys, writes=[P + "py"], inc=(k == 7))
            if t == "s":
                dst, dk = self.hs[0:NS, :], "hs"
            else:
                dst, dk = self.hp[:, t, :], ("hp", t)
            self.post_norm_residual(P, py, P + "py", gpost, dst, dk, tmp, P + "tmp", small, rows)
        kb.barrier()
        kb.pop_scope(mark)

    def even(self, L):
        kb = self.kb
        I = self.inp
        P = "e%d" % L
        mark = kb.push_scope()
        gain = self.load_bcast(P + "gain", I["norm_mix_pre"][L, :], D)
        rstd = self.prenorm_stats(P)
        if "no_attn" not in self.dbg:
            self.even_attn(L, P, gain, rstd)
        if "no_rwkv" not in self.dbg:
            self.even_rwkv(L, P, gain, rstd)
        if "no_comb" not in self.dbg:
            self.even_combine(L, P)
        kb.barrier()
        kb.pop_scope(mark)

    def bk(self, P, n=8):
        kb = self.kb
        banks = [kb.psum(P + "bk%d" % i, [128, 512]) for i in range(n)]
        keys = [P + "bk%d" % i for i in range(n)]
        return banks, keys

    def even_attn(self, L, P0, gain, rstd):
        kb = self.kb
        I = self.inp
        j = L // 2
        P = P0 + "a"
        mark = kb.push_scope()
        w = self.load_w_rows(P + "w", I["w_in_even"][j], 8, 768, c0=0)
        wkeys = [(P + "w", k) for k in range(8)]
        sinkb = self.load_bcast(P + "sink", I["attn_sinks"][j, :], 8)
        rope = kb.sbuf(P + "rope", [128, 1088])
        kb.dma("sp", rope[:], I["consts"][:, C_COS:C_COS + 1088], writes=[P + "rope"])
        bk, bkk = self.bk(P)
        Kc = kb.sbuf(P + "Kc", [128, 129, 64])
        Vc = kb.sbuf(P + "Vc", [128, 129, 64])
        for h in range(8):
            kv = h // 4
            HS = slice(h * 16, (h + 1) * 16)
            for q4 in range(4):
                ks = slice(q4 * 32, (q4 + 1) * 32)
                kb.dma("sp", Kc[HS, ks, :], I["ck"][j, :, ks, kv * 64:(kv + 1) * 64], writes=[P + "Kc"])
                kb.dma("sp", Vc[HS, ks, :], I["cv"][j, :, ks, kv * 64:(kv + 1) * 64], writes=[P + "Vc"])
        kT = kb.sbuf(P + "kT", [128, (NT + 1) * 128], BF16)
        Va = kb.sbuf(P + "Va", [128, NT + 1, 128], BF16)
        xn = kb.sbuf(P + "xn", [128, D], BF16)
        xnT = kb.sbuf(P + "xnT", [128, 8, 128], BF16)
        qk = kb.sbuf(P + "qk", [128, 640])
        vf = kb.sbuf(P + "vf", [128, 128])
        t1 = kb.sbuf(P + "t1", [128, 10, 32]); t2 = kb.sbuf(P + "t2", [128, 10, 32])
        t3 = kb.sbuf(P + "t3", [128, 10, 32]); t4 = kb.sbuf(P + "t4", [128, 10, 32])
        rq = kb.sbuf(P + "rq", [128, 4, 2, 64], BF16)
        rqs = kb.sbuf(P + "rqs", [NS, 8, 64])
        rk = kb.sbuf(P + "rk", [128, 128])
        rkb = kb.sbuf(P + "rkb", [128, 128], BF16)
        qT = kb.sbuf(P + "qT", [128, 4, 128], BF16)
        S = kb.sbuf(P + "S", [128, 4, 256])
        Pb = kb.sbuf(P + "Pb", [128, 4, 256], BF16)
        PT = kb.sbuf(P + "PT", [128, 8, 128], BF16)
        oa = kb.sbuf(P + "oa", [128, 512])
        sm = kb.sbuf(P + "sm", [128, 32])

        def qkv(t, rows=128):
            if t == "s":
                src, sk, rc = self.hs[0:NS, :], "hs", rstd[0:NS, NT:NT + 1]
                cos = rope[0:NS, 1024:1056]
                sin = rope[0:NS, 1056:1088]
            else:
                src, sk, rc = self.hp[:, t, :], ("hp", t), rstd[:, t:t + 1]
                cos = rope[:, t * 32:(t + 1) * 32]
                sin = rope[:, 512 + t * 32:512 + (t + 1) * 32]
            R = slice(0, rows)
            self.norm_transpose(P0, src, sk, rc, gain, xn, P + "xn", bk[7], bkk[7], xnT[:, :, 0:rows], P + "xnT",
                                rows=rows)
            for k in range(8):
                kb.op("pe", lambda e, k=k: e.matmul(bk[0][R, :], xnT[:, k, 0:rows], w[:, k, 0:512],
                                                    start=(k == 0), stop=(k == 7)),
                      reads=[P + "xnT"] + wkeys, writes=[bkk[0]], inc=(k == 7))
            for k in range(8):
                kb.op("pe", lambda e, k=k: e.matmul(bk[1][R, 0:256], xnT[:, k, 0:rows], w[:, k, 512:768],
                                                    start=(k == 0), stop=(k == 7)),
                      reads=[P + "xnT"] + wkeys, writes=[bkk[1]], inc=(k == 7))
            kb.op("act", lambda e: e.copy(out=qk[R, 0:512], in_=bk[0][R, :]), reads=[bkk[0]], writes=[P + "qk"])
            kb.op("act", lambda e: e.copy(out=qk[R, 512:640], in_=bk[1][R, 0:128]), reads=[bkk[1]], writes=[P + "qk"])
            kb.op("act", lambda e: e.copy(out=vf[R, :], in_=bk[1][R, 128:256]), reads=[bkk[1]], writes=[P + "vf"])
            q3 = qk[R, :].rearrange("p (h d) -> p h d", h=10)
            x1, x2 = q3[:, :, 0:32], q3[:, :, 32:64]
            cb = cos.unsqueeze(1).to_broadcast([rows, 10, 32])
            sb_ = sin.unsqueeze(1).to_broadcast([rows, 10, 32])
            kb.op("dve", lambda e: e.tensor_tensor(out=t1[R], in0=x1, in1=cb, op=ALU.mult), reads=[P + "qk", P + "rope"], writes=[P + "t1"])
            kb.op("pool", lambda e: e.tensor_tensor(out=t2[R], in0=x2, in1=sb_, op=ALU.mult), reads=[P + "qk", P + "rope"], writes=[P + "t2"])
            kb.op("dve", lambda e: e.tensor_tensor(out=t3[R], in0=x2, in1=cb, op=ALU.mult), reads=[P + "qk", P + "rope"], writes=[P + "t3"])
            kb.op("pool", lambda e: e.tensor_tensor(out=t4[R], in0=x1, in1=sb_, op=ALU.mult), reads=[P + "qk", P + "rope"], writes=[P + "t4"])
            r3 = rk[R, :].rearrange("p (h d) -> p h d", h=2)
            if t == "s":
                kb.op("dve", lambda e: e.tensor_tensor(out=rqs[:, :, 0:32], in0=t1[R, 0:8, :], in1=t2[R, 0:8, :], op=ALU.subtract),
                      reads=[P + "t1", P + "t2"], writes=[P + "rqs"])
                kb.op("dve", lambda e: e.tensor_tensor(out=rqs[:, :, 32:64], in0=t3[R, 0:8, :], in1=t4[R, 0:8, :], op=ALU.add),
                      reads=[P + "t3", P + "t4"], writes=[P + "rqs"])
            else:
                for kv in range(2):
                    kb.op("dve", lambda e, kv=kv: e.tensor_tensor(out=rq[:, :, kv, 0:32], in0=t1[:, kv * 4:kv * 4 + 4, :],
                                                                  in1=t2[:, kv * 4:kv * 4 + 4, :], op=ALU.subtract),
                          reads=[P + "t1", P + "t2"], writes=[P + "rq"])
                    kb.op("pool", lambda e, kv=kv: e.tensor_tensor(out=rq[:, :, kv, 32:64], in0=t3[:, kv * 4:kv * 4 + 4, :],
                                                                   in1=t4[:, kv * 4:kv * 4 + 4, :], op=ALU.add),
                          reads=[P + "t3", P + "t4"], writes=[P + "rq"])
            kb.op("dve", lambda e: e.tensor_tensor(out=r3[:, :, 0:32], in0=t1[R, 8:10, :], in1=t2[R, 8:10, :], op=ALU.subtract),
                  reads=[P + "t1", P + "t2"], writes=[P + "rk"])
            kb.op("dve", lambda e: e.tensor_tensor(out=r3[:, :, 32:64], in0=t3[R, 8:10, :], in1=t4[R, 8:10, :], op=ALU.add),
                  reads=[P + "t3", P + "t4"], writes=[P + "rk"])
            if t == "s":
                return
            kb.op("act", lambda e: e.copy(out=rkb[:], in_=rk[:]), reads=[P + "rk"], writes=[P + "rkb"])
            kb.op("act", lambda e: e.copy(out=Va[:, t + 1, :], in_=vf[:]), reads=[P + "vf"], writes=[(P + "Va", t + 1)])
            pv = bk[6][:].bitcast(BF16).rearrange("p (k t) -> p k t", k=8)
            for g in range(4):
                kb.op("pe", lambda e, g=g: e.transpose(pv[:, g, :], rq[:, g, :, :].rearrange("p a b -> p (a b)"), self.identb[:]),
                      reads=[P + "rq", "identb"], writes=[bkk[6]], inc=False)
            kb.op("pe", lambda e: e.transpose(pv[:, 4, :], rkb[:], self.identb[:]), reads=[P + "rkb", "identb"], writes=[bkk[6]])
            kb.op("act", lambda e: e.copy(out=qT[:], in_=pv[:, 0:4, :]), reads=[bkk[6]], writes=[P + "qT"])
            kb.op("dve", lambda e: e.tensor_copy(out=kT[:, (t + 1) * 128:(t + 2) * 128], in_=pv[:, 4, :]), reads=[bkk[6]],
                  writes=[(P + "kT", t + 1)])

        qkv(NT - 1)
        pay = kb.sbuf(P + "pay", [128, 256])
        kb.op("dve", lambda e: e.tensor_copy(out=pay[:, 0:128], in_=kT[:, NT * 128:(NT + 1) * 128]), reads=[(P + "kT", NT)], writes=[P + "pay"])
        kb.op("dve", lambda e: e.tensor_copy(out=pay[:, 128:256], in_=Va[:, NT, :]), reads=[(P + "Va", NT)], writes=[P + "pay"])
        ci, co = self.cc_att_in[j].ap(), self.cc_att_out[j].ap()
        kb.dma("sp", ci[:, 0:256], pay[:], reads=[P + "pay"], writes=[P + "ccin"])
        kb.collective(ci, co, [[0, 1, 2, 3], [4, 5, 6, 7]], reads=[P + "ccin"], writes=[P + "ccout"])
        g4 = kb.sbuf(P + "g4", [128, 4, 256])
        kb.dma("sp", g4[:], co.rearrange("(r p) c -> p r c", p=128)[:, :, 0:256], reads=[P + "ccout"], writes=[P + "g4"])
        kb.op("dve", lambda e: e.tensor_scalar(out=pay[:], in0=g4[:, 0, :], scalar1=self.cst[:, C_SELP:C_SELP + 1], scalar2=None,
                                               op0=ALU.mult), reads=[P + "g4", "cst"], writes=[P + "pay"])
        for r in range(1, 4):
            kb.op("dve", lambda e, r=r: e.scalar_tensor_tensor(out=pay[:], in0=g4[:, r, :], scalar=self.cst[:, C_SELP + r:C_SELP + r + 1],
                                                               in1=pay[:], op0=ALU.mult, op1=ALU.add),
                  reads=[P + "g4", "cst", P + "pay"], writes=[P + "pay"])
        kb.op("dve", lambda e: e.tensor_copy(out=kT[:, 0:128], in_=pay[:, 0:128]), reads=[P + "pay"], writes=[(P + "kT", 0)])
        kb.op("dve", lambda e: e.tensor_copy(out=Va[:, 0, :], in_=pay[:, 128:256]), reads=[P + "pay"], writes=[(P + "Va", 0)])

        for t in range(NT):
            qkv(t)
            if t == NT - 1:
                kb.dma("sp", self.out["wkp"][j, :, :], rk[:], reads=[P + "rk"], is_output=True)
                kb.dma("sp", self.out["wvp"][j, :, :], vf[:], reads=[P + "vf"], is_output=True)
            mcol = C_M0 if t == 0 else C_MA
            mask = self.cst[:, mcol:mcol + 256].unsqueeze(1).to_broadcast([128, 4, 256])
            for kv in range(2):
                H = slice(kv * 64, (kv + 1) * 64)
                for g in range(4):
                    b2 = 2 + g // 2
                    kb.op("pe", lambda e, g=g, b2=b2: e.matmul(bk[b2][:, (g % 2) * 256:(g % 2 + 1) * 256], qT[H, g, :],
                                                               kT[H, t * 128:(t + 2) * 128], start=True, stop=True),
                          reads=[P + "qT", (P + "kT", t), (P + "kT", t + 1)], writes=[bkk[b2]])
                for hh in range(2):
                    kb.op("dve", lambda e, hh=hh: e.scalar_tensor_tensor(
                        out=S[:, 2 * hh:2 * hh + 2, :], in0=bk[2 + hh][:].rearrange("p (a b) -> p a b", a=2), scalar=0.125,
                        in1=mask[:, 0:2, :], op0=ALU.mult, op1=ALU.add), reads=[bkk[2 + hh], "cst"], writes=[P + "S"])
                kb.op("dve", lambda e: e.tensor_reduce(out=sm[:, 0:4], in_=S[:], axis=AX.X, op=ALU.max), reads=[P + "S"], writes=[P + "sm"])
                kb.op("dve", lambda e: e.tensor_tensor(out=sm[:, 0:4], in0=sm[:, 0:4], in1=sinkb[:, kv * 4:kv * 4 + 4], op=ALU.max),
                      reads=[P + "sm", P + "sink"], writes=[P + "sm"])
                kb.op("dve", lambda e: e.tensor_scalar(out=sm[:, 4:8], in0=sm[:, 0:4], scalar1=-1.0, scalar2=None, op0=ALU.mult),
                      reads=[P + "sm"], writes=[P + "sm"])
                for g in range(4):
                    kb.op("act", lambda e, g=g: e.activation(out=Pb[:, g, :], in_=S[:, g, :], func=AF.Exp, bias=sm[:, 4 + g:5 + g],
                                                             accum_out=sm[:, 8 + g:9 + g]),
                          reads=[P + "S", P + "sm"], writes=[P + "Pb", P + "sm"])
                kb.op("dve", lambda e: e.tensor_tensor(out=sm[:, 12:16], in0=sinkb[:, kv * 4:kv * 4 + 4], in1=sm[:, 4:8], op=ALU.add),
                      reads=[P + "sm", P + "sink"], writes=[P + "sm"])
                kb.op("act", lambda e: e.activation(out=sm[:, 12:16], in_=sm[:, 12:16], func=AF.Exp), reads=[P + "sm"], writes=[P + "sm"])
                kb.op("dve", lambda e: e.tensor_tensor(out=sm[:, 16:20], in0=sm[:, 8:12], in1=sm[:, 12:16], op=ALU.add),
                      reads=[P + "sm"], writes=[P + "sm"])
                kb.op("dve", lambda e: e.reciprocal(out=sm[:, 20:24], in_=sm[:, 16:20]), reads=[P + "sm"], writes=[P + "sm"])
                pv = bk[4][:].bitcast(BF16).rearrange("p (k t) -> p k t", k=8)
                for g in range(4):
                    for blk in range(2):
                        kb.op("pe", lambda e, g=g, blk=blk: e.transpose(pv[:, g * 2 + blk, :], Pb[:, g, blk * 128:(blk + 1) * 128],
                                                                        self.identb[:]),
                              reads=[P + "Pb", "identb"], writes=[bkk[4]], inc=(g == 3 and blk == 1))
                kb.op("act", lambda e: e.copy(out=PT[:], in_=pv[:]), reads=[bkk[4]], writes=[P + "PT"])
                for g in range(4):
                    for blk in range(2):
                        kb.op("pe", lambda e, g=g, blk=blk: e.matmul(bk[5][:, g * 64:(g + 1) * 64], PT[:, g * 2 + blk, :],
                                                                     Va[:, t + blk, H], start=(blk == 0), stop=(blk == 1)),
                              reads=[P + "PT", (P + "Va", t), (P + "Va", t + 1)], writes=[bkk[5]], inc=(g == 3 and blk == 1))
                kb.op("dve", lambda e, kv=kv: e.tensor_tensor(
                    out=oa[:, kv * 256:(kv + 1) * 256].rearrange("p (g d) -> p g d", g=4),
                    in0=bk[5][:, 0:256].rearrange("p (g d) -> p g d", g=4),
                    in1=sm[:, 20:24].unsqueeze(2).to_broadcast([128, 4, 64]), op=ALU.mult),
                    reads=[bkk[5], P + "sm"], writes=[P + "oa"])
            kb.dma("sp", self.rec_oa.ap()[t, :, :], oa[:], reads=[P + "oa"], writes=[("rec_oa", t)])
            if "oa" in self.dbg:
                kb.dma("sp", self.out["dbg_oa"][t * 128:(t + 1) * 128, :], oa[:], reads=[P + "oa"], is_output=True)

        qkv("s", rows=NS)
        kb.dma("sp", self.out["wks"][j, :, :], rk[0:NS, :], reads=[P + "rk"], is_output=True)
        kb.dma("sp", self.out["wvs"][j, :, :], vf[0:NS, :], reads=[P + "vf"], is_output=True)
        kb.barrier()
        if "no_sattn" in self.dbg:
            kb.pop_scope(mark)
            return
        m2 = kb.push_scope()
        qh = kb.sbuf(P + "qh", [128, 64])
        sc = kb.sbuf(P + "sc", [128, 129])
        pe_ = kb.sbuf(P + "pe", [128, 129])
        oh = kb.sbuf(P + "oh", [128, 64])
        s2 = kb.sbuf(P + "s2", [128, 16])
        oas = kb.sbuf(P + "oas", [128, 512])
        for h in range(8):
            kv = h // 4
            HS = slice(h * 16, (h + 1) * 16)
            kb.dma("sp", Kc[HS, 128, :], rk[0:NS, kv * 64:(kv + 1) * 64], reads=[P + "rk"], writes=[P + "Kc"])
            kb.dma("sp", Vc[HS, 128, :], vf[0:NS, kv * 64:(kv + 1) * 64], reads=[P + "vf"], writes=[P + "Vc"])
            kb.dma("sp", qh[HS, :], rqs[:, h, :], reads=[P + "rqs"], writes=[P + "qh"])
            kb.dma("sp", s2[HS, 0:1], I["attn_sinks"][j, h:h + 1].partition_broadcast(NS), writes=[P + "s2"])
        kb.op("dve", lambda e: e.tensor_tensor(out=Kc[:], in0=Kc[:], in1=qh[:].unsqueeze(1).to_broadcast([128, 129, 64]), op=ALU.mult),
              reads=[P + "Kc", P + "qh"], writes=[P + "Kc"])
        kb.op("dve", lambda e: e.tensor_reduce(out=sc[:], in_=Kc[:], axis=AX.X, op=ALU.add), reads=[P + "Kc"], writes=[P + "sc"])
        kb.op("dve", lambda e: e.tensor_reduce(out=s2[:, 1:2], in_=sc[:], axis=AX.X, op=ALU.max), reads=[P + "sc"], writes=[P + "s2"])
        kb.op("dve", lambda e: e.tensor_scalar(out=s2[:, 2:3], in0=s2[:, 1:2], scalar1=0.125, scalar2=s2[:, 0:1], op0=ALU.mult, op1=ALU.max),
              reads=[P + "s2"], writes=[P + "s2"])
        kb.op("dve", lambda e: e.tensor_scalar(out=s2[:, 3:4], in0=s2[:, 2:3], scalar1=-1.0, scalar2=None, op0=ALU.mult),
              reads=[P + "s2"], writes=[P + "s2"])
        kb.op("act", lambda e: e.activation(out=pe_[:], in_=sc[:], func=AF.Exp, bias=s2[:, 3:4], scale=0.125, accum_out=s2[:, 4:5]),
              reads=[P + "sc", P + "s2"], writes=[P + "pe", P + "s2"])
        kb.op("dve", lambda e: e.tensor_tensor(out=s2[:, 5:6], in0=s2[:, 0:1], in1=s2[:, 3:4], op=ALU.add), reads=[P + "s2"], writes=[P + "s2"])
        kb.op("act", lambda e: e.activation(out=s2[:, 5:6], in_=s2[:, 5:6], func=AF.Exp), reads=[P + "s2"], writes=[P + "s2"])
        kb.op("dve", lambda e: e.tensor_tensor(out=s2[:, 6:7], in0=s2[:, 4:5], in1=s2[:, 5:6], op=ALU.add), reads=[P + "s2"], writes=[P + "s2"])
        kb.op("dve", lambda e: e.reciprocal(out=s2[:, 7:8], in_=s2[:, 6:7]), reads=[P + "s2"], writes=[P + "s2"])
        kb.op("dve", lambda e: e.tensor_tensor(out=Vc[:], in0=Vc[:], in1=pe_[:].unsqueeze(2).to_broadcast([128, 129, 64]), op=ALU.mult),
              reads=[P + "Vc", P + "pe"], writes=[P + "Vc"])
        kb.op("dve", lambda e: e.tensor_reduce(out=oh[:], in_=Vc[:].rearrange("p j d -> p d j"), axis=AX.X, op=ALU.add),
              reads=[P + "Vc"], writes=[P + "oh"])
        kb.op("dve", lambda e: e.tensor_scalar(out=oh[:], in0=oh[:], scalar1=s2[:, 7:8], scalar2=None, op0=ALU.mult),
              reads=[P + "oh", P + "s2"], writes=[P + "oh"])
        kb.op("pool", lambda e: e.memset(oas[:], 0.0), writes=[P + "oas"])
        for h in range(8):
            kb.dma("sp", oas[0:NS, h * 64:(h + 1) * 64], oh[h * 16:(h + 1) * 16, :], reads=[P + "oh"], writes=[P + "oas"])
        kb.dma("sp", self.rec_oa.ap()[NT, :, :], oas[:], reads=[P + "oas"], writes=[("rec_oa", NT)])
        kb.barrier()
        kb.pop_scope(m2)
        kb.pop_scope(mark)

    def even_rwkv(self, L, P0, gain, rstd):
        kb = self.kb
        I = self.inp
        j = L // 2
        P = P0 + "r"
        HDS = 0.5 * DECAY_SCALE
        mark = kb.push_scope()
        w = self.load_w_rows(P + "w", I["w_in_even"][j], 8, BPROJ, c0=768)
        wkeys = [(P + "w", k) for k in range(8)]
        mub = self.load_bcast(P + "mu", I["shift_mu"][j, :], BPROJ)
        w0b = self.load_bcast(P + "w0b", I["decay_w0"][j, :], 512)
        a0b = self.load_bcast(P + "a0b", I["iclr_a0"][j, :], 512)
        kkb = self.load_bcast(P + "kkb", I["key_k"][j, :], 512)
        kab = self.load_bcast(P + "kab", I["key_a"][j, :], 512)
        rkb = self.load_bcast(P + "rkb", I["bonus_r_k"][j, :], 512)
        w2a2 = kb.sbuf(P + "w2a2", [128, 512], BF16)
        kb.dma("pool", w2a2[0:64, :], I["decay_w2"][j, :, :], writes=[P + "w2a2"])
        kb.dma("pool", w2a2[64:128, :], I["iclr_a2"][j, :, :], writes=[P + "w2a2"])
        g2b = kb.sbuf(P + "g2b", [128, 512], BF16)
        kb.dma("pool", g2b[:], I["gate_g2"][j, :, :], writes=[P + "g2b"])
        bk, bkk = self.bk(P)
        S = kb.sbuf(P + "S", [128, 4, 128])
        Sb = kb.sbuf(P + "Sb", [128, 4, 128], BF16)
        zlast = self.zl_dram[j].ap()
        xn = kb.sbuf(P + "xn", [128, D], BF16)
        xnT = kb.sbuf(P + "xnT", [128, 8, 128], BF16)
        zb = kb.sbuf(P + "zb", [128, BPROJ])
        prev = kb.sbuf(P + "prev", [128, BPROJ])
        sm = kb.sbuf(P + "sm", [128, 40])
        f = {}
        for nme in ("logw", "a", "gt", "kk", "kf", "bon", "T1", "T2", "T3"):
            f[nme] = kb.sbuf(P + nme, [128, 512])
        wab = kb.sbuf(P + "wab", [128, 128], BF16)
        sgb = kb.sbuf(P + "sgb", [128, 128], BF16)
        waT = kb.sbuf(P + "waT", [128, 2, 128], BF16)

        def inproj(t, rows=128):
            if t == "s":
                src, sk, rc = self.hs[0:NS, :], "hs", rstd[0:NS, NT:NT + 1]
            else:
                src, sk, rc = self.hp[:, t, :], ("hp", t), rstd[:, t:t + 1]
            R = slice(0, rows)
            self.norm_transpose(P0, src, sk, rc, gain, xn, P + "xn", bk[2], bkk[2], xnT[:, :, 0:rows], P + "xnT", rows=rows)
            for g in range(4):
                wdt = 512 if g < 3 else 256
                b = g % 2
                for k in range(8):
                    kb.op("pe", lambda e, k=k, g=g, b=b, wdt=wdt: e.matmul(bk[b][R, 0:wdt], xnT[:, k, 0:rows],
                                                                          w[:, k, g * 512:g * 512 + wdt], start=(k == 0), stop=(k == 7)),
                          reads=[P + "xnT"] + wkeys, writes=[bkk[b]], inc=(k == 7))
                eng = "act" if g % 2 == 0 else "dve"
                if eng == "act":
                    kb.op("act", lambda e, g=g, b=b, wdt=wdt: e.copy(out=zb[R, g * 512:g * 512 + wdt], in_=bk[b][R, 0:wdt]),
                          reads=[bkk[b]], writes=[P + "zb"])
                else:
                    kb.op("dve", lambda e, g=g, b=b, wdt=wdt: e.tensor_copy(out=zb[R, g * 512:g * 512 + wdt], in_=bk[b][R, 0:wdt]),
                          reads=[bkk[b]], writes=[P + "zb"])

        def elem(rows):
            R = slice(0, rows)
            kb.op("dve", lambda e: e.tensor_tensor(out=prev[R, :], in0=prev[R, :], in1=zb[R, :], op=ALU.subtract),
                  reads=[P + "prev", P + "zb"], writes=[P + "prev"])
            kb.op("pool", lambda e: e.tensor_tensor(out=prev[R, :], in0=prev[R, :], in1=mub[R, :], op=ALU.mult),
                  reads=[P + "prev", P + "mu"], writes=[P + "prev"])
            kb.op("dve", lambda e: e.tensor_tensor(out=prev[R, :], in0=prev[R, :], in1=zb[R, :], op=ALU.add),
                  reads=[P + "prev", P + "zb"], writes=[P + "prev"])
            zs = prev
            r_, k0, v_ = zs[R, 0:512], zs[R, 512:1024], zs[R, 1024:1536]
            kb.op("act", lambda e: e.activation(out=wab[R, 0:64], in_=zs[R, 1536:1600], func=AF.Tanh), reads=[P + "prev"], writes=[P + "wab"])
            kb.op("act", lambda e: e.copy(out=wab[R, 64:128], in_=zs[R, 1600:1664]), reads=[P + "prev"], writes=[P + "wab"])
            th = f["T3"]
            kb.op("act", lambda e: e.activation(out=th[R, 0:128], in_=zs[R, 1664:1792], func=AF.Tanh, scale=0.5), reads=[P + "prev"], writes=[P + "T3"])
            kb.op("dve", lambda e: e.tensor_scalar(out=sgb[R, :], in0=th[R, 0:128], scalar1=0.5, scalar2=0.5, op0=ALU.mult, op1=ALU.add),
                  reads=[P + "T3"], writes=[P + "sgb"])
            pv = bk[2][:].bitcast(BF16).rearrange("p (k t) -> p k t", k=8)
            kb.op("pe", lambda e: e.transpose(pv[:, 0, 0:rows], wab[R, :], self.identb[R, R]), reads=[P + "wab", "identb"], writes=[bkk[2]], inc=False)
            kb.op("pe", lambda e: e.transpose(pv[:, 1, 0:rows], sgb[R, :], self.identb[R, R]), reads=[P + "sgb", "identb"], writes=[bkk[2]])
            kb.op("act", lambda e: e.copy(out=waT[:, :, 0:rows], in_=pv[:, 0:2, 0:rows]), reads=[bkk[2]], writes=[P + "waT"])
            kb.op("pe", lambda e: e.matmul(bk[0][R, :], waT[0:64, 0, 0:rows], w2a2[0:64, :], start=True, stop=True),
                  reads=[P + "waT", P + "w2a2"], writes=[bkk[0]])
            kb.op("pe", lambda e: e.matmul(bk[1][R, :], waT[64:128, 0, 0:rows], w2a2[64:128, :], start=True, stop=True),
                  reads=[P + "waT", P + "w2a2"], writes=[bkk[1]])
            kb.op("pe", lambda e: e.matmul(bk[2][R, :], waT[:, 1, 0:rows], g2b[:, :], start=True, stop=True),
                  reads=[P + "waT", P + "g2b"], writes=[bkk[2]])
            T1, T2, T3 = f["T1"], f["T2"], f["T3"]
            kb.op("dve", lambda e: e.tensor_tensor(out=T1[R, :], in0=bk[0][R, :], in1=w0b[R, :], op=ALU.add), reads=[bkk[0], P + "w0b"], writes=[P + "T1"])
            kb.op("act", lambda e: e.activation(out=T1[R, :], in_=T1[R, :], func=AF.Tanh, scale=0.5), reads=[P + "T1"], writes=[P + "T1"])
            kb.op("dve", lambda e: e.tensor_scalar(out=f["logw"][R, :], in0=T1[R, :], scalar1=-HDS, scalar2=-HDS, op0=ALU.mult, op1=ALU.add),
                  reads=[P + "T1"], writes=[P + "logw"])
            kb.op("dve", lambda e: e.tensor_tensor(out=T2[R, :], in0=bk[1][R, :], in1=a0b[R, :], op=ALU.add), reads=[bkk[1], P + "a0b"], writes=[P + "T2"])
            kb.op("act", lambda e: e.activation(out=T2[R, :], in_=T2[R, :], func=AF.Tanh, scale=0.5), reads=[P + "T2"], writes=[P + "T2"])
            kb.op("dve", lambda e: e.tensor_scalar(out=f["a"][R, :], in0=T2[R, :], scalar1=0.5, scalar2=0.5, op0=ALU.mult, op1=ALU.add),
                  reads=[P + "T2"], writes=[P + "a"])
            kb.op("act", lambda e: e.copy(out=f["gt"][R, :], in_=bk[2][R, :]), reads=[bkk[2]], writes=[P + "gt"])
            kk = f["kk"]
            kb.op("dve", lambda e: e.tensor_tensor(out=kk[R, :], in0=k0, in1=kkb[R, :], op=ALU.mult), reads=[P + "prev", P + "kkb"], writes=[P + "kk"])
            kb.op("pool", lambda e: e.tensor_tensor(out=T3[R, :], in0=kk[R, :], in1=kk[R, :], op=ALU.mult), reads=[P + "kk"], writes=[P + "T3"])
            kb.op("dve", lambda e: e.tensor_reduce(out=sm[R, 0:8], in_=T3[R, :].rearrange("p (h d) -> p h d", h=8), axis=AX.X, op=ALU.add),
                  reads=[P + "T3"], writes=[P + "sm"])
            kb.op("act", lambda e: e.activation(out=sm[R, 8:16], in_=sm[R, 0:8], func=AF.Sqrt), reads=[P + "sm"], writes=[P + "sm"])
            kb.op("dve", lambda e: e.tensor_scalar(out=sm[R, 8:16], in0=sm[R, 8:16], scalar1=1e-12, scalar2=None, op0=ALU.max), reads=[P + "sm"], writes=[P + "sm"])
            kb.op("dve", lambda e: e.reciprocal(out=sm[R, 16:24], in_=sm[R, 8:16]), reads=[P + "sm"], writes=[P + "sm"])
            kb.op("dve", lambda e: e.tensor_tensor(out=kk[R, :].rearrange("p (h d) -> p h d", h=8), in0=kk[R, :].rearrange("p (h d) -> p h d", h=8),
                                                   in1=sm[R, 16:24].unsqueeze(2).to_broadcast([rows, 8, 64]), op=ALU.mult),
                  reads=[P + "kk", P + "sm"], writes=[P + "kk"])
            kb.op("dve", lambda e: e.scalar_tensor_tensor(out=T3[R, :], in0=f["a"][R, :], scalar=-1.0, in1=kab[R, :], op0=ALU.add, op1=ALU.mult),
                  reads=[P + "a", P + "kab"], writes=[P + "T3"])
            kb.op("dve", lambda e: e.scalar_tensor_tensor(out=f["kf"][R, :], in0=T3[R, :], scalar=1.0, in1=k0, op0=ALU.add, op1=ALU.mult),
                  reads=[P + "T3", P + "prev"], writes=[P + "kf"])
            kb.op("pool", lambda e: e.tensor_tensor(out=T3[R, :], in0=r_, in1=f["kf"][R, :], op=ALU.mult), reads=[P + "prev", P + "kf"], writes=[P + "T3"])
            kb.op("pool", lambda e: e.tensor_tensor(out=T3[R, :], in0=T3[R, :], in1=rkb[R, :], op=ALU.mult), reads=[P + "T3", P + "rkb"], writes=[P + "T3"])
            kb.op("dve", lambda e: e.tensor_reduce(out=sm[R, 24:32], in_=T3[R, :].rearrange("p (h d) -> p h d", h=8), axis=AX.X, op=ALU.add),
                  reads=[P + "T3"], writes=[P + "sm"])
            kb.op("dve", lambda e: e.tensor_tensor(out=f["bon"][R, :].rearrange("p (h d) -> p h d", h=8), in0=v_.rearrange("p (h d) -> p h d", h=8),
                                                   in1=sm[R, 24:32].unsqueeze(2).to_broadcast([rows, 8, 64]), op=ALU.mult),
                  reads=[P + "prev", P + "sm"], writes=[P + "bon"])

        inproj(NT - 1)
        ci, co = self.cc_sh_in[j].ap(), self.cc_sh_out[j].ap()
        kb.dma("sp", ci[0:1, :], zb[127:128, :], reads=[P + "zb"], writes=[P + "ccin"])
        kb.collective(ci, co, [[0, 1, 2, 3], [4, 5, 6, 7]], reads=[P + "ccin"], writes=[P + "ccout"])
        mz = kb.push_scope()
        z4 = kb.sbuf(P + "z4", [1, 4, BPROJ])
        zsel = kb.sbuf(P + "zsel", [1, BPROJ])
        kb.dma("sp", z4[0:1, :, :], co.rearrange("(o r) c -> o r c", o=1), reads=[P + "ccout"], writes=[P + "z4"])
        kb.op("dve", lambda e: e.tensor_scalar(out=zsel[0:1, :], in0=z4[0:1, 0, :], scalar1=self.cst[0:1, C_SELP:C_SELP + 1], scalar2=None, op0=ALU.mult),
              reads=[P + "z4", "cst"], writes=[P + "zsel"])
        for r in range(1, 4):
            kb.op("dve", lambda e, r=r: e.scalar_tensor_tensor(out=zsel[0:1, :], in0=z4[0:1, r, :], scalar=self.cst[0:1, C_SELP + r:C_SELP + r + 1],
                                                               in1=zsel[0:1, :], op0=ALU.mult, op1=ALU.add),
                  reads=[P + "z4", "cst", P + "zsel"], writes=[P + "zsel"])
        kb.dma("sp", zlast[0:1, :], zsel[0:1, :], reads=[P + "zsel"], writes=[P + "zlast"])
        kb.barrier()
        kb.pop_scope(mz)
        kb.op("pool", lambda e: e.memset(S[:], 0.0), writes=[P + "S"])
        for c in range(4):
            kb.op("dve", lambda e, c=c: e.tensor_copy(out=S[0:64, c, 64:128], in_=self.cst[0:64, C_ID:C_ID + 64]), reads=["cst", P + "S"], writes=[P + "S"])
            kb.op("dve", lambda e, c=c: e.tensor_copy(out=S[64:128, c, 64:128], in_=self.cst[64:128, C_ID + 64:C_ID + 128]), reads=["cst", P + "S"], writes=[P + "S"])
        kb.op("act", lambda e: e.copy(out=Sb[:], in_=S[:]), reads=[P + "S"], writes=[P + "Sb"])

        m1 = kb.push_scope()
        Gs, Ea, Eb = f["gt"], f["bon"], f["T3"]
        hbs = [{}, {}]
        Vbs, TTs, gcts = [], [], []
        for i2 in range(2):
            for nme in ("At", "Bt", "Kt", "Rt", "Bh", "Kh"):
                hbs[i2][nme] = kb.sbuf(P + "d%d" % i2 + nme, [128, 512], BF16)
            Vbs.append(kb.sbuf(P + "d%dVb" % i2, [128, 8, 128], BF16))
            kb.op("pool", lambda e, i2=i2: e.memset(Vbs[i2][:], 0.0), writes=[P + "d%dVb" % i2])
            TTs.append(kb.sbuf(P + "d%dTT" % i2, [128, 16, 128], BF16))
            gcts.append(kb.sbuf(P + "d%dgct" % i2, [128, 4]))
        Yb_t = kb.sbuf(P + "Yb", [128, 512], BF16)
        cm = {}
        for nme in ("Pa", "PTa", "Pb", "PTb", "Q", "AKT", "RBT", "RKT", "Ub"):
            cm[nme] = kb.sbuf(P + nme, [128, 8, 128], BF16)
        WT = cm["Pb"][:, 0:4, :]
        OPt = Yb_t[:].rearrange("p (a t) -> p a t", a=4)
        tri = self.cst[:, C_TRI:C_TRI + 128]
        trs = self.cst[:, C_TRS:C_TRS + 128]
        low = self.cst[:, C_LOW:C_LOW + 128]
        ones = self.cst[:, C_ONE:C_ONE + 128]

        def X(t):
            i2 = t % 2
            PD = P + "d%d" % i2
            hb, Vb, TT, gct = hbs[i2], Vbs[i2], TTs[i2], gcts[i2]
            inproj(t)
            kb.dma("sp", prev[1:128, :], zb[0:127, :], reads=[P + "zb"], writes=[P + "prev"])
            kb.dma("sp", prev[0:1, :], zlast[0:1, :], reads=[P + "zlast"], writes=[P + "prev"])
            kb.dma("sp", zlast[0:1, :], zb[127:128, :], reads=[P + "zb"], writes=[P + "zlast"])
            if t == NT - 1:
                kb.dma("sp", self.out["shp"][j:j + 1, :], zb[127:128, :], reads=[P + "zb"], is_output=True)
            elem(128)
            kb.dma("sp", self.rec_g.ap()[t, :, :], f["gt"][:], reads=[P + "gt"], writes=[("rec_g", t)])
            kb.dma("sp", self.rec_bo.ap()[t, :, :], f["bon"][:], reads=[P + "bon"], writes=[("rec_bo", t)])
            logw, a_, kk, kf = f["logw"], f["a"], f["kk"], f["kf"]
            T1, T2, T3 = f["T1"], f["T2"], f["T3"]
            zs = prev
            kb.op("pe", lambda e: e.matmul(bk[0][:, :], tri, logw[:, :], start=True, stop=True), reads=["cst", P + "logw"], writes=[bkk[0]])
            kb.op("pe", lambda e: e.matmul(bk[1][:, :], ones, logw[:, :], start=True, stop=True), reads=["cst", P + "logw"], writes=[bkk[1]])
            for h in range(8):
                hp_, c = h % 2, h // 2
                kb.op("pe", lambda e, h=h, hp_=hp_, c=c: e.matmul(bk[2][hp_ * 64:(hp_ + 1) * 64, c:c + 1], logw[:, h * 64:(h + 1) * 64],
                                                                ones[:, 0:1], start=True, stop=True),
                      reads=["cst", P + "logw"], writes=[bkk[2]], inc=(h == 7))
            kb.op("act", lambda e: e.copy(out=Gs[:], in_=bk[0][:, :]), reads=[bkk[0]], writes=[P + "gt"])
            kb.op("act", lambda e: e.activation(out=gct[:], in_=bk[2][:, 0:4], func=AF.Exp), reads=[bkk[2]], writes=[PD + "gct"])
            kb.op("dve", lambda e: e.tensor_tensor(out=T1[:], in0=Gs[:], in1=logw[:], op=ALU.subtract), reads=[P + "gt", P + "logw"], writes=[P + "T1"])
            kb.op("act", lambda e: e.activation(out=Ea[:], in_=T1[:], func=AF.Exp), reads=[P + "T1"], writes=[P + "bon"])
            kb.op("dve", lambda e: e.scalar_tensor_tensor(out=hb["At"][:], in0=kk[:], scalar=-1.0, in1=Ea[:], op0=ALU.mult, op1=ALU.mult),
                  reads=[P + "kk", P + "bon"], writes=[PD + "At"])
            kb.op("act", lambda e: e.activation(out=Eb[:], in_=Gs[:], func=AF.Exp), reads=[P + "gt"], writes=[P + "T3"])
            kb.op("pool", lambda e: e.tensor_tensor(out=hb["Rt"][:], in0=zs[:, 0:512], in1=Eb[:], op=ALU.mult), reads=[P + "prev", P + "T3"], writes=[PD + "Rt"])
            kb.op("pool", lambda e: e.tensor_tensor(out=T2[:], in0=kk[:], in1=a_[:], op=ALU.mult), reads=[P + "kk", P + "a"], writes=[P + "T2"])
            kb.op("act", lambda e: e.activation(out=Ea[:], in_=Gs[:], func=AF.Exp, scale=-1.0), reads=[P + "gt"], writes=[P + "bon"])
            kb.op("dve", lambda e: e.tensor_tensor(out=hb["Bt"][:], in0=T2[:], in1=Ea[:], op=ALU.mult), reads=[P + "T2", P + "bon"], writes=[PD + "Bt"])
            kb.op("pool", lambda e: e.tensor_tensor(out=hb["Kt"][:], in0=kf[:], in1=Ea[:], op=ALU.mult), reads=[P + "kf", P + "bon"], writes=[PD + "Kt"])
            kb.op("dve", lambda e: e.tensor_tensor(out=T1[:], in0=bk[1][:, :], in1=Gs[:], op=ALU.subtract), reads=[bkk[1], P + "gt"], writes=[P + "T1"])
            kb.op("act", lambda e: e.activation(out=Eb[:], in_=T1[:], func=AF.Exp), reads=[P + "T1"], writes=[P + "T3"])
            kb.op("dve", lambda e: e.tensor_tensor(out=hb["Bh"][:], in0=T2[:], in1=Eb[:], op=ALU.mult), reads=[P + "T2", P + "T3"], writes=[PD + "Bh"])
            kb.op("pool", lambda e: e.tensor_tensor(out=hb["Kh"][:], in0=kf[:], in1=Eb[:], op=ALU.mult), reads=[P + "kf", P + "T3"], writes=[PD + "Kh"])
            kb.op("act", lambda e: e.copy(out=Vb[:, :, 0:64], in_=zs[:, 1024:1536].rearrange("p (h d) -> p h d", h=8)), reads=[P + "prev"], writes=[PD + "Vb"])
            for qi, nme in enumerate(("At", "Bt", "Kt", "Rt")):
                pv = bk[qi // 2][:].bitcast(BF16).rearrange("p (k t) -> p k t", k=8)
                for c in range(4):
                    kb.op("pe", lambda e, qi=qi, c=c, nme=nme, pv=pv: e.transpose(pv[:, (qi % 2) * 4 + c, :], hb[nme][:, c * 128:(c + 1) * 128], self.identb[:]),
                          reads=[PD + nme, "identb"], writes=[bkk[qi // 2]], inc=(c == 3))
            for half in range(2):
                pv = bk[half][:].bitcast(BF16).rearrange("p (k t) -> p k t", k=8)
                kb.op("act", lambda e, half=half, pv=pv: e.copy(out=TT[:, half * 8:(half + 1) * 8, :], in_=pv[:]), reads=[bkk[half]], writes=[PD + "TT"])

        def Y(t):
            i2 = t % 2
            PD = P + "d%d" % i2
            hb, Vb, TT, gct = dict(hbs[i2]), Vbs[i2], TTs[i2], gcts[i2]
            hb["Yb"] = Yb_t

            def hsl(h):
                return slice((h % 2) * 64, (h % 2) * 64 + 64), h // 2

            def HI(h):
                return (h % 2) * 4 + h // 2

            def mm_pair(dst, l_idx, r_idx, mask, bank0, add_ident=False):
                for hp_ in range(2):
                    b = bank0 + hp_
                    hs_ = slice(hp_ * 64, hp_ * 64 + 64)
                    for c in range(4):
                        kb.op("pe", lambda e, hs_=hs_, c=c, b=b: e.matmul(bk[b][:, c * 128:(c + 1) * 128], TT[hs_, l_idx * 4 + c, :],
                                                                          TT[hs_, r_idx * 4 + c, :], start=True, stop=True),
                              reads=[PD + "TT"], writes=[bkk[b]], inc=(c == 3))
                    kb.op("dve", lambda e, hp_=hp_, b=b: e.tensor_tensor(
                        out=dst[:, hp_ * 4:(hp_ + 1) * 4, :], in0=bk[b][:].rearrange("p (a t) -> p a t", a=4),
                        in1=mask.unsqueeze(1).to_broadcast([128, 4, 128]), op=ALU.mult), reads=[bkk[b], "cst"], writes=[(dst_key[id(dst)], hp_)])

            dst_key = {id(cm[n]): P + n for n in cm}
            A_, B_, K_, R_ = 0, 1, 2, 3
            mm_pair(cm["Pa"], B_, A_, trs, 3)
            mm_pair(cm["PTa"], A_, B_, low, 5)
            mm_pair(cm["AKT"], K_, A_, trs, 3)
            mm_pair(cm["RBT"], B_, R_, tri, 5)
            mm_pair(cm["RKT"], K_, R_, tri, 3)
            Q = cm["Q"]

            def K2(n):
                return [(P + n, 0), (P + n, 1)]
            kb.op("dve", lambda e: e.tensor_tensor(out=Q[:], in0=cm["Pa"][:], in1=self.identb[:].unsqueeze(1).to_broadcast([128, 8, 128]), op=ALU.add),
                  reads=K2("Pa") + ["identb"], writes=K2("Q"))
            cur, nxt = ("Pa", "PTa"), ("Pb", "PTb")
            for step in range(6):
                Pc, PTc = cm[cur[0]], cm[cur[1]]
                Pn, PTn = cm[nxt[0]], cm[nxt[1]]
                for half in range(2):
                    b0 = 3 + half * 2
                    if step < 5:
                        for hi in range(4):
                            h = half * 4 + hi
                            kb.op("pe", lambda e, hi=hi, h=h, b0=b0: e.matmul(bk[b0][:, hi * 128:(hi + 1) * 128], PTc[:, h, :], Pc[:, h, :], start=True, stop=True),
                                  reads=[(P + cur[0], half), (P + cur[1], half)], writes=[bkk[b0]], inc=(hi == 3))
                    for hi in range(4):
                        h = half * 4 + hi
                        kb.op("pe", lambda e, hi=hi, h=h, b0=b0: e.matmul(bk[b0 + 1][:, hi * 128:(hi + 1) * 128], Pc[:, h, :], PTc[:, h, :], start=True, stop=True),
                              reads=[(P + cur[0], half), (P + cur[1], half)], writes=[bkk[b0 + 1]], inc=(hi == 3))
                for half in range(2):
                    b0 = 3 + half * 2
                    hs4 = slice(half * 4, half * 4 + 4)
                    if step < 5:
                        kb.op("act", lambda e, hs4=hs4, b0=b0: e.copy(out=Pn[:, hs4, :], in_=bk[b0][:].rearrange("p (a t) -> p a t", a=4)),
                              reads=[bkk[b0]], writes=[(P + nxt[0], half)])
                    kb.op("act", lambda e, hs4=hs4, b0=b0: e.copy(out=PTn[:, hs4, :], in_=bk[b0 + 1][:].rearrange("p (a t) -> p a t", a=4)),
                          reads=[bkk[b0 + 1]], writes=[(P + nxt[1], half)])
                for half in range(2):
                    b0 = 3 + half * 2
                    hs4 = slice(half * 4, half * 4 + 4)
                    for hi in range(4):
                        h = half * 4 + hi
                        kb.op("pe", lambda e, hi=hi, h=h, b0=b0: e.matmul(bk[7][:, hi * 128:(hi + 1) * 128], PTn[:, h, :], Q[:, h, :], start=True, stop=True),
                              reads=[(P + nxt[1], half), (P + "Q", half)], writes=[bkk[7]], inc=(hi == 3))
                    kb.op("dve", lambda e, hs4=hs4, b0=b0: e.tensor_tensor(out=Q[:, hs4, :], in0=Q[:, hs4, :], in1=bk[7][:].rearrange("p (a t) -> p a t", a=4), op=ALU.add),
                          reads=[bkk[7], (P + "Q", half)], writes=[(P + "Q", half)])
                cur, nxt = nxt, cur
            for h in range(8):
                hs_, c = hsl(h)
                kb.op("pe", lambda e, h=h, hs_=hs_, c=c: e.matmul(bk[3][hs_, c * 128:(c + 1) * 128], hb["At"][:, h * 64:(h + 1) * 64], Q[:, HI(h), :], start=True, stop=True),
                      reads=[PD + "At"] + K2("Q"), writes=[bkk[3]], inc=(h == 7))
            kb.op("act", lambda e: e.copy(out=WT, in_=bk[3][:].rearrange("p (a t) -> p a t", a=4)), reads=[bkk[3]], writes=[(P + "Pb", 0), (P + "Pb", 1)])
            for h in range(8):
                kb.op("pe", lambda e, h=h: e.matmul(bk[4][:, h * 64:(h + 1) * 64], cm["AKT"][:, HI(h), :], Vb[:, h, 0:64], start=True, stop=True),
                      reads=K2("AKT") + [PD + "Vb"], writes=[bkk[4]], inc=(h == 7))
            kb.op("act", lambda e: e.copy(out=hb["Yb"][:], in_=bk[4][:]), reads=[bkk[4]], writes=[P + "Yb"])
            for h in range(8):
                hs_, c = hsl(h)
                b = 5 + h % 2
                o0 = c * 128
                kb.op("pe", lambda e, h=h, b=b, o0=o0: e.matmul(bk[b][:, o0:o0 + 64], Q[:, HI(h), :], hb["Yb"][:, h * 64:(h + 1) * 64], start=True, stop=False),
                      reads=K2("Q") + [P + "Yb"], writes=[bkk[b]], inc=False)
                kb.op("pe", lambda e, hs_=hs_, c=c, b=b, o0=o0: e.matmul(bk[b][:, o0:o0 + 64], WT[hs_, c, :], Sb[hs_, c, 0:64], start=False, stop=True),
                      reads=[(P + "Pb", 0), (P + "Pb", 1), P + "Sb"], writes=[bkk[b]], inc=False)
                kb.op("pe", lambda e, hs_=hs_, c=c, b=b, o0=o0: e.matmul(bk[b][:, o0 + 64:o0 + 128], WT[hs_, c, :], Sb[hs_, c, 64:128], start=True, stop=True),
                      reads=[(P + "Pb", 0), (P + "Pb", 1), P + "Sb"], writes=[bkk[b]], inc=(h >= 6))
            Ub = cm["Ub"]
            for half in range(2):
                kb.op("act", lambda e, half=half: e.copy(out=Ub[:, half * 4:(half + 1) * 4, :], in_=bk[5 + half][:].rearrange("p (a t) -> p a t", a=4)),
                      reads=[bkk[5 + half]], writes=[P + "Ub"])
            for h in range(8):
                hs_, c = hsl(h)
                kb.op("pe", lambda e, h=h, hs_=hs_, c=c: e.matmul(bk[7][:, h * 64:(h + 1) * 64], TT[hs_, R_ * 4 + c, :], Sb[hs_, c, 0:64], start=True, stop=False),
                      reads=[PD + "TT", P + "Sb"], writes=[bkk[7]], inc=False)
                kb.op("pe", lambda e, h=h: e.matmul(bk[7][:, h * 64:(h + 1) * 64], cm["RBT"][:, HI(h), :], Ub[:, HI(h), 0:64], start=False, stop=False),
                      reads=K2("RBT") + [P + "Ub"], writes=[bkk[7]], inc=False)
                kb.op("pe", lambda e, h=h: e.matmul(bk[7][:, h * 64:(h + 1) * 64], cm["RKT"][:, HI(h), :], Vb[:, h, 0:64], start=False, stop=True),
                      reads=K2("RKT") + [PD + "Vb"], writes=[bkk[7]], inc=(h == 7))
            ol = cm["Pa"][:].rearrange("p a b -> p (a b)").bitcast(F32)
            kb.op("act", lambda e: e.copy(out=ol, in_=bk[7][:]), reads=[bkk[7]], writes=K2("Pa"))
            kb.dma("sp", self.rec_ol.ap()[t, :, :], ol, reads=K2("Pa"), writes=[("rec_ol", t)])
            for h in range(8):
                hs_, c = hsl(h)
                kb.op("pe", lambda e, hs_=hs_, c=c: e.matmul(bk[3][hs_, c * 128:(c + 1) * 128], Sb[hs_, c, 64:128], TT[hs_, R_ * 4 + c, :], start=True, stop=False),
                      reads=[PD + "TT", P + "Sb"], writes=[bkk[3]], inc=False)
                kb.op("pe", lambda e, h=h, hs_=hs_, c=c: e.matmul(bk[3][hs_, c * 128:(c + 1) * 128], Ub[:, HI(h), 64:128], cm["RBT"][:, HI(h), :], start=False, stop=True),
                      reads=K2("RBT") + [P + "Ub"], writes=[bkk[3]], inc=(h == 7))
            kb.op("act", lambda e: e.copy(out=OPt, in_=bk[3][:].rearrange("p (a t) -> p a t", a=4)), reads=[bkk[3]], writes=[P + "Yb"])
            kb.dma("sp", self.rec_op.ap()[t, :, :, :], OPt, reads=[P + "Yb"], writes=[("rec_op", t)])
            for h in range(8):
                hs_, c = hsl(h)
                kb.op("pe", lambda e, h=h, hs_=hs_, c=c: e.matmul(bk[4][hs_, c * 128:(c + 1) * 128], hb["Bh"][:, h * 64:(h + 1) * 64], Ub[:, HI(h), :], start=True, stop=False),
                      reads=[PD + "Bh", P + "Ub"], writes=[bkk[4]], inc=False)
                kb.op("pe", lambda e, h=h, hs_=hs_, c=c: e.matmul(bk[4][hs_, c * 128:c * 128 + 64], hb["Kh"][:, h * 64:(h + 1) * 64], Vb[:, h, 0:64], start=False, stop=True),
                      reads=[PD + "Kh", PD + "Vb"], writes=[bkk[4]], inc=(h == 7))
            kb.op("dve", lambda e: e.tensor_tensor(out=S[:], in0=S[:], in1=gct[:].unsqueeze(2).to_broadcast([128, 4, 128]), op=ALU.mult),
                  reads=[P + "S", PD + "gct"], writes=[P + "S"])
            kb.op("dve", lambda e: e.tensor_tensor(out=S[:], in0=S[:], in1=bk[4][:].rearrange("p (a t) -> p a t", a=4), op=ALU.add),
                  reads=[P + "S", bkk[4]], writes=[P + "S"])
            kb.op("act", lambda e: e.copy(out=Sb[:], in_=S[:]), reads=[P + "S"], writes=[P + "Sb"])
        X(0)
        for t in range(NT):
            if t + 1 < NT:
                X(t + 1)
            Y(t)
        kb.barrier()
        kb.pop_scope(m1)
        kb.dma("sp", self.cc_st_in[j].ap()[:, :], S[:].rearrange("p a b -> p (a b)"), reads=[P + "S"], writes=[P0 + "ccst"])
        kb.collective(self.cc_st_in[j].ap(), self.cc_st_out[j].ap(), [[0, 1, 2, 3], [4, 5, 6, 7]], reads=[P0 + "ccst"], writes=[P0 + "ccsto"])

        m2 = kb.push_scope()
        inproj("s", rows=NS)
        kb.dma("sp", self.out["shs"][j, :, :], zb[0:NS, :], reads=[P + "zb"], is_output=True)
        kb.dma("sp", prev[0:NS, :], I["sshift"][j, :, :], writes=[P + "prev"])
        elem(NS)
        RS = slice(0, NS)
        zs = prev
        kb.op("act", lambda e: e.activation(out=f["T1"][RS, :], in_=f["logw"][RS, :], func=AF.Exp), reads=[P + "logw"], writes=[P + "T1"])
        srcs = [(zs[RS, 0:512], P + "prev"), (f["T1"][RS, :], P + "T1"), (f["kf"][RS, :], P + "kf"), (zs[RS, 1024:1536], P + "prev"),
                (f["kk"][RS, :], P + "kk"), (f["a"][RS, :], P + "a")]
        vec = kb.sbuf(P + "vec", [128, 6, 64])
        St = kb.sbuf(P + "St", [128, 64, 64])
        Tm = kb.sbuf(P + "Tm", [128, 64, 64])
        gn2 = kb.sbuf(P + "gn2", [128, 2, 64])
        for h in range(8):
            HS = slice(h * 16, (h + 1) * 16)
            for qi, (sap, skey) in enumerate(srcs):
                kb.dma("sp", vec[HS, qi, :], sap[:, h * 64:(h + 1) * 64], reads=[skey], writes=[P + "vec"])
            kb.dma("sp", St[HS, :, :].rearrange("p a b -> p (a b)"), I["swkv"][j, :, h, :], writes=[P + "St"])
            kb.dma("sp", gn2[HS, 0, :], I["gn_gain"][j, h * 64:(h + 1) * 64].partition_broadcast(NS), writes=[P + "gn2"])
            kb.dma("sp", gn2[HS, 1, :], I["gn_bias"][j, h * 64:(h + 1) * 64].partition_broadcast(NS), writes=[P + "gn2"])
        sv = kb.sbuf(P + "sv", [128, 8, 64])
        s3 = kb.sbuf(P + "s3", [128, 8])

        def bi(x):
            return x.unsqueeze(1).to_broadcast([128, 64, 64])

        def bj(x):
            return x.unsqueeze(2).to_broadcast([128, 64, 64])
        r_, w_, k_, v_, kk_, a_ = (vec[:, i, :] for i in range(6))
        kb.op("dve", lambda e: e.tensor_tensor(out=Tm[:], in0=St[:], in1=bi(kk_), op=ALU.mult), reads=[P + "St", P + "vec"], writes=[P + "Tm"])
        kb.op("dve", lambda e: e.tensor_reduce(out=sv[:, 0, :], in_=Tm[:], axis=AX.X, op=ALU.add), reads=[P + "Tm"], writes=[P + "sv"])
        kb.op("pool", lambda e: e.tensor_tensor(out=St[:], in0=St[:], in1=bi(w_), op=ALU.mult), reads=[P + "St", P + "vec"], writes=[P + "St"])
        kb.op("dve", lambda e: e.tensor_tensor(out=sv[:, 1, :], in0=kk_, in1=a_, op=ALU.mult), reads=[P + "vec"], writes=[P + "sv"])
        kb.op("dve", lambda e: e.tensor_tensor(out=Tm[:], in0=bj(sv[:, 0, :]), in1=bi(sv[:, 1, :]), op=ALU.mult), reads=[P + "sv", P + "Tm"], writes=[P + "Tm"])
        kb.op("dve", lambda e: e.tensor_tensor(out=St[:], in0=St[:], in1=Tm[:], op=ALU.subtract), reads=[P + "St", P + "Tm"], writes=[P + "St"])
        kb.op("pool", lambda e: e.tensor_tensor(out=Tm[:], in0=bj(v_), in1=bi(k_), op=ALU.mult), reads=[P + "vec", P + "Tm"], writes=[P + "Tm"])
        kb.op("dve", lambda e: e.tensor_tensor(out=St[:], in0=St[:], in1=Tm[:], op=ALU.add), reads=[P + "St", P + "Tm"], writes=[P + "St"])
        for h in range(8):
            kb.dma("sp", self.out["wkvs"][j, :, h, :], St[h * 16:(h + 1) * 16, :, :].rearrange("p a b -> p (a b)"), reads=[P + "St"], is_output=True)
        kb.op("dve", lambda e: e.tensor_tensor(out=Tm[:], in0=St[:], in1=bi(r_), op=ALU.mult), reads=[P + "St", P + "vec"], writes=[P + "Tm"])
        kb.op("dve", lambda e: e.tensor_reduce(out=sv[:, 2, :], in_=Tm[:], axis=AX.X, op=ALU.add), reads=[P + "Tm"], writes=[P + "sv"])
        o_ = sv[:, 2, :]
        kb.op("dve", lambda e: e.tensor_reduce(out=s3[:, 0:1], in_=o_, axis=AX.X, op=ALU.add), reads=[P + "sv"], writes=[P + "s3"])
        kb.op("dve", lambda e: e.tensor_scalar(out=s3[:, 1:2], in0=s3[:, 0:1], scalar1=-1.0 / 64, scalar2=None, op0=ALU.mult), reads=[P + "s3"], writes=[P + "s3"])
        kb.op("dve", lambda e: e.tensor_scalar(out=sv[:, 3, :], in0=o_, scalar1=s3[:, 1:2], scalar2=None, op0=ALU.add), reads=[P + "sv", P + "s3"], writes=[P + "sv"])
        kb.op("dve", lambda e: e.tensor_tensor(out=sv[:, 4, :], in0=sv[:, 3, :], in1=sv[:, 3, :], op=ALU.mult), reads=[P + "sv"], writes=[P + "sv"])
        kb.op("dve", lambda e: e.tensor_reduce(out=s3[:, 2:3], in_=sv[:, 4, :], axis=AX.X, op=ALU.add), reads=[P + "sv"], writes=[P + "s3"])
        kb.op("act", lambda e: e.activation(out=s3[:, 3:4], in_=s3[:, 2:3], func=AF.Sqrt, bias=self.epsr[:, 2:3], scale=1.0 / 64), reads=[P + "s3", "epsr"], writes=[P + "s3"])
        kb.op("dve", lambda e: e.reciprocal(out=s3[:, 4:5], in_=s3[:, 3:4]), reads=[P + "s3"], writes=[P + "s3"])
        kb.op("dve", lambda e: e.scalar_tensor_tensor(out=sv[:, 5, :], in0=sv[:, 3, :], scalar=s3[:, 4:5], in1=gn2[:, 0, :], op0=ALU.mult, op1=ALU.mult),
              reads=[P + "sv", P + "s3", P + "gn2"], writes=[P + "sv"])
        kb.op("dve", lambda e: e.tensor_tensor(out=sv[:, 5, :], in0=sv[:, 5, :], in1=gn2[:, 1, :], op=ALU.add), reads=[P + "sv", P + "gn2"], writes=[P + "sv"])
        obs = kb.sbuf(P + "obs", [128, 512])
        kb.op("pool", lambda e: e.memset(obs[:], 0.0), writes=[P + "obs"])
        for h in range(8):
            kb.dma("sp", obs[0:NS, h * 64:(h + 1) * 64], sv[h * 16:(h + 1) * 16, 5, :], reads=[P + "sv"], writes=[P + "obs"])
        kb.op("dve", lambda e: e.tensor_tensor(out=obs[RS, :], in0=obs[RS, :], in1=f["bon"][RS, :], op=ALU.add), reads=[P + "obs", P + "bon"], writes=[P + "obs"])
        kb.op("dve", lambda e: e.tensor_tensor(out=obs[RS, :], in0=obs[RS, :], in1=f["gt"][RS, :], op=ALU.mult), reads=[P + "obs", P + "gt"], writes=[P + "obs"])
        kb.dma("sp", self.rec_ol.ap()[NT, :, :], obs[:], reads=[P + "obs"], writes=[("rec_ol", NT)])
        kb.barrier()
        kb.pop_scope(m2)
        kb.pop_scope(mark)

    def even_combine(self, L, P0):
        kb = self.kb
        I = self.inp
        j = L // 2
        P = P0 + "c"
        mark = kb.push_scope()
        gpost = self.load_bcast(P0 + "gpost", I["norm_mix_post"][L, :], D)
        gng = self.load_bcast(P + "gng", I["gn_gain"][j, :], 512)
        gnb = self.load_bcast(P + "gnb", I["gn_bias"][j, :], 512)
        wout = self.load_w_rows(P + "wout", I["w_out_even"][j], 8, D)
        wkeys = [(P + "wout", k) for k in range(8)]
        bk, bkk = self.bk(P)
        Hstb = kb.sbuf(P + "Hstb", [128, 4, 64], BF16)
        m0 = kb.push_scope()
        G4 = kb.sbuf(P + "G4", [128, 4, 4, 128])
        kb.dma("sp", G4[:].rearrange("p r c x -> p r (c x)"), self.cc_st_out[j].ap().rearrange("(r p) x -> p r x", p=128),
               reads=[P0 + "ccsto"], writes=[P + "G4"])
        PhT = kb.sbuf(P + "PhT", [128, 4, 4, 64])
        Hs = kb.sbuf(P + "Hs", [128, 5, 4, 64])
        Hst = kb.sbuf(P + "Hst", [128, 4, 64])
        idf = self.ident_f()
        for r in range(4):
            for h in range(8):
                hp_ = h % 2
                hs_ = slice(hp_ * 64, hp_ * 64 + 64)
                c = h // 2
                b = hp_ * 2 + r // 2
                o0 = ((r % 2) * 4 + c) * 64
                kb.op("pe", lambda e, r=r, hs_=hs_, c=c, b=b, o0=o0: e.matmul(bk[b][hs_, o0:o0 + 64], G4[hs_, r, c, 64:128], idf[hs_, hs_], start=True, stop=True),
                      reads=[P + "G4", "cst"], writes=[bkk[b]])
        for b in range(4):
            hs_ = slice((b // 2) * 64, (b // 2) * 64 + 64)
            r0 = (b % 2) * 2
            kb.op("dve", lambda e, b=b, hs_=hs_, r0=r0: e.tensor_copy(out=PhT[hs_, r0:r0 + 2, :, :].rearrange("p r c x -> p (r c x)"), in_=bk[b][hs_, :]),
                  reads=[bkk[b]], writes=[P + "PhT"])
        kb.op("dve", lambda e: e.tensor_copy(out=Hs[:, 1, :, :], in_=G4[:, 0, :, 0:64]), reads=[P + "G4"], writes=[P + "Hs"])
        for r in range(1, 4):
            for h in range(8):
                hp_ = h % 2
                hs_ = slice(hp_ * 64, hp_ * 64 + 64)
                c = h // 2
                kb.op("pe", lambda e, r=r, hs_=hs_, c=c, hp_=hp_: e.matmul(bk[4 + hp_][hs_, c * 64:(c + 1) * 64], PhT[hs_, r, c, :], Hs[hs_, r, c, :], start=True, stop=True),
                      reads=[P + "PhT", P + "Hs"], writes=[bkk[4 + hp_]])
            for hp_ in range(2):
                hs_ = slice(hp_ * 64, hp_ * 64 + 64)
                kb.op("dve", lambda e, r=r, hs_=hs_, hp_=hp_: e.tensor_tensor(out=Hs[hs_, r + 1, :, :], in0=bk[4 + hp_][hs_, 0:256].rearrange("p (c x) -> p c x", c=4),
                                                                            in1=G4[hs_, r, :, 0:64], op=ALU.add), reads=[bkk[4 + hp_], P + "G4", P + "Hs"], writes=[P + "Hs"])
        sel = self.cst[:, C_SELH:C_SELH + 4]
        kb.op("dve", lambda e: e.tensor_scalar(out=Hst[:], in0=Hs[:, 1, :, :], scalar1=sel[:, 1:2], scalar2=None, op0=ALU.mult), reads=[P + "Hs", "cst"], writes=[P + "Hst"])
        for r in (2, 3):
            kb.op("dve", lambda e, r=r: e.scalar_tensor_tensor(out=Hst[:], in0=Hs[:, r, :, :], scalar=sel[:, r:r + 1], in1=Hst[:], op0=ALU.mult, op1=ALU.add),
                  reads=[P + "Hs", "cst", P + "Hst"], writes=[P + "Hst"])
        kb.op("act", lambda e: e.copy(out=Hstb[:], in_=Hst[:]), reads=[P + "Hst"], writes=[P + "Hstb"])
        o4 = kb.sbuf(P + "o4", [64, 2, 4, 64])
        for h in range(8):
            hp_ = h % 2
            hs_ = slice(hp_ * 64, hp_ * 64 + 64)
            c = h // 2
            kb.op("pe", lambda e, hs_=hs_, c=c, hp_=hp_: e.matmul(bk[6 + hp_][0:64, c * 64:(c + 1) * 64], Hs[hs_, 4, c, :], idf[hs_, hs_], start=True, stop=True),
                  reads=[P + "Hs", "cst"], writes=[bkk[6 + hp_]])
        for hp_ in range(2):
            kb.op("dve", lambda e, hp_=hp_: e.tensor_copy(out=o4[:, hp_, :, :].rearrange("p c x -> p (c x)"), in_=bk[6 + hp_][0:64, 0:256]), reads=[bkk[6 + hp_]], writes=[P + "o4"])
        for h in range(8):
            kb.dma("sp", self.out["wkvp"][j, h, :, :], o4[:, h % 2, h // 2, :], reads=[P + "o4"], is_output=True)
        kb.barrier()
        kb.pop_scope(m0)

        rec = [{n: kb.sbuf(P + n + str(i), [128, 512]) for n in ("ol", "bo", "g", "oa")} for i in range(2)]
        opt = [kb.sbuf(P + "opt%d" % i, [128, 4, 128], BF16) for i in range(2)]
        o = kb.sbuf(P + "o", [128, 512])
        xc = kb.sbuf(P + "xc", [128, 512])
        sq = kb.sbuf(P + "sq", [128, 512])
        ycat = kb.sbuf(P + "ycat", [128, D], BF16)
        oT = kb.sbuf(P + "oT", [128, 8, 128], BF16)
        tmp = kb.sbuf(P + "tmp", [128, D])
        sm = kb.sbuf(P + "sm", [128, 32])
        small = kb.sbuf(P0 + "small", [128, 4])
        units = [(t, 128) for t in range(NT)] + [("s", NS)]
        for i, (t, rows) in enumerate(units):
            b = i % 2
            ti = NT if t == "s" else t
            R = slice(0, rows)
            rk_ = {n: P + n + str(b) for n in ("ol", "bo", "g", "oa")}
            kb.dma("sp", rec[b]["ol"][:], self.rec_ol.ap()[ti, :, :], reads=[("rec_ol", ti)], writes=[rk_["ol"]])
            kb.dma("sp", rec[b]["oa"][:], self.rec_oa.ap()[ti, :, :], reads=[("rec_oa", ti)], writes=[rk_["oa"]])
            if t != "s":
                kb.dma("sp", rec[b]["bo"][:], self.rec_bo.ap()[ti, :, :], reads=[("rec_bo", ti)], writes=[rk_["bo"]])
                kb.dma("sp", rec[b]["g"][:], self.rec_g.ap()[ti, :, :], reads=[("rec_g", ti)], writes=[rk_["g"]])
                kb.dma("sp", opt[b][:], self.rec_op.ap()[ti, :, :, :], reads=[("rec_op", ti)], writes=[P + "opt%d" % b])
                for h in range(8):
                    hp_ = h % 2
                    hs_ = slice(hp_ * 64, hp_ * 64 + 64)
                    c = h // 2
                    kb.op("pe", lambda e, hs_=hs_, c=c, b=b, hp_=hp_: e.matmul(bk[6 + hp_][:, c * 64:(c + 1) * 64], opt[b][hs_, c, :], Hstb[hs_, c, :], start=True, stop=True),
                          reads=[P + "opt%d" % b, P + "Hstb"], writes=[bkk[6 + hp_]])
                for hp_ in range(2):
                    kb.op("dve", lambda e, b=b, hp_=hp_: e.tensor_tensor(
                        out=o[:].rearrange("p (c q d) -> p c q d", c=4, q=2)[:, :, hp_, :], in0=bk[6 + hp_][:, 0:256].rearrange("p (c d) -> p c d", c=4),
                        in1=rec[b]["ol"][:].rearrange("p (c q d) -> p c q d", c=4, q=2)[:, :, hp_, :], op=ALU.add),
                        reads=[bkk[6 + hp_], rk_["ol"]], writes=[P + "o"])
                if "ob" in self.dbg:
                    pass
                o3 = o[:].rearrange("p (h d) -> p h d", h=8)
                x3 = xc[:].rearrange("p (h d) -> p h d", h=8)
                kb.op("dve", lambda e: e.tensor_reduce(out=sm[:, 0:8], in_=o3, axis=AX.X, op=ALU.add), reads=[P + "o"], writes=[P + "sm"])
                kb.op("dve", lambda e: e.tensor_scalar(out=sm[:, 8:16], in0=sm[:, 0:8], scalar1=-1.0 / 64, scalar2=None, op0=ALU.mult), reads=[P + "sm"], writes=[P + "sm"])
                kb.op("dve", lambda e: e.tensor_tensor(out=x3, in0=o3, in1=sm[:, 8:16].unsqueeze(2).to_broadcast([128, 8, 64]), op=ALU.add),
                      reads=[P + "o", P + "sm"], writes=[P + "xc"])
                kb.op("pool", lambda e: e.tensor_tensor(out=sq[:], in0=xc[:], in1=xc[:], op=ALU.mult), reads=[P + "xc"], writes=[P + "sq"])
                kb.op("dve", lambda e: e.tensor_reduce(out=sm[:, 16:24], in_=sq[:].rearrange("p (h d) -> p h d", h=8), axis=AX.X, op=ALU.add),
                      reads=[P + "sq"], writes=[P + "sm"])
                kb.op("act", lambda e: e.activation(out=sm[:, 16:24], in_=sm[:, 16:24], func=AF.Sqrt, bias=self.epsr[:, 2:3], scale=1.0 / 64),
                      reads=[P + "sm", "epsr"], writes=[P + "sm"])
                kb.op("dve", lambda e: e.reciprocal(out=sm[:, 24:32], in_=sm[:, 16:24]), reads=[P + "sm"], writes=[P + "sm"])
                kb.op("dve", lambda e: e.tensor_tensor(out=x3, in0=x3, in1=sm[:, 24:32].unsqueeze(2).to_broadcast([128, 8, 64]), op=ALU.mult),
                      reads=[P + "xc", P + "sm"], writes=[P + "xc"])
                kb.op("pool", lambda e: e.tensor_tensor(out=xc[:], in0=xc[:], in1=gng[:], op=ALU.mult), reads=[P + "xc", P + "gng"], writes=[P + "xc"])
                kb.op("pool", lambda e: e.tensor_tensor(out=xc[:], in0=xc[:], in1=gnb[:], op=ALU.add), reads=[P + "xc", P + "gnb"], writes=[P + "xc"])
                kb.op("dve", lambda e, b=b: e.tensor_tensor(out=xc[:], in0=xc[:], in1=rec[b]["bo"][:], op=ALU.add), reads=[P + "xc", rk_["bo"]], writes=[P + "xc"])
                kb.op("dve", lambda e, b=b: e.tensor_tensor(out=ycat[:, 512:1024], in0=xc[:], in1=rec[b]["g"][:], op=ALU.mult), reads=[P + "xc", rk_["g"]], writes=[P + "ycat"])
                if "ob" in self.dbg:
                    kb.op("dve", lambda e, b=b: e.tensor_tensor(out=sq[:], in0=xc[:], in1=rec[b]["g"][:], op=ALU.mult), reads=[P + "xc", rk_["g"]], writes=[P + "sq"])
                    kb.dma("sp", self.out["dbg_ob"][t * 128:(t + 1) * 128, :], sq[:], reads=[P + "sq"], is_output=True)
            else:
                kb.op("act", lambda e, b=b: e.copy(out=ycat[R, 512:1024], in_=rec[b]["ol"][R, :]), reads=[rk_["ol"]], writes=[P + "ycat"])
            kb.op("act", lambda e, b=b: e.copy(out=ycat[R, 0:512], in_=rec[b]["oa"][R, :]), reads=[rk_["oa"]], writes=[P + "ycat"])
            pv = bk[1][:].bitcast(BF16).rearrange("p (k t) -> p k t", k=8)
            for k in range(8):
                kb.op("pe", lambda e, k=k: e.transpose(pv[:, k, 0:rows], ycat[R, k * 128:(k + 1) * 128], self.identb[R, R]),
                      reads=[P + "ycat", "identb"], writes=[bkk[1]], inc=(k == 7))
            kb.op("act", lambda e: e.copy(out=oT[:, :, 0:rows], in_=pv[:, :, 0:rows]), reads=[bkk[1]], writes=[P + "oT"])
            pyb = (bk[2], bk[3]) if b == 0 else (bk[4], bk[5])
            pyk = (bkk[2], bkk[3]) if b == 0 else (bkk[4], bkk[5])
            for half in range(2):
                for k in range(8):
                    kb.op("pe", lambda e, k=k, half=half: e.matmul(pyb[half][R, :], oT[:, k, 0:rows], wout[:, k, half * 512:(half + 1) * 512],
                                                                   start=(k == 0), stop=(k == 7)),
                          reads=[P + "oT"] + wkeys, writes=[pyk[half]], inc=(k == 7))
            if t == "s":
                dst, dk = self.hs[0:NS, :], "hs"
            else:
                dst, dk = self.hp[:, t, :], ("hp", t)
            self.post_norm_residual2(P0, pyb, pyk, gpost, dst, dk, tmp, P + "tmp", small, rows)
        kb.barrier()
        kb.pop_scope(mark)

    def post_norm_residual2(self, pfx, pyb, pyk, gain, dst_ap, dst_key, tmp, tmp_key, small, rows=128):
        kb = self.kb
        sk = pfx + "small"
        R = slice(0, rows)
        for half in range(2):
            kb.op("act", lambda e, half=half: e.activation(out=tmp[R, half * 512:(half + 1) * 512], in_=pyb[half][R, :], func=AF.Square,
                                                           accum_out=small[R, half:half + 1]),
                  reads=[pyk[half]], writes=[tmp_key, sk])
        kb.op("dve", lambda e: e.tensor_tensor(out=small[R, 0:1], in0=small[R, 0:1], in1=small[R, 1:2], op=ALU.add), reads=[sk], writes=[sk])
        kb.op("act", lambda e: e.activation(out=small[R, 1:2], in_=small[R, 0:1], func=AF.Sqrt, bias=self.epsr[R, 0:1], scale=1.0 / D),
              reads=[sk, "epsr"], writes=[sk])
        kb.op("dve", lambda e: e.reciprocal(out=small[R, 2:3], in_=small[R, 1:2]), reads=[sk], writes=[sk])
        for half in range(2):
            kb.op("dve", lambda e, half=half: e.scalar_tensor_tensor(out=tmp[R, half * 512:(half + 1) * 512], in0=pyb[half][R, :],
                                                                     scalar=small[R, 2:3], in1=gain[R, half * 512:(half + 1) * 512],
                                                                     op0=ALU.mult, op1=ALU.mult),
                  reads=[pyk[half], sk, pfx + "gpost"], writes=[tmp_key])
        kb.op("pool", lambda e: e.tensor_tensor(out=dst_ap, in0=dst_ap, in1=tmp[R, :], op=ALU.add), reads=[tmp_key, dst_key], writes=[dst_key])

    def write_y(self):
        kb = self.kb
        for t in range(NT):
            kb.dma("sp", self.out["yp"][t * 128:(t + 1) * 128, :], self.hp[:, t, :], reads=[("hp", t)],
                   is_output=True)
        kb.dma("sp", self.out["ys"][:, :], self.hs[0:NS, :], reads=["hs"], is_output=True)

    def build(self):
        self.declare()
        self.setup()
        for kind, L in self.plan:
            if kind == "ffn":
                self.ffn(L)
            elif kind == "odd":
                self.odd(L)
            elif kind == "even":
                self.even(L)
        self.write_y()
        self.kb.finish()
        self.kb.close()
        return self.nc


def make_consts(c):
    q = c % 4
    cst = np.zeros((128, NCONST), np.float32)
    cst[:, C_ID:C_ID + 128] = np.eye(128, dtype=np.float32)
    s = np.arange(128)[:, None]
    t = np.arange(128)[None, :]
    cst[:, C_TRI:C_TRI + 128] = (s <= t)
    cst[:, C_TRS:C_TRS + 128] = (s < t)
    cst[:, C_LOW:C_LOW + 128] = (s > t)
    cst[:, C_ONE:C_ONE + 128] = 1.0
    qi = np.arange(128)[:, None]
    kj = np.arange(256)[None, :]
    vis = (kj >= qi) & (kj <= qi + 128)
    cst[:, C_MA:C_MA + 256] = np.where(vis, 0.0, NEG)
    vis0 = vis & ((q > 0) | (kj >= 128))
    cst[:, C_M0:C_M0 + 256] = np.where(vis0, 0.0, NEG)
    inv = np.power(np.float32(10000.0), -np.arange(32, dtype=np.float32) / np.float32(32)).astype(np.float32)
    pos = (q * 2048 + np.arange(2048)).astype(np.float32)
    ang = (pos[:, None] * inv[None, :]).astype(np.float32)
    cst[:, C_COS:C_COS + 512] = np.cos(ang).astype(np.float32).reshape(16, 128, 32).transpose(1, 0, 2).reshape(128, 512)
    cst[:, C_SIN:C_SIN + 512] = np.sin(ang).astype(np.float32).reshape(16, 128, 32).transpose(1, 0, 2).reshape(128, 512)
    angs = (np.float32(8192.0) * inv).astype(np.float32)
    cst[:, C_COSS:C_COSS + 32] = np.cos(angs)[None, :]
    cst[:, C_SINS:C_SINS + 32] = np.sin(angs)[None, :]
    if q > 0:
        cst[:, C_SELP + q - 1] = 1.0
    cst[:, C_SELH + q] = 1.0
    return cst


_WEIGHT_KEYS = ["norm_mix_pre", "norm_mix_post", "norm_ffn_pre", "norm_ffn_post", "w_in_even", "attn_sinks", "shift_mu",
                "decay_w0", "decay_w2", "iclr_a0", "iclr_a2", "gate_g2", "key_k", "key_a", "gn_gain", "gn_bias",
                "w_out_even", "w_in_odd", "sgu_ln_gain", "sgu_ln_bias", "sgu_w", "sgu_b", "w_out_odd", "ffn_w_gate",
                "ffn_w_up", "ffn_conv_w", "ffn_conv_b", "ffn_w_down"]

FULL_PLAN = [("even", 0), ("ffn", 0), ("odd", 1), ("ffn", 1), ("even", 2), ("ffn", 2), ("odd", 3), ("ffn", 3)]


def _core_inputs(c, inp, shared):
    b, q = c // 4, c % 4
    f = lambda a: np.ascontiguousarray(a, dtype=np.float32)
    d = dict(shared)
    d["xp"] = f(inp["x_prompt"][b, q * 2048:(q + 1) * 2048])
    d["xs"] = f(inp["x_sample"][c * NS:(c + 1) * NS, 0])
    d["ck"] = f(inp["cache_win_k"][:, c * NS:(c + 1) * NS].reshape(2, NS, 128, 128))
    d["cv"] = f(inp["cache_win_v"][:, c * NS:(c + 1) * NS].reshape(2, NS, 128, 128))
    d["swkv"] = f(inp["state_wkv"][:, c * NS:(c + 1) * NS].reshape(2, NS, 8, 4096))
    d["sshift"] = f(inp["state_shift"][:, c * NS:(c + 1) * NS])
    d["sconv"] = f(inp["state_ffn_conv"][:, c * NS:(c + 1) * NS])
    d["consts"] = make_consts(c)
    return d


def kernel(**inputs):
    inp = {k: np.asarray(v) for k, v in inputs.items()}
    shared = {k: np.ascontiguousarray(inp[k], dtype=np.float32) for k in _WEIGHT_KEYS}
    shared["bonus_r_k"] = np.ascontiguousarray(inp["bonus_r_k"], dtype=np.float32).reshape(2, 512)
    prog = Prog(FULL_PLAN)
    nc = prog.build()
    in_maps = [_core_inputs(c, inp, shared) for c in range(8)]
    res = run_bass_kernel_spmd(nc, in_maps, core_ids=list(range(8))).results
    f32 = np.float32
    y_prompt = np.zeros((2, 8192, D), f32)
    y_sample = np.zeros((128, 1, D), f32)
    wkp = np.zeros((2, 2, 128, 2, 64), f32)
    wvp = np.zeros((2, 2, 128, 2, 64), f32)
    wks = np.zeros((2, 128, 1, 2, 64), f32)
    wvs = np.zeros((2, 128, 1, 2, 64), f32)
    wkvp = np.zeros((2, 2, 8, 64, 64), f32)
    wkvs = np.zeros((2, 128, 8, 64, 64), f32)
    shp = np.zeros((2, 2, BPROJ), f32)
    shs = np.zeros((2, 128, BPROJ), f32)
    sguv = np.zeros((2, 128, 1, D), f32)
    convp = np.zeros((4, 2, 2, DFF), f32)
    convs = np.zeros((4, 128, 2, DFF), f32)
    for c in range(8):
        b, q = c // 4, c % 4
        r = res[c]
        sl = slice(c * NS, (c + 1) * NS)
        y_prompt[b, q * 2048:(q + 1) * 2048] = r["yp"]
        y_sample[sl, 0] = r["ys"]
        wks[:, sl, 0] = r["wks"].reshape(2, NS, 2, 64)
        wvs[:, sl, 0] = r["wvs"].reshape(2, NS, 2, 64)
        wkvs[:, sl] = r["wkvs"].reshape(2, NS, 8, 64, 64)
        shs[:, sl] = r["shs"]
        sguv[:, sl, 0] = r["sguv"]
        convs[:, sl] = r["convs"]
        if q == 3:
            wkp[:, b] = r["wkp"].reshape(2, 128, 2, 64)
            wvp[:, b] = r["wvp"].reshape(2, 128, 2, 64)
            wkvp[:, b] = r["wkvp"]
            shp[:, b] = r["shp"]
            convp[:, b] = r["convp"]
    return (y_prompt, y_sample, wkp, wvp, wks, wvs, wkvp, wkvs, shp, shs, sguv, convp, convs)
```

```python
import numpy as np
import concourse.bass as bass
import concourse.mybir as mybir
from concourse.bass_utils import run_bass_kernel_spmd

F32 = mybir.dt.float32
BF16 = mybir.dt.bfloat16
AF = mybir.ActivationFunctionType
ALU = mybir.AluOpType
AX = mybir.AxisListType

NT = 16
NS = 16
D = 1024
DFF = 2816
NFC = 22
EVEN_PROJ = 2560
BPROJ = 1792
RMS_EPS = 1e-6
LN_EPS = 1e-5
GN_EPS = 64e-5
DECAY_SCALE = 0.606531
NEG = -1e30

C_ID = 0
C_TRI = 128
C_TRS = 256
C_LOW = 384
C_ONE = 512
C_MA = 640
C_M0 = 896
C_SELP = 1152
C_SELH = 1156
NCST = 1160
C_COS = 1160
C_SIN = 1672
C_COSS = 2184
C_SINS = 2216
NCONST = 2248


class _Eng:
    def __init__(self, name, handle, is_pe=False):
        self.name = name
        self.h = handle
        self.is_pe = is_pe
        self.sem = None
        self.count = 0
        self.waited = {}
        self.dsems = []
        self.dcount = []
        self.dnext = 0


class KB:
    def __init__(self, nc, n_dma_sems=8, n_cc=12):
        self.nc = nc
        self.E = {
            "pe": _Eng("pe", nc.tensor, True),
            "act": _Eng("act", nc.scalar),
            "dve": _Eng("dve", nc.vector),
            "pool": _Eng("pool", nc.gpsimd),
            "sp": _Eng("sp", nc.sync),
        }
        self.lastw = {}
        self.readers = {}
        self.ctx = []
        self.sem_by_id = {}
        self.out_events = []
        self.n_ops = 0
        self.block = self.enter(nc.Block())
        for name, e in self.E.items():
            e.sem = self.enter(nc.semaphore("sem_" + name))
            self.sem_by_id[id(e.sem)] = e.sem
        for name in ("sp", "pool"):
            e = self.E[name]
            for i in range(n_dma_sems):
                s = self.enter(nc.semaphore("dsem_%s_%d" % (name, i)))
                self.sem_by_id[id(s)] = s
                e.dsems.append(s)
                e.dcount.append(0)
        self.cc_sems = []
        for i in range(n_cc):
            s = self.enter(nc.semaphore("ccsem_%d" % i))
            self.sem_by_id[id(s)] = s
            self.cc_sems.append(s)
        self.cc_used = 0
        self.all_events = {}
        self.exclusive = set()

    def enter(self, cm):
        v = cm.__enter__()
        self.ctx.append(cm)
        return v

    def push_scope(self):
        return len(self.ctx)

    def pop_scope(self, mark):
        while len(self.ctx) > mark:
            self.ctx.pop().__exit__(None, None, None)

    def sbuf(self, name, shape, dtype=F32):
        return self.enter(self.nc.sbuf_tensor(name, list(shape), dtype))

    def psum(self, name, shape, dtype=F32):
        self.exclusive.add(name)
        return self.enter(self.nc.psum_tensor(name, list(shape), dtype))

    def _deps(self, reads, writes, raw_keys=()):
        deps = {}
        raw = {}

        def add(sid, val, is_raw):
            if deps.get(sid, 0) < val:
                deps[sid] = val
            if is_raw and raw.get(sid, 0) < val:
                raw[sid] = val

        for r in reads:
            ev = self.lastw.get(r)
            if ev is not None:
                add(ev[0], ev[1], True)
        for w in writes:
            ev = self.lastw.get(w)
            if ev is not None:
                add(ev[0], ev[1], w in raw_keys)
            for sid, val in self.readers.get(w, {}).items():
                add(sid, val, False)
        self._raw = raw
        return deps

    def _record(self, ev, reads, writes):
        sid, val = ev
        if self.all_events.get(sid, 0) < val:
            self.all_events[sid] = val
        for r in reads:
            d = self.readers.setdefault(r, {})
            if d.get(sid, 0) < val:
                d[sid] = val
        for w in writes:
            self.lastw[w] = ev
            self.readers[w] = {}

    def _emit_waits(self, e, deps, raw=None):
        for sid, val in deps.items():
            if sid == id(e.sem):
                if e.is_pe:
                    continue
                if raw is not None:
                    val = raw.get(sid, 0)
                    if val == 0:
                        continue
            if e.waited.get(sid, 0) >= val:
                continue
            e.waited[sid] = val
            e.h.wait_ge(self.sem_by_id[sid], val)

    def op(self, eng, fn, reads=(), writes=(), inc=True):
        e = self.E[eng]
        ex = [r for r in reads if r in self.exclusive]
        if ex:
            writes = list(writes) + ex
        deps = self._deps(reads, writes, raw_keys=ex)
        self._emit_waits(e, deps, self._raw)
        if inc:
            e.count += 1
            val = e.count
        else:
            val = e.count + 1
        ins = fn(e.h)
        if inc:
            ins.then_inc(e.sem, 1)
        ev = (id(e.sem), val)
        self._record(ev, reads, writes)
        self.n_ops += 1
        return ev

    def dma(self, eng, out, in_, reads=(), writes=(), is_output=False, **kw):
        e = self.E[eng]
        deps = self._deps(reads, writes)
        i = e.dnext
        e.dnext = (e.dnext + 1) % len(e.dsems)
        s = e.dsems[i]
        if e.dcount[i] > 0:
            deps[id(s)] = max(deps.get(id(s), 0), 16 * e.dcount[i])
        self._emit_waits(e, deps)
        e.dcount[i] += 1
        val = 16 * e.dcount[i]
        e.h.dma_start(out=out, in_=in_, **kw).then_inc(s, 16)
        ev = (id(s), val)
        self._record(ev, reads, writes)
        if is_output:
            self.out_events.append(ev)
        self.n_ops += 1
        return ev

    def collective(self, in_ap, out_ap, groups, reads=(), writes=()):
        e = self.E["pool"]
        self._emit_waits(e, self._deps(reads, writes))
        s = self.cc_sems[self.cc_used]
        self.cc_used += 1
        e.h.collective_compute("AllGather", ALU.bypass, replica_groups=groups,
                               ins=[in_ap], outs=[out_ap]).then_inc(s)
        ev = (id(s), 1)
        self._record(ev, reads, writes)
        return ev

    def barrier(self):
        for name, e in self.E.items():
            self._emit_waits(e, dict(self.all_events))

    def finish(self):
        e = self.E["sp"]
        self._emit_waits(e, dict(self.all_events))

    def close(self):
        self.pop_scope(0)


def _bcast_row(ap1d, n=128):
    return ap1d.partition_broadcast(n)


class Prog:
    def __init__(self, plan, dbg=()):
        self.plan = plan
        self.dbg = dbg
        nc = self.nc = bass.Bass("TRN2", target_bir_lowering=False)
        self.kb = KB(nc)
        self.inp = {}
        self.out = {}
        self.uid = 0

    def din(self, name, shape):
        self.inp[name] = self.nc.dram_tensor(name, list(shape), F32, kind="ExternalInput").ap()
        return self.inp[name]

    def dout(self, name, shape):
        self.out[name] = self.nc.dram_tensor(name, list(shape), F32, kind="ExternalOutput").ap()
        return self.out[name]

    def nm(self, s):
        self.uid += 1
        return "%s_%d" % (s, self.uid)

    def declare(self):
        di = self.din
        di("xp", [NT * 128, D]); di("xs", [NS, D])
        di("ck", [2, NS, 128, 128]); di("cv", [2, NS, 128, 128])
        di("swkv", [2, NS, 8, 4096]); di("sshift", [2, NS, BPROJ]); di("sconv", [4, NS, 2, DFF])
        di("norm_mix_pre", [4, D]); di("norm_mix_post", [4, D]); di("norm_ffn_pre", [4, D]); di("norm_ffn_post", [4, D])
        di("w_in_even", [2, D, EVEN_PROJ]); di("attn_sinks", [2, 8]); di("shift_mu", [2, BPROJ])
        di("decay_w0", [2, 512]); di("decay_w2", [2, 64, 512]); di("iclr_a0", [2, 512]); di("iclr_a2", [2, 64, 512])
        di("gate_g2", [2, 128, 512]); di("key_k", [2, 512]); di("key_a", [2, 512]); di("bonus_r_k", [2, 512])
        di("gn_gain", [2, 512]); di("gn_bias", [2, 512]); di("w_out_even", [2, D, D])
        di("w_in_odd", [2, D, 2 * D]); di("sgu_ln_gain", [2, D]); di("sgu_ln_bias", [2, D])
        di("sgu_w", [2, 8, 128, 128]); di("sgu_b", [2, 8, 128]); di("w_out_odd", [2, D, D])
        di("ffn_w_gate", [4, D, DFF]); di("ffn_w_up", [4, D, DFF]); di("ffn_conv_w", [4, 3, DFF])
        di("ffn_conv_b", [4, DFF]); di("ffn_w_down", [4, DFF, D])
        di("consts", [128, NCONST])
        do = self.dout
        do("yp", [NT * 128, D]); do("ys", [NS, D])
        do("wkp", [2, 128, 128]); do("wvp", [2, 128, 128]); do("wks", [2, NS, 128]); do("wvs", [2, NS, 128])
        do("wkvp", [2, 8, 64, 64]); do("wkvs", [2, NS, 8, 4096]); do("shp", [2, BPROJ]); do("shs", [2, NS, BPROJ])
        do("sguv", [2, NS, D]); do("convp", [4, 2, DFF]); do("convs", [4, NS, 2, DFF])
        nc = self.nc
        self.cc_ffn_in = [nc.dram_tensor("ccfi%d" % l, [128, 44], F32) for l in range(4)]
        self.cc_ffn_out = [nc.dram_tensor("ccfo%d" % l, [512, 44], F32) for l in range(4)]
        self.cc_att_in = [nc.dram_tensor("ccai%d" % l, [128, 256], F32) for l in range(2)]
        self.cc_att_out = [nc.dram_tensor("ccao%d" % l, [512, 256], F32) for l in range(2)]
        self.cc_sh_in = [nc.dram_tensor("ccshi%d" % l, [1, BPROJ], F32) for l in range(2)]
        self.cc_sh_out = [nc.dram_tensor("ccsho%d" % l, [4, BPROJ], F32) for l in range(2)]
        self.zl_dram = [nc.dram_tensor("zl%d" % l, [1, BPROJ], F32) for l in range(2)]
        self.rec_op = nc.dram_tensor("rec_op", [NT, 128, 4, 128], BF16)
        if "oa" in self.dbg:
            self.dout("dbg_oa", [NT * 128, 512])
        if "ob" in self.dbg:
            self.dout("dbg_ob", [NT * 128, 512])
        self.cc_st_in = [nc.dram_tensor("ccsi%d" % l, [128, 512], F32) for l in range(2)]
        self.cc_st_out = [nc.dram_tensor("ccso%d" % l, [512, 512], F32) for l in range(2)]
        self.rec_oa = nc.dram_tensor("rec_oa", [NT + 1, 128, 512], F32)
        self.rec_ol = nc.dram_tensor("rec_ol", [NT + 1, 128, 512], F32)
        self.rec_bo = nc.dram_tensor("rec_bo", [NT + 1, 128, 512], F32)
        self.rec_g = nc.dram_tensor("rec_g", [NT + 1, 128, 512], F32)

    def setup(self):
        kb = self.kb
        self.hp = kb.sbuf("hp", [128, NT, D])
        self.hs = kb.sbuf("hs", [128, D])
        self.cst = kb.sbuf("cst", [128, NCST])
        self.identb = kb.sbuf("identb", [128, 128], BF16)
        self.epsr = kb.sbuf("epsr", [128, 4])
        kb.dma("sp", self.cst[:], self.inp["consts"][:, 0:NCST], writes=["cst"])
        for t in range(NT):
            kb.dma("sp", self.hp[:, t, :], self.inp["xp"][t * 128:(t + 1) * 128, :], writes=[("hp", t)])
        kb.dma("sp", self.hs[0:NS, :], self.inp["xs"][:, :], writes=["hs"])
        kb.op("dve", lambda e: e.tensor_copy(out=self.identb[:], in_=self.cst[:, C_ID:C_ID + 128]),
              reads=["cst"], writes=["identb"])
        kb.op("pool", lambda e: e.memset(self.epsr[:, 0:1], RMS_EPS), writes=["epsr"])
        kb.op("pool", lambda e: e.memset(self.epsr[:, 1:2], LN_EPS), writes=["epsr"])
        kb.op("pool", lambda e: e.memset(self.epsr[:, 2:3], GN_EPS), writes=["epsr"])
        kb.op("pool", lambda e: e.memset(self.epsr[:, 3:4], 0.0), writes=["epsr"])

    def ident_f(self):
        return self.cst[:, C_ID:C_ID + 128]

    def prenorm_stats(self, pfx, junk=None):
        kb = self.kb
        ss = kb.sbuf(pfx + "ss", [128, NT + 1])
        rstd = kb.sbuf(pfx + "rstd", [128, NT + 1])
        mj = kb.push_scope()
        junk = kb.sbuf(pfx + "junk", [128, D], BF16)
        kb.op("pool", lambda e: e.memset(ss[:], 0.0), writes=[pfx + "ss"])
        for t in range(NT):
            kb.op("act", lambda e, t=t: e.activation(out=junk[:], in_=self.hp[:, t, :], func=AF.Square,
                                                     accum_out=ss[:, t:t + 1]),
                  reads=[("hp", t)], writes=[pfx + "junk", pfx + "ss"])
        kb.op("act", lambda e: e.activation(out=junk[0:NS, :], in_=self.hs[0:NS, :], func=AF.Square,
                                            accum_out=ss[0:NS, NT:NT + 1]),
              reads=["hs"], writes=[pfx + "junk", pfx + "ss"])
        kb.op("act", lambda e: e.activation(out=rstd[:], in_=ss[:], func=AF.Sqrt, bias=self.epsr[:, 0:1],
                                            scale=1.0 / D), reads=[pfx + "ss", "epsr"], writes=[pfx + "rstd"])
        kb.op("dve", lambda e: e.reciprocal(out=rstd[:], in_=rstd[:]), reads=[pfx + "rstd"], writes=[pfx + "rstd"])
        kb.barrier()
        kb.pop_scope(mj)
        return rstd

    def load_bcast(self, name, src1d, n):
        kb = self.kb
        t = kb.sbuf(name, [128, n])
        kb.dma("sp", t[:], _bcast_row(src1d), writes=[name])
        return t

    def norm_transpose(self, pfx, src_ap, src_key, rstd_col, gain, xn, xn_key, psT, ps_key, dst_ap, dst_key,
                       rows=128, c0=0, c1=None):
        kb = self.kb
        kb.op("dve", lambda e: e.scalar_tensor_tensor(out=xn[0:rows, :], in0=src_ap, scalar=rstd_col,
                                                      in1=gain[0:rows, :], op0=ALU.mult, op1=ALU.mult),
              reads=[src_key, pfx + "rstd", pfx + "gain"], writes=[xn_key])
        pv = psT[:].bitcast(BF16).rearrange("p (k t) -> p k t", k=8)
        for k in range(8):
            kb.op("pe", lambda e, k=k: e.transpose(pv[:, k, 0:rows], xn[0:rows, k * 128:(k + 1) * 128],
                                                   self.identb[0:rows, 0:rows]),
                  reads=[xn_key, "identb"], writes=[ps_key], inc=(k == 7))
        if c1 is None:
            c1 = rows
        kb.op("act", lambda e: e.copy(out=dst_ap, in_=pv[:, :, c0:c1]), reads=[ps_key], writes=[dst_key])

    def post_norm_residual(self, pfx, ps2, ps_key, gain, dst_ap, dst_key, tmp, tmp_key, small, rows=128):
        kb = self.kb
        sk = pfx + "small"
        kb.op("act", lambda e: e.activation(out=tmp[0:rows, :], in_=ps2[0:rows, :], func=AF.Square,
                                            accum_out=small[0:rows, 0:1]),
              reads=[ps_key], writes=[tmp_key, sk])
        kb.op("act", lambda e: e.activation(out=small[0:rows, 1:2], in_=small[0:rows, 0:1], func=AF.Sqrt,
                                            bias=self.epsr[0:rows, 0:1], scale=1.0 / D),
              reads=[sk, "epsr"], writes=[sk])
        kb.op("dve", lambda e: e.reciprocal(out=small[0:rows, 2:3], in_=small[0:rows, 1:2]), reads=[sk], writes=[sk])
        kb.op("dve", lambda e: e.scalar_tensor_tensor(out=tmp[0:rows, :], in0=ps2[0:rows, :],
                                                      scalar=small[0:rows, 2:3], in1=gain[0:rows, :],
                                                      op0=ALU.mult, op1=ALU.mult),
              reads=[ps_key, sk, pfx + "gpost"], writes=[tmp_key])
        kb.op("pool", lambda e: e.tensor_tensor(out=dst_ap, in0=dst_ap, in1=tmp[0:rows, :], op=ALU.add),
              reads=[tmp_key, dst_key], writes=[dst_key])

    def load_w_rows(self, name, src2d, kchunks, ncols, c0=0):
        kb = self.kb
        w = kb.sbuf(name, [128, kchunks, ncols], BF16)
        for k in range(kchunks):
            kb.dma("pool", w[:, k, :], src2d[k * 128:(k + 1) * 128, c0:c0 + ncols], writes=[(name, k)])
        return w

    def ffn(self, L):
        kb = self.kb
        P = "f%d" % L
        I = self.inp
        mark = kb.push_scope()
        gain = self.load_bcast(P + "gain", I["norm_ffn_pre"][L, :], D)
        gpost = self.load_bcast(P + "gpost", I["norm_ffn_post"][L, :], D)
        rstd = self.prenorm_stats(P)
        cwb = kb.sbuf(P + "cwb", [128, NFC, 4])
        stf = kb.sbuf(P + "stf", [128, NFC, 2 * NS])
        gl = kb.sbuf(P + "gl", [128, 2, NFC])
        gsall = kb.sbuf(P + "gsall", [128, NFC, NS])
        halo = kb.sbuf(P + "halo", [128, 2, NFC])
        m2 = kb.push_scope()
        cwt = kb.sbuf(P + "cwt", [4, DFF])
        kb.dma("sp", cwt[0:3, :], I["ffn_conv_w"][L, :, :], writes=[P + "cwt"])
        kb.dma("sp", cwt[3:4, :], I["ffn_conv_b"][L:L + 1, :], writes=[P + "cwt"])
        stt = kb.sbuf(P + "stt", [2 * NS, DFF])
        kb.dma("sp", stt[:], I["sconv"][L].rearrange("s j f -> (s j) f"), writes=[P + "stt"])
        psA = kb.psum(P + "psA", [128, 512])
        psB = kb.psum(P + "psB", [128, 1024])
        for fc in range(NFC):
            kb.op("pe", lambda e, fc=fc: e.transpose(psA[:, fc * 4:fc * 4 + 4], cwt[0:4, fc * 128:(fc + 1) * 128],
                                                     self.ident_f()[0:4, 0:4]),
                  reads=[P + "cwt", "cst"], writes=[P + "psA"], inc=(fc == NFC - 1))
            kb.op("pe", lambda e, fc=fc: e.transpose(psB[:, fc * 32:fc * 32 + 32],
                                                     stt[0:32, fc * 128:(fc + 1) * 128],
                                                     self.ident_f()[0:32, 0:32]),
                  reads=[P + "stt", "cst"], writes=[P + "psB"], inc=(fc == NFC - 1))
        kb.op("dve", lambda e: e.tensor_copy(out=cwb[:].rearrange("p a b -> p (a b)"), in_=psA[:, 0:NFC * 4]),
              reads=[P + "psA"], writes=[P + "cwb"])
        kb.op("dve", lambda e: e.tensor_copy(out=stf[:].rearrange("p a b -> p (a b)"), in_=psB[:, 0:NFC * 32]),
              reads=[P + "psB"], writes=[P + "stf"])
        kb.barrier()
        kb.pop_scope(m2)
        kb.dma("sp", self.out["convs"][L, :, 0, :], I["sconv"][L, :, 1, :], is_output=True)

        for sb in (1, 0):
            self.ffn_sb(L, sb, P, gain, gpost, rstd, cwb, stf, gl, gsall, halo)
            if sb == 1:
                I_ = self.cc_ffn_in[L]
                O_ = self.cc_ffn_out[L]
                kb.dma("sp", I_.ap()[:, :], gl[:].rearrange("p a b -> p (a b)"), reads=[P + "gl"],
                       writes=[P + "ccin"])
                kb.collective(I_.ap(), O_.ap(), [[0, 1, 2, 3], [4, 5, 6, 7]], reads=[P + "ccin"],
                              writes=[P + "ccout"])
                g4 = kb.sbuf(P + "g4", [128, 4, 44])
                kb.dma("sp", g4[:], O_.ap().rearrange("(r p) c -> p r c", p=128), reads=[P + "ccout"],
                       writes=[P + "g4"])
                hv = halo[:].rearrange("p a b -> p (a b)")
                kb.op("dve", lambda e: e.tensor_scalar(out=hv, in0=g4[:, 0, :], scalar1=self.cst[:, C_SELP:C_SELP + 1],
                                                       scalar2=None, op0=ALU.mult),
                      reads=[P + "g4", "cst"], writes=[P + "halo"])
                for r in range(1, 4):
                    kb.op("dve", lambda e, r=r: e.scalar_tensor_tensor(
                        out=hv, in0=g4[:, r, :], scalar=self.cst[:, C_SELP + r:C_SELP + r + 1], in1=hv,
                        op0=ALU.mult, op1=ALU.add), reads=[P + "g4", "cst", P + "halo"], writes=[P + "halo"])
                m3 = kb.push_scope()
                psC = kb.psum(P + "psC", [128, 512])
                otr = kb.sbuf(P + "otr", [44, 128])
                kb.op("pe", lambda e: e.transpose(psC[0:44, 0:128], gl[:].rearrange("p a b -> p (a b)"),
                                                  self.ident_f()), reads=[P + "gl", "cst"], writes=[P + "psC"])
                kb.op("dve", lambda e: e.tensor_copy(out=otr[:], in_=psC[0:44, 0:128]), reads=[P + "psC"],
                      writes=[P + "otr"])
                for j in range(2):
                    kb.dma("sp", self.out["convp"][L, j, :].rearrange("(c p) -> c p", p=128),
                           otr[j * NFC:(j + 1) * NFC, :], reads=[P + "otr"], is_output=True)
                psD = kb.psum(P + "psD", [128, 3, 1024])
                osr = kb.sbuf(P + "osr", [NS, DFF])
                pv = psD[:].rearrange("p a b -> p (a b)")
                for fc in range(NFC):
                    kb.op("pe", lambda e, fc=fc: e.transpose(pv[0:NS, fc * 128:(fc + 1) * 128], gsall[:, fc, :],
                                                             self.ident_f()),
                          reads=[P + "gsall", "cst"], writes=[P + "psD"], inc=(fc == NFC - 1))
                kb.op("dve", lambda e: e.tensor_copy(out=osr[:], in_=pv[0:NS, 0:DFF]), reads=[P + "psD"],
                      writes=[P + "osr"])
                kb.dma("sp", self.out["convs"][L, :, 1, :], osr[:], reads=[P + "osr"], is_output=True)
                kb.barrier()
                kb.pop_scope(m3)
        kb.barrier()
        kb.pop_scope(mark)

    def ffn_sb(self, L, sb, P0, gain, gpost, rstd, cwb, stf, gl, gsall, halo):
        kb = self.kb
        I = self.inp
        P = "%ss%d" % (P0, sb)
        t0 = sb * 8
        npr = 1024
        ncol = npr + (NS if sb == 1 else 0)
        mark = kb.push_scope()
        hff = kb.sbuf(P + "hff", [128, NFC, ncol], BF16)
        m1 = kb.push_scope()
        xnT = kb.sbuf(P + "xnT", [128, 8, ncol + 2], BF16)
        xn = [kb.sbuf(P + "xn%d" % i, [128, D], BF16) for i in range(2)]
        psT = [kb.psum(P + "psT%d" % i, [128, 512]) for i in range(2)]
        pg = [kb.psum(P + "pg%d" % i, [128, 512]) for i in range(3)]
        pu = [kb.psum(P + "pu%d" % i, [128, 512]) for i in range(3)]
        tiles = list(range(t0, t0 + 8))
        for i, t in enumerate(tiles):
            b = i % 2
            self.norm_transpose(P0, self.hp[:, t, :], ("hp", t), rstd[:, t:t + 1], gain, xn[b], P + "xn%d" % b,
                                psT[b], P + "psT%d" % b, xnT[:, :, i * 128:(i + 1) * 128], (P + "xnT", i))
        if sb == 1:
            self.norm_transpose(P0, self.hs[0:NS, :], "hs", rstd[0:NS, NT:NT + 1], gain, xn[0], P + "xn0",
                                psT[0], P + "psT0", xnT[:, :, npr:npr + NS], (P + "xnT", 8), rows=NS)
            self.norm_transpose(P0, self.hp[:, 7, :], ("hp", 7), rstd[:, 7:8], gain, xn[1], P + "xn1",
                                psT[1], P + "psT1", xnT[:, :, ncol:ncol + 2], (P + "xnT", 9), c0=126, c1=128)
        xkeys = [(P + "xnT", i) for i in range(10 if sb == 1 else 8)]
        NB3 = 3
        G = [kb.sbuf(P + "G%d" % i, [128, npr + 2]) for i in range(NB3)]
        U = [kb.sbuf(P + "U%d" % i, [128, ncol]) for i in range(NB3)]
        C = [kb.sbuf(P + "C%d" % i, [128, npr]) for i in range(2)]
        TM = [kb.sbuf(P + "TM%d" % i, [128, npr]) for i in range(1)]
        GS = [kb.sbuf(P + "GS%d" % i, [128, 2 * NS]) for i in range(NB3)]
        NWB = 2
        wg = [kb.sbuf(P + "wg%d" % i, [128, 8, 256], BF16) for i in range(NWB)]
        wu = [kb.sbuf(P + "wu%d" % i, [128, 8, 256], BF16) for i in range(NWB)]
        slot = 0

        def issue_w(fg):
            wb = fg % NWB
            kb.dma("pool", wg[wb][:], I["ffn_w_gate"][L, :, fg * 256:(fg + 1) * 256].rearrange("(k p) n -> p k n", p=128),
                   writes=[P + "wg%d" % wb])
            kb.dma("pool", wu[wb][:], I["ffn_w_up"][L, :, fg * 256:(fg + 1) * 256].rearrange("(k p) n -> p k n", p=128),
                   writes=[P + "wu%d" % wb])
        issue_w(0)
        slot_of = {}

        def front(fc):
            nonlocal slot
            fg, fi = fc // 2, fc % 2
            wb = fg % NWB
            if fi == 0 and fg + 1 < NFC // 2:
                issue_w(fg + 1)
            gb = fc % NB3
            Gk, Uk, Ck, GSk = P + "G%d" % gb, P + "U%d" % gb, P + "C%d" % gb, P + "GS%d" % gb
            for tg in range(2):
                s3 = slot % 3
                slot += 1
                for k in range(8):
                    kb.op("pe", lambda e, k=k, tg=tg, s3=s3, wb=wb, fi=fi: e.matmul(
                        pg[s3][:, :], wg[wb][:, k, fi * 128:(fi + 1) * 128], xnT[:, k, tg * 512:(tg + 1) * 512],
                        start=(k == 0), stop=(k == 7)),
                        reads=[P + "wg%d" % wb] + xkeys, writes=[P + "pg%d" % s3], inc=(k == 7))
                for k in range(8):
                    kb.op("pe", lambda e, k=k, tg=tg, s3=s3, wb=wb, fi=fi: e.matmul(
                        pu[s3][:, :], wu[wb][:, k, fi * 128:(fi + 1) * 128], xnT[:, k, tg * 512:(tg + 1) * 512],
                        start=(k == 0), stop=(k == 7)),
                        reads=[P + "wu%d" % wb] + xkeys, writes=[P + "pu%d" % s3], inc=(k == 7))
                kb.op("act", lambda e, tg=tg, s3=s3, gb=gb: e.copy(out=G[gb][:, 2 + tg * 512:2 + (tg + 1) * 512],
                                                                  in_=pg[s3][:, :]),
                      reads=[P + "pg%d" % s3], writes=[Gk])
                kb.op("act", lambda e, tg=tg, s3=s3, gb=gb: e.copy(out=U[gb][:, tg * 512:(tg + 1) * 512],
                                                                  in_=pu[s3][:, :]),
                      reads=[P + "pu%d" % s3], writes=[Uk])
            if sb == 1:
                s3 = slot % 3
                slot += 1
                for k in range(8):
                    kb.op("pe", lambda e, k=k, s3=s3, wb=wb, fi=fi: e.matmul(
                        pg[s3][:, 0:NS + 2], wg[wb][:, k, fi * 128:(fi + 1) * 128], xnT[:, k, npr:npr + NS + 2],
                        start=(k == 0), stop=(k == 7)),
                        reads=[P + "wg%d" % wb] + xkeys, writes=[P + "pg%d" % s3], inc=(k == 7))
                for k in range(8):
                    kb.op("pe", lambda e, k=k, s3=s3, wb=wb, fi=fi: e.matmul(
                        pu[s3][:, 0:NS], wu[wb][:, k, fi * 128:(fi + 1) * 128], xnT[:, k, npr:npr + NS],
                        start=(k == 0), stop=(k == 7)),
                        reads=[P + "wu%d" % wb] + xkeys, writes=[P + "pu%d" % s3], inc=(k == 7))
                kb.op("act", lambda e, s3=s3, gb=gb: e.copy(out=GS[gb][:, 0:NS], in_=pg[s3][:, 0:NS]),
                      reads=[P + "pg%d" % s3], writes=[GSk])
                kb.op("act", lambda e, s3=s3, gb=gb: e.copy(out=G[gb][:, 0:2], in_=pg[s3][:, NS:NS + 2]),
                      reads=[P + "pg%d" % s3], writes=[Gk])
                kb.op("act", lambda e, s3=s3, gb=gb: e.copy(out=U[gb][:, npr:npr + NS], in_=pu[s3][:, 0:NS]),
                      reads=[P + "pu%d" % s3], writes=[Uk])
            else:
                kb.op("pool", lambda e, gb=gb, fc=fc: e.tensor_copy(out=G[gb][:, 0:2], in_=halo[:, :, fc]),
                      reads=[P0 + "halo"], writes=[Gk])

        def tail(fc):
            gb = fc % NB3
            Gk, Uk, Ck, GSk = P + "G%d" % gb, P + "U%d" % gb, P + "C%d" % gb, P + "GS%d" % gb
            w0, w1, w2, bb = (cwb[:, fc, i:i + 1] for i in range(4))
            cb = fc % 2
            Ck = P + "C%d" % cb
            kb.op("pool", lambda e, gb=gb, w1=w1: e.tensor_tensor(
                out=TM[0][:, :], in0=G[gb][:, 1:1 + npr], in1=w1.to_broadcast([128, npr]), op=ALU.mult),
                reads=[Gk, P0 + "cwb"], writes=[P + "TM0"])
            kb.op("dve", lambda e, gb=gb, cb=cb, w2=w2, bb=bb: e.tensor_scalar(
                out=C[cb][:, :], in0=G[gb][:, 2:2 + npr], scalar1=w2, scalar2=bb, op0=ALU.mult, op1=ALU.add),
                reads=[Gk, P0 + "cwb"], writes=[Ck])
            kb.op("dve", lambda e, gb=gb, cb=cb, w0=w0: e.scalar_tensor_tensor(
                out=C[cb][:, :], in0=G[gb][:, 0:npr], scalar=w0, in1=C[cb][:, :], op0=ALU.mult, op1=ALU.add),
                reads=[Gk, P0 + "cwb", Ck], writes=[Ck])
            kb.op("dve", lambda e, cb=cb: e.tensor_tensor(
                out=C[cb][:, :], in0=C[cb][:, :], in1=TM[0][:, :], op=ALU.add),
                reads=[P + "TM0", Ck], writes=[Ck])
            kb.op("act", lambda e, cb=cb: e.activation(out=C[cb][:, :], in_=C[cb][:, :], func=AF.Gelu_apprx_tanh),
                  reads=[Ck], writes=[Ck])
            kb.op("dve", lambda e, gb=gb, cb=cb, fc=fc: e.tensor_tensor(out=hff[:, fc, 0:npr], in0=C[cb][:, :],
                                                                in1=U[gb][:, 0:npr], op=ALU.mult),
                  reads=[Ck, Uk], writes=[(P + "hff", fc)])
            if sb == 1:
                kb.op("pool", lambda e, gb=gb, fc=fc: e.tensor_copy(out=gl[:, :, fc], in_=G[gb][:, npr:npr + 2]),
                      reads=[Gk], writes=[P0 + "gl"])
                kb.op("pool", lambda e, gb=gb, fc=fc: e.tensor_copy(out=gsall[:, fc, :], in_=GS[gb][:, 0:NS]),
                      reads=[GSk], writes=[P0 + "gsall"])
                stv = stf[:, fc, :].rearrange("p (s j) -> p s j", j=2)
                cs = GS[gb][:, NS:2 * NS]
                kb.op("dve", lambda e, gb=gb, w2=w2, bb=bb, cs=cs: e.tensor_scalar(
                    out=cs, in0=GS[gb][:, 0:NS], scalar1=w2, scalar2=bb, op0=ALU.mult, op1=ALU.add),
                    reads=[GSk, P0 + "cwb"], writes=[GSk])
                kb.op("dve", lambda e, w1=w1, cs=cs, stv=stv: e.scalar_tensor_tensor(
                    out=cs, in0=stv[:, :, 1], scalar=w1, in1=cs, op0=ALU.mult, op1=ALU.add),
                    reads=[GSk, P0 + "cwb", P0 + "stf"], writes=[GSk])
                kb.op("dve", lambda e, w0=w0, cs=cs, stv=stv: e.scalar_tensor_tensor(
                    out=cs, in0=stv[:, :, 0], scalar=w0, in1=cs, op0=ALU.mult, op1=ALU.add),
                    reads=[GSk, P0 + "cwb", P0 + "stf"], writes=[GSk])
                kb.op("act", lambda e, cs=cs: e.activation(out=cs, in_=cs, func=AF.Gelu_apprx_tanh),
                      reads=[GSk], writes=[GSk])
                kb.op("dve", lambda e, gb=gb, fc=fc, cs=cs: e.tensor_tensor(
                    out=hff[:, fc, npr:npr + NS], in0=cs, in1=U[gb][:, npr:npr + NS], op=ALU.mult),
                    reads=[GSk, Uk], writes=[(P + "hff", fc)])

        for fc in range(NFC):
            front(fc)
            if fc >= 1:
                tail(fc - 1)
        tail(NFC - 1)
        kb.barrier()
        kb.pop_scope(m1)
        m2 = kb.push_scope()
        wd = kb.sbuf(P + "wd", [128, NFC, D], BF16)
        for f2 in range(NFC // 2):
            kb.dma("pool", wd[:, 2 * f2:2 * f2 + 2, :],
                   I["ffn_w_down"][L, f2 * 256:(f2 + 1) * 256, :].rearrange("(f p) n -> p f n", p=128),
                   writes=[(P + "wd", f2)])
        wkeys = [(P + "wd", f2) for f2 in range(NFC // 2)]
        hkeys = [(P + "hff", fc) for fc in range(NFC)]
        py = [kb.psum(P + "py%d" % i, [128, 1024]) for i in range(2)]
        tmp = [kb.sbuf(P + "tmp%d" % i, [128, D]) for i in range(2)]
        small = kb.sbuf(P + "small", [128, 4])
        units = [(t, 128) for t in tiles] + ([("s", NS)] if sb == 1 else [])
        for i, (t, rows) in enumerate(units):
            b = i % 2
            off = (i * 128) if t != "s" else npr
            for half in range(2):
                for f in range(NFC):
                    kb.op("pe", lambda e, f=f, half=half, b=b, off=off, rows=rows: e.matmul(
                        py[b][0:rows, half * 512:(half + 1) * 512], hff[:, f, off:off + rows],
                        wd[:, f, half * 512:(half + 1) * 512], start=(f == 0), stop=(f == NFC - 1)),
                        reads=wkeys + hkeys, writes=[P + "py%d" % b], inc=(f == NFC - 1))
            if t == "s":
                dst, dk = self.hs[0:NS, :], "hs"
            else:
                dst, dk = self.hp[:, t, :], ("hp", t)
            self.post_norm_residual(P0, py[b], P + "py%d" % b, gpost, dst, dk, tmp[b], P + "tmp%d" % b, small, rows)
        kb.barrier()
        kb.pop_scope(m2)
        kb.pop_scope(mark)

    def odd(self, L):
        kb = self.kb
        I = self.inp
        P = "o%d" % L
        j = L // 2
        mark = kb.push_scope()
        gain = self.load_bcast(P + "gain", I["norm_mix_pre"][L, :], D)
        gpost = self.load_bcast(P + "gpost", I["norm_mix_post"][L, :], D)
        lng = self.load_bcast(P + "lng", I["sgu_ln_gain"][j, :], D)
        lnb = self.load_bcast(P + "lnb", I["sgu_ln_bias"][j, :], D)
        rstd = self.prenorm_stats(P)
        win = self.load_w_rows(P + "win", I["w_in_odd"][j], 8, 2 * D)
        wout = self.load_w_rows(P + "wout", I["w_out_odd"][j], 8, D)
        winkeys = [(P + "win", k) for k in range(8)]
        woutkeys = [(P + "wout", k) for k in range(8)]
        wsT = kb.sbuf(P + "wsT", [128, 8, 128], BF16)
        bsb = kb.sbuf(P + "bsb", [128, 8])
        w00 = kb.sbuf(P + "w00", [NS, 8])
        b0 = kb.sbuf(P + "b0", [NS, 8])
        kb.dma("sp", w00[:], I["sgu_w"][j, :, 0, 0].partition_broadcast(NS), writes=[P + "w00"],
               allow_slow_non_contiguous=True)
        kb.dma("sp", b0[:], I["sgu_b"][j, :, 0].partition_broadcast(NS), writes=[P + "b0"],
               allow_slow_non_contiguous=True)
        m0 = kb.push_scope()
        wsn = kb.sbuf(P + "wsn", [128, 8, 128])
        kb.dma("sp", wsn[:], I["sgu_w"][j].rearrange("h t s -> t h s"), writes=[P + "wsn"])
        bsn = kb.sbuf(P + "bsn", [8, 128])
        kb.dma("sp", bsn[:], I["sgu_b"][j, :, :], writes=[P + "bsn"])
        p0 = kb.psum(P + "p0", [128, 1024])
        p1 = kb.psum(P + "p1", [128, 512])
        for h in range(8):
            kb.op("pe", lambda e, h=h: e.transpose(p0[:, h * 128:(h + 1) * 128], wsn[:, h, :], self.ident_f()),
                  reads=[P + "wsn", "cst"], writes=[P + "p0"], inc=(h == 7))
        kb.op("pe", lambda e: e.transpose(p1[:, 0:8], bsn[0:8, :], self.ident_f()[0:8, 0:8]),
              reads=[P + "bsn", "cst"], writes=[P + "p1"])
        kb.op("dve", lambda e: e.tensor_tensor(
            out=wsT[:], in0=p0[:].rearrange("p (h t) -> p h t", h=8),
            in1=self.cst[:, C_TRI:C_TRI + 128].unsqueeze(1).to_broadcast([128, 8, 128]), op=ALU.mult),
            reads=[P + "p0", "cst"], writes=[P + "wsT"])
        kb.op("dve", lambda e: e.tensor_copy(out=bsb[:], in_=p1[:, 0:8]), reads=[P + "p1"], writes=[P + "bsb"])
        kb.barrier()
        kb.pop_scope(m0)

        xn = kb.sbuf(P + "xn", [128, D], BF16)
        xnT = [kb.sbuf(P + "xnT%d" % i, [128, 8, 128], BF16) for i in range(2)]
        u = kb.sbuf(P + "u", [128, D])
        v = kb.sbuf(P + "v", [128, D])
        vn = kb.sbuf(P + "vn", [128, D])
        vnb = kb.sbuf(P + "vnb", [128, D], BF16)
        y = kb.sbuf(P + "y", [128, D])
        yb = kb.sbuf(P + "yb", [128, D], BF16)
        yT = kb.sbuf(P + "yT", [128, 8, 128], BF16)
        tmp = kb.sbuf(P + "tmp", [128, D])
        sm = kb.sbuf(P + "sm", [128, 8])
        small = kb.sbuf(P + "small", [128, 4])
        pT = kb.psum(P + "pT", [128, 512])
        pz = [kb.psum(P + "pz%d" % i, [128, 512]) for i in range(2)]
        pm = kb.psum(P + "pm", [128, 1024])
        py = kb.psum(P + "py", [128, 1024])
        pyT = kb.psum(P + "pyT", [128, 512])
        units = [(t, 128) for t in range(NT)] + [("s", NS)]
        for i, (t, rows) in enumerate(units):
            b = i % 2
            if t == "s":
                src, sk, rc = self.hs[0:NS, :], "hs", rstd[0:NS, NT:NT + 1]
            else:
                src, sk, rc = self.hp[:, t, :], ("hp", t), rstd[:, t:t + 1]
            self.norm_transpose(P, src, sk, rc, gain, xn, P + "xn", pT, P + "pT", xnT[b][:, :, 0:rows],
                                P + "xnT%d" % b, rows=rows)
            for g in range(4):
                pb = g % 2
                for k in range(8):
                    kb.op("pe", lambda e, k=k, g=g, pb=pb, b=b, rows=rows: e.matmul(
                        pz[pb][0:rows, :], xnT[b][:, k, 0:rows], win[:, k, g * 512:(g + 1) * 512],
                        start=(k == 0), stop=(k == 7)),
                        reads=[P + "xnT%d" % b] + winkeys, writes=[P + "pz%d" % pb], inc=(k == 7))
                if g < 2:
                    kb.op("act", lambda e, g=g, pb=pb, rows=rows: e.activation(
                        out=u[0:rows, g * 512:(g + 1) * 512], in_=pz[pb][0:rows, :], func=AF.Gelu),
                        reads=[P + "pz%d" % pb], writes=[P + "u"])
                else:
                    kb.op("act", lambda e, g=g, pb=pb, rows=rows: e.activation(
                        out=v[0:rows, (g - 2) * 512:(g - 1) * 512], in_=pz[pb][0:rows, :], func=AF.Gelu,
                        accum_out=sm[0:rows, g - 2:g - 1]),
                        reads=[P + "pz%d" % pb], writes=[P + "v", P + "sm"])
            kb.op("dve", lambda e, rows=rows: e.tensor_scalar(
                out=sm[0:rows, 2:3], in0=sm[0:rows, 0:1], scalar1=sm[0:rows, 1:2], scalar2=-1.0 / D,
                op0=ALU.add, op1=ALU.mult), reads=[P + "sm"], writes=[P + "sm"])
            kb.op("act", lambda e, rows=rows: e.activation(
                out=tmp[0:rows, :], in_=v[0:rows, :], func=AF.Square, bias=sm[0:rows, 2:3],
                accum_out=sm[0:rows, 3:4]), reads=[P + "v", P + "sm"], writes=[P + "tmp", P + "sm"])
            kb.op("act", lambda e, rows=rows: e.activation(
                out=sm[0:rows, 4:5], in_=sm[0:rows, 3:4], func=AF.Sqrt, bias=self.epsr[0:rows, 1:2],
                scale=1.0 / D), reads=[P + "sm", "epsr"], writes=[P + "sm"])
            kb.op("dve", lambda e, rows=rows: e.reciprocal(out=sm[0:rows, 5:6], in_=sm[0:rows, 4:5]),
                  reads=[P + "sm"], writes=[P + "sm"])
            kb.op("dve", lambda e, rows=rows: e.tensor_scalar(
                out=vn[0:rows, :], in0=v[0:rows, :], scalar1=sm[0:rows, 2:3], scalar2=sm[0:rows, 5:6],
                op0=ALU.add, op1=ALU.mult), reads=[P + "v", P + "sm"], writes=[P + "vn"])
            kb.op("dve", lambda e, rows=rows: e.tensor_tensor(out=vn[0:rows, :], in0=vn[0:rows, :],
                                                              in1=lng[0:rows, :], op=ALU.mult),
                  reads=[P + "vn", P + "lng"], writes=[P + "vn"])
            kb.op("pool", lambda e, rows=rows: e.tensor_tensor(out=vn[0:rows, :], in0=vn[0:rows, :],
                                                               in1=lnb[0:rows, :], op=ALU.add),
                  reads=[P + "vn", P + "lnb"], writes=[P + "vn"])
            if t == "s":
                kb.dma("sp", self.out["sguv"][j, :, :], vn[0:NS, :], reads=[P + "vn"], is_output=True)
                y3 = y[0:NS, :].rearrange("p (h c) -> p h c", h=8)
                kb.op("dve", lambda e: e.tensor_tensor(
                    out=y3, in0=vn[0:NS, :].rearrange("p (h c) -> p h c", h=8),
                    in1=w00[:].unsqueeze(2).to_broadcast([NS, 8, 128]), op=ALU.mult),
                    reads=[P + "vn", P + "w00"], writes=[P + "y"])
                kb.op("dve", lambda e: e.tensor_tensor(
                    out=y3, in0=y3, in1=b0[:].unsqueeze(2).to_broadcast([NS, 8, 128]), op=ALU.add),
                    reads=[P + "y", P + "b0"], writes=[P + "y"])
            else:
                kb.op("act", lambda e: e.copy(out=vnb[:], in_=vn[:]), reads=[P + "vn"], writes=[P + "vnb"])
                for h in range(8):
                    kb.op("pe", lambda e, h=h: e.matmul(pm[:, h * 128:(h + 1) * 128], wsT[:, h, :],
                                                        vnb[:, h * 128:(h + 1) * 128], start=True, stop=True),
                          reads=[P + "wsT", P + "vnb"], writes=[P + "pm"], inc=(h == 7))
                kb.op("dve", lambda e: e.tensor_tensor(
                    out=y[:].rearrange("p (h c) -> p h c", h=8), in0=pm[:].rearrange("p (h c) -> p h c", h=8),
                    in1=bsb[:].unsqueeze(2).to_broadcast([128, 8, 128]), op=ALU.add),
                    reads=[P + "pm", P + "bsb"], writes=[P + "y"])
            kb.op("dve", lambda e, rows=rows: e.tensor_tensor(out=yb[0:rows, :], in0=y[0:rows, :],
                                                              in1=u[0:rows, :], op=ALU.mult),
                  reads=[P + "y", P + "u"], writes=[P + "yb"])
            pv = pyT[:].bitcast(BF16).rearrange("p (k t) -> p k t", k=8)
            for k in range(8):
                kb.op("pe", lambda e, k=k, rows=rows: e.transpose(pv[:, k, 0:rows], yb[0:rows, k * 128:(k + 1) * 128],
                                                                  self.identb[0:rows, 0:rows]),
                      reads=[P + "yb", "identb"], writes=[P + "pyT"], inc=(k == 7))
            kb.op("act", lambda e, rows=rows: e.copy(out=yT[:, :, 0:rows], in_=pv[:, :, 0:rows]),
                  reads=[P + "pyT"], writes=[P + "yT"])
            for half in range(2):
                for k in range(8):
                    kb.op("pe", lambda e, k=k, half=half, rows=rows: e.matmul(
                        py[0:rows, half * 512:(half + 1) * 512], yT[:, k, 0:rows],
                        wout[:, k, half * 512:(half + 1) * 512], start=(k == 0), stop=(k == 7)),
                        reads=[P + "yT"] + woutkeys, writes=[P + "py"], inc=(k == 7))
            if t == "s":
                dst, dk = self.hs[0:NS, :], "hs"
            else:
                dst, dk = self.hp[:, t, :], ("hp", t)
            self.post_norm_residual(P, py, P + "py", gpost, dst, dk, tmp, P + "tmp", small, rows)
        kb.barrier()
        kb.pop_scope(mark)

    def even(self, L):
        kb = self.kb
        I = self.inp
        P = "e%d" % L
        mark = kb.push_scope()
        gain = self.load_bcast(P + "gain", I["norm_mix_pre"][L, :], D)
        rstd = self.prenorm_stats(P)
        if "no_attn" not in self.dbg:
            self.even_attn(L, P, gain, rstd)
        if "no_rwkv" not in self.dbg:
            self.even_rwkv(L, P, gain, rstd)
        if "no_comb" not in self.dbg:
            self.even_combine(L, P)
        kb.barrier()
        kb.pop_scope(mark)

    def bk(self, P, n=8):
        kb = self.kb
        banks = [kb.psum(P + "bk%d" % i, [128, 512]) for i in range(n)]
        keys = [P + "bk%d" % i for i in range(n)]
        return banks, keys

    def even_attn(self, L, P0, gain, rstd):
        kb = self.kb
        I = self.inp
        j = L // 2
        P = P0 + "a"
        mark = kb.push_scope()
        w = self.load_w_rows(P + "w", I["w_in_even"][j], 8, 768, c0=0)
        wkeys = [(P + "w", k) for k in range(8)]
        sinkb = self.load_bcast(P + "sink", I["attn_sinks"][j, :], 8)
        rope = kb.sbuf(P + "rope", [128, 1088])
        kb.dma("sp", rope[:], I["consts"][:, C_COS:C_COS + 1088], writes=[P + "rope"])
        bk, bkk = self.bk(P)
        Kc = kb.sbuf(P + "Kc", [128, 129, 64])
        Vc = kb.sbuf(P + "Vc", [128, 129, 64])
        for h in range(8):
            kv = h // 4
            HS = slice(h * 16, (h + 1) * 16)
            for q4 in range(4):
                ks = slice(q4 * 32, (q4 + 1) * 32)
                kb.dma("sp", Kc[HS, ks, :], I["ck"][j, :, ks, kv * 64:(kv + 1) * 64], writes=[P + "Kc"])
                kb.dma("sp", Vc[HS, ks, :], I["cv"][j, :, ks, kv * 64:(kv + 1) * 64], writes=[P + "Vc"])
        kT = kb.sbuf(P + "kT", [128, (NT + 1) * 128], BF16)
        Va = kb.sbuf(P + "Va", [128, NT + 1, 128], BF16)
        xn = kb.sbuf(P + "xn", [128, D], BF16)
        xnT = kb.sbuf(P + "xnT", [128, 8, 128], BF16)
        qk = kb.sbuf(P + "qk", [128, 640])
        vf = kb.sbuf(P + "vf", [128, 128])
        t1 = kb.sbuf(P + "t1", [128, 10, 32]); t2 = kb.sbuf(P + "t2", [128, 10, 32])
        t3 = kb.sbuf(P + "t3", [128, 10, 32]); t4 = kb.sbuf(P + "t4", [128, 10, 32])
        rq = kb.sbuf(P + "rq", [128, 4, 2, 64], BF16)
        rqs = kb.sbuf(P + "rqs", [NS, 8, 64])
        rk = kb.sbuf(P + "rk", [128, 128])
        rkb = kb.sbuf(P + "rkb", [128, 128], BF16)
        qT = kb.sbuf(P + "qT", [128, 4, 128], BF16)
        S = kb.sbuf(P + "S", [128, 4, 256])
        Pb = kb.sbuf(P + "Pb", [128, 4, 256], BF16)
        PT = kb.sbuf(P + "PT", [128, 8, 128], BF16)
        oa = kb.sbuf(P + "oa", [128, 512])
        sm = kb.sbuf(P + "sm", [128, 32])

        def qkv(t, rows=128):
            if t == "s":
                src, sk, rc = self.hs[0:NS, :], "hs", rstd[0:NS, NT:NT + 1]
                cos = rope[0:NS, 1024:1056]
                sin = rope[0:NS, 1056:1088]
            else:
                src, sk, rc = self.hp[:, t, :], ("hp", t), rstd[:, t:t + 1]
                cos = rope[:, t * 32:(t + 1) * 32]
                sin = rope[:, 512 + t * 32:512 + (t + 1) * 32]
            R = slice(0, rows)
            self.norm_transpose(P0, src, sk, rc, gain, xn, P + "xn", bk[7], bkk[7], xnT[:, :, 0:rows], P + "xnT",
                                rows=rows)
            for k in range(8):
                kb.op("pe", lambda e, k=k: e.matmul(bk[0][R, :], xnT[:, k, 0:rows], w[:, k, 0:512],
                                                    start=(k == 0), stop=(k == 7)),
                      reads=[P + "xnT"] + wkeys, writes=[bkk[0]], inc=(k == 7))
            for k in range(8):
                kb.op("pe", lambda e, k=k: e.matmul(bk[1][R, 0:256], xnT[:, k, 0:rows], w[:, k, 512:768],
                                                    start=(k == 0), stop=(k == 7)),
                      reads=[P + "xnT"] + wkeys, writes=[bkk[1]], inc=(k == 7))
            kb.op("act", lambda e: e.copy(out=qk[R, 0:512], in_=bk[0][R, :]), reads=[bkk[0]], writes=[P + "qk"])
            kb.op("act", lambda e: e.copy(out=qk[R, 512:640], in_=bk[1][R, 0:128]), reads=[bkk[1]], writes=[P + "qk"])
            kb.op("act", lambda e: e.copy(out=vf[R, :], in_=bk[1][R, 128:256]), reads=[bkk[1]], writes=[P + "vf"])
            q3 = qk[R, :].rearrange("p (h d) -> p h d", h=10)
            x1, x2 = q3[:, :, 0:32], q3[:, :, 32:64]
            cb = cos.unsqueeze(1).to_broadcast([rows, 10, 32])
            sb_ = sin.unsqueeze(1).to_broadcast([rows, 10, 32])
            kb.op("dve", lambda e: e.tensor_tensor(out=t1[R], in0=x1, in1=cb, op=ALU.mult), reads=[P + "qk", P + "rope"], writes=[P + "t1"])
            kb.op("pool", lambda e: e.tensor_tensor(out=t2[R], in0=x2, in1=sb_, op=ALU.mult), reads=[P + "qk", P + "rope"], writes=[P + "t2"])
            kb.op("dve", lambda e: e.tensor_tensor(out=t3[R], in0=x2, in1=cb, op=ALU.mult), reads=[P + "qk", P + "rope"], writes=[P + "t3"])
            kb.op("pool", lambda e: e.tensor_tensor(out=t4[R], in0=x1, in1=sb_, op=ALU.mult), reads=[P + "qk", P + "rope"], writes=[P + "t4"])
            r3 = rk[R, :].rearrange("p (h d) -> p h d", h=2)
            if t == "s":
                kb.op("dve", lambda e: e.tensor_tensor(out=rqs[:, :, 0:32], in0=t1[R, 0:8, :], in1=t2[R, 0:8, :], op=ALU.subtract),
                      reads=[P + "t1", P + "t2"], writes=[P + "rqs"])
                kb.op("dve", lambda e: e.tensor_tensor(out=rqs[:, :, 32:64], in0=t3[R, 0:8, :], in1=t4[R, 0:8, :], op=ALU.add),
                      reads=[P + "t3", P + "t4"], writes=[P + "rqs"])
            else:
                for kv in range(2):
                    kb.op("dve", lambda e, kv=kv: e.tensor_tensor(out=rq[:, :, kv, 0:32], in0=t1[:, kv * 4:kv * 4 + 4, :],
                                                                  in1=t2[:, kv * 4:kv * 4 + 4, :], op=ALU.subtract),
                          reads=[P + "t1", P + "t2"], writes=[P + "rq"])
                    kb.op("pool", lambda e, kv=kv: e.tensor_tensor(out=rq[:, :, kv, 32:64], in0=t3[:, kv * 4:kv * 4 + 4, :],
                                                                   in1=t4[:, kv * 4:kv * 4 + 4, :], op=ALU.add),
                          reads=[P + "t3", P + "t4"], writes=[P + "rq"])
            kb.op("dve", lambda e: e.tensor_tensor(out=r3[:, :, 0:32], in0=t1[R, 8:10, :], in1=t2[R, 8:10, :], op=ALU.subtract),
                  reads=[P + "t1", P + "t2"], writes=[P + "rk"])
            kb.op("dve", lambda e: e.tensor_tensor(out=r3[:, :, 32:64], in0=t3[R, 8:10, :], in1=t4[R, 8:10, :], op=ALU.add),
                  reads=[P + "t3", P + "t4"], writes=[P + "rk"])
            if t == "s":
                return
            kb.op("act", lambda e: e.copy(out=rkb[:], in_=rk[:]), reads=[P + "rk"], writes=[P + "rkb"])
            kb.op("act", lambda e: e.copy(out=Va[:, t + 1, :], in_=vf[:]), reads=[P + "vf"], writes=[(P + "Va", t + 1)])
            pv = bk[6][:].bitcast(BF16).rearrange("p (k t) -> p k t", k=8)
            for g in range(4):
                kb.op("pe", lambda e, g=g: e.transpose(pv[:, g, :], rq[:, g, :, :].rearrange("p a b -> p (a b)"), self.identb[:]),
                      reads=[P + "rq", "identb"], writes=[bkk[6]], inc=False)
            kb.op("pe", lambda e: e.transpose(pv[:, 4, :], rkb[:], self.identb[:]), reads=[P + "rkb", "identb"], writes=[bkk[6]])
            kb.op("act", lambda e: e.copy(out=qT[:], in_=pv[:, 0:4, :]), reads=[bkk[6]], writes=[P + "qT"])
            kb.op("dve", lambda e: e.tensor_copy(out=kT[:, (t + 1) * 128:(t + 2) * 128], in_=pv[:, 4, :]), reads=[bkk[6]],
                  writes=[(P + "kT", t + 1)])

        qkv(NT - 1)
        pay = kb.sbuf(P + "pay", [128, 256])
        kb.op("dve", lambda e: e.tensor_copy(out=pay[:, 0:128], in_=kT[:, NT * 128:(NT + 1) * 128]), reads=[(P + "kT", NT)], writes=[P + "pay"])
        kb.op("dve", lambda e: e.tensor_copy(out=pay[:, 128:256], in_=Va[:, NT, :]), reads=[(P + "Va", NT)], writes=[P + "pay"])
        ci, co = self.cc_att_in[j].ap(), self.cc_att_out[j].ap()
        kb.dma("sp", ci[:, 0:256], pay[:], reads=[P + "pay"], writes=[P + "ccin"])
        kb.collective(ci, co, [[0, 1, 2, 3], [4, 5, 6, 7]], reads=[P + "ccin"], writes=[P + "ccout"])
        g4 = kb.sbuf(P + "g4", [128, 4, 256])
        kb.dma("sp", g4[:], co.rearrange("(r p) c -> p r c", p=128)[:, :, 0:256], reads=[P + "ccout"], writes=[P + "g4"])
        kb.op("dve", lambda e: e.tensor_scalar(out=pay[:], in0=g4[:, 0, :], scalar1=self.cst[:, C_SELP:C_SELP + 1], scalar2=None,
                                               op0=ALU.mult), reads=[P + "g4", "cst"], writes=[P + "pay"])
        for r in range(1, 4):
            kb.op("dve", lambda e, r=r: e.scalar_tensor_tensor(out=pay[:], in0=g4[:, r, :], scalar=self.cst[:, C_SELP + r:C_SELP + r + 1],
                                                               in1=pay[:], op0=ALU.mult, op1=ALU.add),
                  reads=[P + "g4", "cst", P + "pay"], writes=[P + "pay"])
        kb.op("dve", lambda e: e.tensor_copy(out=kT[:, 0:128], in_=pay[:, 0:128]), reads=[P + "pay"], writes=[(P + "kT", 0)])
        kb.op("dve", lambda e: e.tensor_copy(out=Va[:, 0, :], in_=pay[:, 128:256]), reads=[P + "pay"], writes=[(P + "Va", 0)])

        for t in range(NT):
            qkv(t)
            if t == NT - 1:
                kb.dma("sp", self.out["wkp"][j, :, :], rk[:], reads=[P + "rk"], is_output=True)
                kb.dma("sp", self.out["wvp"][j, :, :], vf[:], reads=[P + "vf"], is_output=True)
            mcol = C_M0 if t == 0 else C_MA
            mask = self.cst[:, mcol:mcol + 256].unsqueeze(1).to_broadcast([128, 4, 256])
            for kv in range(2):
                H = slice(kv * 64, (kv + 1) * 64)
                for g in range(4):
                    b2 = 2 + g // 2
                    kb.op("pe", lambda e, g=g, b2=b2: e.matmul(bk[b2][:, (g % 2) * 256:(g % 2 + 1) * 256], qT[H, g, :],
                                                               kT[H, t * 128:(t + 2) * 128], start=True, stop=True),
                          reads=[P + "qT", (P + "kT", t), (P + "kT", t + 1)], writes=[bkk[b2]])
                for hh in range(2):
                    kb.op("dve", lambda e, hh=hh: e.scalar_tensor_tensor(
                        out=S[:, 2 * hh:2 * hh + 2, :], in0=bk[2 + hh][:].rearrange("p (a b) -> p a b", a=2), scalar=0.125,
                        in1=mask[:, 0:2, :], op0=ALU.mult, op1=ALU.add), reads=[bkk[2 + hh], "cst"], writes=[P + "S"])
                kb.op("dve", lambda e: e.tensor_reduce(out=sm[:, 0:4], in_=S[:], axis=AX.X, op=ALU.max), reads=[P + "S"], writes=[P + "sm"])
                kb.op("dve", lambda e: e.tensor_tensor(out=sm[:, 0:4], in0=sm[:, 0:4], in1=sinkb[:, kv * 4:kv * 4 + 4], op=ALU.max),
                      reads=[P + "sm", P + "sink"], writes=[P + "sm"])
                kb.op("dve", lambda e: e.tensor_scalar(out=sm[:, 4:8], in0=sm[:, 0:4], scalar1=-1.0, scalar2=None, op0=ALU.mult),
                      reads=[P + "sm"], writes=[P + "sm"])
                for g in range(4):
                    kb.op("act", lambda e, g=g: e.activation(out=Pb[:, g, :], in_=S[:, g, :], func=AF.Exp, bias=sm[:, 4 + g:5 + g],
                                                             accum_out=sm[:, 8 + g:9 + g]),
                          reads=[P + "S", P + "sm"], writes=[P + "Pb", P + "sm"])
                kb.op("dve", lambda e: e.tensor_tensor(out=sm[:, 12:16], in0=sinkb[:, kv * 4:kv * 4 + 4], in1=sm[:, 4:8], op=ALU.add),
                      reads=[P + "sm", P + "sink"], writes=[P + "sm"])
                kb.op("act", lambda e: e.activation(out=sm[:, 12:16], in_=sm[:, 12:16], func=AF.Exp), reads=[P + "sm"], writes=[P + "sm"])
                kb.op("dve", lambda e: e.tensor_tensor(out=sm[:, 16:20], in0=sm[:, 8:12], in1=sm[:, 12:16], op=ALU.add),
                      reads=[P + "sm"], writes=[P + "sm"])
                kb.op("dve", lambda e: e.reciprocal(out=sm[:, 20:24], in_=sm[:, 16:20]), reads=[P + "sm"], writes=[P + "sm"])
                pv = bk[4][:].bitcast(BF16).rearrange("p (k t) -> p k t", k=8)
                for g in range(4):
                    for blk in range(2):
                        kb.op("pe", lambda e, g=g, blk=blk: e.transpose(pv[:, g * 2 + blk, :], Pb[:, g, blk * 128:(blk + 1) * 128],
                                                                        self.identb[:]),
                              reads=[P + "Pb", "identb"], writes=[bkk[4]], inc=(g == 3 and blk == 1))
                kb.op("act", lambda e: e.copy(out=PT[:], in_=pv[:]), reads=[bkk[4]], writes=[P + "PT"])
                for g in range(4):
                    for blk in range(2):
                        kb.op("pe", lambda e, g=g, blk=blk: e.matmul(bk[5][:, g * 64:(g + 1) * 64], PT[:, g * 2 + blk, :],
                                                                     Va[:, t + blk, H], start=(blk == 0), stop=(blk == 1)),
                              reads=[P + "PT", (P + "Va", t), (P + "Va", t + 1)], writes=[bkk[5]], inc=(g == 3 and blk == 1))
                kb.op("dve", lambda e, kv=kv: e.tensor_tensor(
                    out=oa[:, kv * 256:(kv + 1) * 256].rearrange("p (g d) -> p g d", g=4),
                    in0=bk[5][:, 0:256].rearrange("p (g d) -> p g d", g=4),
                    in1=sm[:, 20:24].unsqueeze(2).to_broadcast([128, 4, 64]), op=ALU.mult),
                    reads=[bkk[5], P + "sm"], writes=[P + "oa"])
            kb.dma("sp", self.rec_oa.ap()[t, :, :], oa[:], reads=[P + "oa"], writes=[("rec_oa", t)])
            if "oa" in self.dbg:
                kb.dma("sp", self.out["dbg_oa"][t * 128:(t + 1) * 128, :], oa[:], reads=[P + "oa"], is_output=True)

        qkv("s", rows=NS)
        kb.dma("sp", self.out["wks"][j, :, :], rk[0:NS, :], reads=[P + "rk"], is_output=True)
        kb.dma("sp", self.out["wvs"][j, :, :], vf[0:NS, :], reads=[P + "vf"], is_output=True)
        kb.barrier()
        if "no_sattn" in self.dbg:
            kb.pop_scope(mark)
            return
        m2 = kb.push_scope()
        qh = kb.sbuf(P + "qh", [128, 64])
        sc = kb.sbuf(P + "sc", [128, 129])
        pe_ = kb.sbuf(P + "pe", [128, 129])
        oh = kb.sbuf(P + "oh", [128, 64])
        s2 = kb.sbuf(P + "s2", [128, 16])
        oas = kb.sbuf(P + "oas", [128, 512])
        for h in range(8):
            kv = h // 4
            HS = slice(h * 16, (h + 1) * 16)
            kb.dma("sp", Kc[HS, 128, :], rk[0:NS, kv * 64:(kv + 1) * 64], reads=[P + "rk"], writes=[P + "Kc"])
            kb.dma("sp", Vc[HS, 128, :], vf[0:NS, kv * 64:(kv + 1) * 64], reads=[P + "vf"], writes=[P + "Vc"])
            kb.dma("sp", qh[HS, :], rqs[:, h, :], reads=[P + "rqs"], writes=[P + "qh"])
            kb.dma("sp", s2[HS, 0:1], I["attn_sinks"][j, h:h + 1].partition_broadcast(NS), writes=[P + "s2"])
        kb.op("dve", lambda e: e.tensor_tensor(out=Kc[:], in0=Kc[:], in1=qh[:].unsqueeze(1).to_broadcast([128, 129, 64]), op=ALU.mult),
              reads=[P + "Kc", P + "qh"], writes=[P + "Kc"])
        kb.op("dve", lambda e: e.tensor_reduce(out=sc[:], in_=Kc[:], axis=AX.X, op=ALU.add), reads=[P + "Kc"], writes=[P + "sc"])
        kb.op("dve", lambda e: e.tensor_reduce(out=s2[:, 1:2], in_=sc[:], axis=AX.X, op=ALU.max), reads=[P + "sc"], writes=[P + "s2"])
        kb.op("dve", lambda e: e.tensor_scalar(out=s2[:, 2:3], in0=s2[:, 1:2], scalar1=0.125, scalar2=s2[:, 0:1], op0=ALU.mult, op1=ALU.max),
              reads=[P + "s2"], writes=[P + "s2"])
        kb.op("dve", lambda e: e.tensor_scalar(out=s2[:, 3:4], in0=s2[:, 2:3], scalar1=-1.0, scalar2=None, op0=ALU.mult),
              reads=[P + "s2"], writes=[P + "s2"])
        kb.op("act", lambda e: e.activation(out=pe_[:], in_=sc[:], func=AF.Exp, bias=s2[:, 3:4], scale=0.125, accum_out=s2[:, 4:5]),
              reads=[P + "sc", P + "s2"], writes=[P + "pe", P + "s2"])
        kb.op("dve", lambda e: e.tensor_tensor(out=s2[:, 5:6], in0=s2[:, 0:1], in1=s2[:, 3:4], op=ALU.add), reads=[P + "s2"], writes=[P + "s2"])
        kb.op("act", lambda e: e.activation(out=s2[:, 5:6], in_=s2[:, 5:6], func=AF.Exp), reads=[P + "s2"], writes=[P + "s2"])
        kb.op("dve", lambda e: e.tensor_tensor(out=s2[:, 6:7], in0=s2[:, 4:5], in1=s2[:, 5:6], op=ALU.add), reads=[P + "s2"], writes=[P + "s2"])
        kb.op("dve", lambda e: e.reciprocal(out=s2[:, 7:8], in_=s2[:, 6:7]), reads=[P + "s2"], writes=[P + "s2"])
        kb.op("dve", lambda e: e.tensor_tensor(out=Vc[:], in0=Vc[:], in1=pe_[:].unsqueeze(2).to_broadcast([128, 129, 64]), op=ALU.mult),
              reads=[P + "Vc", P + "pe"], writes=[P + "Vc"])
        kb.op("dve", lambda e: e.tensor_reduce(out=oh[:], in_=Vc[:].rearrange("p j d -> p d j"), axis=AX.X, op=ALU.add),
              reads=[P + "Vc"], writes=[P + "oh"])
        kb.op("dve", lambda e: e.tensor_scalar(out=oh[:], in0=oh[:], scalar1=s2[:, 7:8], scalar2=None, op0=ALU.mult),
              reads=[P + "oh", P + "s2"], writes=[P + "oh"])
        kb.op("pool", lambda e: e.memset(oas[:], 0.0), writes=[P + "oas"])
        for h in range(8):
            kb.dma("sp", oas[0:NS, h * 64:(h + 1) * 64], oh[h * 16:(h + 1) * 16, :], reads=[P + "oh"], writes=[P + "oas"])
        kb.dma("sp", self.rec_oa.ap()[NT, :, :], oas[:], reads=[P + "oas"], writes=[("rec_oa", NT)])
        kb.barrier()
        kb.pop_scope(m2)
        kb.pop_scope(mark)

    def even_rwkv(self, L, P0, gain, rstd):
        kb = self.kb
        I = self.inp
        j = L // 2
        P = P0 + "r"
        HDS = 0.5 * DECAY_SCALE
        mark = kb.push_scope()
        w = self.load_w_rows(P + "w", I["w_in_even"][j], 8, BPROJ, c0=768)
        wkeys = [(P + "w", k) for k in range(8)]
        mub = self.load_bcast(P + "mu", I["shift_mu"][j, :], BPROJ)
        w0b = self.load_bcast(P + "w0b", I["decay_w0"][j, :], 512)
        a0b = self.load_bcast(P + "a0b", I["iclr_a0"][j, :], 512)
        kkb = self.load_bcast(P + "kkb", I["key_k"][j, :], 512)
        kab = self.load_bcast(P + "kab", I["key_a"][j, :], 512)
        rkb = self.load_bcast(P + "rkb", I["bonus_r_k"][j, :], 512)
        w2a2 = kb.sbuf(P + "w2a2", [128, 512], BF16)
        kb.dma("pool", w2a2[0:64, :], I["decay_w2"][j, :, :], writes=[P + "w2a2"])
        kb.dma("pool", w2a2[64:128, :], I["iclr_a2"][j, :, :], writes=[P + "w2a2"])
        g2b = kb.sbuf(P + "g2b", [128, 512], BF16)
        kb.dma("pool", g2b[:], I["gate_g2"][j, :, :], writes=[P + "g2b"])
        bk, bkk = self.bk(P)
        S = kb.sbuf(P + "S", [128, 4, 128])
        Sb = kb.sbuf(P + "Sb", [128, 4, 128], BF16)
        zlast = self.zl_dram[j].ap()
        xn = kb.sbuf(P + "xn", [128, D], BF16)
        xnT = kb.sbuf(P + "xnT", [128, 8, 128], BF16)
        zb = kb.sbuf(P + "zb", [128, BPROJ])
        prev = kb.sbuf(P + "prev", [128, BPROJ])
        sm = kb.sbuf(P + "sm", [128, 40])
        f = {}
        for nme in ("logw", "a", "gt", "kk", "kf", "bon", "T1", "T2", "T3"):
            f[nme] = kb.sbuf(P + nme, [128, 512])
        wab = kb.sbuf(P + "wab", [128, 128], BF16)
        sgb = kb.sbuf(P + "sgb", [128, 128], BF16)
        waT = kb.sbuf(P + "waT", [128, 2, 128], BF16)

        def inproj(t, rows=128):
            if t == "s":
                src, sk, rc = self.hs[0:NS, :], "hs", rstd[0:NS, NT:NT + 1]
            else:
                src, sk, rc = self.hp[:, t, :], ("hp", t), rstd[:, t:t + 1]
            R = slice(0, rows)
            self.norm_transpose(P0, src, sk, rc, gain, xn, P + "xn", bk[2], bkk[2], xnT[:, :, 0:rows], P + "xnT", rows=rows)
            for g in range(4):
                wdt = 512 if g < 3 else 256
                b = g % 2
                for k in range(8):
                    kb.op("pe", lambda e, k=k, g=g, b=b, wdt=wdt: e.matmul(bk[b][R, 0:wdt], xnT[:, k, 0:rows],
                                                                          w[:, k, g * 512:g * 512 + wdt], start=(k == 0), stop=(k == 7)),
                          reads=[P + "xnT"] + wkeys, writes=[bkk[b]], inc=(k == 7))
                eng = "act" if g % 2 == 0 else "dve"
                if eng == "act":
                    kb.op("act", lambda e, g=g, b=b, wdt=wdt: e.copy(out=zb[R, g * 512:g * 512 + wdt], in_=bk[b][R, 0:wdt]),
                          reads=[bkk[b]], writes=[P + "zb"])
                else:
                    kb.op("dve", lambda e, g=g, b=b, wdt=wdt: e.tensor_copy(out=zb[R, g * 512:g * 512 + wdt], in_=bk[b][R, 0:wdt]),
                          reads=[bkk[b]], writes=[P + "zb"])

        def elem(rows):
            R = slice(0, rows)
            kb.op("dve", lambda e: e.tensor_tensor(out=prev[R, :], in0=prev[R, :], in1=zb[R, :], op=ALU.subtract),
                  reads=[P + "prev", P + "zb"], writes=[P + "prev"])
            kb.op("dve", lambda e: e.tensor_tensor(out=prev[R, :], in0=prev[R, :], in1=mub[R, :], op=ALU.mult),
                  reads=[P + "prev", P + "mu"], writes=[P + "prev"])
            kb.op("dve", lambda e: e.tensor_tensor(out=prev[R, :], in0=prev[R, :], in1=zb[R, :], op=ALU.add),
                  reads=[P + "prev", P + "zb"], writes=[P + "prev"])
            zs = prev
            r_, k0, v_ = zs[R, 0:512], zs[R, 512:1024], zs[R, 1024:1536]
            kb.op("act", lambda e: e.activation(out=wab[R, 0:64], in_=zs[R, 1536:1600], func=AF.Tanh), reads=[P + "prev"], writes=[P + "wab"])
            yield
            kb.op("act", lambda e: e.copy(out=wab[R, 64:128], in_=zs[R, 1600:1664]), reads=[P + "prev"], writes=[P + "wab"])
            th = f["T3"]
            kb.op("act", lambda e: e.activation(out=th[R, 0:128], in_=zs[R, 1664:1792], func=AF.Tanh, scale=0.5), reads=[P + "prev"], writes=[P + "T3"])
            kb.op("dve", lambda e: e.tensor_scalar(out=sgb[R, :], in0=th[R, 0:128], scalar1=0.5, scalar2=0.5, op0=ALU.mult, op1=ALU.add),
                  reads=[P + "T3"], writes=[P + "sgb"])
            pv = bk[2][:].bitcast(BF16).rearrange("p (k t) -> p k t", k=8)
            kb.op("pe", lambda e: e.transpose(pv[:, 0, 0:rows], wab[R, :], self.identb[R, R]), reads=[P + "wab", "identb"], writes=[bkk[2]], inc=False)
            kb.op("pe", lambda e: e.transpose(pv[:, 1, 0:rows], sgb[R, :], self.identb[R, R]), reads=[P + "sgb", "identb"], writes=[bkk[2]])
            yield
            kb.op("act", lambda e: e.copy(out=waT[:, :, 0:rows], in_=pv[:, 0:2, 0:rows]), reads=[bkk[2]], writes=[P + "waT"])
            kb.op("pe", lambda e: e.matmul(bk[0][R, :], waT[0:64, 0, 0:rows], w2a2[0:64, :], start=True, stop=True),
                  reads=[P + "waT", P + "w2a2"], writes=[bkk[0]])
            kb.op("pe", lambda e: e.matmul(bk[1][R, :], waT[64:128, 0, 0:rows], w2a2[64:128, :], start=True, stop=True),
                  reads=[P + "waT", P + "w2a2"], writes=[bkk[1]])
            kb.op("pe", lambda e: e.matmul(bk[2][R, :], waT[:, 1, 0:rows], g2b[:, :], start=True, stop=True),
                  reads=[P + "waT", P + "g2b"], writes=[bkk[2]])
            T1, T2, T3 = f["T1"], f["T2"], f["T3"]
            kb.op("dve", lambda e: e.tensor_tensor(out=T1[R, :], in0=bk[0][R, :], in1=w0b[R, :], op=ALU.add), reads=[bkk[0], P + "w0b"], writes=[P + "T1"])
            yield
            kb.op("act", lambda e: e.activation(out=T1[R, :], in_=T1[R, :], func=AF.Tanh, scale=0.5), reads=[P + "T1"], writes=[P + "T1"])
            kb.op("dve", lambda e: e.tensor_scalar(out=f["logw"][R, :], in0=T1[R, :], scalar1=-HDS, scalar2=-HDS, op0=ALU.mult, op1=ALU.add),
                  reads=[P + "T1"], writes=[P + "logw"])
            kb.op("dve", lambda e: e.tensor_tensor(out=T2[R, :], in0=bk[1][R, :], in1=a0b[R, :], op=ALU.add), reads=[bkk[1], P + "a0b"], writes=[P + "T2"])
            kb.op("act", lambda e: e.activation(out=T2[R, :], in_=T2[R, :], func=AF.Tanh, scale=0.5), reads=[P + "T2"], writes=[P + "T2"])
            kb.op("dve", lambda e: e.tensor_scalar(out=f["a"][R, :], in0=T2[R, :], scalar1=0.5, scalar2=0.5, op0=ALU.mult, op1=ALU.add),
                  reads=[P + "T2"], writes=[P + "a"])
            yield
            kb.op("act", lambda e: e.copy(out=f["gt"][R, :], in_=bk[2][R, :]), reads=[bkk[2]], writes=[P + "gt"])
            kk = f["kk"]
            kb.op("dve", lambda e: e.tensor_tensor(out=kk[R, :], in0=k0, in1=kkb[R, :], op=ALU.mult), reads=[P + "prev", P + "kkb"], writes=[P + "kk"])
            kb.op("dve", lambda e: e.tensor_tensor(out=T3[R, :], in0=kk[R, :], in1=kk[R, :], op=ALU.mult), reads=[P + "kk"], writes=[P + "T3"])
            kb.op("dve", lambda e: e.tensor_reduce(out=sm[R, 0:8], in_=T3[R, :].rearrange("p (h d) -> p h d", h=8), axis=AX.X, op=ALU.add),
                  reads=[P + "T3"], writes=[P + "sm"])
            kb.op("act", lambda e: e.activation(out=sm[R, 8:16], in_=sm[R, 0:8], func=AF.Sqrt), reads=[P + "sm"], writes=[P + "sm"])
            yield
            kb.op("dve", lambda e: e.tensor_scalar(out=sm[R, 8:16], in0=sm[R, 8:16], scalar1=1e-12, scalar2=None, op0=ALU.max), reads=[P + "sm"], writes=[P + "sm"])
            kb.op("dve", lambda e: e.reciprocal(out=sm[R, 16:24], in_=sm[R, 8:16]), reads=[P + "sm"], writes=[P + "sm"])
            kb.op("dve", lambda e: e.tensor_tensor(out=kk[R, :].rearrange("p (h d) -> p h d", h=8), in0=kk[R, :].rearrange("p (h d) -> p h d", h=8),
                                                   in1=sm[R, 16:24].unsqueeze(2).to_broadcast([rows, 8, 64]), op=ALU.mult),
                  reads=[P + "kk", P + "sm"], writes=[P + "kk"])
            kb.op("dve", lambda e: e.scalar_tensor_tensor(out=T3[R, :], in0=f["a"][R, :], scalar=-1.0, in1=kab[R, :], op0=ALU.add, op1=ALU.mult),
                  reads=[P + "a", P + "kab"], writes=[P + "T3"])
            kb.op("dve", lambda e: e.scalar_tensor_tensor(out=f["kf"][R, :], in0=T3[R, :], scalar=1.0, in1=k0, op0=ALU.add, op1=ALU.mult),
                  reads=[P + "T3", P + "prev"], writes=[P + "kf"])
            yield
            kb.op("dve", lambda e: e.tensor_tensor(out=T3[R, :], in0=r_, in1=f["kf"][R, :], op=ALU.mult), reads=[P + "prev", P + "kf"], writes=[P + "T3"])
            kb.op("dve", lambda e: e.tensor_tensor(out=T3[R, :], in0=T3[R, :], in1=rkb[R, :], op=ALU.mult), reads=[P + "T3", P + "rkb"], writes=[P + "T3"])
            kb.op("dve", lambda e: e.tensor_reduce(out=sm[R, 24:32], in_=T3[R, :].rearrange("p (h d) -> p h d", h=8), axis=AX.X, op=ALU.add),
                  reads=[P + "T3"], writes=[P + "sm"])
            kb.op("dve", lambda e: e.tensor_tensor(out=f["bon"][R, :].rearrange("p (h d) -> p h d", h=8), in0=v_.rearrange("p (h d) -> p h d", h=8),
                                                   in1=sm[R, 24:32].unsqueeze(2).to_broadcast([rows, 8, 64]), op=ALU.mult),
                  reads=[P + "prev", P + "sm"], writes=[P + "bon"])

        inproj(NT - 1)
        ci, co = self.cc_sh_in[j].ap(), self.cc_sh_out[j].ap()
        kb.dma("sp", ci[0:1, :], zb[127:128, :], reads=[P + "zb"], writes=[P + "ccin"])
        kb.collective(ci, co, [[0, 1, 2, 3], [4, 5, 6, 7]], reads=[P + "ccin"], writes=[P + "ccout"])
        mz = kb.push_scope()
        z4 = kb.sbuf(P + "z4", [1, 4, BPROJ])
        zsel = kb.sbuf(P + "zsel", [1, BPROJ])
        kb.dma("sp", z4[0:1, :, :], co.rearrange("(o r) c -> o r c", o=1), reads=[P + "ccout"], writes=[P + "z4"])
        kb.op("dve", lambda e: e.tensor_scalar(out=zsel[0:1, :], in0=z4[0:1, 0, :], scalar1=self.cst[0:1, C_SELP:C_SELP + 1], scalar2=None, op0=ALU.mult),
              reads=[P + "z4", "cst"], writes=[P + "zsel"])
        for r in range(1, 4):
            kb.op("dve", lambda e, r=r: e.scalar_tensor_tensor(out=zsel[0:1, :], in0=z4[0:1, r, :], scalar=self.cst[0:1, C_SELP + r:C_SELP + r + 1],
                                                               in1=zsel[0:1, :], op0=ALU.mult, op1=ALU.add),
                  reads=[P + "z4", "cst", P + "zsel"], writes=[P + "zsel"])
        kb.dma("sp", zlast[0:1, :], zsel[0:1, :], reads=[P + "zsel"], writes=[P + "zlast"])
        kb.barrier()
        kb.pop_scope(mz)
        kb.op("pool", lambda e: e.memset(S[:], 0.0), writes=[P + "S"])
        for c in range(4):
            kb.op("dve", lambda e, c=c: e.tensor_copy(out=S[0:64, c, 64:128], in_=self.cst[0:64, C_ID:C_ID + 64]), reads=["cst", P + "S"], writes=[P + "S"])
            kb.op("dve", lambda e, c=c: e.tensor_copy(out=S[64:128, c, 64:128], in_=self.cst[64:128, C_ID + 64:C_ID + 128]), reads=["cst", P + "S"], writes=[P + "S"])
        kb.op("act", lambda e: e.copy(out=Sb[:], in_=S[:]), reads=[P + "S"], writes=[P + "Sb"])

        m1 = kb.push_scope()
        Gs, Ea, Eb = f["gt"], f["bon"], f["T3"]
        hbs = [{}, {}]
        Vbs, TTs, gcts = [], [], []
        for i2 in range(2):
            for nme in ("At", "Bt", "Kt", "Rt", "Bh", "Kh"):
                hbs[i2][nme] = kb.sbuf(P + "d%d" % i2 + nme, [128, 512], BF16)
            Vbs.append(kb.sbuf(P + "d%dVb" % i2, [128, 8, 128], BF16))
            kb.op("pool", lambda e, i2=i2: e.memset(Vbs[i2][:], 0.0), writes=[P + "d%dVb" % i2])
            TTs.append(kb.sbuf(P + "d%dTT" % i2, [128, 16, 128], BF16))
            gcts.append(kb.sbuf(P + "d%dgct" % i2, [128, 4]))
        Yb_t = kb.sbuf(P + "Yb", [128, 512], BF16)
        cm = {}
        for nme in ("Pa", "PTa", "Pb", "PTb", "Q", "AKT", "RBT", "RKT", "Ub"):
            cm[nme] = kb.sbuf(P + nme, [128, 8, 128], BF16)
        WT = cm["Pb"][:, 0:4, :]
        OPt = Yb_t[:].rearrange("p (a t) -> p a t", a=4)
        tri = self.cst[:, C_TRI:C_TRI + 128]
        trs = self.cst[:, C_TRS:C_TRS + 128]
        low = self.cst[:, C_LOW:C_LOW + 128]
        ones = self.cst[:, C_ONE:C_ONE + 128]

        def X(t):
            i2 = t % 2
            PD = P + "d%d" % i2
            hb, Vb, TT, gct = hbs[i2], Vbs[i2], TTs[i2], gcts[i2]
            inproj(t)
            kb.dma("sp", prev[1:128, :], zb[0:127, :], reads=[P + "zb"], writes=[P + "prev"])
            kb.dma("sp", prev[0:1, :], zlast[0:1, :], reads=[P + "zlast"], writes=[P + "prev"])
            kb.dma("sp", zlast[0:1, :], zb[127:128, :], reads=[P + "zb"], writes=[P + "zlast"])
            if t == NT - 1:
                kb.dma("sp", self.out["shp"][j:j + 1, :], zb[127:128, :], reads=[P + "zb"], is_output=True)
            yield from elem(128)
            kb.dma("sp", self.rec_g.ap()[t, :, :], f["gt"][:], reads=[P + "gt"], writes=[("rec_g", t)])
            yield
            kb.dma("sp", self.rec_bo.ap()[t, :, :], f["bon"][:], reads=[P + "bon"], writes=[("rec_bo", t)])
            logw, a_, kk, kf = f["logw"], f["a"], f["kk"], f["kf"]
            T1, T2, T3 = f["T1"], f["T2"], f["T3"]
            zs = prev
            kb.op("pe", lambda e: e.matmul(bk[0][:, :], tri, logw[:, :], start=True, stop=True), reads=["cst", P + "logw"], writes=[bkk[0]])
            kb.op("pe", lambda e: e.matmul(bk[1][:, :], ones, logw[:, :], start=True, stop=True), reads=["cst", P + "logw"], writes=[bkk[1]])
            for h in range(8):
                hp_, c = h % 2, h // 2
                kb.op("pe", lambda e, h=h, hp_=hp_, c=c: e.matmul(bk[2][hp_ * 64:(hp_ + 1) * 64, c:c + 1], logw[:, h * 64:(h + 1) * 64],
                                                                ones[:, 0:1], start=True, stop=True),
                      reads=["cst", P + "logw"], writes=[bkk[2]], inc=(h == 7))
            kb.op("act", lambda e: e.copy(out=Gs[:], in_=bk[0][:, :]), reads=[bkk[0]], writes=[P + "gt"])
            kb.op("act", lambda e: e.activation(out=gct[:], in_=bk[2][:, 0:4], func=AF.Exp), reads=[bkk[2]], writes=[PD + "gct"])
            yield
            kb.op("dve", lambda e: e.tensor_tensor(out=T1[:], in0=Gs[:], in1=logw[:], op=ALU.subtract), reads=[P + "gt", P + "logw"], writes=[P + "T1"])
            kb.op("act", lambda e: e.activation(out=Ea[:], in_=T1[:], func=AF.Exp), reads=[P + "T1"], writes=[P + "bon"])
            kb.op("dve", lambda e: e.scalar_tensor_tensor(out=hb["At"][:], in0=kk[:], scalar=-1.0, in1=Ea[:], op0=ALU.mult, op1=ALU.mult),
                  reads=[P + "kk", P + "bon"], writes=[PD + "At"])
            kb.op("act", lambda e: e.activation(out=Eb[:], in_=Gs[:], func=AF.Exp), reads=[P + "gt"], writes=[P + "T3"])
            kb.op("dve", lambda e: e.tensor_tensor(out=hb["Rt"][:], in0=zs[:, 0:512], in1=Eb[:], op=ALU.mult), reads=[P + "prev", P + "T3"], writes=[PD + "Rt"])
            yield
            kb.op("dve", lambda e: e.tensor_tensor(out=T2[:], in0=kk[:], in1=a_[:], op=ALU.mult), reads=[P + "kk", P + "a"], writes=[P + "T2"])
            kb.op("act", lambda e: e.activation(out=Ea[:], in_=Gs[:], func=AF.Exp, scale=-1.0), reads=[P + "gt"], writes=[P + "bon"])
            kb.op("dve", lambda e: e.tensor_tensor(out=hb["Bt"][:], in0=T2[:], in1=Ea[:], op=ALU.mult), reads=[P + "T2", P + "bon"], writes=[PD + "Bt"])
            kb.op("dve", lambda e: e.tensor_tensor(out=hb["Kt"][:], in0=kf[:], in1=Ea[:], op=ALU.mult), reads=[P + "kf", P + "bon"], writes=[PD + "Kt"])
            kb.op("dve", lambda e: e.tensor_tensor(out=T1[:], in0=bk[1][:, :], in1=Gs[:], op=ALU.subtract), reads=[bkk[1], P + "gt"], writes=[P + "T1"])
            yield
            kb.op("act", lambda e: e.activation(out=Eb[:], in_=T1[:], func=AF.Exp), reads=[P + "T1"], writes=[P + "T3"])
            kb.op("dve", lambda e: e.tensor_tensor(out=hb["Bh"][:], in0=T2[:], in1=Eb[:], op=ALU.mult), reads=[P + "T2", P + "T3"], writes=[PD + "Bh"])
            kb.op("dve", lambda e: e.tensor_tensor(out=hb["Kh"][:], in0=kf[:], in1=Eb[:], op=ALU.mult), reads=[P + "kf", P + "T3"], writes=[PD + "Kh"])
            kb.op("act", lambda e: e.copy(out=Vb[:, :, 0:64], in_=zs[:, 1024:1536].rearrange("p (h d) -> p h d", h=8)), reads=[P + "prev"], writes=[PD + "Vb"])
            for qi, nme in enumerate(("At", "Bt", "Kt", "Rt")):
                pv = bk[qi // 2][:].bitcast(BF16).rearrange("p (k t) -> p k t", k=8)
                for c in range(4):
                    kb.op("pe", lambda e, qi=qi, c=c, nme=nme, pv=pv: e.transpose(pv[:, (qi % 2) * 4 + c, :], hb[nme][:, c * 128:(c + 1) * 128], self.identb[:]),
                          reads=[PD + nme, "identb"], writes=[bkk[qi // 2]], inc=(c == 3))
            for half in range(2):
                pv = bk[half][:].bitcast(BF16).rearrange("p (k t) -> p k t", k=8)
                kb.op("act", lambda e, half=half, pv=pv: e.copy(out=TT[:, half * 8:(half + 1) * 8, :], in_=pv[:]), reads=[bkk[half]], writes=[PD + "TT"])

        def Y(t):
            i2 = t % 2
            PD = P + "d%d" % i2
            hb, Vb, TT, gct = dict(hbs[i2]), Vbs[i2], TTs[i2], gcts[i2]
            hb["Yb"] = Yb_t

            def hsl(h):
                return slice((h % 2) * 64, (h % 2) * 64 + 64), h // 2

            def HI(h):
                return (h % 2) * 4 + h // 2

            def mm_pair(dst, l_idx, r_idx, mask, bank0, add_ident=False):
                for hp_ in range(2):
                    b = bank0 + hp_
                    hs_ = slice(hp_ * 64, hp_ * 64 + 64)
                    for c in range(4):
                        kb.op("pe", lambda e, hs_=hs_, c=c, b=b: e.matmul(bk[b][:, c * 128:(c + 1) * 128], TT[hs_, l_idx * 4 + c, :],
                                                                          TT[hs_, r_idx * 4 + c, :], start=True, stop=True),
                              reads=[PD + "TT"], writes=[bkk[b]], inc=(c == 3))
                    kb.op("dve", lambda e, hp_=hp_, b=b: e.tensor_tensor(
                        out=dst[:, hp_ * 4:(hp_ + 1) * 4, :], in0=bk[b][:].rearrange("p (a t) -> p a t", a=4),
                        in1=mask.unsqueeze(1).to_broadcast([128, 4, 128]), op=ALU.mult), reads=[bkk[b], "cst"], writes=[(dst_key[id(dst)], hp_)])

            dst_key = {id(cm[n]): P + n for n in cm}
            A_, B_, K_, R_ = 0, 1, 2, 3
            mm_pair(cm["Pa"], B_, A_, trs, 3)
            yield
            mm_pair(cm["PTa"], A_, B_, low, 5)
            yield
            mm_pair(cm["AKT"], K_, A_, trs, 3)
            yield
            mm_pair(cm["RBT"], B_, R_, tri, 5)
            yield
            mm_pair(cm["RKT"], K_, R_, tri, 3)
            yield
            Q = cm["Q"]

            def K2(n):
                return [(P + n, 0), (P + n, 1)]
            kb.op("dve", lambda e: e.tensor_tensor(out=Q[:], in0=cm["Pa"][:], in1=self.identb[:].unsqueeze(1).to_broadcast([128, 8, 128]), op=ALU.add),
                  reads=K2("Pa") + ["identb"], writes=K2("Q"))
            cur, nxt = ("Pa", "PTa"), ("Pb", "PTb")
            for step in range(6):
                Pc, PTc = cm[cur[0]], cm[cur[1]]
                Pn, PTn = cm[nxt[0]], cm[nxt[1]]
                for half in range(2):
                    b0 = 3 + half * 2
                    if step < 5:
                        for hi in range(4):
                            h = half * 4 + hi
                            kb.op("pe", lambda e, hi=hi, h=h, b0=b0: e.matmul(bk[b0][:, hi * 128:(hi + 1) * 128], PTc[:, h, :], Pc[:, h, :], start=True, stop=True),
                                  reads=[(P + cur[0], half), (P + cur[1], half)], writes=[bkk[b0]], inc=(hi == 3))
                    for hi in range(4):
                        h = half * 4 + hi
                        kb.op("pe", lambda e, hi=hi, h=h, b0=b0: e.matmul(bk[b0 + 1][:, hi * 128:(hi + 1) * 128], Pc[:, h, :], PTc[:, h, :], start=True, stop=True),
                              reads=[(P + cur[0], half), (P + cur[1], half)], writes=[bkk[b0 + 1]], inc=(hi == 3))
                for half in range(2):
                    b0 = 3 + half * 2
                    hs4 = slice(half * 4, half * 4 + 4)
                    if step < 5:
                        kb.op("act", lambda e, hs4=hs4, b0=b0: e.copy(out=Pn[:, hs4, :], in_=bk[b0][:].rearrange("p (a t) -> p a t", a=4)),
                              reads=[bkk[b0]], writes=[(P + nxt[0], half)])
                    kb.op("act", lambda e, hs4=hs4, b0=b0: e.copy(out=PTn[:, hs4, :], in_=bk[b0 + 1][:].rearrange("p (a t) -> p a t", a=4)),
                          reads=[bkk[b0 + 1]], writes=[(P + nxt[1], half)])
                for half in range(2):
                    b0 = 3 + half * 2
                    hs4 = slice(half * 4, half * 4 + 4)
                    for hi in range(4):
                        h = half * 4 + hi
                        kb.op("pe", lambda e, hi=hi, h=h, b0=b0: e.matmul(bk[7][:, hi * 128:(hi + 1) * 128], PTn[:, h, :], Q[:, h, :], start=True, stop=True),
                              reads=[(P + nxt[1], half), (P + "Q", half)], writes=[bkk[7]], inc=(hi == 3))
                    kb.op("dve", lambda e, hs4=hs4, b0=b0: e.tensor_tensor(out=Q[:, hs4, :], in0=Q[:, hs4, :], in1=bk[7][:].rearrange("p (a t) -> p a t", a=4), op=ALU.add),
                          reads=[bkk[7], (P + "Q", half)], writes=[(P + "Q", half)])
                cur, nxt = nxt, cur
                yield
            for h in range(8):
                hs_, c = hsl(h)
                kb.op("pe", lambda e, h=h, hs_=hs_, c=c: e.matmul(bk[3][hs_, c * 128:(c + 1) * 128], hb["At"][:, h * 64:(h + 1) * 64], Q[:, HI(h), :], start=True, stop=True),
                      reads=[PD + "At"] + K2("Q"), writes=[bkk[3]], inc=(h == 7))
            kb.op("act", lambda e: e.copy(out=WT, in_=bk[3][:].rearrange("p (a t) -> p a t", a=4)), reads=[bkk[3]], writes=[(P + "Pb", 0), (P + "Pb", 1)])
            for h in range(8):
                kb.op("pe", lambda e, h=h: e.matmul(bk[4][:, h * 64:(h + 1) * 64], cm["AKT"][:, HI(h), :], Vb[:, h, 0:64], start=True, stop=True),
                      reads=K2("AKT") + [PD + "Vb"], writes=[bkk[4]], inc=(h == 7))
            kb.op("act", lambda e: e.copy(out=hb["Yb"][:], in_=bk[4][:]), reads=[bkk[4]], writes=[P + "Yb"])
            for h in range(8):
                hs_, c = hsl(h)
                b = 5 + h % 2
                o0 = c * 128
                kb.op("pe", lambda e, h=h, b=b, o0=o0: e.matmul(bk[b][:, o0:o0 + 64], Q[:, HI(h), :], hb["Yb"][:, h * 64:(h + 1) * 64], start=True, stop=False),
                      reads=K2("Q") + [P + "Yb"], writes=[bkk[b]], inc=False)
                kb.op("pe", lambda e, hs_=hs_, c=c, b=b, o0=o0: e.matmul(bk[b][:, o0:o0 + 64], WT[hs_, c, :], Sb[hs_, c, 0:64], start=False, stop=True),
                      reads=[(P + "Pb", 0), (P + "Pb", 1), P + "Sb"], writes=[bkk[b]], inc=False)
                kb.op("pe", lambda e, hs_=hs_, c=c, b=b, o0=o0: e.matmul(bk[b][:, o0 + 64:o0 + 128], WT[hs_, c, :], Sb[hs_, c, 64:128], start=True, stop=True),
                      reads=[(P + "Pb", 0), (P + "Pb", 1), P + "Sb"], writes=[bkk[b]], inc=(h >= 6))
            Ub = cm["Ub"]
            for half in range(2):
                kb.op("act", lambda e, half=half: e.copy(out=Ub[:, half * 4:(half + 1) * 4, :], in_=bk[5 + half][:].rearrange("p (a t) -> p a t", a=4)),
                      reads=[bkk[5 + half]], writes=[P + "Ub"])
            for h in range(8):
                hs_, c = hsl(h)
                kb.op("pe", lambda e, h=h, hs_=hs_, c=c: e.matmul(bk[7][:, h * 64:(h + 1) * 64], TT[hs_, R_ * 4 + c, :], Sb[hs_, c, 0:64], start=True, stop=False),
                      reads=[PD + "TT", P + "Sb"], writes=[bkk[7]], inc=False)
                kb.op("pe", lambda e, h=h: e.matmul(bk[7][:, h * 64:(h + 1) * 64], cm["RBT"][:, HI(h), :], Ub[:, HI(h), 0:64], start=False, stop=False),
                      reads=K2("RBT") + [P + "Ub"], writes=[bkk[7]], inc=False)
                kb.op("pe", lambda e, h=h: e.matmul(bk[7][:, h * 64:(h + 1) * 64], cm["RKT"][:, HI(h), :], Vb[:, h, 0:64], start=False, stop=True),
                      reads=K2("RKT") + [PD + "Vb"], writes=[bkk[7]], inc=(h == 7))
            ol = cm["Pa"][:].rearrange("p a b -> p (a b)").bitcast(F32)
            kb.op("act", lambda e: e.copy(out=ol, in_=bk[7][:]), reads=[bkk[7]], writes=K2("Pa"))
            yield
            kb.dma("pool", self.rec_ol.ap()[t, :, :], ol, reads=K2("Pa"), writes=[("rec_ol", t)])
            for h in range(8):
                hs_, c = hsl(h)
                kb.op("pe", lambda e, hs_=hs_, c=c: e.matmul(bk[3][hs_, c * 128:(c + 1) * 128], Sb[hs_, c, 64:128], TT[hs_, R_ * 4 + c, :], start=True, stop=False),
                      reads=[PD + "TT", P + "Sb"], writes=[bkk[3]], inc=False)
                kb.op("pe", lambda e, h=h, hs_=hs_, c=c: e.matmul(bk[3][hs_, c * 128:(c + 1) * 128], Ub[:, HI(h), 64:128], cm["RBT"][:, HI(h), :], start=False, stop=True),
                      reads=K2("RBT") + [P + "Ub"], writes=[bkk[3]], inc=(h == 7))
            kb.op("act", lambda e: e.copy(out=OPt, in_=bk[3][:].rearrange("p (a t) -> p a t", a=4)), reads=[bkk[3]], writes=[P + "Yb"])
            kb.dma("pool", self.rec_op.ap()[t, :, :, :], OPt, reads=[P + "Yb"], writes=[("rec_op", t)])
            for h in range(8):
                hs_, c = hsl(h)
                kb.op("pe", lambda e, h=h, hs_=hs_, c=c: e.matmul(bk[4][hs_, c * 128:(c + 1) * 128], hb["Bh"][:, h * 64:(h + 1) * 64], Ub[:, HI(h), :], start=True, stop=False),
                      reads=[PD + "Bh", P + "Ub"], writes=[bkk[4]], inc=False)
                kb.op("pe", lambda e, h=h, hs_=hs_, c=c: e.matmul(bk[4][hs_, c * 128:c * 128 + 64], hb["Kh"][:, h * 64:(h + 1) * 64], Vb[:, h, 0:64], start=False, stop=True),
                      reads=[PD + "Kh", PD + "Vb"], writes=[bkk[4]], inc=(h == 7))
            kb.op("dve", lambda e: e.tensor_tensor(out=S[:], in0=S[:], in1=gct[:].unsqueeze(2).to_broadcast([128, 4, 128]), op=ALU.mult),
                  reads=[P + "S", PD + "gct"], writes=[P + "S"])
            kb.op("dve", lambda e: e.tensor_tensor(out=S[:], in0=S[:], in1=bk[4][:].rearrange("p (a t) -> p a t", a=4), op=ALU.add),
                  reads=[P + "S", bkk[4]], writes=[P + "S"])
            yield
            kb.op("act", lambda e: e.copy(out=Sb[:], in_=S[:]), reads=[P + "S"], writes=[P + "Sb"])
        for _ in X(0):
            pass
        for t in range(NT):
            gens = [Y(t)] + ([X(t + 1)] if t + 1 < NT else [])
            while gens:
                for g in list(gens):
                    try:
                        next(g)
                    except StopIteration:
                        gens.remove(g)
        kb.barrier()
        kb.pop_scope(m1)
        kb.dma("sp", self.cc_st_in[j].ap()[:, :], S[:].rearrange("p a b -> p (a b)"), reads=[P + "S"], writes=[P0 + "ccst"])
        kb.collective(self.cc_st_in[j].ap(), self.cc_st_out[j].ap(), [[0, 1, 2, 3], [4, 5, 6, 7]], reads=[P0 + "ccst"], writes=[P0 + "ccsto"])

        m2 = kb.push_scope()
        inproj("s", rows=NS)
        kb.dma("sp", self.out["shs"][j, :, :], zb[0:NS, :], reads=[P + "zb"], is_output=True)
        kb.dma("sp", prev[0:NS, :], I["sshift"][j, :, :], writes=[P + "prev"])
        for _ in elem(NS):
            pass
        RS = slice(0, NS)
        zs = prev
        kb.op("act", lambda e: e.activation(out=f["T1"][RS, :], in_=f["logw"][RS, :], func=AF.Exp), reads=[P + "logw"], writes=[P + "T1"])
        srcs = [(zs[RS, 0:512], P + "prev"), (f["T1"][RS, :], P + "T1"), (f["kf"][RS, :], P + "kf"), (zs[RS, 1024:1536], P + "prev"),
                (f["kk"][RS, :], P + "kk"), (f["a"][RS, :], P + "a")]
        vec = kb.sbuf(P + "vec", [128, 6, 64])
        St = kb.sbuf(P + "St", [128, 64, 64])
        Tm = kb.sbuf(P + "Tm", [128, 64, 64])
        gn2 = kb.sbuf(P + "gn2", [128, 2, 64])
        for h in range(8):
            HS = slice(h * 16, (h + 1) * 16)
            for qi, (sap, skey) in enumerate(srcs):
                kb.dma("sp", vec[HS, qi, :], sap[:, h * 64:(h + 1) * 64], reads=[skey], writes=[P + "vec"])
            kb.dma("sp", St[HS, :, :].rearrange("p a b -> p (a b)"), I["swkv"][j, :, h, :], writes=[P + "St"])
            kb.dma("sp", gn2[HS, 0, :], I["gn_gain"][j, h * 64:(h + 1) * 64].partition_broadcast(NS), writes=[P + "gn2"])
            kb.dma("sp", gn2[HS, 1, :], I["gn_bias"][j, h * 64:(h + 1) * 64].partition_broadcast(NS), writes=[P + "gn2"])
        sv = kb.sbuf(P + "sv", [128, 8, 64])
        s3 = kb.sbuf(P + "s3", [128, 8])

        def bi(x):
            return x.unsqueeze(1).to_broadcast([128, 64, 64])

        def bj(x):
            return x.unsqueeze(2).to_broadcast([128, 64, 64])
        r_, w_, k_, v_, kk_, a_ = (vec[:, i, :] for i in range(6))
        kb.op("dve", lambda e: e.tensor_tensor(out=Tm[:], in0=St[:], in1=bi(kk_), op=ALU.mult), reads=[P + "St", P + "vec"], writes=[P + "Tm"])
        kb.op("dve", lambda e: e.tensor_reduce(out=sv[:, 0, :], in_=Tm[:], axis=AX.X, op=ALU.add), reads=[P + "Tm"], writes=[P + "sv"])
        kb.op("dve", lambda e: e.tensor_tensor(out=St[:], in0=St[:], in1=bi(w_), op=ALU.mult), reads=[P + "St", P + "vec"], writes=[P + "St"])
        kb.op("dve", lambda e: e.tensor_tensor(out=sv[:, 1, :], in0=kk_, in1=a_, op=ALU.mult), reads=[P + "vec"], writes=[P + "sv"])
        kb.op("dve", lambda e: e.tensor_tensor(out=Tm[:], in0=bj(sv[:, 0, :]), in1=bi(sv[:, 1, :]), op=ALU.mult), reads=[P + "sv", P + "Tm"], writes=[P + "Tm"])
        kb.op("dve", lambda e: e.tensor_tensor(out=St[:], in0=St[:], in1=Tm[:], op=ALU.subtract), reads=[P + "St", P + "Tm"], writes=[P + "St"])
        kb.op("dve", lambda e: e.tensor_tensor(out=Tm[:], in0=bj(v_), in1=bi(k_), op=ALU.mult), reads=[P + "vec", P + "Tm"], writes=[P + "Tm"])
        kb.op("dve", lambda e: e.tensor_tensor(out=St[:], in0=St[:], in1=Tm[:], op=ALU.add), reads=[P + "St", P + "Tm"], writes=[P + "St"])
        for h in range(8):
            kb.dma("sp", self.out["wkvs"][j, :, h, :], St[h * 16:(h + 1) * 16, :, :].rearrange("p a b -> p (a b)"), reads=[P + "St"], is_output=True)
        kb.op("dve", lambda e: e.tensor_tensor(out=Tm[:], in0=St[:], in1=bi(r_), op=ALU.mult), reads=[P + "St", P + "vec"], writes=[P + "Tm"])
        kb.op("dve", lambda e: e.tensor_reduce(out=sv[:, 2, :], in_=Tm[:], axis=AX.X, op=ALU.add), reads=[P + "Tm"], writes=[P + "sv"])
        o_ = sv[:, 2, :]
        kb.op("dve", lambda e: e.tensor_reduce(out=s3[:, 0:1], in_=o_, axis=AX.X, op=ALU.add), reads=[P + "sv"], writes=[P + "s3"])
        kb.op("dve", lambda e: e.tensor_scalar(out=s3[:, 1:2], in0=s3[:, 0:1], scalar1=-1.0 / 64, scalar2=None, op0=ALU.mult), reads=[P + "s3"], writes=[P + "s3"])
        kb.op("dve", lambda e: e.tensor_scalar(out=sv[:, 3, :], in0=o_, scalar1=s3[:, 1:2], scalar2=None, op0=ALU.add), reads=[P + "sv", P + "s3"], writes=[P + "sv"])
        kb.op("dve", lambda e: e.tensor_tensor(out=sv[:, 4, :], in0=sv[:, 3, :], in1=sv[:, 3, :], op=ALU.mult), reads=[P + "sv"], writes=[P + "sv"])
        kb.op("dve", lambda e: e.tensor_reduce(out=s3[:, 2:3], in_=sv[:, 4, :], axis=AX.X, op=ALU.add), reads=[P + "sv"], writes=[P + "s3"])
        kb.op("act", lambda e: e.activation(out=s3[:, 3:4], in_=s3[:, 2:3], func=AF.Sqrt, bias=self.epsr[:, 2:3], scale=1.0 / 64), reads=[P + "s3", "epsr"], writes=[P + "s3"])
        kb.op("dve", lambda e: e.reciprocal(out=s3[:, 4:5], in_=s3[:, 3:4]), reads=[P + "s3"], writes=[P + "s3"])
        kb.op("dve", lambda e: e.scalar_tensor_tensor(out=sv[:, 5, :], in0=sv[:, 3, :], scalar=s3[:, 4:5], in1=gn2[:, 0, :], op0=ALU.mult, op1=ALU.mult),
              reads=[P + "sv", P + "s3", P + "gn2"], writes=[P + "sv"])
        kb.op("dve", lambda e: e.tensor_tensor(out=sv[:, 5, :], in0=sv[:, 5, :], in1=gn2[:, 1, :], op=ALU.add), reads=[P + "sv", P + "gn2"], writes=[P + "sv"])
        obs = kb.sbuf(P + "obs", [128, 512])
        kb.op("pool", lambda e: e.memset(obs[:], 0.0), writes=[P + "obs"])
        for h in range(8):
            kb.dma("sp", obs[0:NS, h * 64:(h + 1) * 64], sv[h * 16:(h + 1) * 16, 5, :], reads=[P + "sv"], writes=[P + "obs"])
        kb.op("dve", lambda e: e.tensor_tensor(out=obs[RS, :], in0=obs[RS, :], in1=f["bon"][RS, :], op=ALU.add), reads=[P + "obs", P + "bon"], writes=[P + "obs"])
        kb.op("dve", lambda e: e.tensor_tensor(out=obs[RS, :], in0=obs[RS, :], in1=f["gt"][RS, :], op=ALU.mult), reads=[P + "obs", P + "gt"], writes=[P + "obs"])
        kb.dma("sp", self.rec_ol.ap()[NT, :, :], obs[:], reads=[P + "obs"], writes=[("rec_ol", NT)])
        kb.barrier()
        kb.pop_scope(m2)
        kb.pop_scope(mark)

    def even_combine(self, L, P0):
        kb = self.kb
        I = self.inp
        j = L // 2
        P = P0 + "c"
        mark = kb.push_scope()
        gpost = self.load_bcast(P0 + "gpost", I["norm_mix_post"][L, :], D)
        gng = self.load_bcast(P + "gng", I["gn_gain"][j, :], 512)
        gnb = self.load_bcast(P + "gnb", I["gn_bias"][j, :], 512)
        wout = self.load_w_rows(P + "wout", I["w_out_even"][j], 8, D)
        wkeys = [(P + "wout", k) for k in range(8)]
        bk, bkk = self.bk(P)
        Hstb = kb.sbuf(P + "Hstb", [128, 4, 64], BF16)
        m0 = kb.push_scope()
        G4 = kb.sbuf(P + "G4", [128, 4, 4, 128])
        kb.dma("sp", G4[:].rearrange("p r c x -> p r (c x)"), self.cc_st_out[j].ap().rearrange("(r p) x -> p r x", p=128),
               reads=[P0 + "ccsto"], writes=[P + "G4"])
        PhT = kb.sbuf(P + "PhT", [128, 4, 4, 64])
        Hs = kb.sbuf(P + "Hs", [128, 5, 4, 64])
        Hst = kb.sbuf(P + "Hst", [128, 4, 64])
        idf = self.ident_f()
        for r in range(4):
            for h in range(8):
                hp_ = h % 2
                hs_ = slice(hp_ * 64, hp_ * 64 + 64)
                c = h // 2
                b = hp_ * 2 + r // 2
                o0 = ((r % 2) * 4 + c) * 64
                kb.op("pe", lambda e, r=r, hs_=hs_, c=c, b=b, o0=o0: e.matmul(bk[b][hs_, o0:o0 + 64], G4[hs_, r, c, 64:128], idf[hs_, hs_], start=True, stop=True),
                      reads=[P + "G4", "cst"], writes=[bkk[b]])
        for b in range(4):
            hs_ = slice((b // 2) * 64, (b // 2) * 64 + 64)
            r0 = (b % 2) * 2
            kb.op("dve", lambda e, b=b, hs_=hs_, r0=r0: e.tensor_copy(out=PhT[hs_, r0:r0 + 2, :, :].rearrange("p r c x -> p (r c x)"), in_=bk[b][hs_, :]),
                  reads=[bkk[b]], writes=[P + "PhT"])
        kb.op("dve", lambda e: e.tensor_copy(out=Hs[:, 1, :, :], in_=G4[:, 0, :, 0:64]), reads=[P + "G4"], writes=[P + "Hs"])
        for r in range(1, 4):
            for h in range(8):
                hp_ = h % 2
                hs_ = slice(hp_ * 64, hp_ * 64 + 64)
                c = h // 2
                kb.op("pe", lambda e, r=r, hs_=hs_, c=c, hp_=hp_: e.matmul(bk[4 + hp_][hs_, c * 64:(c + 1) * 64], PhT[hs_, r, c, :], Hs[hs_, r, c, :], start=True, stop=True),
                      reads=[P + "PhT", P + "Hs"], writes=[bkk[4 + hp_]])
            for hp_ in range(2):
                hs_ = slice(hp_ * 64, hp_ * 64 + 64)
                kb.op("dve", lambda e, r=r, hs_=hs_, hp_=hp_: e.tensor_tensor(out=Hs[hs_, r + 1, :, :], in0=bk[4 + hp_][hs_, 0:256].rearrange("p (c x) -> p c x", c=4),
                                                                            in1=G4[hs_, r, :, 0:64], op=ALU.add), reads=[bkk[4 + hp_], P + "G4", P + "Hs"], writes=[P + "Hs"])
        sel = self.cst[:, C_SELH:C_SELH + 4]
        kb.op("dve", lambda e: e.tensor_scalar(out=Hst[:], in0=Hs[:, 1, :, :], scalar1=sel[:, 1:2], scalar2=None, op0=ALU.mult), reads=[P + "Hs", "cst"], writes=[P + "Hst"])
        for r in (2, 3):
            kb.op("dve", lambda e, r=r: e.scalar_tensor_tensor(out=Hst[:], in0=Hs[:, r, :, :], scalar=sel[:, r:r + 1], in1=Hst[:], op0=ALU.mult, op1=ALU.add),
                  reads=[P + "Hs", "cst", P + "Hst"], writes=[P + "Hst"])
        kb.op("act", lambda e: e.copy(out=Hstb[:], in_=Hst[:]), reads=[P + "Hst"], writes=[P + "Hstb"])
        o4 = kb.sbuf(P + "o4", [64, 2, 4, 64])
        for h in range(8):
            hp_ = h % 2
            hs_ = slice(hp_ * 64, hp_ * 64 + 64)
            c = h // 2
            kb.op("pe", lambda e, hs_=hs_, c=c, hp_=hp_: e.matmul(bk[6 + hp_][0:64, c * 64:(c + 1) * 64], Hs[hs_, 4, c, :], idf[hs_, hs_], start=True, stop=True),
                  reads=[P + "Hs", "cst"], writes=[bkk[6 + hp_]])
        for hp_ in range(2):
            kb.op("dve", lambda e, hp_=hp_: e.tensor_copy(out=o4[:, hp_, :, :].rearrange("p c x -> p (c x)"), in_=bk[6 + hp_][0:64, 0:256]), reads=[bkk[6 + hp_]], writes=[P + "o4"])
        for h in range(8):
            kb.dma("sp", self.out["wkvp"][j, h, :, :], o4[:, h % 2, h // 2, :], reads=[P + "o4"], is_output=True)
        kb.barrier()
        kb.pop_scope(m0)

        rec = [{n: kb.sbuf(P + n + str(i), [128, 512]) for n in ("ol", "bo", "g", "oa")} for i in range(2)]
        opt = [kb.sbuf(P + "opt%d" % i, [128, 4, 128], BF16) for i in range(2)]
        o = kb.sbuf(P + "o", [128, 512])
        xc = kb.sbuf(P + "xc", [128, 512])
        sq = kb.sbuf(P + "sq", [128, 512])
        ycat = kb.sbuf(P + "ycat", [128, D], BF16)
        oT = kb.sbuf(P + "oT", [128, 8, 128], BF16)
        tmps = [kb.sbuf(P + "tmp%d" % i, [128, D]) for i in range(2)]
        sm = kb.sbuf(P + "sm", [128, 32])
        small = kb.sbuf(P0 + "small", [128, 4])
        units = [(t, 128) for t in range(NT)] + [("s", NS)]
        for i, (t, rows) in enumerate(units):
            b = i % 2
            ti = NT if t == "s" else t
            R = slice(0, rows)
            rk_ = {n: P + n + str(b) for n in ("ol", "bo", "g", "oa")}
            kb.dma("sp", rec[b]["ol"][:], self.rec_ol.ap()[ti, :, :], reads=[("rec_ol", ti)], writes=[rk_["ol"]])
            kb.dma("sp", rec[b]["oa"][:], self.rec_oa.ap()[ti, :, :], reads=[("rec_oa", ti)], writes=[rk_["oa"]])
            if t != "s":
                kb.dma("sp", rec[b]["bo"][:], self.rec_bo.ap()[ti, :, :], reads=[("rec_bo", ti)], writes=[rk_["bo"]])
                kb.dma("sp", rec[b]["g"][:], self.rec_g.ap()[ti, :, :], reads=[("rec_g", ti)], writes=[rk_["g"]])
                kb.dma("sp", opt[b][:], self.rec_op.ap()[ti, :, :, :], reads=[("rec_op", ti)], writes=[P + "opt%d" % b])
                for h in range(8):
                    hp_ = h % 2
                    hs_ = slice(hp_ * 64, hp_ * 64 + 64)
                    c = h // 2
                    kb.op("pe", lambda e, hs_=hs_, c=c, b=b, hp_=hp_: e.matmul(bk[6 + hp_][:, c * 64:(c + 1) * 64], opt[b][hs_, c, :], Hstb[hs_, c, :], start=True, stop=True),
                          reads=[P + "opt%d" % b, P + "Hstb"], writes=[bkk[6 + hp_]])
                for hp_ in range(2):
                    kb.op("dve", lambda e, b=b, hp_=hp_: e.tensor_tensor(
                        out=o[:].rearrange("p (c q d) -> p c q d", c=4, q=2)[:, :, hp_, :], in0=bk[6 + hp_][:, 0:256].rearrange("p (c d) -> p c d", c=4),
                        in1=rec[b]["ol"][:].rearrange("p (c q d) -> p c q d", c=4, q=2)[:, :, hp_, :], op=ALU.add),
                        reads=[bkk[6 + hp_], rk_["ol"]], writes=[P + "o"])
                if "ob" in self.dbg:
                    pass
                o3 = o[:].rearrange("p (h d) -> p h d", h=8)
                x3 = xc[:].rearrange("p (h d) -> p h d", h=8)
                kb.op("dve", lambda e: e.tensor_reduce(out=sm[:, 0:8], in_=o3, axis=AX.X, op=ALU.add), reads=[P + "o"], writes=[P + "sm"])
                kb.op("dve", lambda e: e.tensor_scalar(out=sm[:, 8:16], in0=sm[:, 0:8], scalar1=-1.0 / 64, scalar2=None, op0=ALU.mult), reads=[P + "sm"], writes=[P + "sm"])
                kb.op("dve", lambda e: e.tensor_tensor(out=x3, in0=o3, in1=sm[:, 8:16].unsqueeze(2).to_broadcast([128, 8, 64]), op=ALU.add),
                      reads=[P + "o", P + "sm"], writes=[P + "xc"])
                kb.op("dve", lambda e: e.tensor_tensor(out=sq[:], in0=xc[:], in1=xc[:], op=ALU.mult), reads=[P + "xc"], writes=[P + "sq"])
                kb.op("dve", lambda e: e.tensor_reduce(out=sm[:, 16:24], in_=sq[:].rearrange("p (h d) -> p h d", h=8), axis=AX.X, op=ALU.add),
                      reads=[P + "sq"], writes=[P + "sm"])
                kb.op("act", lambda e: e.activation(out=sm[:, 16:24], in_=sm[:, 16:24], func=AF.Sqrt, bias=self.epsr[:, 2:3], scale=1.0 / 64),
                      reads=[P + "sm", "epsr"], writes=[P + "sm"])
                kb.op("dve", lambda e: e.reciprocal(out=sm[:, 24:32], in_=sm[:, 16:24]), reads=[P + "sm"], writes=[P + "sm"])
                kb.op("dve", lambda e: e.tensor_tensor(out=x3, in0=x3, in1=sm[:, 24:32].unsqueeze(2).to_broadcast([128, 8, 64]), op=ALU.mult),
                      reads=[P + "xc", P + "sm"], writes=[P + "xc"])
                kb.op("dve", lambda e: e.tensor_tensor(out=xc[:], in0=xc[:], in1=gng[:], op=ALU.mult), reads=[P + "xc", P + "gng"], writes=[P + "xc"])
                kb.op("dve", lambda e: e.tensor_tensor(out=xc[:], in0=xc[:], in1=gnb[:], op=ALU.add), reads=[P + "xc", P + "gnb"], writes=[P + "xc"])
                kb.op("dve", lambda e, b=b: e.tensor_tensor(out=xc[:], in0=xc[:], in1=rec[b]["bo"][:], op=ALU.add), reads=[P + "xc", rk_["bo"]], writes=[P + "xc"])
                kb.op("dve", lambda e, b=b: e.tensor_tensor(out=ycat[:, 512:1024], in0=xc[:], in1=rec[b]["g"][:], op=ALU.mult), reads=[P + "xc", rk_["g"]], writes=[P + "ycat"])
                if "ob" in self.dbg:
                    kb.op("dve", lambda e, b=b: e.tensor_tensor(out=sq[:], in0=xc[:], in1=rec[b]["g"][:], op=ALU.mult), reads=[P + "xc", rk_["g"]], writes=[P + "sq"])
                    kb.dma("sp", self.out["dbg_ob"][t * 128:(t + 1) * 128, :], sq[:], reads=[P + "sq"], is_output=True)
            else:
                kb.op("act", lambda e, b=b: e.copy(out=ycat[R, 512:1024], in_=rec[b]["ol"][R, :]), reads=[rk_["ol"]], writes=[P + "ycat"])
            kb.op("act", lambda e, b=b: e.copy(out=ycat[R, 0:512], in_=rec[b]["oa"][R, :]), reads=[rk_["oa"]], writes=[P + "ycat"])
            pv = bk[1][:].bitcast(BF16).rearrange("p (k t) -> p k t", k=8)
            for k in range(8):
                kb.op("pe", lambda e, k=k: e.transpose(pv[:, k, 0:rows], ycat[R, k * 128:(k + 1) * 128], self.identb[R, R]),
                      reads=[P + "ycat", "identb"], writes=[bkk[1]], inc=(k == 7))
            kb.op("act", lambda e: e.copy(out=oT[:, :, 0:rows], in_=pv[:, :, 0:rows]), reads=[bkk[1]], writes=[P + "oT"])
            pyb = (bk[2], bk[3]) if b == 0 else (bk[4], bk[5])
            pyk = (bkk[2], bkk[3]) if b == 0 else (bkk[4], bkk[5])
            for half in range(2):
                for k in range(8):
                    kb.op("pe", lambda e, k=k, half=half: e.matmul(pyb[half][R, :], oT[:, k, 0:rows], wout[:, k, half * 512:(half + 1) * 512],
                                                                   start=(k == 0), stop=(k == 7)),
                          reads=[P + "oT"] + wkeys, writes=[pyk[half]], inc=(k == 7))
            if t == "s":
                dst, dk = self.hs[0:NS, :], "hs"
            else:
                dst, dk = self.hp[:, t, :], ("hp", t)
            self.post_norm_residual2(P0, pyb, pyk, gpost, dst, dk, tmps[b], P + "tmp%d" % b, small, rows)
        kb.barrier()
        kb.pop_scope(mark)

    def post_norm_residual2(self, pfx, pyb, pyk, gain, dst_ap, dst_key, tmp, tmp_key, small, rows=128):
        kb = self.kb
        sk = pfx + "small"
        R = slice(0, rows)
        for half in range(2):
            kb.op("act", lambda e, half=half: e.activation(out=tmp[R, half * 512:(half + 1) * 512], in_=pyb[half][R, :], func=AF.Square,
                                                           accum_out=small[R, half:half + 1]),
                  reads=[pyk[half]], writes=[tmp_key, sk])
        kb.op("dve", lambda e: e.tensor_tensor(out=small[R, 0:1], in0=small[R, 0:1], in1=small[R, 1:2], op=ALU.add), reads=[sk], writes=[sk])
        kb.op("act", lambda e: e.activation(out=small[R, 1:2], in_=small[R, 0:1], func=AF.Sqrt, bias=self.epsr[R, 0:1], scale=1.0 / D),
              reads=[sk, "epsr"], writes=[sk])
        kb.op("dve", lambda e: e.reciprocal(out=small[R, 2:3], in_=small[R, 1:2]), reads=[sk], writes=[sk])
        for half in range(2):
            kb.op("dve", lambda e, half=half: e.scalar_tensor_tensor(out=tmp[R, half * 512:(half + 1) * 512], in0=pyb[half][R, :],
                                                                     scalar=small[R, 2:3], in1=gain[R, half * 512:(half + 1) * 512],
                                                                     op0=ALU.mult, op1=ALU.mult),
                  reads=[pyk[half], sk, pfx + "gpost"], writes=[tmp_key])
        kb.op("pool", lambda e: e.tensor_tensor(out=dst_ap, in0=dst_ap, in1=tmp[R, :], op=ALU.add), reads=[tmp_key, dst_key], writes=[dst_key])

    def write_y(self):
        kb = self.kb
        for t in range(NT):
            kb.dma("sp", self.out["yp"][t * 128:(t + 1) * 128, :], self.hp[:, t, :], reads=[("hp", t)],
                   is_output=True)
        kb.dma("sp", self.out["ys"][:, :], self.hs[0:NS, :], reads=["hs"], is_output=True)

    def build(self):
        self.declare()
        self.setup()
        for kind, L in self.plan:
            if kind == "ffn":
                self.ffn(L)
            elif kind == "odd":
                self.odd(L)
            elif kind == "even":
                self.even(L)
        self.write_y()
        self.kb.finish()
        self.kb.close()
        return self.nc


def make_consts(c):
    q = c % 4
    cst = np.zeros((128, NCONST), np.float32)
    cst[:, C_ID:C_ID + 128] = np.eye(128, dtype=np.float32)
    s = np.arange(128)[:, None]
    t = np.arange(128)[None, :]
    cst[:, C_TRI:C_TRI + 128] = (s <= t)
    cst[:, C_TRS:C_TRS + 128] = (s < t)
    cst[:, C_LOW:C_LOW + 128] = (s > t)
    cst[:, C_ONE:C_ONE + 128] = 1.0
    qi = np.arange(128)[:, None]
    kj = np.arange(256)[None, :]
    vis = (kj >= qi) & (kj <= qi + 128)
    cst[:, C_MA:C_MA + 256] = np.where(vis, 0.0, NEG)
    vis0 = vis & ((q > 0) | (kj >= 128))
    cst[:, C_M0:C_M0 + 256] = np.where(vis0, 0.0, NEG)
    inv = np.power(np.float32(10000.0), -np.arange(32, dtype=np.float32) / np.float32(32)).astype(np.float32)
    pos = (q * 2048 + np.arange(2048)).astype(np.float32)
    ang = (pos[:, None] * inv[None, :]).astype(np.float32)
    cst[:, C_COS:C_COS + 512] = np.cos(ang).astype(np.float32).reshape(16, 128, 32).transpose(1, 0, 2).reshape(128, 512)
    cst[:, C_SIN:C_SIN + 512] = np.sin(ang).astype(np.float32).reshape(16, 128, 32).transpose(1, 0, 2).reshape(128, 512)
    angs = (np.float32(8192.0) * inv).astype(np.float32)
    cst[:, C_COSS:C_COSS + 32] = np.cos(angs)[None, :]
    cst[:, C_SINS:C_SINS + 32] = np.sin(angs)[None, :]
    if q > 0:
        cst[:, C_SELP + q - 1] = 1.0
    cst[:, C_SELH + q] = 1.0
    return cst


_WEIGHT_KEYS = ["norm_mix_pre", "norm_mix_post", "norm_ffn_pre", "norm_ffn_post", "w_in_even", "attn_sinks", "shift_mu",
                "decay_w0", "decay_w2", "iclr_a0", "iclr_a2", "gate_g2", "key_k", "key_a", "gn_gain", "gn_bias",
                "w_out_even", "w_in_odd", "sgu_ln_gain", "sgu_ln_bias", "sgu_w", "sgu_b", "w_out_odd", "ffn_w_gate",
                "ffn_w_up", "ffn_conv_w", "ffn_conv_b", "ffn_w_down"]

FULL_PLAN = [("even", 0), ("ffn", 0), ("odd", 1), ("ffn", 1), ("even", 2), ("ffn", 2), ("odd", 3), ("ffn", 3)]


def _core_inputs(c, inp, shared):
    b, q = c // 4, c % 4
    f = lambda a: np.ascontiguousarray(a, dtype=np.float32)
    d = dict(shared)
    d["xp"] = f(inp["x_prompt"][b, q * 2048:(q + 1) * 2048])
    d["xs"] = f(inp["x_sample"][c * NS:(c + 1) * NS, 0])
    d["ck"] = f(inp["cache_win_k"][:, c * NS:(c + 1) * NS].reshape(2, NS, 128, 128))
    d["cv"] = f(inp["cache_win_v"][:, c * NS:(c + 1) * NS].reshape(2, NS, 128, 128))
    d["swkv"] = f(inp["state_wkv"][:, c * NS:(c + 1) * NS].reshape(2, NS, 8, 4096))
    d["sshift"] = f(inp["state_shift"][:, c * NS:(c + 1) * NS])
    d["sconv"] = f(inp["state_ffn_conv"][:, c * NS:(c + 1) * NS])
    d["consts"] = make_consts(c)
    return d


def kernel(**inputs):
    inp = {k: np.asarray(v) for k, v in inputs.items()}
    shared = {k: np.ascontiguousarray(inp[k], dtype=np.float32) for k in _WEIGHT_KEYS}
    shared["bonus_r_k"] = np.ascontiguousarray(inp["bonus_r_k"], dtype=np.float32).reshape(2, 512)
    prog = Prog(FULL_PLAN)
    nc = prog.build()
    in_maps = [_core_inputs(c, inp, shared) for c in range(8)]
    res = run_bass_kernel_spmd(nc, in_maps, core_ids=list(range(8))).results
    f32 = np.float32
    y_prompt = np.zeros((2, 8192, D), f32)
    y_sample = np.zeros((128, 1, D), f32)
    wkp = np.zeros((2, 2, 128, 2, 64), f32)
    wvp = np.zeros((2, 2, 128, 2, 64), f32)
    wks = np.zeros((2, 128, 1, 2, 64), f32)
    wvs = np.zeros((2, 128, 1, 2, 64), f32)
    wkvp = np.zeros((2, 2, 8, 64, 64), f32)
    wkvs = np.zeros((2, 128, 8, 64, 64), f32)
    shp = np.zeros((2, 2, BPROJ), f32)
    shs = np.zeros((2, 128, BPROJ), f32)
    sguv = np.zeros((2, 128, 1, D), f32)
    convp = np.zeros((4, 2, 2, DFF), f32)
    convs = np.zeros((4, 128, 2, DFF), f32)
    for c in range(8):
        b, q = c // 4, c % 4
        r = res[c]
        sl = slice(c * NS, (c + 1) * NS)
        y_prompt[b, q * 2048:(q + 1) * 2048] = r["yp"]
        y_sample[sl, 0] = r["ys"]
        wks[:, sl, 0] = r["wks"].reshape(2, NS, 2, 64)
        wvs[:, sl, 0] = r["wvs"].reshape(2, NS, 2, 64)
        wkvs[:, sl] = r["wkvs"].reshape(2, NS, 8, 64, 64)
        shs[:, sl] = r["shs"]
        sguv[:, sl, 0] = r["sguv"]
        convs[:, sl] = r["convs"]
        if q == 3:
            wkp[:, b] = r["wkp"].reshape(2, 128, 2, 64)
            wvp[:, b] = r["wvp"].reshape(2, 128, 2, 64)
            wkvp[:, b] = r["wkvp"]
            shp[:, b] = r["shp"]
            convp[:, b] = r["convp"]
    return (y_prompt, y_sample, wkp, wvp, wks, wvs, wkvp, wkvs, shp, shs, sguv, convp, convs)
```

```python
import numpy as np
import concourse.bass as bass
import concourse.mybir as mybir
from concourse.bass_utils import run_bass_kernel_spmd

F32 = mybir.dt.float32
BF16 = mybir.dt.bfloat16
AF = mybir.ActivationFunctionType
ALU = mybir.AluOpType
AX = mybir.AxisListType

NT = 16
NS = 16
D = 1024
DFF = 2816
NFC = 22
EVEN_PROJ = 2560
BPROJ = 1792
RMS_EPS = 1e-6
LN_EPS = 1e-5
GN_EPS = 64e-5
DECAY_SCALE = 0.606531
NEG = -1e30
RELAX_SAME_ENGINE = False

C_ID = 0
C_TRI = 128
C_TRS = 256
C_LOW = 384
C_ONE = 512
C_MA = 640
C_M0 = 896
C_SELP = 1152
C_SELH = 1156
NCST = 1160
C_COS = 1160
C_SIN = 1672
C_COSS = 2184
C_SINS = 2216
NCONST = 2248


class _Eng:
    def __init__(self, name, handle, is_pe=False):
        self.name = name
        self.h = handle
        self.is_pe = is_pe
        self.sem = None
        self.count = 0
        self.waited = {}
        self.dsems = []
        self.dcount = []
        self.dnext = 0


class KB:
    def __init__(self, nc, n_dma_sems=8, n_cc=12):
        self.nc = nc
        self.E = {
            "pe": _Eng("pe", nc.tensor, True),
            "act": _Eng("act", nc.scalar),
            "dve": _Eng("dve", nc.vector),
            "pool": _Eng("pool", nc.gpsimd),
            "sp": _Eng("sp", nc.sync),
        }
        self.lastw = {}
        self.readers = {}
        self.ctx = []
        self.sem_by_id = {}
        self.out_events = []
        self.n_ops = 0
        self.block = self.enter(nc.Block())
        for name, e in self.E.items():
            e.sem = self.enter(nc.semaphore("sem_" + name))
            self.sem_by_id[id(e.sem)] = e.sem
        for name in ("sp", "pool"):
            e = self.E[name]
            for i in range(n_dma_sems):
                s = self.enter(nc.semaphore("dsem_%s_%d" % (name, i)))
                self.sem_by_id[id(s)] = s
                e.dsems.append(s)
                e.dcount.append(0)
        self.cc_sems = []
        for i in range(n_cc):
            s = self.enter(nc.semaphore("ccsem_%d" % i))
            self.sem_by_id[id(s)] = s
            self.cc_sems.append(s)
        self.cc_used = 0
        self.all_events = {}
        self.exclusive = set()

    def enter(self, cm):
        v = cm.__enter__()
        self.ctx.append(cm)
        return v

    def push_scope(self):
        return len(self.ctx)

    def pop_scope(self, mark):
        while len(self.ctx) > mark:
            self.ctx.pop().__exit__(None, None, None)

    def sbuf(self, name, shape, dtype=F32):
        return self.enter(self.nc.sbuf_tensor(name, list(shape), dtype))

    def psum(self, name, shape, dtype=F32):
        self.exclusive.add(name)
        return self.enter(self.nc.psum_tensor(name, list(shape), dtype))

    def _deps(self, reads, writes, raw_keys=()):
        deps = {}
        raw = {}

        def add(sid, val, is_raw):
            if deps.get(sid, 0) < val:
                deps[sid] = val
            if is_raw and raw.get(sid, 0) < val:
                raw[sid] = val

        for r in reads:
            ev = self.lastw.get(r)
            if ev is not None:
                add(ev[0], ev[1], True)
        for w in writes:
            ev = self.lastw.get(w)
            if ev is not None:
                add(ev[0], ev[1], w in raw_keys)
            for sid, val in self.readers.get(w, {}).items():
                add(sid, val, False)
        self._raw = raw
        return deps

    def _record(self, ev, reads, writes):
        sid, val = ev
        if self.all_events.get(sid, 0) < val:
            self.all_events[sid] = val
        for r in reads:
            d = self.readers.setdefault(r, {})
            if d.get(sid, 0) < val:
                d[sid] = val
        for w in writes:
            self.lastw[w] = ev
            self.readers[w] = {}

    def _emit_waits(self, e, deps, raw=None):
        for sid, val in deps.items():
            if sid == id(e.sem):
                if e.is_pe:
                    continue
                if raw is not None and RELAX_SAME_ENGINE:
                    val = raw.get(sid, 0)
                    if val == 0:
                        continue
            if e.waited.get(sid, 0) >= val:
                continue
            e.waited[sid] = val
            e.h.wait_ge(self.sem_by_id[sid], val)

    def op(self, eng, fn, reads=(), writes=(), inc=True):
        e = self.E[eng]
        ex = [r for r in reads if r in self.exclusive]
        if ex:
            writes = list(writes) + ex
        deps = self._deps(reads, writes, raw_keys=ex)
        self._emit_waits(e, deps, self._raw)
        if inc:
            e.count += 1
            val = e.count
        else:
            val = e.count + 1
        ins = fn(e.h)
        if inc:
            ins.then_inc(e.sem, 1)
        ev = (id(e.sem), val)
        self._record(ev, reads, writes)
        self.n_ops += 1
        return ev

    def dma(self, eng, out, in_, reads=(), writes=(), is_output=False, **kw):
        e = self.E[eng]
        deps = self._deps(reads, writes)
        i = e.dnext
        e.dnext = (e.dnext + 1) % len(e.dsems)
        s = e.dsems[i]
        if e.dcount[i] > 0:
            deps[id(s)] = max(deps.get(id(s), 0), 16 * e.dcount[i])
        self._emit_waits(e, deps)
        e.dcount[i] += 1
        val = 16 * e.dcount[i]
        e.h.dma_start(out=out, in_=in_, **kw).then_inc(s, 16)
        ev = (id(s), val)
        self._record(ev, reads, writes)
        if is_output:
            self.out_events.append(ev)
        self.n_ops += 1
        return ev

    def collective(self, in_ap, out_ap, groups, reads=(), writes=()):
        e = self.E["pool"]
        self._emit_waits(e, self._deps(reads, writes))
        s = self.cc_sems[self.cc_used]
        self.cc_used += 1
        e.h.collective_compute("AllGather", ALU.bypass, replica_groups=groups,
                               ins=[in_ap], outs=[out_ap]).then_inc(s)
        ev = (id(s), 1)
        self._record(ev, reads, writes)
        return ev

    def barrier(self):
        for name, e in self.E.items():
            self._emit_waits(e, dict(self.all_events))

    def finish(self):
        e = self.E["sp"]
        self._emit_waits(e, dict(self.all_events))

    def close(self):
        self.pop_scope(0)


def _bcast_row(ap1d, n=128):
    return ap1d.partition_broadcast(n)


class Prog:
    def __init__(self, plan, dbg=()):
        self.plan = plan
        self.dbg = dbg
        nc = self.nc = bass.Bass("TRN2", target_bir_lowering=False)
        self.kb = KB(nc)
        self.inp = {}
        self.out = {}
        self.uid = 0

    def din(self, name, shape):
        self.inp[name] = self.nc.dram_tensor(name, list(shape), F32, kind="ExternalInput").ap()
        return self.inp[name]

    def dout(self, name, shape):
        self.out[name] = self.nc.dram_tensor(name, list(shape), F32, kind="ExternalOutput").ap()
        return self.out[name]

    def nm(self, s):
        self.uid += 1
        return "%s_%d" % (s, self.uid)

    def declare(self):
        di = self.din
        di("xp", [NT * 128, D]); di("xs", [NS, D])
        di("ck", [2, NS, 128, 128]); di("cv", [2, NS, 128, 128])
        di("swkv", [2, NS, 8, 4096]); di("sshift", [2, NS, BPROJ]); di("sconv", [4, NS, 2, DFF])
        di("norm_mix_pre", [4, D]); di("norm_mix_post", [4, D]); di("norm_ffn_pre", [4, D]); di("norm_ffn_post", [4, D])
        di("w_in_even", [2, D, EVEN_PROJ]); di("attn_sinks", [2, 8]); di("shift_mu", [2, BPROJ])
        di("decay_w0", [2, 512]); di("decay_w2", [2, 64, 512]); di("iclr_a0", [2, 512]); di("iclr_a2", [2, 64, 512])
        di("gate_g2", [2, 128, 512]); di("key_k", [2, 512]); di("key_a", [2, 512]); di("bonus_r_k", [2, 512])
        di("gn_gain", [2, 512]); di("gn_bias", [2, 512]); di("w_out_even", [2, D, D])
        di("w_in_odd", [2, D, 2 * D]); di("sgu_ln_gain", [2, D]); di("sgu_ln_bias", [2, D])
        di("sgu_w", [2, 8, 128, 128]); di("sgu_b", [2, 8, 128]); di("w_out_odd", [2, D, D])
        di("ffn_w_gate", [4, D, DFF]); di("ffn_w_up", [4, D, DFF]); di("ffn_conv_w", [4, 3, DFF])
        di("ffn_conv_b", [4, DFF]); di("ffn_w_down", [4, DFF, D])
        di("consts", [128, NCONST])
        do = self.dout
        do("yp", [NT * 128, D]); do("ys", [NS, D])
        do("wkp", [2, 128, 128]); do("wvp", [2, 128, 128]); do("wks", [2, NS, 128]); do("wvs", [2, NS, 128])
        do("wkvp", [2, 8, 64, 64]); do("wkvs", [2, NS, 8, 4096]); do("shp", [2, BPROJ]); do("shs", [2, NS, BPROJ])
        do("sguv", [2, NS, D]); do("convp", [4, 2, DFF]); do("convs", [4, NS, 2, DFF])
        nc = self.nc
        self.cc_ffn_in = [nc.dram_tensor("ccfi%d" % l, [128, 44], F32) for l in range(4)]
        self.cc_ffn_out = [nc.dram_tensor("ccfo%d" % l, [512, 44], F32) for l in range(4)]
        self.cc_att_in = [nc.dram_tensor("ccai%d" % l, [128, 256], F32) for l in range(2)]
        self.cc_att_out = [nc.dram_tensor("ccao%d" % l, [512, 256], F32) for l in range(2)]
        self.cc_sh_in = [nc.dram_tensor("ccshi%d" % l, [1, BPROJ], F32) for l in range(2)]
        self.cc_sh_out = [nc.dram_tensor("ccsho%d" % l, [4, BPROJ], F32) for l in range(2)]
        self.zl_dram = [nc.dram_tensor("zl%d" % l, [1, BPROJ], F32) for l in range(2)]
        self.rec_op = nc.dram_tensor("rec_op", [NT, 128, 4, 128], BF16)
        if "oa" in self.dbg:
            self.dout("dbg_oa", [NT * 128, 512])
        if "ob" in self.dbg:
            self.dout("dbg_ob", [NT * 128, 512])
        self.cc_st_in = [nc.dram_tensor("ccsi%d" % l, [128, 512], F32) for l in range(2)]
        self.cc_st_out = [nc.dram_tensor("ccso%d" % l, [512, 512], F32) for l in range(2)]
        self.rec_oa = nc.dram_tensor("rec_oa", [NT + 1, 128, 512], F32)
        self.rec_ol = nc.dram_tensor("rec_ol", [NT + 1, 128, 512], F32)
        self.rec_bo = nc.dram_tensor("rec_bo", [NT + 1, 128, 512], F32)
        self.rec_g = nc.dram_tensor("rec_g", [NT + 1, 128, 512], F32)

    def setup(self):
        kb = self.kb
        self.hp = kb.sbuf("hp", [128, NT, D])
        self.hs = kb.sbuf("hs", [128, D])
        self.cst = kb.sbuf("cst", [128, NCST])
        self.identb = kb.sbuf("identb", [128, 128], BF16)
        self.epsr = kb.sbuf("epsr", [128, 4])
        kb.dma("sp", self.cst[:], self.inp["consts"][:, 0:NCST], writes=["cst"])
        for t in range(NT):
            kb.dma("sp", self.hp[:, t, :], self.inp["xp"][t * 128:(t + 1) * 128, :], writes=[("hp", t)])
        kb.dma("sp", self.hs[0:NS, :], self.inp["xs"][:, :], writes=["hs"])
        kb.op("dve", lambda e: e.tensor_copy(out=self.identb[:], in_=self.cst[:, C_ID:C_ID + 128]),
              reads=["cst"], writes=["identb"])
        kb.op("pool", lambda e: e.memset(self.epsr[:, 0:1], RMS_EPS), writes=["epsr"])
        kb.op("pool", lambda e: e.memset(self.epsr[:, 1:2], LN_EPS), writes=["epsr"])
        kb.op("pool", lambda e: e.memset(self.epsr[:, 2:3], GN_EPS), writes=["epsr"])
        kb.op("pool", lambda e: e.memset(self.epsr[:, 3:4], 0.0), writes=["epsr"])

    def ident_f(self):
        return self.cst[:, C_ID:C_ID + 128]

    def prenorm_stats(self, pfx, junk=None):
        kb = self.kb
        ss = kb.sbuf(pfx + "ss", [128, NT + 1])
        rstd = kb.sbuf(pfx + "rstd", [128, NT + 1])
        mj = kb.push_scope()
        junk = kb.sbuf(pfx + "junk", [128, D], BF16)
        kb.op("pool", lambda e: e.memset(ss[:], 0.0), writes=[pfx + "ss"])
        for t in range(NT):
            kb.op("act", lambda e, t=t: e.activation(out=junk[:], in_=self.hp[:, t, :], func=AF.Square,
                                                     accum_out=ss[:, t:t + 1]),
                  reads=[("hp", t)], writes=[pfx + "junk", pfx + "ss"])
        kb.op("act", lambda e: e.activation(out=junk[0:NS, :], in_=self.hs[0:NS, :], func=AF.Square,
                                            accum_out=ss[0:NS, NT:NT + 1]),
              reads=["hs"], writes=[pfx + "junk", pfx + "ss"])
        kb.op("act", lambda e: e.activation(out=rstd[:], in_=ss[:], func=AF.Sqrt, bias=self.epsr[:, 0:1],
                                            scale=1.0 / D), reads=[pfx + "ss", "epsr"], writes=[pfx + "rstd"])
        kb.op("dve", lambda e: e.reciprocal(out=rstd[:], in_=rstd[:]), reads=[pfx + "rstd"], writes=[pfx + "rstd"])
        kb.barrier()
        kb.pop_scope(mj)
        return rstd

    def load_bcast(self, name, src1d, n):
        kb = self.kb
        t = kb.sbuf(name, [128, n])
        kb.dma("sp", t[:], _bcast_row(src1d), writes=[name])
        return t

    def norm_transpose(self, pfx, src_ap, src_key, rstd_col, gain, xn, xn_key, psT, ps_key, dst_ap, dst_key,
                       rows=128, c0=0, c1=None):
        kb = self.kb
        kb.op("dve", lambda e: e.scalar_tensor_tensor(out=xn[0:rows, :], in0=src_ap, scalar=rstd_col,
                                                      in1=gain[0:rows, :], op0=ALU.mult, op1=ALU.mult),
              reads=[src_key, pfx + "rstd", pfx + "gain"], writes=[xn_key])
        pv = psT[:].bitcast(BF16).rearrange("p (k t) -> p k t", k=8)
        for k in range(8):
            kb.op("pe", lambda e, k=k: e.transpose(pv[:, k, 0:rows], xn[0:rows, k * 128:(k + 1) * 128],
                                                   self.identb[0:rows, 0:rows]),
                  reads=[xn_key, "identb"], writes=[ps_key], inc=(k == 7))
        if c1 is None:
            c1 = rows
        kb.op("act", lambda e: e.copy(out=dst_ap, in_=pv[:, :, c0:c1]), reads=[ps_key], writes=[dst_key])

    def post_norm_residual(self, pfx, ps2, ps_key, gain, dst_ap, dst_key, tmp, tmp_key, small, rows=128):
        kb = self.kb
        sk = pfx + "small"
        kb.op("act", lambda e: e.activation(out=tmp[0:rows, :], in_=ps2[0:rows, :], func=AF.Square,
                                            accum_out=small[0:rows, 0:1]),
              reads=[ps_key], writes=[tmp_key, sk])
        kb.op("act", lambda e: e.activation(out=small[0:rows, 1:2], in_=small[0:rows, 0:1], func=AF.Sqrt,
                                            bias=self.epsr[0:rows, 0:1], scale=1.0 / D),
              reads=[sk, "epsr"], writes=[sk])
        kb.op("dve", lambda e: e.reciprocal(out=small[0:rows, 2:3], in_=small[0:rows, 1:2]), reads=[sk], writes=[sk])
        kb.op("dve", lambda e: e.scalar_tensor_tensor(out=tmp[0:rows, :], in0=ps2[0:rows, :],
                                                      scalar=small[0:rows, 2:3], in1=gain[0:rows, :],
                                                      op0=ALU.mult, op1=ALU.mult),
              reads=[ps_key, sk, pfx + "gpost"], writes=[tmp_key])
        kb.op("pool", lambda e: e.tensor_tensor(out=dst_ap, in0=dst_ap, in1=tmp[0:rows, :], op=ALU.add),
              reads=[tmp_key, dst_key], writes=[dst_key])

    def load_w_rows(self, name, src2d, kchunks, ncols, c0=0):
        kb = self.kb
        w = kb.sbuf(name, [128, kchunks, ncols], BF16)
        for k in range(kchunks):
            kb.dma("pool", w[:, k, :], src2d[k * 128:(k + 1) * 128, c0:c0 + ncols], writes=[(name, k)])
        return w

    def ffn(self, L):
        kb = self.kb
        P = "f%d" % L
        I = self.inp
        mark = kb.push_scope()
        gain = self.load_bcast(P + "gain", I["norm_ffn_pre"][L, :], D)
        gpost = self.load_bcast(P + "gpost", I["norm_ffn_post"][L, :], D)
        rstd = self.prenorm_stats(P)
        cwb = kb.sbuf(P + "cwb", [128, NFC, 4])
        stf = kb.sbuf(P + "stf", [128, NFC, 2 * NS])
        gl = kb.sbuf(P + "gl", [128, 2, NFC])
        gsall = kb.sbuf(P + "gsall", [128, NFC, NS])
        halo = kb.sbuf(P + "halo", [128, 2, NFC])
        m2 = kb.push_scope()
        cwt = kb.sbuf(P + "cwt", [4, DFF])
        kb.dma("sp", cwt[0:3, :], I["ffn_conv_w"][L, :, :], writes=[P + "cwt"])
        kb.dma("sp", cwt[3:4, :], I["ffn_conv_b"][L:L + 1, :], writes=[P + "cwt"])
        stt = kb.sbuf(P + "stt", [2 * NS, DFF])
        kb.dma("sp", stt[:], I["sconv"][L].rearrange("s j f -> (s j) f"), writes=[P + "stt"])
        psA = kb.psum(P + "psA", [128, 512])
        psB = kb.psum(P + "psB", [128, 1024])
        for fc in range(NFC):
            kb.op("pe", lambda e, fc=fc: e.transpose(psA[:, fc * 4:fc * 4 + 4], cwt[0:4, fc * 128:(fc + 1) * 128],
                                                     self.ident_f()[0:4, 0:4]),
                  reads=[P + "cwt", "cst"], writes=[P + "psA"], inc=(fc == NFC - 1))
            kb.op("pe", lambda e, fc=fc: e.transpose(psB[:, fc * 32:fc * 32 + 32],
                                                     stt[0:32, fc * 128:(fc + 1) * 128],
                                                     self.ident_f()[0:32, 0:32]),
                  reads=[P + "stt", "cst"], writes=[P + "psB"], inc=(fc == NFC - 1))
        kb.op("dve", lambda e: e.tensor_copy(out=cwb[:].rearrange("p a b -> p (a b)"), in_=psA[:, 0:NFC * 4]),
              reads=[P + "psA"], writes=[P + "cwb"])
        kb.op("dve", lambda e: e.tensor_copy(out=stf[:].rearrange("p a b -> p (a b)"), in_=psB[:, 0:NFC * 32]),
              reads=[P + "psB"], writes=[P + "stf"])
        kb.barrier()
        kb.pop_scope(m2)
        kb.dma("sp", self.out["convs"][L, :, 0, :], I["sconv"][L, :, 1, :], is_output=True)

        for sb in (1, 0):
            self.ffn_sb(L, sb, P, gain, gpost, rstd, cwb, stf, gl, gsall, halo)
            if sb == 1:
                I_ = self.cc_ffn_in[L]
                O_ = self.cc_ffn_out[L]
                kb.dma("sp", I_.ap()[:, :], gl[:].rearrange("p a b -> p (a b)"), reads=[P + "gl"],
                       writes=[P + "ccin"])
                kb.collective(I_.ap(), O_.ap(), [[0, 1, 2, 3], [4, 5, 6, 7]], reads=[P + "ccin"],
                              writes=[P + "ccout"])
                g4 = kb.sbuf(P + "g4", [128, 4, 44])
                kb.dma("sp", g4[:], O_.ap().rearrange("(r p) c -> p r c", p=128), reads=[P + "ccout"],
                       writes=[P + "g4"])
                hv = halo[:].rearrange("p a b -> p (a b)")
                kb.op("dve", lambda e: e.tensor_scalar(out=hv, in0=g4[:, 0, :], scalar1=self.cst[:, C_SELP:C_SELP + 1],
                                                       scalar2=None, op0=ALU.mult),
                      reads=[P + "g4", "cst"], writes=[P + "halo"])
                for r in range(1, 4):
                    kb.op("dve", lambda e, r=r: e.scalar_tensor_tensor(
                        out=hv, in0=g4[:, r, :], scalar=self.cst[:, C_SELP + r:C_SELP + r + 1], in1=hv,
                        op0=ALU.mult, op1=ALU.add), reads=[P + "g4", "cst", P + "halo"], writes=[P + "halo"])
                m3 = kb.push_scope()
                psC = kb.psum(P + "psC", [128, 512])
                otr = kb.sbuf(P + "otr", [44, 128])
                kb.op("pe", lambda e: e.transpose(psC[0:44, 0:128], gl[:].rearrange("p a b -> p (a b)"),
                                                  self.ident_f()), reads=[P + "gl", "cst"], writes=[P + "psC"])
                kb.op("dve", lambda e: e.tensor_copy(out=otr[:], in_=psC[0:44, 0:128]), reads=[P + "psC"],
                      writes=[P + "otr"])
                for j in range(2):
                    kb.dma("sp", self.out["convp"][L, j, :].rearrange("(c p) -> c p", p=128),
                           otr[j * NFC:(j + 1) * NFC, :], reads=[P + "otr"], is_output=True)
                psD = kb.psum(P + "psD", [128, 3, 1024])
                osr = kb.sbuf(P + "osr", [NS, DFF])
                pv = psD[:].rearrange("p a b -> p (a b)")
                for fc in range(NFC):
                    kb.op("pe", lambda e, fc=fc: e.transpose(pv[0:NS, fc * 128:(fc + 1) * 128], gsall[:, fc, :],
                                                             self.ident_f()),
                          reads=[P + "gsall", "cst"], writes=[P + "psD"], inc=(fc == NFC - 1))
                kb.op("dve", lambda e: e.tensor_copy(out=osr[:], in_=pv[0:NS, 0:DFF]), reads=[P + "psD"],
                      writes=[P + "osr"])
                kb.dma("sp", self.out["convs"][L, :, 1, :], osr[:], reads=[P + "osr"], is_output=True)
                kb.barrier()
                kb.pop_scope(m3)
        kb.barrier()
        kb.pop_scope(mark)

    def ffn_sb(self, L, sb, P0, gain, gpost, rstd, cwb, stf, gl, gsall, halo):
        kb = self.kb
        I = self.inp
        P = "%ss%d" % (P0, sb)
        t0 = sb * 8
        npr = 1024
        ncol = npr + (NS if sb == 1 else 0)
        mark = kb.push_scope()
        hff = kb.sbuf(P + "hff", [128, NFC, ncol], BF16)
        m1 = kb.push_scope()
        xnT = kb.sbuf(P + "xnT", [128, 8, ncol + 2], BF16)
        xn = [kb.sbuf(P + "xn%d" % i, [128, D], BF16) for i in range(2)]
        psT = [kb.psum(P + "psT%d" % i, [128, 512]) for i in range(2)]
        pg = [kb.psum(P + "pg%d" % i, [128, 512]) for i in range(3)]
        pu = [kb.psum(P + "pu%d" % i, [128, 512]) for i in range(3)]
        tiles = list(range(t0, t0 + 8))
        for i, t in enumerate(tiles):
            b = i % 2
            self.norm_transpose(P0, self.hp[:, t, :], ("hp", t), rstd[:, t:t + 1], gain, xn[b], P + "xn%d" % b,
                                psT[b], P + "psT%d" % b, xnT[:, :, i * 128:(i + 1) * 128], (P + "xnT", i))
        if sb == 1:
            self.norm_transpose(P0, self.hs[0:NS, :], "hs", rstd[0:NS, NT:NT + 1], gain, xn[0], P + "xn0",
                                psT[0], P + "psT0", xnT[:, :, npr:npr + NS], (P + "xnT", 8), rows=NS)
            self.norm_transpose(P0, self.hp[:, 7, :], ("hp", 7), rstd[:, 7:8], gain, xn[1], P + "xn1",
                                psT[1], P + "psT1", xnT[:, :, ncol:ncol + 2], (P + "xnT", 9), c0=126, c1=128)
        xkeys = [(P + "xnT", i) for i in range(10 if sb == 1 else 8)]
        NB3 = 3
        G = [kb.sbuf(P + "G%d" % i, [128, npr + 2]) for i in range(NB3)]
        U = [kb.sbuf(P + "U%d" % i, [128, ncol]) for i in range(NB3)]
        C = [kb.sbuf(P + "C%d" % i, [128, npr]) for i in range(2)]
        TM = [kb.sbuf(P + "TM%d" % i, [128, npr]) for i in range(1)]
        GS = [kb.sbuf(P + "GS%d" % i, [128, 2 * NS]) for i in range(NB3)]
        NWB = 2
        wg = [kb.sbuf(P + "wg%d" % i, [128, 8, 256], BF16) for i in range(NWB)]
        wu = [kb.sbuf(P + "wu%d" % i, [128, 8, 256], BF16) for i in range(NWB)]
        slot = 0

        def issue_w(fg):
            wb = fg % NWB
            kb.dma("pool", wg[wb][:], I["ffn_w_gate"][L, :, fg * 256:(fg + 1) * 256].rearrange("(k p) n -> p k n", p=128),
                   writes=[P + "wg%d" % wb])
            kb.dma("pool", wu[wb][:], I["ffn_w_up"][L, :, fg * 256:(fg + 1) * 256].rearrange("(k p) n -> p k n", p=128),
                   writes=[P + "wu%d" % wb])
        issue_w(0)
        slot_of = {}

        def front(fc):
            nonlocal slot
            fg, fi = fc // 2, fc % 2
            wb = fg % NWB
            if fi == 0 and fg + 1 < NFC // 2:
                issue_w(fg + 1)
            gb = fc % NB3
            Gk, Uk, Ck, GSk = P + "G%d" % gb, P + "U%d" % gb, P + "C%d" % gb, P + "GS%d" % gb
            for tg in range(2):
                s3 = slot % 3
                slot += 1
                for k in range(8):
                    kb.op("pe", lambda e, k=k, tg=tg, s3=s3, wb=wb, fi=fi: e.matmul(
                        pg[s3][:, :], wg[wb][:, k, fi * 128:(fi + 1) * 128], xnT[:, k, tg * 512:(tg + 1) * 512],
                        start=(k == 0), stop=(k == 7)),
                        reads=[P + "wg%d" % wb] + xkeys, writes=[P + "pg%d" % s3], inc=(k == 7))
                for k in range(8):
                    kb.op("pe", lambda e, k=k, tg=tg, s3=s3, wb=wb, fi=fi: e.matmul(
                        pu[s3][:, :], wu[wb][:, k, fi * 128:(fi + 1) * 128], xnT[:, k, tg * 512:(tg + 1) * 512],
                        start=(k == 0), stop=(k == 7)),
                        reads=[P + "wu%d" % wb] + xkeys, writes=[P + "pu%d" % s3], inc=(k == 7))
                kb.op("act", lambda e, tg=tg, s3=s3, gb=gb: e.copy(out=G[gb][:, 2 + tg * 512:2 + (tg + 1) * 512],
                                                                  in_=pg[s3][:, :]),
                      reads=[P + "pg%d" % s3], writes=[Gk])
                kb.op("act", lambda e, tg=tg, s3=s3, gb=gb: e.copy(out=U[gb][:, tg * 512:(tg + 1) * 512],
                                                                  in_=pu[s3][:, :]),
                      reads=[P + "pu%d" % s3], writes=[Uk])
            if sb == 1:
                s3 = slot % 3
                slot += 1
                for k in range(8):
                    kb.op("pe", lambda e, k=k, s3=s3, wb=wb, fi=fi: e.matmul(
                        pg[s3][:, 0:NS + 2], wg[wb][:, k, fi * 128:(fi + 1) * 128], xnT[:, k, npr:npr + NS + 2],
                        start=(k == 0), stop=(k == 7)),
                        reads=[P + "wg%d" % wb] + xkeys, writes=[P + "pg%d" % s3], inc=(k == 7))
                for k in range(8):
                    kb.op("pe", lambda e, k=k, s3=s3, wb=wb, fi=fi: e.matmul(
                        pu[s3][:, 0:NS], wu[wb][:, k, fi * 128:(fi + 1) * 128], xnT[:, k, npr:npr + NS],
                        start=(k == 0), stop=(k == 7)),
                        reads=[P + "wu%d" % wb] + xkeys, writes=[P + "pu%d" % s3], inc=(k == 7))
                kb.op("act", lambda e, s3=s3, gb=gb: e.copy(out=GS[gb][:, 0:NS], in_=pg[s3][:, 0:NS]),
                      reads=[P + "pg%d" % s3], writes=[GSk])
                kb.op("act", lambda e, s3=s3, gb=gb: e.copy(out=G[gb][:, 0:2], in_=pg[s3][:, NS:NS + 2]),
                      reads=[P + "pg%d" % s3], writes=[Gk])
                kb.op("act", lambda e, s3=s3, gb=gb: e.copy(out=U[gb][:, npr:npr + NS], in_=pu[s3][:, 0:NS]),
                      reads=[P + "pu%d" % s3], writes=[Uk])
            else:
                kb.op("pool", lambda e, gb=gb, fc=fc: e.tensor_copy(out=G[gb][:, 0:2], in_=halo[:, :, fc]),
                      reads=[P0 + "halo"], writes=[Gk])

        def tail(fc):
            gb = fc % NB3
            Gk, Uk, Ck, GSk = P + "G%d" % gb, P + "U%d" % gb, P + "C%d" % gb, P + "GS%d" % gb
            w0, w1, w2, bb = (cwb[:, fc, i:i + 1] for i in range(4))
            cb = fc % 2
            Ck = P + "C%d" % cb
            kb.op("pool", lambda e, gb=gb, w1=w1: e.tensor_tensor(
                out=TM[0][:, :], in0=G[gb][:, 1:1 + npr], in1=w1.to_broadcast([128, npr]), op=ALU.mult),
                reads=[Gk, P0 + "cwb"], writes=[P + "TM0"])
            kb.op("dve", lambda e, gb=gb, cb=cb, w2=w2, bb=bb: e.tensor_scalar(
                out=C[cb][:, :], in0=G[gb][:, 2:2 + npr], scalar1=w2, scalar2=bb, op0=ALU.mult, op1=ALU.add),
                reads=[Gk, P0 + "cwb"], writes=[Ck])
            kb.op("dve", lambda e, gb=gb, cb=cb, w0=w0: e.scalar_tensor_tensor(
                out=C[cb][:, :], in0=G[gb][:, 0:npr], scalar=w0, in1=C[cb][:, :], op0=ALU.mult, op1=ALU.add),
                reads=[Gk, P0 + "cwb", Ck], writes=[Ck])
            kb.op("dve", lambda e, cb=cb: e.tensor_tensor(
                out=C[cb][:, :], in0=C[cb][:, :], in1=TM[0][:, :], op=ALU.add),
                reads=[P + "TM0", Ck], writes=[Ck])
            kb.op("act", lambda e, cb=cb: e.activation(out=C[cb][:, :], in_=C[cb][:, :], func=AF.Gelu_apprx_tanh),
                  reads=[Ck], writes=[Ck])
            kb.op("dve", lambda e, gb=gb, cb=cb, fc=fc: e.tensor_tensor(out=hff[:, fc, 0:npr], in0=C[cb][:, :],
                                                                in1=U[gb][:, 0:npr], op=ALU.mult),
                  reads=[Ck, Uk], writes=[(P + "hff", fc)])
            if sb == 1:
                kb.op("pool", lambda e, gb=gb, fc=fc: e.tensor_copy(out=gl[:, :, fc], in_=G[gb][:, npr:npr + 2]),
                      reads=[Gk], writes=[P0 + "gl"])
                kb.op("pool", lambda e, gb=gb, fc=fc: e.tensor_copy(out=gsall[:, fc, :], in_=GS[gb][:, 0:NS]),
                      reads=[GSk], writes=[P0 + "gsall"])
                stv = stf[:, fc, :].rearrange("p (s j) -> p s j", j=2)
                cs = GS[gb][:, NS:2 * NS]
                kb.op("dve", lambda e, gb=gb, w2=w2, bb=bb, cs=cs: e.tensor_scalar(
                    out=cs, in0=GS[gb][:, 0:NS], scalar1=w2, scalar2=bb, op0=ALU.mult, op1=ALU.add),
                    reads=[GSk, P0 + "cwb"], writes=[GSk])
                kb.op("dve", lambda e, w1=w1, cs=cs, stv=stv: e.scalar_tensor_tensor(
                    out=cs, in0=stv[:, :, 1], scalar=w1, in1=cs, op0=ALU.mult, op1=ALU.add),
                    reads=[GSk, P0 + "cwb", P0 + "stf"], writes=[GSk])
                kb.op("dve", lambda e, w0=w0, cs=cs, stv=stv: e.scalar_tensor_tensor(
                    out=cs, in0=stv[:, :, 0], scalar=w0, in1=cs, op0=ALU.mult, op1=ALU.add),
                    reads=[GSk, P0 + "cwb", P0 + "stf"], writes=[GSk])
                kb.op("act", lambda e, cs=cs: e.activation(out=cs, in_=cs, func=AF.Gelu_apprx_tanh),
                      reads=[GSk], writes=[GSk])
                kb.op("dve", lambda e, gb=gb, fc=fc, cs=cs: e.tensor_tensor(
                    out=hff[:, fc, npr:npr + NS], in0=cs, in1=U[gb][:, npr:npr + NS], op=ALU.mult),
                    reads=[GSk, Uk], writes=[(P + "hff", fc)])

        for fc in range(NFC):
            front(fc)
            if fc >= 1:
                tail(fc - 1)
        tail(NFC - 1)
        kb.barrier()
        kb.pop_scope(m1)
        m2 = kb.push_scope()
        wd = kb.sbuf(P + "wd", [128, NFC, D], BF16)
        for f2 in range(NFC // 2):
            kb.dma("pool", wd[:, 2 * f2:2 * f2 + 2, :],
                   I["ffn_w_down"][L, f2 * 256:(f2 + 1) * 256, :].rearrange("(f p) n -> p f n", p=128),
                   writes=[(P + "wd", f2)])
        wkeys = [(P + "wd", f2) for f2 in range(NFC // 2)]
        hkeys = [(P + "hff", fc) for fc in range(NFC)]
        py = [kb.psum(P + "py%d" % i, [128, 1024]) for i in range(2)]
        tmp = [kb.sbuf(P + "tmp%d" % i, [128, D]) for i in range(2)]
        small = kb.sbuf(P + "small", [128, 4])
        units = [(t, 128) for t in tiles] + ([("s", NS)] if sb == 1 else [])
        for i, (t, rows) in enumerate(units):
            b = i % 2
            off = (i * 128) if t != "s" else npr
            for half in range(2):
                for f in range(NFC):
                    kb.op("pe", lambda e, f=f, half=half, b=b, off=off, rows=rows: e.matmul(
                        py[b][0:rows, half * 512:(half + 1) * 512], hff[:, f, off:off + rows],
                        wd[:, f, half * 512:(half + 1) * 512], start=(f == 0), stop=(f == NFC - 1)),
                        reads=wkeys + hkeys, writes=[P + "py%d" % b], inc=(f == NFC - 1))
            if t == "s":
                dst, dk = self.hs[0:NS, :], "hs"
            else:
                dst, dk = self.hp[:, t, :], ("hp", t)
            self.post_norm_residual(P0, py[b], P + "py%d" % b, gpost, dst, dk, tmp[b], P + "tmp%d" % b, small, rows)
        kb.barrier()
        kb.pop_scope(m2)
        kb.pop_scope(mark)

    def odd(self, L):
        kb = self.kb
        I = self.inp
        P = "o%d" % L
        j = L // 2
        mark = kb.push_scope()
        gain = self.load_bcast(P + "gain", I["norm_mix_pre"][L, :], D)
        gpost = self.load_bcast(P + "gpost", I["norm_mix_post"][L, :], D)
        lng = self.load_bcast(P + "lng", I["sgu_ln_gain"][j, :], D)
        lnb = self.load_bcast(P + "lnb", I["sgu_ln_bias"][j, :], D)
        rstd = self.prenorm_stats(P)
        win = self.load_w_rows(P + "win", I["w_in_odd"][j], 8, 2 * D)
        wout = self.load_w_rows(P + "wout", I["w_out_odd"][j], 8, D)
        winkeys = [(P + "win", k) for k in range(8)]
        woutkeys = [(P + "wout", k) for k in range(8)]
        wsT = kb.sbuf(P + "wsT", [128, 8, 128], BF16)
        bsb = kb.sbuf(P + "bsb", [128, 8])
        w00 = kb.sbuf(P + "w00", [NS, 8])
        b0 = kb.sbuf(P + "b0", [NS, 8])
        kb.dma("sp", w00[:], I["sgu_w"][j, :, 0, 0].partition_broadcast(NS), writes=[P + "w00"],
               allow_slow_non_contiguous=True)
        kb.dma("sp", b0[:], I["sgu_b"][j, :, 0].partition_broadcast(NS), writes=[P + "b0"],
               allow_slow_non_contiguous=True)
        m0 = kb.push_scope()
        wsn = kb.sbuf(P + "wsn", [128, 8, 128])
        kb.dma("sp", wsn[:], I["sgu_w"][j].rearrange("h t s -> t h s"), writes=[P + "wsn"])
        bsn = kb.sbuf(P + "bsn", [8, 128])
        kb.dma("sp", bsn[:], I["sgu_b"][j, :, :], writes=[P + "bsn"])
        p0 = kb.psum(P + "p0", [128, 1024])
        p1 = kb.psum(P + "p1", [128, 512])
        for h in range(8):
            kb.op("pe", lambda e, h=h: e.transpose(p0[:, h * 128:(h + 1) * 128], wsn[:, h, :], self.ident_f()),
                  reads=[P + "wsn", "cst"], writes=[P + "p0"], inc=(h == 7))
        kb.op("pe", lambda e: e.transpose(p1[:, 0:8], bsn[0:8, :], self.ident_f()[0:8, 0:8]),
              reads=[P + "bsn", "cst"], writes=[P + "p1"])
        kb.op("dve", lambda e: e.tensor_tensor(
            out=wsT[:], in0=p0[:].rearrange("p (h t) -> p h t", h=8),
            in1=self.cst[:, C_TRI:C_TRI + 128].unsqueeze(1).to_broadcast([128, 8, 128]), op=ALU.mult),
            reads=[P + "p0", "cst"], writes=[P + "wsT"])
        kb.op("dve", lambda e: e.tensor_copy(out=bsb[:], in_=p1[:, 0:8]), reads=[P + "p1"], writes=[P + "bsb"])
        kb.barrier()
        kb.pop_scope(m0)

        xn = kb.sbuf(P + "xn", [128, D], BF16)
        xnT = [kb.sbuf(P + "xnT%d" % i, [128, 8, 128], BF16) for i in range(2)]
        u = kb.sbuf(P + "u", [128, D])
        v = kb.sbuf(P + "v", [128, D])
        vn = kb.sbuf(P + "vn", [128, D])
        vnb = kb.sbuf(P + "vnb", [128, D], BF16)
        y = kb.sbuf(P + "y", [128, D])
        yb = kb.sbuf(P + "yb", [128, D], BF16)
        yT = kb.sbuf(P + "yT", [128, 8, 128], BF16)
        tmp = kb.sbuf(P + "tmp", [128, D])
        sm = kb.sbuf(P + "sm", [128, 8])
        small = kb.sbuf(P + "small", [128, 4])
        pT = kb.psum(P + "pT", [128, 512])
        pz = [kb.psum(P + "pz%d" % i, [128, 512]) for i in range(2)]
        pm = kb.psum(P + "pm", [128, 1024])
        py = kb.psum(P + "py", [128, 1024])
        pyT = kb.psum(P + "pyT", [128, 512])
        units = [(t, 128) for t in range(NT)] + [("s", NS)]
        for i, (t, rows) in enumerate(units):
            b = i % 2
            if t == "s":
                src, sk, rc = self.hs[0:NS, :], "hs", rstd[0:NS, NT:NT + 1]
            else:
                src, sk, rc = self.hp[:, t, :], ("hp", t), rstd[:, t:t + 1]
            self.norm_transpose(P, src, sk, rc, gain, xn, P + "xn", pT, P + "pT", xnT[b][:, :, 0:rows],
                                P + "xnT%d" % b, rows=rows)
            for g in range(4):
                pb = g % 2
                for k in range(8):
                    kb.op("pe", lambda e, k=k, g=g, pb=pb, b=b, rows=rows: e.matmul(
                        pz[pb][0:rows, :], xnT[b][:, k, 0:rows], win[:, k, g * 512:(g + 1) * 512],
                        start=(k == 0), stop=(k == 7)),
                        reads=[P + "xnT%d" % b] + winkeys, writes=[P + "pz%d" % pb], inc=(k == 7))
                if g < 2:
                    kb.op("act", lambda e, g=g, pb=pb, rows=rows: e.activation(
                        out=u[0:rows, g * 512:(g + 1) * 512], in_=pz[pb][0:rows, :], func=AF.Gelu),
                        reads=[P + "pz%d" % pb], writes=[P + "u"])
                else:
                    kb.op("act", lambda e, g=g, pb=pb, rows=rows: e.activation(
                        out=v[0:rows, (g - 2) * 512:(g - 1) * 512], in_=pz[pb][0:rows, :], func=AF.Gelu,
                        accum_out=sm[0:rows, g - 2:g - 1]),
                        reads=[P + "pz%d" % pb], writes=[P + "v", P + "sm"])
            kb.op("dve", lambda e, rows=rows: e.tensor_scalar(
                out=sm[0:rows, 2:3], in0=sm[0:rows, 0:1], scalar1=sm[0:rows, 1:2], scalar2=-1.0 / D,
                op0=ALU.add, op1=ALU.mult), reads=[P + "sm"], writes=[P + "sm"])
            kb.op("act", lambda e, rows=rows: e.activation(
                out=tmp[0:rows, :], in_=v[0:rows, :], func=AF.Square, bias=sm[0:rows, 2:3],
                accum_out=sm[0:rows, 3:4]), reads=[P + "v", P + "sm"], writes=[P + "tmp", P + "sm"])
            kb.op("act", lambda e, rows=rows: e.activation(
                out=sm[0:rows, 4:5], in_=sm[0:rows, 3:4], func=AF.Sqrt, bias=self.epsr[0:rows, 1:2],
                scale=1.0 / D), reads=[P + "sm", "epsr"], writes=[P + "sm"])
            kb.op("dve", lambda e, rows=rows: e.reciprocal(out=sm[0:rows, 5:6], in_=sm[0:rows, 4:5]),
                  reads=[P + "sm"], writes=[P + "sm"])
            kb.op("dve", lambda e, rows=rows: e.tensor_scalar(
                out=vn[0:rows, :], in0=v[0:rows, :], scalar1=sm[0:rows, 2:3], scalar2=sm[0:rows, 5:6],
                op0=ALU.add, op1=ALU.mult), reads=[P + "v", P + "sm"], writes=[P + "vn"])
            kb.op("dve", lambda e, rows=rows: e.tensor_tensor(out=vn[0:rows, :], in0=vn[0:rows, :],
                                                              in1=lng[0:rows, :], op=ALU.mult),
                  reads=[P + "vn", P + "lng"], writes=[P + "vn"])
            kb.op("pool", lambda e, rows=rows: e.tensor_tensor(out=vn[0:rows, :], in0=vn[0:rows, :],
                                                               in1=lnb[0:rows, :], op=ALU.add),
                  reads=[P + "vn", P + "lnb"], writes=[P + "vn"])
            if t == "s":
                kb.dma("sp", self.out["sguv"][j, :, :], vn[0:NS, :], reads=[P + "vn"], is_output=True)
                y3 = y[0:NS, :].rearrange("p (h c) -> p h c", h=8)
                kb.op("dve", lambda e: e.tensor_tensor(
                    out=y3, in0=vn[0:NS, :].rearrange("p (h c) -> p h c", h=8),
                    in1=w00[:].unsqueeze(2).to_broadcast([NS, 8, 128]), op=ALU.mult),
                    reads=[P + "vn", P + "w00"], writes=[P + "y"])
                kb.op("dve", lambda e: e.tensor_tensor(
                    out=y3, in0=y3, in1=b0[:].unsqueeze(2).to_broadcast([NS, 8, 128]), op=ALU.add),
                    reads=[P + "y", P + "b0"], writes=[P + "y"])
            else:
                kb.op("act", lambda e: e.copy(out=vnb[:], in_=vn[:]), reads=[P + "vn"], writes=[P + "vnb"])
                for h in range(8):
                    kb.op("pe", lambda e, h=h: e.matmul(pm[:, h * 128:(h + 1) * 128], wsT[:, h, :],
                                                        vnb[:, h * 128:(h + 1) * 128], start=True, stop=True),
                          reads=[P + "wsT", P + "vnb"], writes=[P + "pm"], inc=(h == 7))
                kb.op("dve", lambda e: e.tensor_tensor(
                    out=y[:].rearrange("p (h c) -> p h c", h=8), in0=pm[:].rearrange("p (h c) -> p h c", h=8),
                    in1=bsb[:].unsqueeze(2).to_broadcast([128, 8, 128]), op=ALU.add),
                    reads=[P + "pm", P + "bsb"], writes=[P + "y"])
            kb.op("dve", lambda e, rows=rows: e.tensor_tensor(out=yb[0:rows, :], in0=y[0:rows, :],
                                                              in1=u[0:rows, :], op=ALU.mult),
                  reads=[P + "y", P + "u"], writes=[P + "yb"])
            pv = pyT[:].bitcast(BF16).rearrange("p (k t) -> p k t", k=8)
            for k in range(8):
                kb.op("pe", lambda e, k=k, rows=rows: e.transpose(pv[:, k, 0:rows], yb[0:rows, k * 128:(k + 1) * 128],
                                                                  self.identb[0:rows, 0:rows]),
                      reads=[P + "yb", "identb"], writes=[P + "pyT"], inc=(k == 7))
            kb.op("act", lambda e, rows=rows: e.copy(out=yT[:, :, 0:rows], in_=pv[:, :, 0:rows]),
                  reads=[P + "pyT"], writes=[P + "yT"])
            for half in range(2):
                for k in range(8):
                    kb.op("pe", lambda e, k=k, half=half, rows=rows: e.matmul(
                        py[0:rows, half * 512:(half + 1) * 512], yT[:, k, 0:rows],
                        wout[:, k, half * 512:(half + 1) * 512], start=(k == 0), stop=(k == 7)),
                        reads=[P + "yT"] + woutkeys, writes=[P + "py"], inc=(k == 7))
            if t == "s":
                dst, dk = self.hs[0:NS, :], "hs"
            else:
                dst, dk = self.hp[:, t, :], ("hp", t)
            self.post_norm_residual(P, py, P + "py", gpost, dst, dk, tmp, P + "tmp", small, rows)
        kb.barrier()
        kb.pop_scope(mark)

    def even(self, L):
        kb = self.kb
        I = self.inp
        P = "e%d" % L
        mark = kb.push_scope()
        gain = self.load_bcast(P + "gain", I["norm_mix_pre"][L, :], D)
        rstd = self.prenorm_stats(P)
        if "no_attn" not in self.dbg:
            self.even_attn(L, P, gain, rstd)
        if "no_rwkv" not in self.dbg:
            self.even_rwkv(L, P, gain, rstd)
        if "no_comb" not in self.dbg:
            self.even_combine(L, P)
        kb.barrier()
        kb.pop_scope(mark)

    def bk(self, P, n=8):
        kb = self.kb
        banks = [kb.psum(P + "bk%d" % i, [128, 512]) for i in range(n)]
        keys = [P + "bk%d" % i for i in range(n)]
        return banks, keys

    def even_attn(self, L, P0, gain, rstd):
        kb = self.kb
        I = self.inp
        j = L // 2
        P = P0 + "a"
        mark = kb.push_scope()
        w = self.load_w_rows(P + "w", I["w_in_even"][j], 8, 768, c0=0)
        wkeys = [(P + "w", k) for k in range(8)]
        sinkb = self.load_bcast(P + "sink", I["attn_sinks"][j, :], 8)
        rope = kb.sbuf(P + "rope", [128, 1088])
        kb.dma("sp", rope[:], I["consts"][:, C_COS:C_COS + 1088], writes=[P + "rope"])
        bk, bkk = self.bk(P)
        Kc = kb.sbuf(P + "Kc", [128, 129, 64])
        Vc = kb.sbuf(P + "Vc", [128, 129, 64])
        for h in range(8):
            kv = h // 4
            HS = slice(h * 16, (h + 1) * 16)
            for q4 in range(4):
                ks = slice(q4 * 32, (q4 + 1) * 32)
                kb.dma("sp", Kc[HS, ks, :], I["ck"][j, :, ks, kv * 64:(kv + 1) * 64], writes=[P + "Kc"])
                kb.dma("sp", Vc[HS, ks, :], I["cv"][j, :, ks, kv * 64:(kv + 1) * 64], writes=[P + "Vc"])
        kT = kb.sbuf(P + "kT", [128, (NT + 1) * 128], BF16)
        Va = kb.sbuf(P + "Va", [128, NT + 1, 128], BF16)
        xn = kb.sbuf(P + "xn", [128, D], BF16)
        xnT = kb.sbuf(P + "xnT", [128, 8, 128], BF16)
        qk = kb.sbuf(P + "qk", [128, 640])
        vf = kb.sbuf(P + "vf", [128, 128])
        t1 = kb.sbuf(P + "t1", [128, 10, 32]); t2 = kb.sbuf(P + "t2", [128, 10, 32])
        t3 = kb.sbuf(P + "t3", [128, 10, 32]); t4 = kb.sbuf(P + "t4", [128, 10, 32])
        rq = kb.sbuf(P + "rq", [128, 4, 2, 64], BF16)
        rqs = kb.sbuf(P + "rqs", [NS, 8, 64])
        rk = kb.sbuf(P + "rk", [128, 128])
        rkb = kb.sbuf(P + "rkb", [128, 128], BF16)
        qT = kb.sbuf(P + "qT", [128, 4, 128], BF16)
        S = kb.sbuf(P + "S", [128, 4, 256])
        Pb = kb.sbuf(P + "Pb", [128, 4, 256], BF16)
        PT = kb.sbuf(P + "PT", [128, 8, 128], BF16)
        oa = kb.sbuf(P + "oa", [128, 512])
        sm = kb.sbuf(P + "sm", [128, 32])

        def qkv(t, rows=128):
            if t == "s":
                src, sk, rc = self.hs[0:NS, :], "hs", rstd[0:NS, NT:NT + 1]
                cos = rope[0:NS, 1024:1056]
                sin = rope[0:NS, 1056:1088]
            else:
                src, sk, rc = self.hp[:, t, :], ("hp", t), rstd[:, t:t + 1]
                cos = rope[:, t * 32:(t + 1) * 32]
                sin = rope[:, 512 + t * 32:512 + (t + 1) * 32]
            R = slice(0, rows)
            self.norm_transpose(P0, src, sk, rc, gain, xn, P + "xn", bk[7], bkk[7], xnT[:, :, 0:rows], P + "xnT",
                                rows=rows)
            for k in range(8):
                kb.op("pe", lambda e, k=k: e.matmul(bk[0][R, :], xnT[:, k, 0:rows], w[:, k, 0:512],
                                                    start=(k == 0), stop=(k == 7)),
                      reads=[P + "xnT"] + wkeys, writes=[bkk[0]], inc=(k == 7))
            for k in range(8):
                kb.op("pe", lambda e, k=k: e.matmul(bk[1][R, 0:256], xnT[:, k, 0:rows], w[:, k, 512:768],
                                                    start=(k == 0), stop=(k == 7)),
                      reads=[P + "xnT"] + wkeys, writes=[bkk[1]], inc=(k == 7))
            kb.op("act", lambda e: e.copy(out=qk[R, 0:512], in_=bk[0][R, :]), reads=[bkk[0]], writes=[P + "qk"])
            kb.op("act", lambda e: e.copy(out=qk[R, 512:640], in_=bk[1][R, 0:128]), reads=[bkk[1]], writes=[P + "qk"])
            kb.op("act", lambda e: e.copy(out=vf[R, :], in_=bk[1][R, 128:256]), reads=[bkk[1]], writes=[P + "vf"])
            q3 = qk[R, :].rearrange("p (h d) -> p h d", h=10)
            x1, x2 = q3[:, :, 0:32], q3[:, :, 32:64]
            cb = cos.unsqueeze(1).to_broadcast([rows, 10, 32])
            sb_ = sin.unsqueeze(1).to_broadcast([rows, 10, 32])
            kb.op("dve", lambda e: e.tensor_tensor(out=t1[R], in0=x1, in1=cb, op=ALU.mult), reads=[P + "qk", P + "rope"], writes=[P + "t1"])
            kb.op("pool", lambda e: e.tensor_tensor(out=t2[R], in0=x2, in1=sb_, op=ALU.mult), reads=[P + "qk", P + "rope"], writes=[P + "t2"])
            kb.op("dve", lambda e: e.tensor_tensor(out=t3[R], in0=x2, in1=cb, op=ALU.mult), reads=[P + "qk", P + "rope"], writes=[P + "t3"])
            kb.op("pool", lambda e: e.tensor_tensor(out=t4[R], in0=x1, in1=sb_, op=ALU.mult), reads=[P + "qk", P + "rope"], writes=[P + "t4"])
            r3 = rk[R, :].rearrange("p (h d) -> p h d", h=2)
            if t == "s":
                kb.op("dve", lambda e: e.tensor_tensor(out=rqs[:, :, 0:32], in0=t1[R, 0:8, :], in1=t2[R, 0:8, :], op=ALU.subtract),
                      reads=[P + "t1", P + "t2"], writes=[P + "rqs"])
                kb.op("dve", lambda e: e.tensor_tensor(out=rqs[:, :, 32:64], in0=t3[R, 0:8, :], in1=t4[R, 0:8, :], op=ALU.add),
                      reads=[P + "t3", P + "t4"], writes=[P + "rqs"])
            else:
                for kv in range(2):
                    kb.op("dve", lambda e, kv=kv: e.tensor_tensor(out=rq[:, :, kv, 0:32], in0=t1[:, kv * 4:kv * 4 + 4, :],
                                                                  in1=t2[:, kv * 4:kv * 4 + 4, :], op=ALU.subtract),
                          reads=[P + "t1", P + "t2"], writes=[P + "rq"])
                    kb.op("pool", lambda e, kv=kv: e.tensor_tensor(out=rq[:, :, kv, 32:64], in0=t3[:, kv * 4:kv * 4 + 4, :],
                                                                   in1=t4[:, kv * 4:kv * 4 + 4, :], op=ALU.add),
                          reads=[P + "t3", P + "t4"], writes=[P + "rq"])
            kb.op("dve", lambda e: e.tensor_tensor(out=r3[:, :, 0:32], in0=t1[R, 8:10, :], in1=t2[R, 8:10, :], op=ALU.subtract),
                  reads=[P + "t1", P + "t2"], writes=[P + "rk"])
            kb.op("dve", lambda e: e.tensor_tensor(out=r3[:, :, 32:64], in0=t3[R, 8:10, :], in1=t4[R, 8:10, :], op=ALU.add),
                  reads=[P + "t3", P + "t4"], writes=[P + "rk"])
            if t == "s":
                return
            kb.op("act", lambda e: e.copy(out=rkb[:], in_=rk[:]), reads=[P + "rk"], writes=[P + "rkb"])
            kb.op("act", lambda e: e.copy(out=Va[:, t + 1, :], in_=vf[:]), reads=[P + "vf"], writes=[(P + "Va", t + 1)])
            pv = bk[6][:].bitcast(BF16).rearrange("p (k t) -> p k t", k=8)
            for g in range(4):
                kb.op("pe", lambda e, g=g: e.transpose(pv[:, g, :], rq[:, g, :, :].rearrange("p a b -> p (a b)"), self.identb[:]),
                      reads=[P + "rq", "identb"], writes=[bkk[6]], inc=False)
            kb.op("pe", lambda e: e.transpose(pv[:, 4, :], rkb[:], self.identb[:]), reads=[P + "rkb", "identb"], writes=[bkk[6]])
            kb.op("act", lambda e: e.copy(out=qT[:], in_=pv[:, 0:4, :]), reads=[bkk[6]], writes=[P + "qT"])
            kb.op("dve", lambda e: e.tensor_copy(out=kT[:, (t + 1) * 128:(t + 2) * 128], in_=pv[:, 4, :]), reads=[bkk[6]],
                  writes=[(P + "kT", t + 1)])

        qkv(NT - 1)
        pay = kb.sbuf(P + "pay", [128, 256])
        kb.op("dve", lambda e: e.tensor_copy(out=pay[:, 0:128], in_=kT[:, NT * 128:(NT + 1) * 128]), reads=[(P + "kT", NT)], writes=[P + "pay"])
        kb.op("dve", lambda e: e.tensor_copy(out=pay[:, 128:256], in_=Va[:, NT, :]), reads=[(P + "Va", NT)], writes=[P + "pay"])
        ci, co = self.cc_att_in[j].ap(), self.cc_att_out[j].ap()
        kb.dma("sp", ci[:, 0:256], pay[:], reads=[P + "pay"], writes=[P + "ccin"])
        kb.collective(ci, co, [[0, 1, 2, 3], [4, 5, 6, 7]], reads=[P + "ccin"], writes=[P + "ccout"])
        g4 = kb.sbuf(P + "g4", [128, 4, 256])
        kb.dma("sp", g4[:], co.rearrange("(r p) c -> p r c", p=128)[:, :, 0:256], reads=[P + "ccout"], writes=[P + "g4"])
        kb.op("dve", lambda e: e.tensor_scalar(out=pay[:], in0=g4[:, 0, :], scalar1=self.cst[:, C_SELP:C_SELP + 1], scalar2=None,
                                               op0=ALU.mult), reads=[P + "g4", "cst"], writes=[P + "pay"])
        for r in range(1, 4):
            kb.op("dve", lambda e, r=r: e.scalar_tensor_tensor(out=pay[:], in0=g4[:, r, :], scalar=self.cst[:, C_SELP + r:C_SELP + r + 1],
                                                               in1=pay[:], op0=ALU.mult, op1=ALU.add),
                  reads=[P + "g4", "cst", P + "pay"], writes=[P + "pay"])
        kb.op("dve", lambda e: e.tensor_copy(out=kT[:, 0:128], in_=pay[:, 0:128]), reads=[P + "pay"], writes=[(P + "kT", 0)])
        kb.op("dve", lambda e: e.tensor_copy(out=Va[:, 0, :], in_=pay[:, 128:256]), reads=[P + "pay"], writes=[(P + "Va", 0)])

        for t in range(NT):
            qkv(t)
            if t == NT - 1:
                kb.dma("sp", self.out["wkp"][j, :, :], rk[:], reads=[P + "rk"], is_output=True)
                kb.dma("sp", self.out["wvp"][j, :, :], vf[:], reads=[P + "vf"], is_output=True)
            mcol = C_M0 if t == 0 else C_MA
            mask = self.cst[:, mcol:mcol + 256].unsqueeze(1).to_broadcast([128, 4, 256])
            for kv in range(2):
                H = slice(kv * 64, (kv + 1) * 64)
                for g in range(4):
                    b2 = 2 + g // 2
                    kb.op("pe", lambda e, g=g, b2=b2: e.matmul(bk[b2][:, (g % 2) * 256:(g % 2 + 1) * 256], qT[H, g, :],
                                                               kT[H, t * 128:(t + 2) * 128], start=True, stop=True),
                          reads=[P + "qT", (P + "kT", t), (P + "kT", t + 1)], writes=[bkk[b2]])
                for hh in range(2):
                    kb.op("dve", lambda e, hh=hh: e.scalar_tensor_tensor(
                        out=S[:, 2 * hh:2 * hh + 2, :], in0=bk[2 + hh][:].rearrange("p (a b) -> p a b", a=2), scalar=0.125,
                        in1=mask[:, 0:2, :], op0=ALU.mult, op1=ALU.add), reads=[bkk[2 + hh], "cst"], writes=[P + "S"])
                kb.op("dve", lambda e: e.tensor_reduce(out=sm[:, 0:4], in_=S[:], axis=AX.X, op=ALU.max), reads=[P + "S"], writes=[P + "sm"])
                kb.op("dve", lambda e: e.tensor_tensor(out=sm[:, 0:4], in0=sm[:, 0:4], in1=sinkb[:, kv * 4:kv * 4 + 4], op=ALU.max),
                      reads=[P + "sm", P + "sink"], writes=[P + "sm"])
                kb.op("dve", lambda e: e.tensor_scalar(out=sm[:, 4:8], in0=sm[:, 0:4], scalar1=-1.0, scalar2=None, op0=ALU.mult),
                      reads=[P + "sm"], writes=[P + "sm"])
                for g in range(4):
                    kb.op("act", lambda e, g=g: e.activation(out=Pb[:, g, :], in_=S[:, g, :], func=AF.Exp, bias=sm[:, 4 + g:5 + g],
                                                             accum_out=sm[:, 8 + g:9 + g]),
                          reads=[P + "S", P + "sm"], writes=[P + "Pb", P + "sm"])
                kb.op("dve", lambda e: e.tensor_tensor(out=sm[:, 12:16], in0=sinkb[:, kv * 4:kv * 4 + 4], in1=sm[:, 4:8], op=ALU.add),
                      reads=[P + "sm", P + "sink"], writes=[P + "sm"])
                kb.op("act", lambda e: e.activation(out=sm[:, 12:16], in_=sm[:, 12:16], func=AF.Exp), reads=[P + "sm"], writes=[P + "sm"])
                kb.op("dve", lambda e: e.tensor_tensor(out=sm[:, 16:20], in0=sm[:, 8:12], in1=sm[:, 12:16], op=ALU.add),
                      reads=[P + "sm"], writes=[P + "sm"])
                kb.op("dve", lambda e: e.reciprocal(out=sm[:, 20:24], in_=sm[:, 16:20]), reads=[P + "sm"], writes=[P + "sm"])
                pv = bk[4][:].bitcast(BF16).rearrange("p (k t) -> p k t", k=8)
                for g in range(4):
                    for blk in range(2):
                        kb.op("pe", lambda e, g=g, blk=blk: e.transpose(pv[:, g * 2 + blk, :], Pb[:, g, blk * 128:(blk + 1) * 128],
                                                                        self.identb[:]),
                              reads=[P + "Pb", "identb"], writes=[bkk[4]], inc=(g == 3 and blk == 1))
                kb.op("act", lambda e: e.copy(out=PT[:], in_=pv[:]), reads=[bkk[4]], writes=[P + "PT"])
                for g in range(4):
                    for blk in range(2):
                        kb.op("pe", lambda e, g=g, blk=blk: e.matmul(bk[5][:, g * 64:(g + 1) * 64], PT[:, g * 2 + blk, :],
                                                                     Va[:, t + blk, H], start=(blk == 0), stop=(blk == 1)),
                              reads=[P + "PT", (P + "Va", t), (P + "Va", t + 1)], writes=[bkk[5]], inc=(g == 3 and blk == 1))
                kb.op("dve", lambda e, kv=kv: e.tensor_tensor(
                    out=oa[:, kv * 256:(kv + 1) * 256].rearrange("p (g d) -> p g d", g=4),
                    in0=bk[5][:, 0:256].rearrange("p (g d) -> p g d", g=4),
                    in1=sm[:, 20:24].unsqueeze(2).to_broadcast([128, 4, 64]), op=ALU.mult),
                    reads=[bkk[5], P + "sm"], writes=[P + "oa"])
            kb.dma("sp", self.rec_oa.ap()[t, :, :], oa[:], reads=[P + "oa"], writes=[("rec_oa", t)])
            if "oa" in self.dbg:
                kb.dma("sp", self.out["dbg_oa"][t * 128:(t + 1) * 128, :], oa[:], reads=[P + "oa"], is_output=True)

        qkv("s", rows=NS)
        kb.dma("sp", self.out["wks"][j, :, :], rk[0:NS, :], reads=[P + "rk"], is_output=True)
        kb.dma("sp", self.out["wvs"][j, :, :], vf[0:NS, :], reads=[P + "vf"], is_output=True)
        kb.barrier()
        if "no_sattn" in self.dbg:
            kb.pop_scope(mark)
            return
        m2 = kb.push_scope()
        qh = kb.sbuf(P + "qh", [128, 64])
        sc = kb.sbuf(P + "sc", [128, 129])
        pe_ = kb.sbuf(P + "pe", [128, 129])
        oh = kb.sbuf(P + "oh", [128, 64])
        s2 = kb.sbuf(P + "s2", [128, 16])
        oas = kb.sbuf(P + "oas", [128, 512])
        for h in range(8):
            kv = h // 4
            HS = slice(h * 16, (h + 1) * 16)
            kb.dma("sp", Kc[HS, 128, :], rk[0:NS, kv * 64:(kv + 1) * 64], reads=[P + "rk"], writes=[P + "Kc"])
            kb.dma("sp", Vc[HS, 128, :], vf[0:NS, kv * 64:(kv + 1) * 64], reads=[P + "vf"], writes=[P + "Vc"])
            kb.dma("sp", qh[HS, :], rqs[:, h, :], reads=[P + "rqs"], writes=[P + "qh"])
            kb.dma("sp", s2[HS, 0:1], I["attn_sinks"][j, h:h + 1].partition_broadcast(NS), writes=[P + "s2"])
        kb.op("dve", lambda e: e.tensor_tensor(out=Kc[:], in0=Kc[:], in1=qh[:].unsqueeze(1).to_broadcast([128, 129, 64]), op=ALU.mult),
              reads=[P + "Kc", P + "qh"], writes=[P + "Kc"])
        kb.op("dve", lambda e: e.tensor_reduce(out=sc[:], in_=Kc[:], axis=AX.X, op=ALU.add), reads=[P + "Kc"], writes=[P + "sc"])
        kb.op("dve", lambda e: e.tensor_reduce(out=s2[:, 1:2], in_=sc[:], axis=AX.X, op=ALU.max), reads=[P + "sc"], writes=[P + "s2"])
        kb.op("dve", lambda e: e.tensor_scalar(out=s2[:, 2:3], in0=s2[:, 1:2], scalar1=0.125, scalar2=s2[:, 0:1], op0=ALU.mult, op1=ALU.max),
              reads=[P + "s2"], writes=[P + "s2"])
        kb.op("dve", lambda e: e.tensor_scalar(out=s2[:, 3:4], in0=s2[:, 2:3], scalar1=-1.0, scalar2=None, op0=ALU.mult),
              reads=[P + "s2"], writes=[P + "s2"])
        kb.op("act", lambda e: e.activation(out=pe_[:], in_=sc[:], func=AF.Exp, bias=s2[:, 3:4], scale=0.125, accum_out=s2[:, 4:5]),
              reads=[P + "sc", P + "s2"], writes=[P + "pe", P + "s2"])
        kb.op("dve", lambda e: e.tensor_tensor(out=s2[:, 5:6], in0=s2[:, 0:1], in1=s2[:, 3:4], op=ALU.add), reads=[P + "s2"], writes=[P + "s2"])
        kb.op("act", lambda e: e.activation(out=s2[:, 5:6], in_=s2[:, 5:6], func=AF.Exp), reads=[P + "s2"], writes=[P + "s2"])
        kb.op("dve", lambda e: e.tensor_tensor(out=s2[:, 6:7], in0=s2[:, 4:5], in1=s2[:, 5:6], op=ALU.add), reads=[P + "s2"], writes=[P + "s2"])
        kb.op("dve", lambda e: e.reciprocal(out=s2[:, 7:8], in_=s2[:, 6:7]), reads=[P + "s2"], writes=[P + "s2"])
        kb.op("dve", lambda e: e.tensor_tensor(out=Vc[:], in0=Vc[:], in1=pe_[:].unsqueeze(2).to_broadcast([128, 129, 64]), op=ALU.mult),
              reads=[P + "Vc", P + "pe"], writes=[P + "Vc"])
        kb.op("dve", lambda e: e.tensor_reduce(out=oh[:], in_=Vc[:].rearrange("p j d -> p d j"), axis=AX.X, op=ALU.add),
              reads=[P + "Vc"], writes=[P + "oh"])
        kb.op("dve", lambda e: e.tensor_scalar(out=oh[:], in0=oh[:], scalar1=s2[:, 7:8], scalar2=None, op0=ALU.mult),
              reads=[P + "oh", P + "s2"], writes=[P + "oh"])
        kb.op("pool", lambda e: e.memset(oas[:], 0.0), writes=[P + "oas"])
        for h in range(8):
            kb.dma("sp", oas[0:NS, h * 64:(h + 1) * 64], oh[h * 16:(h + 1) * 16, :], reads=[P + "oh"], writes=[P + "oas"])
        kb.dma("sp", self.rec_oa.ap()[NT, :, :], oas[:], reads=[P + "oas"], writes=[("rec_oa", NT)])
        kb.barrier()
        kb.pop_scope(m2)
        kb.pop_scope(mark)

    def even_rwkv(self, L, P0, gain, rstd):
        kb = self.kb
        I = self.inp
        j = L // 2
        P = P0 + "r"
        HDS = 0.5 * DECAY_SCALE
        mark = kb.push_scope()
        w = self.load_w_rows(P + "w", I["w_in_even"][j], 8, BPROJ, c0=768)
        wkeys = [(P + "w", k) for k in range(8)]
        mub = self.load_bcast(P + "mu", I["shift_mu"][j, :], BPROJ)
        w0b = self.load_bcast(P + "w0b", I["decay_w0"][j, :], 512)
        a0b = self.load_bcast(P + "a0b", I["iclr_a0"][j, :], 512)
        kkb = self.load_bcast(P + "kkb", I["key_k"][j, :], 512)
        kab = self.load_bcast(P + "kab", I["key_a"][j, :], 512)
        rkb = self.load_bcast(P + "rkb", I["bonus_r_k"][j, :], 512)
        w2a2 = kb.sbuf(P + "w2a2", [128, 512], BF16)
        kb.dma("pool", w2a2[0:64, :], I["decay_w2"][j, :, :], writes=[P + "w2a2"])
        kb.dma("pool", w2a2[64:128, :], I["iclr_a2"][j, :, :], writes=[P + "w2a2"])
        g2b = kb.sbuf(P + "g2b", [128, 512], BF16)
        kb.dma("pool", g2b[:], I["gate_g2"][j, :, :], writes=[P + "g2b"])
        bk, bkk = self.bk(P)
        S = kb.sbuf(P + "S", [128, 4, 128])
        Sb = kb.sbuf(P + "Sb", [128, 4, 128], BF16)
        zlast = self.zl_dram[j].ap()
        xn = kb.sbuf(P + "xn", [128, D], BF16)
        xnT = kb.sbuf(P + "xnT", [128, 8, 128], BF16)
        zb = kb.sbuf(P + "zb", [128, BPROJ])
        prev = kb.sbuf(P + "prev", [128, BPROJ])
        sm = kb.sbuf(P + "sm", [128, 40])
        f = {}
        for nme in ("logw", "a", "gt", "kk", "kf", "bon", "T1", "T2", "T3"):
            f[nme] = kb.sbuf(P + nme, [128, 512])
        wab = kb.sbuf(P + "wab", [128, 128], BF16)
        sgb = kb.sbuf(P + "sgb", [128, 128], BF16)
        waT = kb.sbuf(P + "waT", [128, 2, 128], BF16)

        def inproj(t, rows=128):
            if t == "s":
                src, sk, rc = self.hs[0:NS, :], "hs", rstd[0:NS, NT:NT + 1]
            else:
                src, sk, rc = self.hp[:, t, :], ("hp", t), rstd[:, t:t + 1]
            R = slice(0, rows)
            self.norm_transpose(P0, src, sk, rc, gain, xn, P + "xn", bk[2], bkk[2], xnT[:, :, 0:rows], P + "xnT", rows=rows)
            for g in range(4):
                wdt = 512 if g < 3 else 256
                b = g % 2
                for k in range(8):
                    kb.op("pe", lambda e, k=k, g=g, b=b, wdt=wdt: e.matmul(bk[b][R, 0:wdt], xnT[:, k, 0:rows],
                                                                          w[:, k, g * 512:g * 512 + wdt], start=(k == 0), stop=(k == 7)),
                          reads=[P + "xnT"] + wkeys, writes=[bkk[b]], inc=(k == 7))
                eng = "act" if g % 2 == 0 else "dve"
                if eng == "act":
                    kb.op("act", lambda e, g=g, b=b, wdt=wdt: e.copy(out=zb[R, g * 512:g * 512 + wdt], in_=bk[b][R, 0:wdt]),
                          reads=[bkk[b]], writes=[P + "zb"])
                else:
                    kb.op("dve", lambda e, g=g, b=b, wdt=wdt: e.tensor_copy(out=zb[R, g * 512:g * 512 + wdt], in_=bk[b][R, 0:wdt]),
                          reads=[bkk[b]], writes=[P + "zb"])

        def elem(rows):
            R = slice(0, rows)
            kb.op("dve", lambda e: e.tensor_tensor(out=prev[R, :], in0=prev[R, :], in1=zb[R, :], op=ALU.subtract),
                  reads=[P + "prev", P + "zb"], writes=[P + "prev"])
            kb.op("dve", lambda e: e.tensor_tensor(out=prev[R, :], in0=prev[R, :], in1=mub[R, :], op=ALU.mult),
                  reads=[P + "prev", P + "mu"], writes=[P + "prev"])
            kb.op("dve", lambda e: e.tensor_tensor(out=prev[R, :], in0=prev[R, :], in1=zb[R, :], op=ALU.add),
                  reads=[P + "prev", P + "zb"], writes=[P + "prev"])
            zs = prev
            r_, k0, v_ = zs[R, 0:512], zs[R, 512:1024], zs[R, 1024:1536]
            kb.op("act", lambda e: e.activation(out=wab[R, 0:64], in_=zs[R, 1536:1600], func=AF.Tanh), reads=[P + "prev"], writes=[P + "wab"])
            yield
            kb.op("act", lambda e: e.copy(out=wab[R, 64:128], in_=zs[R, 1600:1664]), reads=[P + "prev"], writes=[P + "wab"])
            th = f["T3"]
            kb.op("act", lambda e: e.activation(out=th[R, 0:128], in_=zs[R, 1664:1792], func=AF.Tanh, scale=0.5), reads=[P + "prev"], writes=[P + "T3"])
            kb.op("dve", lambda e: e.tensor_scalar(out=sgb[R, :], in0=th[R, 0:128], scalar1=0.5, scalar2=0.5, op0=ALU.mult, op1=ALU.add),
                  reads=[P + "T3"], writes=[P + "sgb"])
            pv = bk[2][:].bitcast(BF16).rearrange("p (k t) -> p k t", k=8)
            kb.op("pe", lambda e: e.transpose(pv[:, 0, 0:rows], wab[R, :], self.identb[R, R]), reads=[P + "wab", "identb"], writes=[bkk[2]], inc=False)
            kb.op("pe", lambda e: e.transpose(pv[:, 1, 0:rows], sgb[R, :], self.identb[R, R]), reads=[P + "sgb", "identb"], writes=[bkk[2]])
            yield
            kb.op("act", lambda e: e.copy(out=waT[:, :, 0:rows], in_=pv[:, 0:2, 0:rows]), reads=[bkk[2]], writes=[P + "waT"])
            kb.op("pe", lambda e: e.matmul(bk[0][R, :], waT[0:64, 0, 0:rows], w2a2[0:64, :], start=True, stop=True),
                  reads=[P + "waT", P + "w2a2"], writes=[bkk[0]])
            kb.op("pe", lambda e: e.matmul(bk[1][R, :], waT[64:128, 0, 0:rows], w2a2[64:128, :], start=True, stop=True),
                  reads=[P + "waT", P + "w2a2"], writes=[bkk[1]])
            kb.op("pe", lambda e: e.matmul(bk[2][R, :], waT[:, 1, 0:rows], g2b[:, :], start=True, stop=True),
                  reads=[P + "waT", P + "g2b"], writes=[bkk[2]])
            T1, T2, T3 = f["T1"], f["T2"], f["T3"]
            kb.op("dve", lambda e: e.tensor_tensor(out=T1[R, :], in0=bk[0][R, :], in1=w0b[R, :], op=ALU.add), reads=[bkk[0], P + "w0b"], writes=[P + "T1"])
            yield
            kb.op("act", lambda e: e.activation(out=T1[R, :], in_=T1[R, :], func=AF.Tanh, scale=0.5), reads=[P + "T1"], writes=[P + "T1"])
            kb.op("dve", lambda e: e.tensor_scalar(out=f["logw"][R, :], in0=T1[R, :], scalar1=-HDS, scalar2=-HDS, op0=ALU.mult, op1=ALU.add),
                  reads=[P + "T1"], writes=[P + "logw"])
            kb.op("dve", lambda e: e.tensor_tensor(out=T2[R, :], in0=bk[1][R, :], in1=a0b[R, :], op=ALU.add), reads=[bkk[1], P + "a0b"], writes=[P + "T2"])
            kb.op("act", lambda e: e.activation(out=T2[R, :], in_=T2[R, :], func=AF.Tanh, scale=0.5), reads=[P + "T2"], writes=[P + "T2"])
            kb.op("dve", lambda e: e.tensor_scalar(out=f["a"][R, :], in0=T2[R, :], scalar1=0.5, scalar2=0.5, op0=ALU.mult, op1=ALU.add),
                  reads=[P + "T2"], writes=[P + "a"])
            yield
            kb.op("act", lambda e: e.copy(out=f["gt"][R, :], in_=bk[2][R, :]), reads=[bkk[2]], writes=[P + "gt"])
            kk = f["kk"]
            kb.op("dve", lambda e: e.tensor_tensor(out=kk[R, :], in0=k0, in1=kkb[R, :], op=ALU.mult), reads=[P + "prev", P + "kkb"], writes=[P + "kk"])
            kb.op("dve", lambda e: e.tensor_tensor(out=T3[R, :], in0=kk[R, :], in1=kk[R, :], op=ALU.mult), reads=[P + "kk"], writes=[P + "T3"])
            kb.op("dve", lambda e: e.tensor_reduce(out=sm[R, 0:8], in_=T3[R, :].rearrange("p (h d) -> p h d", h=8), axis=AX.X, op=ALU.add),
                  reads=[P + "T3"], writes=[P + "sm"])
            kb.op("act", lambda e: e.activation(out=sm[R, 8:16], in_=sm[R, 0:8], func=AF.Sqrt), reads=[P + "sm"], writes=[P + "sm"])
            yield
            kb.op("dve", lambda e: e.tensor_scalar(out=sm[R, 8:16], in0=sm[R, 8:16], scalar1=1e-12, scalar2=None, op0=ALU.max), reads=[P + "sm"], writes=[P + "sm"])
            kb.op("dve", lambda e: e.reciprocal(out=sm[R, 16:24], in_=sm[R, 8:16]), reads=[P + "sm"], writes=[P + "sm"])
            kb.op("dve", lambda e: e.tensor_tensor(out=kk[R, :].rearrange("p (h d) -> p h d", h=8), in0=kk[R, :].rearrange("p (h d) -> p h d", h=8),
                                                   in1=sm[R, 16:24].unsqueeze(2).to_broadcast([rows, 8, 64]), op=ALU.mult),
                  reads=[P + "kk", P + "sm"], writes=[P + "kk"])
            kb.op("dve", lambda e: e.scalar_tensor_tensor(out=T3[R, :], in0=f["a"][R, :], scalar=-1.0, in1=kab[R, :], op0=ALU.add, op1=ALU.mult),
                  reads=[P + "a", P + "kab"], writes=[P + "T3"])
            kb.op("dve", lambda e: e.scalar_tensor_tensor(out=f["kf"][R, :], in0=T3[R, :], scalar=1.0, in1=k0, op0=ALU.add, op1=ALU.mult),
                  reads=[P + "T3", P + "prev"], writes=[P + "kf"])
            yield
            kb.op("dve", lambda e: e.tensor_tensor(out=T3[R, :], in0=r_, in1=f["kf"][R, :], op=ALU.mult), reads=[P + "prev", P + "kf"], writes=[P + "T3"])
            kb.op("dve", lambda e: e.tensor_tensor(out=T3[R, :], in0=T3[R, :], in1=rkb[R, :], op=ALU.mult), reads=[P + "T3", P + "rkb"], writes=[P + "T3"])
            kb.op("dve", lambda e: e.tensor_reduce(out=sm[R, 24:32], in_=T3[R, :].rearrange("p (h d) -> p h d", h=8), axis=AX.X, op=ALU.add),
                  reads=[P + "T3"], writes=[P + "sm"])
            kb.op("dve", lambda e: e.tensor_tensor(out=f["bon"][R, :].rearrange("p (h d) -> p h d", h=8), in0=v_.rearrange("p (h d) -> p h d", h=8),
                                                   in1=sm[R, 24:32].unsqueeze(2).to_broadcast([rows, 8, 64]), op=ALU.mult),
                  reads=[P + "prev", P + "sm"], writes=[P + "bon"])

        inproj(NT - 1)
        ci, co = self.cc_sh_in[j].ap(), self.cc_sh_out[j].ap()
        kb.dma("sp", ci[0:1, :], zb[127:128, :], reads=[P + "zb"], writes=[P + "ccin"])
        kb.collective(ci, co, [[0, 1, 2, 3], [4, 5, 6, 7]], reads=[P + "ccin"], writes=[P + "ccout"])
        mz = kb.push_scope()
        z4 = kb.sbuf(P + "z4", [1, 4, BPROJ])
        zsel = kb.sbuf(P + "zsel", [1, BPROJ])
        kb.dma("sp", z4[0:1, :, :], co.rearrange("(o r) c -> o r c", o=1), reads=[P + "ccout"], writes=[P + "z4"])
        kb.op("dve", lambda e: e.tensor_scalar(out=zsel[0:1, :], in0=z4[0:1, 0, :], scalar1=self.cst[0:1, C_SELP:C_SELP + 1], scalar2=None, op0=ALU.mult),
              reads=[P + "z4", "cst"], writes=[P + "zsel"])
        for r in range(1, 4):
            kb.op("dve", lambda e, r=r: e.scalar_tensor_tensor(out=zsel[0:1, :], in0=z4[0:1, r, :], scalar=self.cst[0:1, C_SELP + r:C_SELP + r + 1],
                                                               in1=zsel[0:1, :], op0=ALU.mult, op1=ALU.add),
                  reads=[P + "z4", "cst", P + "zsel"], writes=[P + "zsel"])
        kb.dma("sp", zlast[0:1, :], zsel[0:1, :], reads=[P + "zsel"], writes=[P + "zlast"])
        kb.barrier()
        kb.pop_scope(mz)
        kb.op("pool", lambda e: e.memset(S[:], 0.0), writes=[P + "S"])
        for c in range(4):
            kb.op("dve", lambda e, c=c: e.tensor_copy(out=S[0:64, c, 64:128], in_=self.cst[0:64, C_ID:C_ID + 64]), reads=["cst", P + "S"], writes=[P + "S"])
            kb.op("dve", lambda e, c=c: e.tensor_copy(out=S[64:128, c, 64:128], in_=self.cst[64:128, C_ID + 64:C_ID + 128]), reads=["cst", P + "S"], writes=[P + "S"])
        kb.op("act", lambda e: e.copy(out=Sb[:], in_=S[:]), reads=[P + "S"], writes=[P + "Sb"])

        m1 = kb.push_scope()
        Gs, Ea, Eb = f["gt"], f["bon"], f["T3"]
        hbs = [{}, {}]
        Vbs, TTs, gcts = [], [], []
        for i2 in range(2):
            for nme in ("At", "Bt", "Kt", "Rt", "Bh", "Kh"):
                hbs[i2][nme] = kb.sbuf(P + "d%d" % i2 + nme, [128, 512], BF16)
            Vbs.append(kb.sbuf(P + "d%dVb" % i2, [128, 8, 128], BF16))
            kb.op("pool", lambda e, i2=i2: e.memset(Vbs[i2][:], 0.0), writes=[P + "d%dVb" % i2])
            TTs.append(kb.sbuf(P + "d%dTT" % i2, [128, 16, 128], BF16))
            gcts.append(kb.sbuf(P + "d%dgct" % i2, [128, 4]))
        Yb_t = kb.sbuf(P + "Yb", [128, 512], BF16)
        cm = {}
        for nme in ("Pa", "PTa", "Pb", "PTb", "Q", "AKT", "RBT", "RKT", "Ub"):
            cm[nme] = kb.sbuf(P + nme, [128, 8, 128], BF16)
        WT = cm["Pb"][:, 0:4, :]
        OPt = Yb_t[:].rearrange("p (a t) -> p a t", a=4)
        tri = self.cst[:, C_TRI:C_TRI + 128]
        trs = self.cst[:, C_TRS:C_TRS + 128]
        low = self.cst[:, C_LOW:C_LOW + 128]
        ones = self.cst[:, C_ONE:C_ONE + 128]

        def X(t):
            i2 = t % 2
            PD = P + "d%d" % i2
            hb, Vb, TT, gct = hbs[i2], Vbs[i2], TTs[i2], gcts[i2]
            inproj(t)
            kb.dma("sp", prev[1:128, :], zb[0:127, :], reads=[P + "zb"], writes=[P + "prev"])
            kb.dma("sp", prev[0:1, :], zlast[0:1, :], reads=[P + "zlast"], writes=[P + "prev"])
            kb.dma("sp", zlast[0:1, :], zb[127:128, :], reads=[P + "zb"], writes=[P + "zlast"])
            if t == NT - 1:
                kb.dma("sp", self.out["shp"][j:j + 1, :], zb[127:128, :], reads=[P + "zb"], is_output=True)
            yield from elem(128)
            kb.dma("sp", self.rec_g.ap()[t, :, :], f["gt"][:], reads=[P + "gt"], writes=[("rec_g", t)])
            yield
            kb.dma("sp", self.rec_bo.ap()[t, :, :], f["bon"][:], reads=[P + "bon"], writes=[("rec_bo", t)])
            logw, a_, kk, kf = f["logw"], f["a"], f["kk"], f["kf"]
            T1, T2, T3 = f["T1"], f["T2"], f["T3"]
            zs = prev
            kb.op("pe", lambda e: e.matmul(bk[0][:, :], tri, logw[:, :], start=True, stop=True), reads=["cst", P + "logw"], writes=[bkk[0]])
            kb.op("pe", lambda e: e.matmul(bk[1][:, :], ones, logw[:, :], start=True, stop=True), reads=["cst", P + "logw"], writes=[bkk[1]])
            for h in range(8):
                hp_, c = h % 2, h // 2
                kb.op("pe", lambda e, h=h, hp_=hp_, c=c: e.matmul(bk[2][hp_ * 64:(hp_ + 1) * 64, c:c + 1], logw[:, h * 64:(h + 1) * 64],
                                                                ones[:, 0:1], start=True, stop=True),
                      reads=["cst", P + "logw"], writes=[bkk[2]], inc=(h == 7))
            kb.op("act", lambda e: e.copy(out=Gs[:], in_=bk[0][:, :]), reads=[bkk[0]], writes=[P + "gt"])
            kb.op("act", lambda e: e.activation(out=gct[:], in_=bk[2][:, 0:4], func=AF.Exp), reads=[bkk[2]], writes=[PD + "gct"])
            yield
            kb.op("dve", lambda e: e.tensor_tensor(out=T1[:], in0=Gs[:], in1=logw[:], op=ALU.subtract), reads=[P + "gt", P + "logw"], writes=[P + "T1"])
            kb.op("act", lambda e: e.activation(out=Ea[:], in_=T1[:], func=AF.Exp), reads=[P + "T1"], writes=[P + "bon"])
            kb.op("dve", lambda e: e.scalar_tensor_tensor(out=hb["At"][:], in0=kk[:], scalar=-1.0, in1=Ea[:], op0=ALU.mult, op1=ALU.mult),
                  reads=[P + "kk", P + "bon"], writes=[PD + "At"])
            kb.op("act", lambda e: e.activation(out=Eb[:], in_=Gs[:], func=AF.Exp), reads=[P + "gt"], writes=[P + "T3"])
            kb.op("dve", lambda e: e.tensor_tensor(out=hb["Rt"][:], in0=zs[:, 0:512], in1=Eb[:], op=ALU.mult), reads=[P + "prev", P + "T3"], writes=[PD + "Rt"])
            yield
            kb.op("dve", lambda e: e.tensor_tensor(out=T2[:], in0=kk[:], in1=a_[:], op=ALU.mult), reads=[P + "kk", P + "a"], writes=[P + "T2"])
            kb.op("act", lambda e: e.activation(out=Ea[:], in_=Gs[:], func=AF.Exp, scale=-1.0), reads=[P + "gt"], writes=[P + "bon"])
            kb.op("dve", lambda e: e.tensor_tensor(out=hb["Bt"][:], in0=T2[:], in1=Ea[:], op=ALU.mult), reads=[P + "T2", P + "bon"], writes=[PD + "Bt"])
            kb.op("dve", lambda e: e.tensor_tensor(out=hb["Kt"][:], in0=kf[:], in1=Ea[:], op=ALU.mult), reads=[P + "kf", P + "bon"], writes=[PD + "Kt"])
            kb.op("dve", lambda e: e.tensor_tensor(out=T1[:], in0=bk[1][:, :], in1=Gs[:], op=ALU.subtract), reads=[bkk[1], P + "gt"], writes=[P + "T1"])
            yield
            kb.op("act", lambda e: e.activation(out=Eb[:], in_=T1[:], func=AF.Exp), reads=[P + "T1"], writes=[P + "T3"])
            kb.op("dve", lambda e: e.tensor_tensor(out=hb["Bh"][:], in0=T2[:], in1=Eb[:], op=ALU.mult), reads=[P + "T2", P + "T3"], writes=[PD + "Bh"])
            kb.op("dve", lambda e: e.tensor_tensor(out=hb["Kh"][:], in0=kf[:], in1=Eb[:], op=ALU.mult), reads=[P + "kf", P + "T3"], writes=[PD + "Kh"])
            kb.op("act", lambda e: e.copy(out=Vb[:, :, 0:64], in_=zs[:, 1024:1536].rearrange("p (h d) -> p h d", h=8)), reads=[P + "prev"], writes=[PD + "Vb"])
            for qi, nme in enumerate(("At", "Bt", "Kt", "Rt")):
                pv = bk[qi // 2][:].bitcast(BF16).rearrange("p (k t) -> p k t", k=8)
                for c in range(4):
                    kb.op("pe", lambda e, qi=qi, c=c, nme=nme, pv=pv: e.transpose(pv[:, (qi % 2) * 4 + c, :], hb[nme][:, c * 128:(c + 1) * 128], self.identb[:]),
                          reads=[PD + nme, "identb"], writes=[bkk[qi // 2]], inc=(c == 3))
            for half in range(2):
                pv = bk[half][:].bitcast(BF16).rearrange("p (k t) -> p k t", k=8)
                kb.op("act", lambda e, half=half, pv=pv: e.copy(out=TT[:, half * 8:(half + 1) * 8, :], in_=pv[:]), reads=[bkk[half]], writes=[PD + "TT"])

        def Y(t):
            i2 = t % 2
            PD = P + "d%d" % i2
            hb, Vb, TT, gct = dict(hbs[i2]), Vbs[i2], TTs[i2], gcts[i2]
            hb["Yb"] = Yb_t

            def hsl(h):
                return slice((h % 2) * 64, (h % 2) * 64 + 64), h // 2

            def HI(h):
                return (h % 2) * 4 + h // 2

            def mm_pair(dst, l_idx, r_idx, mask, bank0, add_ident=False):
                for hp_ in range(2):
                    b = bank0 + hp_
                    hs_ = slice(hp_ * 64, hp_ * 64 + 64)
                    for c in range(4):
                        kb.op("pe", lambda e, hs_=hs_, c=c, b=b: e.matmul(bk[b][:, c * 128:(c + 1) * 128], TT[hs_, l_idx * 4 + c, :],
                                                                          TT[hs_, r_idx * 4 + c, :], start=True, stop=True),
                              reads=[PD + "TT"], writes=[bkk[b]], inc=(c == 3))
                    kb.op("dve", lambda e, hp_=hp_, b=b: e.tensor_tensor(
                        out=dst[:, hp_ * 4:(hp_ + 1) * 4, :], in0=bk[b][:].rearrange("p (a t) -> p a t", a=4),
                        in1=mask.unsqueeze(1).to_broadcast([128, 4, 128]), op=ALU.mult), reads=[bkk[b], "cst"], writes=[(dst_key[id(dst)], hp_)])

            dst_key = {id(cm[n]): P + n for n in cm}
            A_, B_, K_, R_ = 0, 1, 2, 3
            mm_pair(cm["Pa"], B_, A_, trs, 3)
            yield
            mm_pair(cm["PTa"], A_, B_, low, 5)
            yield
            mm_pair(cm["AKT"], K_, A_, trs, 3)
            yield
            mm_pair(cm["RBT"], B_, R_, tri, 5)
            yield
            mm_pair(cm["RKT"], K_, R_, tri, 3)
            yield
            Q = cm["Q"]

            def K2(n):
                return [(P + n, 0), (P + n, 1)]
            kb.op("dve", lambda e: e.tensor_tensor(out=Q[:], in0=cm["Pa"][:], in1=self.identb[:].unsqueeze(1).to_broadcast([128, 8, 128]), op=ALU.add),
                  reads=K2("Pa") + ["identb"], writes=K2("Q"))
            cur, nxt = ("Pa", "PTa"), ("Pb", "PTb")
            for step in range(6):
                Pc, PTc = cm[cur[0]], cm[cur[1]]
                Pn, PTn = cm[nxt[0]], cm[nxt[1]]
                for half in range(2):
                    b0 = 3 + half * 2
                    if step < 5:
                        for hi in range(4):
                            h = half * 4 + hi
                            kb.op("pe", lambda e, hi=hi, h=h, b0=b0: e.matmul(bk[b0][:, hi * 128:(hi + 1) * 128], PTc[:, h, :], Pc[:, h, :], start=True, stop=True),
                                  reads=[(P + cur[0], half), (P + cur[1], half)], writes=[bkk[b0]], inc=(hi == 3))
                    for hi in range(4):
                        h = half * 4 + hi
                        kb.op("pe", lambda e, hi=hi, h=h, b0=b0: e.matmul(bk[b0 + 1][:, hi * 128:(hi + 1) * 128], Pc[:, h, :], PTc[:, h, :], start=True, stop=True),
                              reads=[(P + cur[0], half), (P + cur[1], half)], writes=[bkk[b0 + 1]], inc=(hi == 3))
                for half in range(2):
                    b0 = 3 + half * 2
                    hs4 = slice(half * 4, half * 4 + 4)
                    if step < 5:
                        kb.op("act", lambda e, hs4=hs4, b0=b0: e.copy(out=Pn[:, hs4, :], in_=bk[b0][:].rearrange("p (a t) -> p a t", a=4)),
                              reads=[bkk[b0]], writes=[(P + nxt[0], half)])
                    kb.op("act", lambda e, hs4=hs4, b0=b0: e.copy(out=PTn[:, hs4, :], in_=bk[b0 + 1][:].rearrange("p (a t) -> p a t", a=4)),
                          reads=[bkk[b0 + 1]], writes=[(P + nxt[1], half)])
                for half in range(2):
                    b0 = 3 + half * 2
                    hs4 = slice(half * 4, half * 4 + 4)
                    for hi in range(4):
                        h = half * 4 + hi
                        kb.op("pe", lambda e, hi=hi, h=h, b0=b0: e.matmul(bk[7][:, hi * 128:(hi + 1) * 128], PTn[:, h, :], Q[:, h, :], start=True, stop=True),
                              reads=[(P + nxt[1], half), (P + "Q", half)], writes=[bkk[7]], inc=(hi == 3))
                    kb.op("dve", lambda e, hs4=hs4, b0=b0: e.tensor_tensor(out=Q[:, hs4, :], in0=Q[:, hs4, :], in1=bk[7][:].rearrange("p (a t) -> p a t", a=4), op=ALU.add),
                          reads=[bkk[7], (P + "Q", half)], writes=[(P + "Q", half)])
                cur, nxt = nxt, cur
                yield
            for h in range(8):
                hs_, c = hsl(h)
                kb.op("pe", lambda e, h=h, hs_=hs_, c=c: e.matmul(bk[3][hs_, c * 128:(c + 1) * 128], hb["At"][:, h * 64:(h + 1) * 64], Q[:, HI(h), :], start=True, stop=True),
                      reads=[PD + "At"] + K2("Q"), writes=[bkk[3]], inc=(h == 7))
            kb.op("act", lambda e: e.copy(out=WT, in_=bk[3][:].rearrange("p (a t) -> p a t", a=4)), reads=[bkk[3]], writes=[(P + "Pb", 0), (P + "Pb", 1)])
            for h in range(8):
                kb.op("pe", lambda e, h=h: e.matmul(bk[4][:, h * 64:(h + 1) * 64], cm["AKT"][:, HI(h), :], Vb[:, h, 0:64], start=True, stop=True),
                      reads=K2("AKT") + [PD + "Vb"], writes=[bkk[4]], inc=(h == 7))
            kb.op("act", lambda e: e.copy(out=hb["Yb"][:], in_=bk[4][:]), reads=[bkk[4]], writes=[P + "Yb"])
            for h in range(8):
                hs_, c = hsl(h)
                b = 5 + h % 2
                o0 = c * 128
                kb.op("pe", lambda e, h=h, b=b, o0=o0: e.matmul(bk[b][:, o0:o0 + 64], Q[:, HI(h), :], hb["Yb"][:, h * 64:(h + 1) * 64], start=True, stop=False),
                      reads=K2("Q") + [P + "Yb"], writes=[bkk[b]], inc=False)
                kb.op("pe", lambda e, hs_=hs_, c=c, b=b, o0=o0: e.matmul(bk[b][:, o0:o0 + 64], WT[hs_, c, :], Sb[hs_, c, 0:64], start=False, stop=True),
                      reads=[(P + "Pb", 0), (P + "Pb", 1), P + "Sb"], writes=[bkk[b]], inc=False)
                kb.op("pe", lambda e, hs_=hs_, c=c, b=b, o0=o0: e.matmul(bk[b][:, o0 + 64:o0 + 128], WT[hs_, c, :], Sb[hs_, c, 64:128], start=True, stop=True),
                      reads=[(P + "Pb", 0), (P + "Pb", 1), P + "Sb"], writes=[bkk[b]], inc=(h >= 6))
            Ub = cm["Ub"]
            for half in range(2):
                kb.op("act", lambda e, half=half: e.copy(out=Ub[:, half * 4:(half + 1) * 4, :], in_=bk[5 + half][:].rearrange("p (a t) -> p a t", a=4)),
                      reads=[bkk[5 + half]], writes=[P + "Ub"])
            for h in range(8):
                hs_, c = hsl(h)
                kb.op("pe", lambda e, h=h, hs_=hs_, c=c: e.matmul(bk[7][:, h * 64:(h + 1) * 64], TT[hs_, R_ * 4 + c, :], Sb[hs_, c, 0:64], start=True, stop=False),
                      reads=[PD + "TT", P + "Sb"], writes=[bkk[7]], inc=False)
                kb.op("pe", lambda e, h=h: e.matmul(bk[7][:, h * 64:(h + 1) * 64], cm["RBT"][:, HI(h), :], Ub[:, HI(h), 0:64], start=False, stop=False),
                      reads=K2("RBT") + [P + "Ub"], writes=[bkk[7]], inc=False)
                kb.op("pe", lambda e, h=h: e.matmul(bk[7][:, h * 64:(h + 1) * 64], cm["RKT"][:, HI(h), :], Vb[:, h, 0:64], start=False, stop=True),
                      reads=K2("RKT") + [PD + "Vb"], writes=[bkk[7]], inc=(h == 7))
            ol = cm["Pa"][:].rearrange("p a b -> p (a b)").bitcast(F32)
            kb.op("act", lambda e: e.copy(out=ol, in_=bk[7][:]), reads=[bkk[7]], writes=K2("Pa"))
            yield
            kb.dma("pool", self.rec_ol.ap()[t, :, :], ol, reads=K2("Pa"), writes=[("rec_ol", t)])
            for h in range(8):
                hs_, c = hsl(h)
                kb.op("pe", lambda e, hs_=hs_, c=c: e.matmul(bk[3][hs_, c * 128:(c + 1) * 128], Sb[hs_, c, 64:128], TT[hs_, R_ * 4 + c, :], start=True, stop=False),
                      reads=[PD + "TT", P + "Sb"], writes=[bkk[3]], inc=False)
                kb.op("pe", lambda e, h=h, hs_=hs_, c=c: e.matmul(bk[3][hs_, c * 128:(c + 1) * 128], Ub[:, HI(h), 64:128], cm["RBT"][:, HI(h), :], start=False, stop=True),
                      reads=K2("RBT") + [P + "Ub"], writes=[bkk[3]], inc=(h == 7))
            kb.op("act", lambda e: e.copy(out=OPt, in_=bk[3][:].rearrange("p (a t) -> p a t", a=4)), reads=[bkk[3]], writes=[P + "Yb"])
            kb.dma("pool", self.rec_op.ap()[t, :, :, :], OPt, reads=[P + "Yb"], writes=[("rec_op", t)])
            for h in range(8):
                hs_, c = hsl(h)
                kb.op("pe", lambda e, h=h, hs_=hs_, c=c: e.matmul(bk[4][hs_, c * 128:(c + 1) * 128], hb["Bh"][:, h * 64:(h + 1) * 64], Ub[:, HI(h), :], start=True, stop=False),
                      reads=[PD + "Bh", P + "Ub"], writes=[bkk[4]], inc=False)
                kb.op("pe", lambda e, h=h, hs_=hs_, c=c: e.matmul(bk[4][hs_, c * 128:c * 128 + 64], hb["Kh"][:, h * 64:(h + 1) * 64], Vb[:, h, 0:64], start=False, stop=True),
                      reads=[PD + "Kh", PD + "Vb"], writes=[bkk[4]], inc=(h == 7))
            kb.op("dve", lambda e: e.tensor_tensor(out=S[:], in0=S[:], in1=gct[:].unsqueeze(2).to_broadcast([128, 4, 128]), op=ALU.mult),
                  reads=[P + "S", PD + "gct"], writes=[P + "S"])
            kb.op("dve", lambda e: e.tensor_tensor(out=S[:], in0=S[:], in1=bk[4][:].rearrange("p (a t) -> p a t", a=4), op=ALU.add),
                  reads=[P + "S", bkk[4]], writes=[P + "S"])
            yield
            kb.op("act", lambda e: e.copy(out=Sb[:], in_=S[:]), reads=[P + "S"], writes=[P + "Sb"])
        for _ in X(0):
            pass
        for t in range(NT):
            gens = [Y(t)] + ([X(t + 1)] if t + 1 < NT else [])
            while gens:
                for g in list(gens):
                    try:
                        next(g)
                    except StopIteration:
                        gens.remove(g)
        kb.barrier()
        kb.pop_scope(m1)
        kb.dma("sp", self.cc_st_in[j].ap()[:, :], S[:].rearrange("p a b -> p (a b)"), reads=[P + "S"], writes=[P0 + "ccst"])
        kb.collective(self.cc_st_in[j].ap(), self.cc_st_out[j].ap(), [[0, 1, 2, 3], [4, 5, 6, 7]], reads=[P0 + "ccst"], writes=[P0 + "ccsto"])

        m2 = kb.push_scope()
        inproj("s", rows=NS)
        kb.dma("sp", self.out["shs"][j, :, :], zb[0:NS, :], reads=[P + "zb"], is_output=True)
        kb.dma("sp", prev[0:NS, :], I["sshift"][j, :, :], writes=[P + "prev"])
        for _ in elem(NS):
            pass
        RS = slice(0, NS)
        zs = prev
        kb.op("act", lambda e: e.activation(out=f["T1"][RS, :], in_=f["logw"][RS, :], func=AF.Exp), reads=[P + "logw"], writes=[P + "T1"])
        srcs = [(zs[RS, 0:512], P + "prev"), (f["T1"][RS, :], P + "T1"), (f["kf"][RS, :], P + "kf"), (zs[RS, 1024:1536], P + "prev"),
                (f["kk"][RS, :], P + "kk"), (f["a"][RS, :], P + "a")]
        vec = kb.sbuf(P + "vec", [128, 6, 64])
        St = kb.sbuf(P + "St", [128, 64, 64])
        Tm = kb.sbuf(P + "Tm", [128, 64, 64])
        gn2 = kb.sbuf(P + "gn2", [128, 2, 64])
        for h in range(8):
            HS = slice(h * 16, (h + 1) * 16)
            for qi, (sap, skey) in enumerate(srcs):
                kb.dma("sp", vec[HS, qi, :], sap[:, h * 64:(h + 1) * 64], reads=[skey], writes=[P + "vec"])
            kb.dma("sp", St[HS, :, :].rearrange("p a b -> p (a b)"), I["swkv"][j, :, h, :], writes=[P + "St"])
            kb.dma("sp", gn2[HS, 0, :], I["gn_gain"][j, h * 64:(h + 1) * 64].partition_broadcast(NS), writes=[P + "gn2"])
            kb.dma("sp", gn2[HS, 1, :], I["gn_bias"][j, h * 64:(h + 1) * 64].partition_broadcast(NS), writes=[P + "gn2"])
        sv = kb.sbuf(P + "sv", [128, 8, 64])
        s3 = kb.sbuf(P + "s3", [128, 8])

        def bi(x):
            return x.unsqueeze(1).to_broadcast([128, 64, 64])

        def bj(x):
            return x.unsqueeze(2).to_broadcast([128, 64, 64])
        r_, w_, k_, v_, kk_, a_ = (vec[:, i, :] for i in range(6))
        kb.op("dve", lambda e: e.tensor_tensor(out=Tm[:], in0=St[:], in1=bi(kk_), op=ALU.mult), reads=[P + "St", P + "vec"], writes=[P + "Tm"])
        kb.op("dve", lambda e: e.tensor_reduce(out=sv[:, 0, :], in_=Tm[:], axis=AX.X, op=ALU.add), reads=[P + "Tm"], writes=[P + "sv"])
        kb.op("dve", lambda e: e.tensor_tensor(out=St[:], in0=St[:], in1=bi(w_), op=ALU.mult), reads=[P + "St", P + "vec"], writes=[P + "St"])
        kb.op("dve", lambda e: e.tensor_tensor(out=sv[:, 1, :], in0=kk_, in1=a_, op=ALU.mult), reads=[P + "vec"], writes=[P + "sv"])
        kb.op("dve", lambda e: e.tensor_tensor(out=Tm[:], in0=bj(sv[:, 0, :]), in1=bi(sv[:, 1, :]), op=ALU.mult), reads=[P + "sv", P + "Tm"], writes=[P + "Tm"])
        kb.op("dve", lambda e: e.tensor_tensor(out=St[:], in0=St[:], in1=Tm[:], op=ALU.subtract), reads=[P + "St", P + "Tm"], writes=[P + "St"])
        kb.op("dve", lambda e: e.tensor_tensor(out=Tm[:], in0=bj(v_), in1=bi(k_), op=ALU.mult), reads=[P + "vec", P + "Tm"], writes=[P + "Tm"])
        kb.op("dve", lambda e: e.tensor_tensor(out=St[:], in0=St[:], in1=Tm[:], op=ALU.add), reads=[P + "St", P + "Tm"], writes=[P + "St"])
        for h in range(8):
            kb.dma("sp", self.out["wkvs"][j, :, h, :], St[h * 16:(h + 1) * 16, :, :].rearrange("p a b -> p (a b)"), reads=[P + "St"], is_output=True)
        kb.op("dve", lambda e: e.tensor_tensor(out=Tm[:], in0=St[:], in1=bi(r_), op=ALU.mult), reads=[P + "St", P + "vec"], writes=[P + "Tm"])
        kb.op("dve", lambda e: e.tensor_reduce(out=sv[:, 2, :], in_=Tm[:], axis=AX.X, op=ALU.add), reads=[P + "Tm"], writes=[P + "sv"])
        o_ = sv[:, 2, :]
        kb.op("dve", lambda e: e.tensor_reduce(out=s3[:, 0:1], in_=o_, axis=AX.X, op=ALU.add), reads=[P + "sv"], writes=[P + "s3"])
        kb.op("dve", lambda e: e.tensor_scalar(out=s3[:, 1:2], in0=s3[:, 0:1], scalar1=-1.0 / 64, scalar2=None, op0=ALU.mult), reads=[P + "s3"], writes=[P + "s3"])
        kb.op("dve", lambda e: e.tensor_scalar(out=sv[:, 3, :], in0=o_, scalar1=s3[:, 1:2], scalar2=None, op0=ALU.add), reads=[P + "sv", P + "s3"], writes=[P + "sv"])
        kb.op("dve", lambda e: e.tensor_tensor(out=sv[:, 4, :], in0=sv[:, 3, :], in1=sv[:, 3, :], op=ALU.mult), reads=[P + "sv"], writes=[P + "sv"])
        kb.op("dve", lambda e: e.tensor_reduce(out=s3[:, 2:3], in_=sv[:, 4, :], axis=AX.X, op=ALU.add), reads=[P + "sv"], writes=[P + "s3"])
        kb.op("act", lambda e: e.activation(out=s3[:, 3:4], in_=s3[:, 2:3], func=AF.Sqrt, bias=self.epsr[:, 2:3], scale=1.0 / 64), reads=[P + "s3", "epsr"], writes=[P + "s3"])
        kb.op("dve", lambda e: e.reciprocal(out=s3[:, 4:5], in_=s3[:, 3:4]), reads=[P + "s3"], writes=[P + "s3"])
        kb.op("dve", lambda e: e.scalar_tensor_tensor(out=sv[:, 5, :], in0=sv[:, 3, :], scalar=s3[:, 4:5], in1=gn2[:, 0, :], op0=ALU.mult, op1=ALU.mult),
              reads=[P + "sv", P + "s3", P + "gn2"], writes=[P + "sv"])
        kb.op("dve", lambda e: e.tensor_tensor(out=sv[:, 5, :], in0=sv[:, 5, :], in1=gn2[:, 1, :], op=ALU.add), reads=[P + "sv", P + "gn2"], writes=[P + "sv"])
        obs = kb.sbuf(P + "obs", [128, 512])
        kb.op("pool", lambda e: e.memset(obs[:], 0.0), writes=[P + "obs"])
        for h in range(8):
            kb.dma("sp", obs[0:NS, h * 64:(h + 1) * 64], sv[h * 16:(h + 1) * 16, 5, :], reads=[P + "sv"], writes=[P + "obs"])
        kb.op("dve", lambda e: e.tensor_tensor(out=obs[RS, :], in0=obs[RS, :], in1=f["bon"][RS, :], op=ALU.add), reads=[P + "obs", P + "bon"], writes=[P + "obs"])
        kb.op("dve", lambda e: e.tensor_tensor(out=obs[RS, :], in0=obs[RS, :], in1=f["gt"][RS, :], op=ALU.mult), reads=[P + "obs", P + "gt"], writes=[P + "obs"])
        kb.dma("sp", self.rec_ol.ap()[NT, :, :], obs[:], reads=[P + "obs"], writes=[("rec_ol", NT)])
        kb.barrier()
        kb.pop_scope(m2)
        kb.pop_scope(mark)

    def even_combine(self, L, P0):
        kb = self.kb
        I = self.inp
        j = L // 2
        P = P0 + "c"
        mark = kb.push_scope()
        gpost = self.load_bcast(P0 + "gpost", I["norm_mix_post"][L, :], D)
        gng = self.load_bcast(P + "gng", I["gn_gain"][j, :], 512)
        gnb = self.load_bcast(P + "gnb", I["gn_bias"][j, :], 512)
        wout = self.load_w_rows(P + "wout", I["w_out_even"][j], 8, D)
        wkeys = [(P + "wout", k) for k in range(8)]
        bk, bkk = self.bk(P)
        Hstb = kb.sbuf(P + "Hstb", [128, 4, 64], BF16)
        m0 = kb.push_scope()
        G4 = kb.sbuf(P + "G4", [128, 4, 4, 128])
        kb.dma("sp", G4[:].rearrange("p r c x -> p r (c x)"), self.cc_st_out[j].ap().rearrange("(r p) x -> p r x", p=128),
               reads=[P0 + "ccsto"], writes=[P + "G4"])
        PhT = kb.sbuf(P + "PhT", [128, 4, 4, 64])
        Hs = kb.sbuf(P + "Hs", [128, 5, 4, 64])
        Hst = kb.sbuf(P + "Hst", [128, 4, 64])
        idf = self.ident_f()
        for r in range(4):
            for h in range(8):
                hp_ = h % 2
                hs_ = slice(hp_ * 64, hp_ * 64 + 64)
                c = h // 2
                b = hp_ * 2 + r // 2
                o0 = ((r % 2) * 4 + c) * 64
                kb.op("pe", lambda e, r=r, hs_=hs_, c=c, b=b, o0=o0: e.matmul(bk[b][hs_, o0:o0 + 64], G4[hs_, r, c, 64:128], idf[hs_, hs_], start=True, stop=True),
                      reads=[P + "G4", "cst"], writes=[bkk[b]])
        for b in range(4):
            hs_ = slice((b // 2) * 64, (b // 2) * 64 + 64)
            r0 = (b % 2) * 2
            kb.op("dve", lambda e, b=b, hs_=hs_, r0=r0: e.tensor_copy(out=PhT[hs_, r0:r0 + 2, :, :].rearrange("p r c x -> p (r c x)"), in_=bk[b][hs_, :]),
                  reads=[bkk[b]], writes=[P + "PhT"])
        kb.op("dve", lambda e: e.tensor_copy(out=Hs[:, 1, :, :], in_=G4[:, 0, :, 0:64]), reads=[P + "G4"], writes=[P + "Hs"])
        for r in range(1, 4):
            for h in range(8):
                hp_ = h % 2
                hs_ = slice(hp_ * 64, hp_ * 64 + 64)
                c = h // 2
                kb.op("pe", lambda e, r=r, hs_=hs_, c=c, hp_=hp_: e.matmul(bk[4 + hp_][hs_, c * 64:(c + 1) * 64], PhT[hs_, r, c, :], Hs[hs_, r, c, :], start=True, stop=True),
                      reads=[P + "PhT", P + "Hs"], writes=[bkk[4 + hp_]])
            for hp_ in range(2):
                hs_ = slice(hp_ * 64, hp_ * 64 + 64)
                kb.op("dve", lambda e, r=r, hs_=hs_, hp_=hp_: e.tensor_tensor(out=Hs[hs_, r + 1, :, :], in0=bk[4 + hp_][hs_, 0:256].rearrange("p (c x) -> p c x", c=4),
                                                                            in1=G4[hs_, r, :, 0:64], op=ALU.add), reads=[bkk[4 + hp_], P + "G4", P + "Hs"], writes=[P + "Hs"])
        sel = self.cst[:, C_SELH:C_SELH + 4]
        kb.op("dve", lambda e: e.tensor_scalar(out=Hst[:], in0=Hs[:, 1, :, :], scalar1=sel[:, 1:2], scalar2=None, op0=ALU.mult), reads=[P + "Hs", "cst"], writes=[P + "Hst"])
        for r in (2, 3):
            kb.op("dve", lambda e, r=r: e.scalar_tensor_tensor(out=Hst[:], in0=Hs[:, r, :, :], scalar=sel[:, r:r + 1], in1=Hst[:], op0=ALU.mult, op1=ALU.add),
                  reads=[P + "Hs", "cst", P + "Hst"], writes=[P + "Hst"])
        kb.op("act", lambda e: e.copy(out=Hstb[:], in_=Hst[:]), reads=[P + "Hst"], writes=[P + "Hstb"])
        o4 = kb.sbuf(P + "o4", [64, 2, 4, 64])
        for h in range(8):
            hp_ = h % 2
            hs_ = slice(hp_ * 64, hp_ * 64 + 64)
            c = h // 2
            kb.op("pe", lambda e, hs_=hs_, c=c, hp_=hp_: e.matmul(bk[6 + hp_][0:64, c * 64:(c + 1) * 64], Hs[hs_, 4, c, :], idf[hs_, hs_], start=True, stop=True),
                  reads=[P + "Hs", "cst"], writes=[bkk[6 + hp_]])
        for hp_ in range(2):
            kb.op("dve", lambda e, hp_=hp_: e.tensor_copy(out=o4[:, hp_, :, :].rearrange("p c x -> p (c x)"), in_=bk[6 + hp_][0:64, 0:256]), reads=[bkk[6 + hp_]], writes=[P + "o4"])
        for h in range(8):
            kb.dma("sp", self.out["wkvp"][j, h, :, :], o4[:, h % 2, h // 2, :], reads=[P + "o4"], is_output=True)
        kb.barrier()
        kb.pop_scope(m0)

        rec = [{n: kb.sbuf(P + n + str(i), [128, 512]) for n in ("ol", "bo", "g", "oa")} for i in range(2)]
        opt = [kb.sbuf(P + "opt%d" % i, [128, 4, 128], BF16) for i in range(2)]
        o = kb.sbuf(P + "o", [128, 512])
        xc = kb.sbuf(P + "xc", [128, 512])
        sq = kb.sbuf(P + "sq", [128, 512])
        ycat = kb.sbuf(P + "ycat", [128, D], BF16)
        oT = kb.sbuf(P + "oT", [128, 8, 128], BF16)
        tmps = [kb.sbuf(P + "tmp%d" % i, [128, D]) for i in range(2)]
        sm = kb.sbuf(P + "sm", [128, 32])
        small = kb.sbuf(P0 + "small", [128, 4])
        units = [(t, 128) for t in range(NT)] + [("s", NS)]
        for i, (t, rows) in enumerate(units):
            b = i % 2
            ti = NT if t == "s" else t
            R = slice(0, rows)
            rk_ = {n: P + n + str(b) for n in ("ol", "bo", "g", "oa")}
            kb.dma("sp", rec[b]["ol"][:], self.rec_ol.ap()[ti, :, :], reads=[("rec_ol", ti)], writes=[rk_["ol"]])
            kb.dma("sp", rec[b]["oa"][:], self.rec_oa.ap()[ti, :, :], reads=[("rec_oa", ti)], writes=[rk_["oa"]])
            if t != "s":
                kb.dma("sp", rec[b]["bo"][:], self.rec_bo.ap()[ti, :, :], reads=[("rec_bo", ti)], writes=[rk_["bo"]])
                kb.dma("sp", rec[b]["g"][:], self.rec_g.ap()[ti, :, :], reads=[("rec_g", ti)], writes=[rk_["g"]])
                kb.dma("sp", opt[b][:], self.rec_op.ap()[ti, :, :, :], reads=[("rec_op", ti)], writes=[P + "opt%d" % b])
                for h in range(8):
                    hp_ = h % 2
                    hs_ = slice(hp_ * 64, hp_ * 64 + 64)
                    c = h // 2
                    kb.op("pe", lambda e, hs_=hs_, c=c, b=b, hp_=hp_: e.matmul(bk[6 + hp_][:, c * 64:(c + 1) * 64], opt[b][hs_, c, :], Hstb[hs_, c, :], start=True, stop=True),
                          reads=[P + "opt%d" % b, P + "Hstb"], writes=[bkk[6 + hp_]])
                for hp_ in range(2):
                    kb.op("dve", lambda e, b=b, hp_=hp_: e.tensor_tensor(
                        out=o[:].rearrange("p (c q d) -> p c q d", c=4, q=2)[:, :, hp_, :], in0=bk[6 + hp_][:, 0:256].rearrange("p (c d) -> p c d", c=4),
                        in1=rec[b]["ol"][:].rearrange("p (c q d) -> p c q d", c=4, q=2)[:, :, hp_, :], op=ALU.add),
                        reads=[bkk[6 + hp_], rk_["ol"]], writes=[P + "o"])
                if "ob" in self.dbg:
                    pass
                o3 = o[:].rearrange("p (h d) -> p h d", h=8)
                x3 = xc[:].rearrange("p (h d) -> p h d", h=8)
                kb.op("dve", lambda e: e.tensor_reduce(out=sm[:, 0:8], in_=o3, axis=AX.X, op=ALU.add), reads=[P + "o"], writes=[P + "sm"])
                kb.op("dve", lambda e: e.tensor_scalar(out=sm[:, 8:16], in0=sm[:, 0:8], scalar1=-1.0 / 64, scalar2=None, op0=ALU.mult), reads=[P + "sm"], writes=[P + "sm"])
                kb.op("dve", lambda e: e.tensor_tensor(out=x3, in0=o3, in1=sm[:, 8:16].unsqueeze(2).to_broadcast([128, 8, 64]), op=ALU.add),
                      reads=[P + "o", P + "sm"], writes=[P + "xc"])
                kb.op("dve", lambda e: e.tensor_tensor(out=sq[:], in0=xc[:], in1=xc[:], op=ALU.mult), reads=[P + "xc"], writes=[P + "sq"])
                kb.op("dve", lambda e: e.tensor_reduce(out=sm[:, 16:24], in_=sq[:].rearrange("p (h d) -> p h d", h=8), axis=AX.X, op=ALU.add),
                      reads=[P + "sq"], writes=[P + "sm"])
                kb.op("act", lambda e: e.activation(out=sm[:, 16:24], in_=sm[:, 16:24], func=AF.Sqrt, bias=self.epsr[:, 2:3], scale=1.0 / 64),
                      reads=[P + "sm", "epsr"], writes=[P + "sm"])
                kb.op("dve", lambda e: e.reciprocal(out=sm[:, 24:32], in_=sm[:, 16:24]), reads=[P + "sm"], writes=[P + "sm"])
                kb.op("dve", lambda e: e.tensor_tensor(out=x3, in0=x3, in1=sm[:, 24:32].unsqueeze(2).to_broadcast([128, 8, 64]), op=ALU.mult),
                      reads=[P + "xc", P + "sm"], writes=[P + "xc"])
                kb.op("dve", lambda e: e.tensor_tensor(out=xc[:], in0=xc[:], in1=gng[:], op=ALU.mult), reads=[P + "xc", P + "gng"], writes=[P + "xc"])
                kb.op("dve", lambda e: e.tensor_tensor(out=xc[:], in0=xc[:], in1=gnb[:], op=ALU.add), reads=[P + "xc", P + "gnb"], writes=[P + "xc"])
                kb.op("dve", lambda e, b=b: e.tensor_tensor(out=xc[:], in0=xc[:], in1=rec[b]["bo"][:], op=ALU.add), reads=[P + "xc", rk_["bo"]], writes=[P + "xc"])
                kb.op("dve", lambda e, b=b: e.tensor_tensor(out=ycat[:, 512:1024], in0=xc[:], in1=rec[b]["g"][:], op=ALU.mult), reads=[P + "xc", rk_["g"]], writes=[P + "ycat"])
                if "ob" in self.dbg:
                    kb.op("dve", lambda e, b=b: e.tensor_tensor(out=sq[:], in0=xc[:], in1=rec[b]["g"][:], op=ALU.mult), reads=[P + "xc", rk_["g"]], writes=[P + "sq"])
                    kb.dma("sp", self.out["dbg_ob"][t * 128:(t + 1) * 128, :], sq[:], reads=[P + "sq"], is_output=True)
            else:
                kb.op("act", lambda e, b=b: e.copy(out=ycat[R, 512:1024], in_=rec[b]["ol"][R, :]), reads=[rk_["ol"]], writes=[P + "ycat"])
            kb.op("act", lambda e, b=b: e.copy(out=ycat[R, 0:512], in_=rec[b]["oa"][R, :]), reads=[rk_["oa"]], writes=[P + "ycat"])
            pv = bk[1][:].bitcast(BF16).rearrange("p (k t) -> p k t", k=8)
            for k in range(8):
                kb.op("pe", lambda e, k=k: e.transpose(pv[:, k, 0:rows], ycat[R, k * 128:(k + 1) * 128], self.identb[R, R]),
                      reads=[P + "ycat", "identb"], writes=[bkk[1]], inc=(k == 7))
            kb.op("act", lambda e: e.copy(out=oT[:, :, 0:rows], in_=pv[:, :, 0:rows]), reads=[bkk[1]], writes=[P + "oT"])
            pyb = (bk[2], bk[3]) if b == 0 else (bk[4], bk[5])
            pyk = (bkk[2], bkk[3]) if b == 0 else (bkk[4], bkk[5])
            for half in range(2):
                for k in range(8):
                    kb.op("pe", lambda e, k=k, half=half: e.matmul(pyb[half][R, :], oT[:, k, 0:rows], wout[:, k, half * 512:(half + 1) * 512],
                                                                   start=(k == 0), stop=(k == 7)),
                          reads=[P + "oT"] + wkeys, writes=[pyk[half]], inc=(k == 7))
            if t == "s":
                dst, dk = self.hs[0:NS, :], "hs"
            else:
                dst, dk = self.hp[:, t, :], ("hp", t)
            self.post_norm_residual2(P0, pyb, pyk, gpost, dst, dk, tmps[b], P + "tmp%d" % b, small, rows)
        kb.barrier()
        kb.pop_scope(mark)

    def post_norm_residual2(self, pfx, pyb, pyk, gain, dst_ap, dst_key, tmp, tmp_key, small, rows=128):
        kb = self.kb
        sk = pfx + "small"
        R = slice(0, rows)
        for half in range(2):
            kb.op("act", lambda e, half=half: e.activation(out=tmp[R, half * 512:(half + 1) * 512], in_=pyb[half][R, :], func=AF.Square,
                                                           accum_out=small[R, half:half + 1]),
                  reads=[pyk[half]], writes=[tmp_key, sk])
        kb.op("dve", lambda e: e.tensor_tensor(out=small[R, 0:1], in0=small[R, 0:1], in1=small[R, 1:2], op=ALU.add), reads=[sk], writes=[sk])
        kb.op("act", lambda e: e.activation(out=small[R, 1:2], in_=small[R, 0:1], func=AF.Sqrt, bias=self.epsr[R, 0:1], scale=1.0 / D),
              reads=[sk, "epsr"], writes=[sk])
        kb.op("dve", lambda e: e.reciprocal(out=small[R, 2:3], in_=small[R, 1:2]), reads=[sk], writes=[sk])
        for half in range(2):
            kb.op("dve", lambda e, half=half: e.scalar_tensor_tensor(out=tmp[R, half * 512:(half + 1) * 512], in0=pyb[half][R, :],
                                                                     scalar=small[R, 2:3], in1=gain[R, half * 512:(half + 1) * 512],
                                                                     op0=ALU.mult, op1=ALU.mult),
                  reads=[pyk[half], sk, pfx + "gpost"], writes=[tmp_key])
        kb.op("pool", lambda e: e.tensor_tensor(out=dst_ap, in0=dst_ap, in1=tmp[R, :], op=ALU.add), reads=[tmp_key, dst_key], writes=[dst_key])

    def write_y(self):
        kb = self.kb
        for t in range(NT):
            kb.dma("sp", self.out["yp"][t * 128:(t + 1) * 128, :], self.hp[:, t, :], reads=[("hp", t)],
                   is_output=True)
        kb.dma("sp", self.out["ys"][:, :], self.hs[0:NS, :], reads=["hs"], is_output=True)

    def build(self):
        self.declare()
        self.setup()
        for kind, L in self.plan:
            if kind == "ffn":
                self.ffn(L)
            elif kind == "odd":
                self.odd(L)
            elif kind == "even":
                self.even(L)
        self.write_y()
        self.kb.finish()
        self.kb.close()
        return self.nc


def make_consts(c):
    q = c % 4
    cst = np.zeros((128, NCONST), np.float32)
    cst[:, C_ID:C_ID + 128] = np.eye(128, dtype=np.float32)
    s = np.arange(128)[:, None]
    t = np.arange(128)[None, :]
    cst[:, C_TRI:C_TRI + 128] = (s <= t)
    cst[:, C_TRS:C_TRS + 128] = (s < t)
    cst[:, C_LOW:C_LOW + 128] = (s > t)
    cst[:, C_ONE:C_ONE + 128] = 1.0
    qi = np.arange(128)[:, None]
    kj = np.arange(256)[None, :]
    vis = (kj >= qi) & (kj <= qi + 128)
    cst[:, C_MA:C_MA + 256] = np.where(vis, 0.0, NEG)
    vis0 = vis & ((q > 0) | (kj >= 128))
    cst[:, C_M0:C_M0 + 256] = np.where(vis0, 0.0, NEG)
    inv = np.power(np.float32(10000.0), -np.arange(32, dtype=np.float32) / np.float32(32)).astype(np.float32)
    pos = (q * 2048 + np.arange(2048)).astype(np.float32)
    ang = (pos[:, None] * inv[None, :]).astype(np.float32)
    cst[:, C_COS:C_COS + 512] = np.cos(ang).astype(np.float32).reshape(16, 128, 32).transpose(1, 0, 2).reshape(128, 512)
    cst[:, C_SIN:C_SIN + 512] = np.sin(ang).astype(np.float32).reshape(16, 128, 32).transpose(1, 0, 2).reshape(128, 512)
    angs = (np.float32(8192.0) * inv).astype(np.float32)
    cst[:, C_COSS:C_COSS + 32] = np.cos(angs)[None, :]
    cst[:, C_SINS:C_SINS + 32] = np.sin(angs)[None, :]
    if q > 0:
        cst[:, C_SELP + q - 1] = 1.0
    cst[:, C_SELH + q] = 1.0
    return cst


_WEIGHT_KEYS = ["norm_mix_pre", "norm_mix_post", "norm_ffn_pre", "norm_ffn_post", "w_in_even", "attn_sinks", "shift_mu",
                "decay_w0", "decay_w2", "iclr_a0", "iclr_a2", "gate_g2", "key_k", "key_a", "gn_gain", "gn_bias",
                "w_out_even", "w_in_odd", "sgu_ln_gain", "sgu_ln_bias", "sgu_w", "sgu_b", "w_out_odd", "ffn_w_gate",
                "ffn_w_up", "ffn_conv_w", "ffn_conv_b", "ffn_w_down"]

FULL_PLAN = [("even", 0), ("ffn", 0), ("odd", 1), ("ffn", 1), ("even", 2), ("ffn", 2), ("odd", 3), ("ffn", 3)]


def _core_inputs(c, inp, shared):
    b, q = c // 4, c % 4
    f = lambda a: np.ascontiguousarray(a, dtype=np.float32)
    d = dict(shared)
    d["xp"] = f(inp["x_prompt"][b, q * 2048:(q + 1) * 2048])
    d["xs"] = f(inp["x_sample"][c * NS:(c + 1) * NS, 0])
    d["ck"] = f(inp["cache_win_k"][:, c * NS:(c + 1) * NS].reshape(2, NS, 128, 128))
    d["cv"] = f(inp["cache_win_v"][:, c * NS:(c + 1) * NS].reshape(2, NS, 128, 128))
    d["swkv"] = f(inp["state_wkv"][:, c * NS:(c + 1) * NS].reshape(2, NS, 8, 4096))
    d["sshift"] = f(inp["state_shift"][:, c * NS:(c + 1) * NS])
    d["sconv"] = f(inp["state_ffn_conv"][:, c * NS:(c + 1) * NS])
    d["consts"] = make_consts(c)
    return d


def kernel(**inputs):
    inp = {k: np.asarray(v) for k, v in inputs.items()}
    shared = {k: np.ascontiguousarray(inp[k], dtype=np.float32) for k in _WEIGHT_KEYS}
    shared["bonus_r_k"] = np.ascontiguousarray(inp["bonus_r_k"], dtype=np.float32).reshape(2, 512)
    prog = Prog(FULL_PLAN)
    nc = prog.build()
    in_maps = [_core_inputs(c, inp, shared) for c in range(8)]
    res = run_bass_kernel_spmd(nc, in_maps, core_ids=list(range(8))).results
    f32 = np.float32
    y_prompt = np.zeros((2, 8192, D), f32)
    y_sample = np.zeros((128, 1, D), f32)
    wkp = np.zeros((2, 2, 128, 2, 64), f32)
    wvp = np.zeros((2, 2, 128, 2, 64), f32)
    wks = np.zeros((2, 128, 1, 2, 64), f32)
    wvs = np.zeros((2, 128, 1, 2, 64), f32)
    wkvp = np.zeros((2, 2, 8, 64, 64), f32)
    wkvs = np.zeros((2, 128, 8, 64, 64), f32)
    shp = np.zeros((2, 2, BPROJ), f32)
    shs = np.zeros((2, 128, BPROJ), f32)
    sguv = np.zeros((2, 128, 1, D), f32)
    convp = np.zeros((4, 2, 2, DFF), f32)
    convs = np.zeros((4, 128, 2, DFF), f32)
    for c in range(8):
        b, q = c // 4, c % 4
        r = res[c]
        sl = slice(c * NS, (c + 1) * NS)
        y_prompt[b, q * 2048:(q + 1) * 2048] = r["yp"]
        y_sample[sl, 0] = r["ys"]
        wks[:, sl, 0] = r["wks"].reshape(2, NS, 2, 64)
        wvs[:, sl, 0] = r["wvs"].reshape(2, NS, 2, 64)
        wkvs[:, sl] = r["wkvs"].reshape(2, NS, 8, 64, 64)
        shs[:, sl] = r["shs"]
        sguv[:, sl, 0] = r["sguv"]
        convs[:, sl] = r["convs"]
        if q == 3:
            wkp[:, b] = r["wkp"].reshape(2, 128, 2, 64)
            wvp[:, b] = r["wvp"].reshape(2, 128, 2, 64)
            wkvp[:, b] = r["wkvp"]
            shp[:, b] = r["shp"]
            convp[:, b] = r["convp"]
    return (y_prompt, y_sample, wkp, wvp, wks, wvs, wkvp, wkvs, shp, shs, sguv, convp, convs)
```
